# Optimizing a Trainium2 kernel written in Bass

```python
import math
import jax
import jax.numpy as jnp
from jax import lax
import numpy as np

D_MODEL = 1024
BATCH = 2
SEQ = 8192
DEPTH = 4

GRID_W = 64
CTX_LEN = 256
HEAD_DIM = 64
NA_HEADS = 6
NA_WIN_R = 8
NA_WIN_C = 16
SG_GROUPS = 4
SG_CHUNK = 128
GQA_Q_HEADS = 6
GQA_KV_HEADS = 2
Q_BLOCK = 128
ROPE_THETA = 10000.0

D_A = NA_HEADS * HEAD_DIM
D_B = SG_GROUPS * HEAD_DIM
D_C = GQA_Q_HEADS * HEAD_DIM
D_MIX = D_A + D_B + D_C
D_KV_C = GQA_KV_HEADS * HEAD_DIM
IN_SIZES = (D_A, D_A, D_A, 2 * D_B, D_C, D_KV_C, D_KV_C)
D_IN = 3 * D_A + 2 * D_B + D_C + 2 * D_KV_C
D_FF = int(math.ceil(8 * D_MODEL / 3 / 256)) * 256
N_MOD = 6
ALPHA = (2 * DEPTH) ** 0.25
BETA = (8 * DEPTH) ** -0.25
LN_EPS = 1e-6

kernel_name = "hybrid_na_sgmlp_gqa_deepnorm_prefix"


def layer_norm(x):
    xf = x.astype(jnp.float32)
    mu = jnp.mean(xf, -1, keepdims=True)
    var = jnp.mean(jnp.square(xf - mu), -1, keepdims=True)
    return ((xf - mu) * lax.rsqrt(var + LN_EPS)).astype(x.dtype)


def rms_norm(x, g):
    xf = x.astype(jnp.float32)
    y = xf * lax.rsqrt(jnp.mean(xf * xf, -1, keepdims=True) + LN_EPS)
    return y.astype(x.dtype) * g


def modulate(x, shift, scale):
    return layer_norm(x) * (1 + scale) + shift


def split_cols(p, sizes):
    out, off = [], 0
    for s in sizes:
        out.append(p[..., off:off + s])
        off += s
    return out


def heads(t, n):
    return t.reshape(t.shape[0], t.shape[1], n, HEAD_DIM)


def axial_rope_tables(n_tokens, dtype):
    t = jnp.arange(n_tokens, dtype=jnp.int32)
    row = (t // GRID_W).astype(jnp.float32)
    col = (t % GRID_W).astype(jnp.float32)
    n_freq = HEAD_DIM // 4
    inv = 1.0 / (ROPE_THETA ** (jnp.arange(n_freq, dtype=jnp.float32) / n_freq))
    ang = jnp.stack([row[:, None] * inv, col[:, None] * inv], axis=1)
    return jnp.cos(ang).astype(dtype), jnp.sin(ang).astype(dtype)


def apply_axial_rope(x, cos, sin):
    n_freq = HEAD_DIM // 4
    xr = x.reshape(*x.shape[:-1], 2, 2, n_freq)
    x1, x2 = xr[..., 0, :], xr[..., 1, :]
    c, s = cos[:, None], sin[:, None]
    out = jnp.stack([x1 * c - x2 * s, x2 * c + x1 * s], axis=-2)
    return out.reshape(x.shape)


def attend(q, k, v):
    scale = q.shape[-1] ** -0.5
    s = jnp.einsum('bqkgd,bskd->bkgqs', q, k) * scale
    p = jax.nn.softmax(s.astype(jnp.float32), axis=-1).astype(v.dtype)
    return jnp.einsum('bkgqs,bskd->bqkgd', p, v)


def neighbourhood_attention(q, k, v, k_ctx, v_ctx, rpb):
    B, S, H, Dh = q.shape
    rows = S // GRID_W
    kr = min(NA_WIN_R, rows)
    kcw = NA_WIN_C
    scale = Dh ** -0.5
    qg = q.reshape(B, rows, GRID_W, H, Dh)
    kg = k.reshape(B, rows, GRID_W, H, Dh)
    vg = v.reshape(B, rows, GRID_W, H, Dh)
    r_idx = jnp.arange(rows, dtype=jnp.int32)
    row_start = jnp.clip(r_idx - kr // 2, 0, rows - kr)
    c_idx = jnp.arange(GRID_W, dtype=jnp.int32)
    col_start = jnp.clip(c_idx - kcw // 2, 0, GRID_W - kcw)
    col_win = col_start[:, None] + jnp.arange(kcw, dtype=jnp.int32)[None, :]
    dc = col_win - c_idx[:, None] + (NA_WIN_C - 1)
    n_loc = kr * kcw

    def one_row(args):
        q_row, rs, r = args
        k_rows = lax.dynamic_slice_in_dim(kg, rs, kr, axis=1)
        v_rows = lax.dynamic_slice_in_dim(vg, rs, kr, axis=1)
        k_win = k_rows[:, :, col_win]
        v_win = v_rows[:, :, col_win]
        s_loc = jnp.einsum('bqhd,brqchd->bhqrc', q_row, k_win) * scale
        dr = rs + jnp.arange(kr, dtype=jnp.int32) - r + (NA_WIN_R - 1)
        bias = rpb[:, dr][:, :, dc]
        s_loc = s_loc + jnp.transpose(bias, (0, 2, 1, 3))[None]
        s_ctx = jnp.einsum('bqhd,blhd->bhql', q_row, k_ctx) * scale
        s = jnp.concatenate([s_loc.reshape(B, H, GRID_W, n_loc), s_ctx], axis=-1)
        p = jax.nn.softmax(s.astype(jnp.float32), axis=-1).astype(v.dtype)
        p_loc = p[..., :n_loc].reshape(B, H, GRID_W, kr, kcw)
        p_ctx = p[..., n_loc:]
        return (jnp.einsum('bhqrc,brqchd->bqhd', p_loc, v_win)
                + jnp.einsum('bhql,blhd->bqhd', p_ctx, v_ctx))

    out = lax.map(one_row, (jnp.transpose(qg, (1, 0, 2, 3, 4)), row_start, r_idx))
    return jnp.transpose(out, (1, 0, 2, 3, 4)).reshape(B, S, H, Dh)


def gqa_latent(q, k, v, k_ctx, v_ctx):
    B, S, Hq, Dh = q.shape
    G = Hq // GQA_KV_HEADS
    nb = S // Q_BLOCK
    qb = jnp.transpose(q.reshape(B, nb, Q_BLOCK, GQA_KV_HEADS, G, Dh), (1, 0, 2, 3, 4, 5))
    k_all = jnp.concatenate([k, k_ctx], axis=1)
    v_all = jnp.concatenate([v, v_ctx], axis=1)
    out = lax.map(lambda qblk: attend(qblk, k_all, v_all), qb)
    return jnp.transpose(out, (1, 0, 2, 3, 4, 5)).reshape(B, S, Hq, Dh)


def spatial_gating(z, w_s, b_s, g_sgu):
    B, N, _ = z.shape
    u, v = z[..., :D_B], z[..., D_B:]
    v = layer_norm(v) * g_sgu
    vc = v.reshape(B, N // SG_CHUNK, SG_CHUNK, SG_GROUPS, HEAD_DIM)
    mixed = jnp.einsum('gpq,bnqgc->bnpgc', w_s, vc) + jnp.transpose(b_s)[None, None, :, :, None]
    return u * mixed.reshape(B, N, D_B)


def merge_groups(o_a, o_b, o_c, g_out):
    B, N = o_b.shape[0], o_b.shape[1]
    return jnp.concatenate([
        rms_norm(o_a.reshape(B, N, D_A), g_out[:D_A]),
        rms_norm(o_b, g_out[D_A:D_A + D_B]),
        rms_norm(o_c.reshape(B, N, D_C), g_out[D_A + D_B:]),
    ], axis=-1)


def mixing_sublayer(h, hc, w_in, rpb, w_s, b_s, g_sgu, g_q, g_k, g_out, w_o, need_ctx_out):
    B, S, _ = h.shape
    qa, ka, va, zb, qc, kc, vc = split_cols(h @ w_in, IN_SIZES)
    qa_x, ka_x, va_x, zb_x, qc_x, kc_x, vc_x = split_cols(hc @ w_in, IN_SIZES)
    cos, sin = axial_rope_tables(S, h.dtype)
    ka_x, va_x = heads(ka_x, NA_HEADS), heads(va_x, NA_HEADS)
    kc_x = rms_norm(heads(kc_x, GQA_KV_HEADS), g_k)
    vc_x = heads(vc_x, GQA_KV_HEADS)
    o_a = neighbourhood_attention(heads(qa, NA_HEADS), heads(ka, NA_HEADS), heads(va, NA_HEADS),
                                  ka_x, va_x, rpb)
    o_b = spatial_gating(jax.nn.gelu(zb), w_s, b_s, g_sgu)
    q_c = apply_axial_rope(rms_norm(heads(qc, GQA_Q_HEADS), g_q), cos, sin)
    k_c = apply_axial_rope(rms_norm(heads(kc, GQA_KV_HEADS), g_k), cos, sin)
    o_c = gqa_latent(q_c, k_c, heads(vc, GQA_KV_HEADS), kc_x, vc_x)
    y = merge_groups(o_a, o_b, o_c, g_out) @ w_o
    if not need_ctx_out:
        return y, None
    L = hc.shape[1]
    o_ax = attend(heads(qa_x, NA_HEADS)[:, :, :, None], ka_x, va_x)
    o_bx = spatial_gating(jax.nn.gelu(zb_x), w_s, b_s, g_sgu)
    q_cx = rms_norm(heads(qc_x, GQA_Q_HEADS), g_q).reshape(
        B, L, GQA_KV_HEADS, GQA_Q_HEADS // GQA_KV_HEADS, HEAD_DIM)
    o_cx = attend(q_cx, kc_x, vc_x)
    yc = merge_groups(o_ax, o_bx, o_cx, g_out) @ w_o
    return y, yc


def swiglu(h, w_in, w_out):
    a, b = h @ w_in[:, :D_FF], h @ w_in[:, D_FF:]
    return (jax.nn.silu(a) * b) @ w_out


def setup_inputs(seed: int = 0) -> dict:
    key = jax.random.key(seed)
    ks = jax.random.split(key, 24)
    n = jax.random.normal
    f = jnp.float32
    return {
        "x": n(ks[0], (BATCH, SEQ, D_MODEL), f),
        "c": n(ks[1], (BATCH, D_MODEL), f),
        "ctx": n(ks[2], (BATCH, CTX_LEN, D_MODEL), f),
        "c_ctx": n(ks[3], (D_MODEL,), f),
        "w_mod": n(ks[4], (DEPTH, D_MODEL, N_MOD * D_MODEL), f) * (0.5 * D_MODEL ** -0.5),
        "b_mod": n(ks[5], (DEPTH, N_MOD * D_MODEL), f) * 0.02,
        "w_in": n(ks[6], (DEPTH, D_MODEL, D_IN), f) * D_MODEL ** -0.5,
        "rpb": n(ks[7], (DEPTH, NA_HEADS, 2 * NA_WIN_R - 1, 2 * NA_WIN_C - 1), f) * 0.02,
        "w_s": n(ks[8], (DEPTH, SG_GROUPS, SG_CHUNK, SG_CHUNK), f) * SG_CHUNK ** -0.5,
        "b_s": n(ks[9], (DEPTH, SG_GROUPS, SG_CHUNK), f) * 0.02,
        "g_sgu": 1.0 + 0.02 * n(ks[10], (DEPTH, D_B), f),
        "g_q": 1.0 + 0.02 * n(ks[11], (DEPTH, HEAD_DIM), f),
        "g_k": 1.0 + 0.02 * n(ks[12], (DEPTH, HEAD_DIM), f),
        "g_out": 1.0 + 0.02 * n(ks[13], (DEPTH, D_MIX), f),
        "w_o": n(ks[14], (DEPTH, D_MIX, D_MODEL), f) * (D_MIX ** -0.5 * BETA),
        "ln1_g": 1.0 + 0.02 * n(ks[15], (DEPTH, D_MODEL), f),
        "ln1_b": 0.02 * n(ks[16], (DEPTH, D_MODEL), f),
        "w_ffn_in": n(ks[17], (DEPTH, D_MODEL, 2 * D_FF), f) * D_MODEL ** -0.5,
        "w_ffn_out": n(ks[18], (DEPTH, D_FF, D_MODEL), f) * (D_FF ** -0.5 * BETA),
        "ln2_g": 1.0 + 0.02 * n(ks[19], (DEPTH, D_MODEL), f),
        "ln2_b": 0.02 * n(ks[20], (DEPTH, D_MODEL), f),
    }


def reference(x, c, ctx, c_ctx, w_mod, b_mod, w_in, rpb, w_s, b_s, g_sgu, g_q, g_k, g_out, w_o,
              ln1_g, ln1_b, w_ffn_in, w_ffn_out, ln2_g, ln2_b):
    sc = jax.nn.silu(c)
    sc_ctx = jax.nn.silu(c_ctx)
    for l in range(DEPTH):
        need_ctx_out = l < DEPTH - 1
        mod = split_cols((sc @ w_mod[l] + b_mod[l])[:, None, :], (D_MODEL,) * N_MOD)
        mod_x = split_cols(sc_ctx @ w_mod[l] + b_mod[l], (D_MODEL,) * N_MOD)
        sh1, sc1, g1, sh2, sc2, g2 = mod
        sh1x, sc1x, g1x, sh2x, sc2x, g2x = mod_x
        h = modulate(x, sh1, sc1)
        hc = modulate(ctx, sh1x, sc1x)
        y, yc = mixing_sublayer(h, hc, w_in[l], rpb[l], w_s[l], b_s[l], g_sgu[l], g_q[l], g_k[l],
                                g_out[l], w_o[l], need_ctx_out)
        x = layer_norm(ALPHA * x + g1 * y) * ln1_g[l] + ln1_b[l]
        x = layer_norm(ALPHA * x + g2 * swiglu(modulate(x, sh2, sc2), w_ffn_in[l], w_ffn_out[l])) \
            * ln2_g[l] + ln2_b[l]
        if need_ctx_out:
            ctx = layer_norm(ALPHA * ctx + g1x * yc) * ln1_g[l] + ln1_b[l]
            ctx = layer_norm(ALPHA * ctx + g2x * swiglu(modulate(ctx, sh2x, sc2x), w_ffn_in[l],
                                                         w_ffn_out[l])) * ln2_g[l] + ln2_b[l]
    return x
```

```python
import numpy as np
import concourse.bass as bass
import concourse.mybir as mybir
from concourse.bass_utils import run_bass_kernel_spmd

F32 = mybir.dt.float32
BF16 = mybir.dt.bfloat16
AF = mybir.ActivationFunctionType
ALU = mybir.AluOpType
AX = mybir.AxisListType

L = 4
NT = 18
ALPHA = float(8.0 ** 0.25)
EPS = 1e-6
NLAYERS_BUILD = L


class T:
    __slots__ = ("name", "w", "r")

    def __init__(self, name):
        self.name = name
        self.w = {}
        self.r = {}


ENGS = ["pe", "dve", "act", "pool", "sp"]
PIDV = [None]


class Plan:
    def __init__(self):
        self.ops = {e: [] for e in ENGS}
        self.known = {e: {} for e in ENGS}
        self.dma_cnt = {}
        self.waited = {e: set() for e in ENGS}

    def _res(self, hs):
        out = []
        for t in hs:
            r = t.resolve() if hasattr(t, "resolve") else t
            if isinstance(r, (list, tuple)):
                out.extend(r)
            else:
                out.append(r)
        return out

    def record_begin(self):
        self._rec = []

    def record_end(self):
        r, self._rec = self._rec, None
        return r

    def replay_interleaved(self, lists):
        its = [list(l) for l in lists]
        pos = [0] * len(its)
        while any(pos[j] < len(its[j]) for j in range(len(its))):
            for j in range(len(its)):
                if pos[j] < len(its[j]):
                    self.op(*its[j][pos[j]])
                    pos[j] += 1

    def op(self, eng, fn, reads=(), writes=(), pwrites=(), dma=None):
        reads, writes, pwrites = self._res(reads), self._res(writes), self._res(pwrites)
        if getattr(self, "_rec", None) is not None:
            self._rec.append((eng, fn, reads, writes, pwrites, dma))
            return
        idx = len(self.ops[eng])
        if dma is None:
            tok = (eng, idx + 1)
        else:
            self.dma_cnt[dma] = self.dma_cnt.get(dma, 0) + 1
            tok = ("dma:" + dma, self.dma_cnt[dma])
        need = {}

        def addw(d):
            for k, v in d.items():
                if need.get(k, 0) < v:
                    need[k] = v

        for t in reads:
            addw(t.w)
        for t in writes:
            addw(t.w)
            addw(t.r)
        for t in pwrites:
            addw(t.r)
        waits = []
        kn = self.known[eng]
        for k, v in need.items():
            if kn.get(k, 0) < v:
                kn[k] = v
                waits.append((k, v))
                if not k.startswith("dma:"):
                    self.waited[k].add(v)
        if fn is not None:
            for t in reads:
                if t.r.get(tok[0], 0) < tok[1]:
                    t.r[tok[0]] = tok[1]
            for t in writes:
                t.w = {tok[0]: tok[1]}
                t.r = {}
            for t in pwrites:
                if t.r:
                    t.w = {tok[0]: tok[1]}
                    t.r = {}
                else:
                    t.w[tok[0]] = max(t.w.get(tok[0], 0), tok[1])
        self.ops[eng].append((waits, fn, dma))

    def barrier(self):
        latest = {}
        for e in ENGS:
            n = len(self.ops[e])
            if n:
                latest[e] = n
        for k, v in self.dma_cnt.items():
            latest["dma:" + k] = v
        for e in ENGS:
            waits = []
            kn = self.known[e]
            for k, v in latest.items():
                if k == e:
                    continue
                if not k.startswith("dma:"):
                    vv = v
                    while vv > 0 and (self.ops[k][vv - 1][1] is None or self.ops[k][vv - 1][2] is not None):
                        vv -= 1
                    if vv == 0:
                        continue
                    v = vv
                if kn.get(k, 0) < v:
                    kn[k] = v
                    waits.append((k, v))
                    if not k.startswith("dma:"):
                        self.waited[k].add(v)
            self.ops[e].append((waits, None, None))

    def emit(self, nc, block, sems):
        rank = {}
        for e in ENGS:
            s = sorted(self.waited[e])
            rank[e] = {v: i + 1 for i, v in enumerate(s)}
        plan = self

        def run(eng_name):
            def body(e):
                if eng_name == "pool":
                    PIDV[0] = e.partition_id()
                for i, (waits, fn, dma) in enumerate(plan.ops[eng_name]):
                    for k, v in waits:
                        if k.startswith("dma:"):
                            if k.startswith("dma:cc"):
                                e.wait_ge(sems[k], 1)
                            else:
                                e.wait_ge(sems[k], 16 * v)
                        else:
                            e.wait_ge(sems[k], rank[k][v])
                    if fn is None:
                        continue
                    ins = fn(e)
                    if dma is not None:
                        if dma.startswith("cc"):
                            ins.then_inc(sems["dma:" + dma])
                        else:
                            ins.then_inc(sems["dma:" + dma], 16)
                    elif (i + 1) in rank[eng_name]:
                        ins.then_inc(sems[eng_name], 1)
            return body

        block.tensor(run("pe"))
        block.vector(run("dve"))
        block.scalar(run("act"))
        block.gpsimd(run("pool"))
        block.sync(run("sp"))


def MM(out, l, r, st, sp):
    return lambda e: e.matmul(out, lhsT=l, rhs=r, start=st, stop=sp)


def TR(out, in_, ident):
    return lambda e: e.transpose(out=out, in_=in_, identity=ident)


def ACT(out, in_, func, scale=None, bias=None):
    kw = {}
    if scale is not None:
        kw["scale"] = scale
    if bias is not None:
        kw["bias"] = bias
    return lambda e: e.activation(out=out, in_=in_, func=func, **kw)


def TS(out, in0, s1, s2, op0, op1=None):
    if op1 is None:
        return lambda e: e.tensor_scalar(out=out, in0=in0, scalar1=s1, scalar2=None, op0=op0)
    return lambda e: e.tensor_scalar(out=out, in0=in0, scalar1=s1, scalar2=s2, op0=op0, op1=op1)


def TT(out, in0, in1, op):
    return lambda e: e.tensor_tensor(out=out, in0=in0, in1=in1, op=op)


def STT(out, in0, scalar, in1, op0, op1):
    return lambda e: e.scalar_tensor_tensor(out=out, in0=in0, scalar=scalar, in1=in1, op0=op0, op1=op1)


def CP(out, in_):
    return lambda e: e.tensor_copy(out=out, in_=in_)


def DMA(out, in_):
    return lambda e: e.dma_start(out=out, in_=in_)


def MEMSET(ap, v):
    return lambda e: e.memset(ap, v)


STOP = [None]


def build(nlayers=L, debug_x=False):
    nc = bass.Bass("TRN2", target_bir_lowering=False)
    P = Plan()

    def din(name, shape):
        return nc.dram_tensor(name, shape, F32, kind="ExternalInput").ap()

    x_in = din("x_in", [NT, 128, 1024])
    cvec = din("cvec", [128, 16])
    wmod = din("wmod", [nlayers * 12, 128, 4096])
    bmodT = din("bmodT", [L, 128, 32])
    bmodg = din("bmodg", [L * 2, 1024])
    win = din("win", [nlayers * 5, 128, 4096])
    wo = din("wo", [nlayers * 2, 128, 4096])
    wffi = din("wffi", [nlayers * 11, 128, 4096])
    wffo = din("wffo", [nlayers * 6, 128, 4096])
    lnp = din("lnp", [L * 4, 1024])
    gout = din("gout", [128, L * 8])
    gsgu = din("gsgu", [L, 256])
    gqk = din("gqk", [L, 128])
    wsT = din("wsT", [L, 128, 512])
    bsd = din("bsd", [128, L * 4])
    rope = din("rope", [NT, 128, 64])
    nab = din("nab", [nlayers * 5, 128, 4608])
    identd = din("identd", [128, 128])
    y_out = nc.dram_tensor("y_out", [16, 128, 1024], F32, kind="ExternalOutput").ap()

    gst = nc.dram_tensor("gst", [L * 4, 128, 1024], F32).ap()
    hts = nc.dram_tensor("hts", [NT, 128, 1024], BF16).ap()
    nak_loc = nc.dram_tensor("nak_loc", [16, 128, 384], BF16).ap()
    nav_loc = nc.dram_tensor("nav_loc", [16, 128, 390], BF16).ap()
    ktc_loc = nc.dram_tensor("ktc_loc", [128, 2048], BF16)
    ktc_all = nc.dram_tensor("ktc_all", [512, 2048], BF16)
    vc_loc = nc.dram_tensor("vc_loc", [2048, 130], BF16)
    vc_all = nc.dram_tensor("vc_all", [8192, 130], BF16)
    nak_edge = nc.dram_tensor("nak_edge", [512, 384], BF16)
    nak_eall = nc.dram_tensor("nak_eall", [2048, 384], BF16)
    nav_edge = nc.dram_tensor("nav_edge", [512, 390], BF16)
    nav_eall = nc.dram_tensor("nav_eall", [2048, 390], BF16)

    off = [16512]
    OFF = {}
    LIMIT = 229344

    def sb(name, shape, dt, at=None):
        nbytes = int(np.prod(shape[1:])) * (4 if dt == F32 else 2)
        nbytes = (nbytes + 63) // 64 * 64
        if at is None:
            o = off[0]
            off[0] += nbytes
            assert off[0] <= LIMIT, (name, off[0])
        else:
            o = at
        t = nc.alloc_sbuf_tensor_at(name, shape, dt, offset=o)
        OFF[name] = o
        return t

    X = sb("X", [128, NT, 1024], F32)
    KTC = sb("KTC", [128, 8448], BF16)
    VC = sb("VC", [128, 66, 130], BF16)
    NAK = sb("NAK", [128, 3, 1280], BF16)
    NAV = sb("NAV", [128, 10, 390], BF16)
    NAB = sb("NAB", [128, 4608], BF16)
    BC = sb("BC", [128, 3, 1024], F32)
    WB = [sb(f"WB{i}", [128, 8, 512], BF16) for i in range(2)]
    HT4 = sb("HT4", [128, 4, 8, 128], BF16)
    MERGB = sb("MERGB", [128, 4, 1024], BF16)
    att0 = off[0]
    QTA = sb("QTA", [128, 3, 512], BF16)
    QTC = sb("QTC", [128, 3, 512], BF16)
    PT = [sb(f"PT{i}", [128, 512], BF16) for i in range(2)]
    PTN = sb("PTN", [128, 1024], BF16)
    OT = sb("OT", [128, 768], F32)
    att1 = off[0]
    assert att0 + 11264 <= att1
    GTOK4 = sb("GTOK4", [128, 4, 2816], BF16, at=OFF["KTC"])
    GTTa = sb("GTTa", [128, 2, 22, 128], BF16, at=OFF["KTC"] + 22528)
    GTTb = sb("GTTb", [128, 2, 22, 128], BF16, at=att0)
    assert 22528 + 11264 <= 16896 + 17160
    Y4 = sb("Y4", [128, 4, 1024], F32, at=OFF["NAK"])
    HT4b = sb("HT4b", [128, 4, 8, 128], BF16, at=OFF["MERGB"])
    W2P1 = sb("W2P1", [128, 1024], BF16, at=OFF["NAK"] + 16384)
    MVP_ = [sb(f"MVP{j}", [128, 8], F32, at=OFF["NAK"] + 16384 + 2048 + 64 * j) for j in range(2)]
    ST6P_ = [sb(f"ST6P{j}", [128, 12], F32, at=OFF["NAK"] + 16384 + 2048 + 128 + 64 * j) for j in range(2)]
    assert OFF["NAK"] + 16384 + 2048 + 256 <= OFF["NAB"] + 9216
    assert OFF["NAB"] + 9216 - OFF["NAK"] >= 16384
    WBX = [sb(f"WBX{i}", [128, 8, 512], BF16, at=OFF["KTC"] + 8192 * i) for i in range(4)]
    RLAT = sb("RLAT", [128, 8, 128], BF16, at=att0)
    RCTX = sb("RCTX", [128, 8, 128], BF16, at=att0 + 2048)
    W4A = sb("W4A", [128, 1024], F32)
    ZG = sb("ZG", [128, 512], F32)
    OATT = sb("OATT", [128, 4, 384], F32, at=OFF["W4A"])
    ZG_1 = sb("ZG1", [128, 512], F32, at=OFF["W4A"])
    W2A_0 = sb("W2A", [128, 1024], BF16)
    TMPA_0 = sb("TMPA", [128, 512], F32)
    TMPB_0 = sb("TMPB", [128, 512], F32)
    KQB_0 = sb("KQB", [128, 512], BF16)
    TMPA_1 = sb("TMPA1", [128, 512], F32, at=OFF["PTN"])
    TMPB_1 = sb("TMPB1", [128, 512], F32, at=OFF["OT"])
    KQB_1 = sb("KQB1", [128, 512], BF16, at=OFF["PT0"])
    W2A_1 = sb("W2A1", [128, 1024], BF16, at=OFF["QTA"])
    KAT_1 = sb("KAT1", [128, 4, 128], BF16, at=OFF["QTA"] + 2048)
    VAT_1 = sb("VAT1", [128, 6, 65], BF16, at=OFF["QTC"])
    VCT_1 = sb("VCT1", [128, 2, 65], BF16, at=OFF["QTC"] + 896)
    ROPE_1 = sb("ROPE1", [128, 64], F32, at=OFF["PT1"])
    SS_1 = sb("SS1", [128, 16], F32, at=OFF["PT1"] + 256)
    MV_1 = sb("MV1", [128, 8], F32, at=OFF["PT1"] + 320)
    ST6_1 = sb("ST61", [128, 12], F32, at=OFF["PT1"] + 384)
    VAT_0 = sb("VAT", [128, 6, 65], BF16)
    VCT_0 = sb("VCT", [128, 2, 65], BF16)
    KAT_0 = sb("KAT", [128, 4, 128], BF16)
    IDB = sb("IDB", [128, 128], BF16)
    IDF = sb("IDF", [128, 128], F32)
    ROPE_0 = sb("ROPE", [128, 64], F32)
    S2F = sb("S2F", [128, 16], F32)
    S2B = sb("S2B", [128, 8, 2], BF16)
    ONESB = sb("ONESB", [128, 128], BF16)
    MODF = sb("MODF", [128, L, 32, 2], F32)
    BMT = sb("BMT", [128, 32], F32)
    GOUT = sb("GOUT", [128, L * 8], F32)
    GSGU = sb("GSGU", [128, 256], F32)
    GQK = sb("GQK", [128, 128], F32)
    WST = sb("WST", [128, 512], BF16)
    BS = sb("BS", [128, L * 4], F32)
    ST6_0 = sb("ST6", [128, 12], F32)
    MV_0 = sb("MV", [128, 8], F32)
    SS_0 = sb("SS", [128, 16], F32)
    print("SBUF used", off[0], "of", LIMIT)

    PSF = [nc.alloc_psum_tensor(f"PSF{i}", [128, 512], F32) for i in range(6)]
    PST = [nc.alloc_psum_tensor(f"PST{i}", [128, 1024], BF16) for i in range(2)]
    PSFT = [T(f"psf{i}") for i in range(8)]
    pst_rr = [0]

    XT = [T(f"x{i}") for i in range(NT)]
    tKTC, tVC, tNAK, tNAV, tNAB = T("ktc"), T("vc"), T("nak"), T("nav"), T("nab")
    tBC = [T("bc0"), T("bc1"), T("bc2")]
    tWB = [T("wb0"), T("wb1")]
    tHT = [T(f"ht{i}") for i in range(4)]
    tMERG = [T(f"mg{i}") for i in range(4)]
    tQTA, tQTC = T("qta"), T("qtc")
    tPT = [T("pt0"), T("pt1")]
    tPTN, tOT = T("ptn"), T("ot")
    tPTNb = T("ptnb")
    tGTOK = T("gtok")
    tY4 = [T(f"y4{i}") for i in range(4)]
    tG4 = [T(f"g4{i}") for i in range(4)]
    tGTT4 = [T(f"gtt{i}") for i in range(4)]
    tW4A, tZG = T("w4a"), T("zg")
    CUR = [0]

    class BufP:
        def __init__(self, bufs):
            self.bufs = bufs

        def __getitem__(self, k):
            return self.bufs[CUR[0]][k]

    class HP:
        def __init__(self, hs):
            self.hs = hs

        def resolve(self):
            return self.hs[CUR[0]]
    TMPA, TMPB, KQB = BufP([TMPA_0, TMPA_1]), BufP([TMPB_0, TMPB_1]), BufP([KQB_0, KQB_1])
    W2A, KAT, VAT, VCT = BufP([W2A_0, W2A_1]), BufP([KAT_0, KAT_1]), BufP([VAT_0, VAT_1]), BufP([VCT_0, VCT_1])
    ILV = [False]
    ZGp = BufP([ZG, ZG_1])
    tZGp = HP([tZG, tW4A])
    tVAT_0, tVCT_0, tKAT_0, tW2A_0 = T("vat"), T("vct"), T("kat"), T("w2a")
    W2P = [W2A_0, W2P1]
    MVP, ST6P = MVP_, ST6P_
    tHTb = [T(f"htb{i}") for i in range(4)]
    tW2P = [tW2A_0, T("w2p1")]
    tMVP = [T("mvp0"), T("mvp1")]
    tST6P = [T("st6p0"), T("st6p1")]
    tCONST, tMODF, tLAYC = T("const"), T("modf"), T("layc")
    tW2A = HP([tW2A_0, tQTA])
    tKAT = HP([tKAT_0, tQTA])
    tVAT = HP([tVAT_0, tQTC])
    tVCT = HP([tVCT_0, tQTC])
    tTMPA = HP([T("tmpa"), [tPTN, tPTNb]])
    tTMPB = HP([T("tmpb"), tOT])
    tKQB = HP([T("kqb"), tPT[0]])
    tROPE = HP([T("rope"), tPT[1]])
    tST6 = HP([T("st6"), tPT[1]])
    tMV = HP([T("mv"), tPT[1]])
    tSS = HP([T("ss"), tPT[1]])
    ROPE, SS, MV, ST6 = BufP([ROPE_0, ROPE_1]), BufP([SS_0, SS_1]), BufP([MV_0, MV_1]), BufP([ST6_0, ST6_1])
    tGST, tHTS = T("gst"), [T(f"hts{i}") for i in range(NT)]
    tNAKL, tNAVL, tKTCL, tVCL, tNAKE, tNAVE = T("nakl"), T("navl"), T("ktcl"), T("vcl"), T("nake"), T("nave")
    tKTCA, tVCA, tNAKEA, tNAVEA = T("ktca"), T("vca"), T("nakea"), T("navea")
    tY = T("y")

    wb_rr = [0]
    WBALL = WB + WBX
    tWBALL = tWB + [T(f"wbx{i}") for i in range(4)]
    wb_pool = [6]

    def load_wblock(src_ap):
        i = wb_rr[0] % wb_pool[0]
        wb_rr[0] += 1
        P.op("pool", DMA(WBALL[i][:].rearrange("p a b -> p (a b)"), src_ap), writes=[tWBALL[i]], dma=f"wb{i}")
        return WBALL[i], tWBALL[i]

    ps_rr = [0]

    ps_set_rr = [0, 0]

    def next_ps(lo=0, hi=3):
        if ILV[0]:
            c = CUR[0]
            i = 2 * c + ps_set_rr[c] % 2
            ps_set_rr[c] += 1
            return PSF[i], PSFT[i]
        i = lo + ps_rr[0] % (hi - lo)
        ps_rr[0] += 1
        return PSF[i], PSFT[i]

    def ln_stats(src, srcT, n):
        nch = max(1, n // 512)
        w = n // nch
        for ci in range(nch):
            P.op("dve", (lambda o, i: (lambda e: e.bn_stats(out=o, in_=i)))(ST6[:, ci * 6:(ci + 1) * 6], src[:, ci * w:(ci + 1) * w]),
                 reads=srcT, writes=[tST6] if ci == 0 else (), pwrites=() if ci == 0 else [tST6])
        P.op("dve", (lambda o, i: (lambda e: e.bn_aggr(out=o, in_=i)))(MV[:, 0:2], ST6[:, 0:6 * nch].rearrange('p (n s) -> p n s', s=6)), reads=[tST6], writes=[tMV])
        P.op("dve", TS(MV[:, 2:3], MV[:, 1:2], EPS, None, ALU.add), reads=[tMV], writes=[tMV])
        P.op("act", ACT(MV[:, 2:3], MV[:, 2:3], AF.Ln), reads=[tMV], writes=[tMV])
        P.op("act", ACT(MV[:, 3:4], MV[:, 2:3], AF.Exp, scale=-0.5), reads=[tMV], writes=[tMV])

    def rms_rstd(src, srcT, G, W, tmp, tmpT):
        P.op("dve", TT(tmp[:, 0:G * W], src, src, ALU.mult), reads=srcT, writes=[tmpT])
        P.op("dve", (lambda o, i: (lambda e: e.reduce_sum(out=o, in_=i, axis=AX.X)))(SS[:, 0:G], tmp[:, 0:G * W].rearrange("p (g w) -> p g w", g=G)),
             reads=[tmpT], writes=[tSS])
        P.op("dve", TS(SS[:, 0:G], SS[:, 0:G], 1.0 / W, EPS, ALU.mult, ALU.add), reads=[tSS], writes=[tSS])
        P.op("act", ACT(SS[:, 0:G], SS[:, 0:G], AF.Ln), reads=[tSS], writes=[tSS])
        P.op("act", ACT(SS[:, 8:8 + G], SS[:, 0:G], AF.Exp, scale=-0.5), reads=[tSS], writes=[tSS])


    def qk_norm_rope(src, srcT, H, gain, dst, oscale, dstT, full):
        n = H * 64
        P.op("act", ACT(TMPA[:, 0:n], src, AF.Identity), reads=srcT, writes=[tTMPA])
        rms_rstd(TMPA[:, 0:n], [tTMPA], H, 64, TMPB, tTMPB)
        v3 = lambda ap: ap.rearrange("p (h d) -> p h d", h=H)
        P.op("dve", TT(v3(TMPA[:, 0:n]), v3(TMPA[:, 0:n]), SS[:, 8:8 + H].unsqueeze(2).to_broadcast([128, H, 64]), ALU.mult), reads=[tTMPA, tSS], writes=[tTMPA])
        P.op("dve", STT(v3(TMPA[:, 0:n]), v3(TMPA[:, 0:n]), oscale, gain.unsqueeze(1).to_broadcast([128, H, 64]), ALU.mult, ALU.mult),
             reads=[tTMPA, tLAYC], writes=[tTMPA])
        v5 = lambda ap: ap.rearrange("p (h a s f) -> p h a s f", h=H, a=2, s=2)
        x1 = v5(TMPA[:, 0:n])[:, :, :, 0, :]
        x2 = v5(TMPA[:, 0:n])[:, :, :, 1, :]
        d1 = v5(dst)[:, :, :, 0, :]
        d2 = v5(dst)[:, :, :, 1, :]
        C = ROPE[:, 0:32].rearrange("p (a f) -> p a f", a=2).unsqueeze(1).to_broadcast([128, H, 2, 16])
        S_ = ROPE[:, 32:64].rearrange("p (a f) -> p a f", a=2).unsqueeze(1).to_broadcast([128, H, 2, 16])
        v4 = lambda ap: ap.rearrange("p (h a f) -> p h a f", h=H, a=2)
        t1 = v4(TMPB[:, 0:H * 32])
        t2 = v4(TMPB[:, H * 32:H * 64])
        P.op("dve", TT(t1, x1, C, ALU.mult), reads=[tTMPA, tROPE], writes=[tTMPB])
        P.op("dve", TT(t2, x2, S_, ALU.mult), reads=[tTMPA, tROPE], pwrites=[tTMPB])
        P.op("dve", TT(d1, t1, t2, ALU.subtract), reads=[tTMPB], writes=dstT if full else (), pwrites=() if full else dstT)
        P.op("dve", TT(t1, x2, C, ALU.mult), reads=[tTMPA, tROPE], writes=[tTMPB])
        P.op("dve", TT(t2, x1, S_, ALU.mult), reads=[tTMPA, tROPE], pwrites=[tTMPB])
        P.op("dve", TT(d2, t1, t2, ALU.add), reads=[tTMPB], pwrites=dstT)

    def transpose_tile(src_bf, srcT, nch, dst_fn, dstT, evac):
        if ILV[0]:
            j = CUR[0]
        else:
            j = pst_rr[0] % 2
            pst_rr[0] += 1
        psb, pT = PST[j], PSFT[6 + j]
        for kc in range(nch):
            P.op("pe", TR(psb[:, kc * 128:(kc + 1) * 128], src_bf[:, kc * 128:(kc + 1) * 128], IDB[:, :]),
                 reads=srcT + [tCONST], writes=[pT] if kc == 0 else (), pwrites=() if kc == 0 else [pT])
        for kc in range(nch):
            evac(kc, psb[:, kc * 128:(kc + 1) * 128], pT)

    P.op("sp", DMA(IDF[:, :], identd[:, :]), writes=[tCONST], dma="c0")
    P.op("sp", DMA(S2F[:, :], cvec[:, :]), pwrites=[tCONST], dma="c0")
    P.op("sp", DMA(GOUT[:, :], gout[:, :]), pwrites=[tCONST], dma="c0")
    P.op("sp", DMA(BS[:, :], bsd[:, :]), pwrites=[tCONST], dma="c0")
    for i in range(NT):
        P.op("sp", DMA(X[:, i, :], x_in[i]), writes=[XT[i]], dma="xin")
    P.op("dve", CP(IDB[:, :], IDF[:, :]), reads=[tCONST], pwrites=[tCONST])
    P.op("dve", MEMSET(ONESB[:, :], 1.0), pwrites=[tCONST])
    P.op("act", ACT(S2F[:, :], S2F[:, :], AF.Silu), reads=[tCONST], writes=[tCONST])
    P.op("dve", CP(S2B[:].rearrange("p a b -> p (a b)"), S2F[:, :]), reads=[tCONST], writes=[tCONST])
    for kc in range(8):
        P.op("dve", TS(RLAT[:, kc, :], ONESB[:, :], S2F[:, 2 * kc:2 * kc + 1], None, ALU.mult), reads=[tCONST], pwrites=[tGTOK])
        P.op("dve", TS(RCTX[:, kc, :], ONESB[:, :], S2F[:, 2 * kc + 1:2 * kc + 2], None, ALU.mult), reads=[tCONST], pwrites=[tGTOK])
    P.op("dve", MEMSET(VAT[:, :, :], 1.0), writes=[tVAT])
    P.op("dve", MEMSET(VCT[:, :, :], 1.0), writes=[tVCT])

    for l in range(nlayers):
        P.op("sp", DMA(BMT[:, :], bmodT[l]), writes=[tLAYC], dma="c1")
        psm, psmT = PSF[5], PSFT[5]
        for jj, nbs in enumerate([(0, 1), (2, 3), (6, 7), (8, 9)]):
            for half, nb in enumerate(nbs):
                wbuf, wT = load_wblock(wmod[l * 12 + nb])
                for oc4 in range(4):
                    col = (jj * 8 + half * 4 + oc4) * 2
                    for kc in range(8):
                        first = (jj == 0 and half == 0 and oc4 == 0 and kc == 0)
                        P.op("pe", MM(psm[:, col:col + 2], wbuf[:, kc, oc4 * 128:(oc4 + 1) * 128], S2B[:, kc, :], kc == 0, kc == 7),
                             reads=[wT, tCONST], writes=[psmT] if first else (), pwrites=() if first else [psmT])
        pv = psm[:, 0:64].rearrange("p (a b) -> p a b", b=2)
        for s in range(2):
            P.op("dve", TT(MODF[:, l, :, s], pv[:, :, s], BMT[:, :], ALU.add), reads=[psmT, tLAYC], pwrites=[tMODF])
        for a0 in (8, 24):
            P.op("dve", TS(MODF[:, l, a0:a0 + 8, :], MODF[:, l, a0:a0 + 8, :], 1.0, None, ALU.add), reads=[tMODF], writes=[tMODF])
        for gi, nbs in enumerate([(4, 5), (10, 11)]):
            for half, nb in enumerate(nbs):
                wbuf, wT = load_wblock(wmod[l * 12 + nb])
                P.op("sp", DMA(TMPA[:, :], bmodg[l * 2 + gi, half * 512:(half + 1) * 512].partition_broadcast(128)),
                     writes=[tTMPA], dma="tmpa")
                for s, R in enumerate((RLAT, RCTX)):
                    ps, pT = next_ps(0, 3)
                    for kc in range(8):
                        P.op("pe", MM(ps[:, :], R[:, kc, :], wbuf[:, kc, :], kc == 0, kc == 7), reads=[wT, tGTOK],
                             writes=[pT] if kc == 0 else (), pwrites=() if kc == 0 else [pT])
                    P.op("dve", TT(TMPB[:, :], ps[:, :], TMPA[:, :], ALU.add), reads=[pT, tTMPA], writes=[tTMPB])
                    P.op("sp", DMA(gst[l * 4 + gi * 2 + s][:, half * 512:(half + 1) * 512], TMPB[:, :]), reads=[tTMPB], pwrites=[tGST], dma="gst")
    P.barrier()
    wb_pool[0] = 2
    wb_rr[0] = 0
    if STOP[0] == 'p0':
        return nc, P

    def load_bc(l, sub, ctx):
        P.op("sp", DMA(BC[:, 0, :], gst[l * 4 + sub * 2 + (1 if ctx else 0)]), reads=[tGST], writes=[tBC[0]], dma="bc0")

    def load_ln(l, sub):
        for j in range(2):
            r = l * 4 + sub * 2 + j
            P.op("sp", DMA(BC[:, 1 + j, :], lnp[r, :].partition_broadcast(128)), writes=[tBC[1 + j]], dma=f"bc{1 + j}")

    def ln_mod_transpose(i, l, which, slot):
        s = 1 if i >= 16 else 0
        ln_stats(X[:, i, :], [XT[i]], 1024)
        P.op("dve", TS(W2A[:, :], X[:, i, :], MV[:, 0:1], MV[:, 3:4], ALU.subtract, ALU.mult), reads=[XT[i], tMV], writes=[tW2A])

        def evac(kc, pap, pT):
            sc = MODF[:, l, which * 16 + 8 + kc, s:s + 1]
            sh = MODF[:, l, which * 16 + kc, s:s + 1]
            P.op("dve", TS(HT4[:, slot, kc, :], pap, sc, sh, ALU.mult, ALU.add), reads=[pT, tMODF], pwrites=[tHT[slot]])
        transpose_tile(W2A, [tW2A], 8, None, None, evac)

    def deepnorm_from(i, zsrc, zT, ew="dve"):
        if ew == "dve":
            P.op("dve", STT(W4A[:, :], X[:, i, :], ALPHA, zsrc, ALU.mult, ALU.add), reads=[XT[i]] + zT, writes=[tW4A])
        else:
            P.op(ew, TS(W4A[:, :], X[:, i, :], ALPHA, None, ALU.mult), reads=[XT[i]], writes=[tW4A])
            P.op(ew, TT(W4A[:, :], W4A[:, :], zsrc, ALU.add), reads=[tW4A] + zT, writes=[tW4A])
        ln_stats(W4A[:, :], [tW4A], 1024)
        P.op(ew, TS(W4A[:, :], W4A[:, :], MV[:, 0:1], MV[:, 3:4], ALU.subtract, ALU.mult), reads=[tW4A, tMV], writes=[tW4A])
        P.op("pool", TT(W4A[:, :], W4A[:, :], BC[:, 1, :], ALU.mult), reads=[tW4A, tBC[1]], writes=[tW4A])
        P.op("pool", TT(X[:, i, :], W4A[:, :], BC[:, 2, :], ALU.add), reads=[tW4A, tBC[2]], writes=[XT[i]])

    def deepnorm(i, l, ysrc_fn, yT):
        for nb in range(2):
            P.op("dve", TT(W4A[:, nb * 512:(nb + 1) * 512], ysrc_fn(nb), BC[:, 0, nb * 512:(nb + 1) * 512], ALU.mult),
                 reads=[yT[nb], tBC[0]], writes=[tW4A] if nb == 0 else (), pwrites=() if nb == 0 else [tW4A])
        deepnorm_from(i, W4A[:, :], [tW4A])

    for l in range(nlayers):
        P.op("sp", DMA(GSGU[:, :], gsgu[l, :].partition_broadcast(128)), writes=[tLAYC], dma="c1")
        P.op("sp", DMA(GQK[:, :], gqk[l, :].partition_broadcast(128)), pwrites=[tLAYC], dma="c1")
        P.op("pool", DMA(WST[:, :], wsT[l]), pwrites=[tLAYC], dma="c2")

        CUR[0] = 1
        P.op("dve", MEMSET(VAT[:, :, :], 1.0), writes=[tVAT])
        P.op("dve", MEMSET(VCT[:, :, :], 1.0), pwrites=[tVCT])
        CUR[0] = 0
        P.op("dve", MEMSET(VC[:, 64:66, :], 1.0), writes=[tVC])
        P.op("dve", MEMSET(NAV[:, 8:10, :], 1.0), writes=[tNAV])
        wv, wvT = load_wblock(win[l * 5 + 0])
        wk, wkT = load_wblock(win[l * 5 + 1])
        def p1_tile(i):
            lat = i < 16
            sl = i % 2
            P.op("sp", DMA(ROPE[:, :], rope[i]), writes=[tROPE], dma="rope")
            ln_mod_transpose(i, l, 0, sl)
            P.op("sp", DMA(hts[i], HT4[:, sl].rearrange("p a b -> p (a b)")), reads=[tHT[sl]], writes=[tHTS[i]], dma="hts")
            ps, pT = next_ps(0, 3)
            for kc in range(8):
                P.op("pe", MM(ps[:, :], HT4[:, sl, kc, :], wv[:, kc, :], kc == 0, kc == 7), reads=[tHT[sl], wvT],
                     writes=[pT] if kc == 0 else (), pwrites=() if kc == 0 else [pT])
            if lat:
                P.op("dve", CP(VAT[:, :, 0:64], ps[:, 0:384].rearrange("p (h d) -> p h d", h=6)), reads=[pT], pwrites=[tVAT])
                P.op("dve", CP(VCT[:, :, 0:64], ps[:, 384:512].rearrange("p (h d) -> p h d", h=2)), reads=[pT], pwrites=[tVCT])
                P.op("sp", DMA(nav_loc[i], VAT[:].rearrange("p a b -> p (a b)")), reads=[tVAT], pwrites=[tNAVL], dma="navl")
                P.op("sp", DMA(vc_loc.ap()[i * 128:(i + 1) * 128, :], VCT[:].rearrange("p a b -> p (a b)")), reads=[tVCT], pwrites=[tVCL], dma="vcl")
                if i in (0, 1, 14, 15):
                    e = i if i < 2 else i - 12
                    P.op("sp", DMA(nav_edge.ap()[e * 128:(e + 1) * 128, :], VAT[:].rearrange("p a b -> p (a b)")), reads=[tVAT], pwrites=[tNAVE], dma="nave")
            else:
                j = i - 16
                P.op("dve", CP(NAV[:, 8 + j, :].rearrange("p (h d) -> p h d", h=6)[:, :, 0:64], ps[:, 0:384].rearrange("p (h d) -> p h d", h=6)), reads=[pT], pwrites=[tNAV])
                P.op("dve", CP(VC[:, 64 + j, :].rearrange("p (h d) -> p h d", h=2)[:, :, 0:64], ps[:, 384:512].rearrange("p (h d) -> p h d", h=2)), reads=[pT], pwrites=[tVC])
            ps, pT = next_ps(0, 3)
            for kc in range(8):
                P.op("pe", MM(ps[:, :], HT4[:, sl, kc, :], wk[:, kc, :], kc == 0, kc == 7), reads=[tHT[sl], wkT],
                     writes=[pT] if kc == 0 else (), pwrites=() if kc == 0 else [pT])
            P.op("act", ACT(KQB[:, 0:384], ps[:, 0:384], AF.Identity), reads=[pT], writes=[tKQB])
            qk_norm_rope(ps[:, 384:512], [pT], 2, GQK[:, 64:128], KQB[:, 384:512], 1.0, [tKQB], False)
            kdst = []

            def evac_k(kc, pap, ppT, i=i, lat=lat):
                if kc < 3:
                    if lat:
                        P.op("dve", CP(KAT[:, kc, :], pap), reads=[ppT], pwrites=[tKAT])
                    else:
                        P.op("dve", CP(NAK[:, kc, (8 + i - 16) * 128:(9 + i - 16) * 128], pap), reads=[ppT], pwrites=[tNAK])
                else:
                    if lat:
                        P.op("dve", CP(KAT[:, 3, :], pap), reads=[ppT], pwrites=[tKAT])
                    else:
                        P.op("dve", CP(KTC[:, 8192 + (i - 16) * 128:8192 + (i - 15) * 128], pap), reads=[ppT], pwrites=[tKTC])
            transpose_tile(KQB, [tKQB], 4, None, None, evac_k)
            if lat:
                P.op("sp", DMA(nak_loc[i], KAT[:, 0:3, :].rearrange("p a b -> p (a b)")), reads=[tKAT], pwrites=[tNAKL], dma="nakl")
                P.op("sp", DMA(ktc_loc.ap()[:, i * 128:(i + 1) * 128], KAT[:, 3, :]), reads=[tKAT], pwrites=[tKTCL], dma="ktcl")
                if i in (0, 1, 14, 15):
                    e = i if i < 2 else i - 12
                    P.op("sp", DMA(nak_edge.ap()[e * 128:(e + 1) * 128, :], KAT[:, 0:3, :].rearrange("p a b -> p (a b)")), reads=[tKAT], pwrites=[tNAKE], dma="nake")

        for ia in range(0, NT, 2):
            lists = []
            for i in (ia, ia + 1):
                CUR[0] = i % 2
                ILV[0] = True
                P.record_begin()
                p1_tile(i)
                lists.append(P.record_end())
            CUR[0] = 0
            ILV[0] = False
            P.replay_interleaved(lists)

        if STOP[0] == 'p1':
            return nc, P
        def coll(src, srcT, dst, dstT, key):
            P.op("pool", (lambda s_, d_: (lambda e: e.collective_compute(
                "AllGather", ALU.bypass, replica_groups=[[0, 1, 2, 3], [4, 5, 6, 7]],
                ins=[s_.ap().opt()], outs=[d_.ap().opt()])))(src, dst), reads=[srcT], writes=[dstT], dma=key)
        coll(ktc_loc, tKTCL, ktc_all, tKTCA, f"cc{l}a")
        coll(vc_loc, tVCL, vc_all, tVCA, f"cc{l}b")
        coll(nak_edge, tNAKE, nak_eall, tNAKEA, f"cc{l}c")
        coll(nav_edge, tNAVE, nav_eall, tNAVEA, f"cc{l}d")
        def load_gathered_kv():
            for r in range(4):
                P.op("sp", DMA(KTC[:, r * 2048:(r + 1) * 2048], ktc_all.ap()[r * 128:(r + 1) * 128, :]), reads=[tKTCA], pwrites=[tKTC], dma="ktc")
            for r in range(4):
                P.op("sp", DMA(VC[:, r * 16:(r + 1) * 16, :], vc_all.ap()[r * 2048:(r + 1) * 2048, :].rearrange("(k p) c -> p k c", p=128)),
                     reads=[tVCA], pwrites=[tVC], dma="vc")

        if STOP[0] == 'ex':
            P.barrier()
            return nc, P
        load_ln(l, 0)
        cur_cls = [None]
        for st in range(5):
            tiles = list(range(4 * st, 4 * st + 4)) if st < 4 else [16, 17]
            if l == nlayers - 1 and st == 4:
                continue
            n = len(tiles)
            ctx = st == 4
            load_bc(l, 0, ctx)
            for ti, i in enumerate(tiles):
                P.op("sp", DMA(HT4[:, ti].rearrange("p a b -> p (a b)"), hts[i]), reads=[tHTS[i]], writes=[tHT[ti]], dma=f"ht{ti}")
            wq, wqT = load_wblock(win[l * 5 + 2])
            def proj_qa(ti, i):
                ps, pT = next_ps(0, 3)
                for kc in range(8):
                    P.op("pe", MM(ps[:, 0:384], HT4[:, ti, kc, :], wq[:, kc, 0:384], kc == 0, kc == 7), reads=[tHT[ti], wqT],
                         writes=[pT] if kc == 0 else (), pwrites=() if kc == 0 else [pT])
                P.op("act", ACT(KQB[:, 0:384], ps[:, 0:384], AF.Identity, scale=0.125), reads=[pT], writes=[tKQB])

                def evac_qa(kc, pap, ppT, ti=ti):
                    P.op("dve", CP(QTA[:, kc, ti * 128:(ti + 1) * 128], pap), reads=[ppT], pwrites=[tQTA])
                transpose_tile(KQB, [tKQB], 3, None, None, evac_qa)
            for ta in range(0, n, 2):
                lists = []
                for ti in range(ta, min(n, ta + 2)):
                    CUR[0] = ti % 2
                    ILV[0] = True
                    P.record_begin()
                    proj_qa(ti, tiles[ti])
                    lists.append(P.record_end())
                CUR[0] = 0
                ILV[0] = False
                P.replay_interleaved(lists)
            wq, wqT = load_wblock(win[l * 5 + 3])
            def proj_qc(ti, i):
                P.op("sp", DMA(ROPE[:, :], rope[i]), writes=[tROPE], dma="rope")
                ps, pT = next_ps(0, 3)
                for kc in range(8):
                    P.op("pe", MM(ps[:, 0:384], HT4[:, ti, kc, :], wq[:, kc, 0:384], kc == 0, kc == 7), reads=[tHT[ti], wqT],
                         writes=[pT] if kc == 0 else (), pwrites=() if kc == 0 else [pT])
                qk_norm_rope(ps[:, 0:384], [pT], 6, GQK[:, 0:64], KQB[:, 0:384], 0.125, [tKQB], True)

                def evac_qc(kc, pap, ppT, ti=ti):
                    P.op("dve", CP(QTC[:, kc, ti * 128:(ti + 1) * 128], pap), reads=[ppT], pwrites=[tQTC])
                transpose_tile(KQB, [tKQB], 3, None, None, evac_qc)
            for ta in range(0, n, 2):
                lists = []
                for ti in range(ta, min(n, ta + 2)):
                    CUR[0] = ti % 2
                    ILV[0] = True
                    P.record_begin()
                    proj_qc(ti, tiles[ti])
                    lists.append(P.record_end())
                CUR[0] = 0
                ILV[0] = False
                P.replay_interleaved(lists)
            wq, wqT = load_wblock(win[l * 5 + 4])
            def proj_zb(ti, i):
                ps, pT = next_ps(0, 3)
                for kc in range(8):
                    P.op("pe", MM(ps[:, :], HT4[:, ti, kc, :], wq[:, kc, :], kc == 0, kc == 7), reads=[tHT[ti], wqT],
                         writes=[pT] if kc == 0 else (), pwrites=() if kc == 0 else [pT])
                P.op("act", ACT(ZGp[:, :], ps[:, :], AF.Gelu_apprx_tanh), reads=[pT], writes=[tZGp])
                ln_stats(ZGp[:, 256:512], [tZGp], 256)
                P.op("dve", TS(KQB[:, 0:256], ZGp[:, 256:512], MV[:, 0:1], MV[:, 3:4], ALU.subtract, ALU.mult), reads=[tZGp, tMV], writes=[tKQB])
                ps2, p2T = (PSF[4 + CUR[0]], PSFT[4 + CUR[0]]) if ILV[0] else (PSF[5], PSFT[5])
                for g in range(4):
                    P.op("pe", MM(ps2[:, g * 64:(g + 1) * 64], WST[:, g * 128:(g + 1) * 128], KQB[:, g * 64:(g + 1) * 64], True, True),
                         reads=[tKQB, tLAYC], writes=[p2T] if g == 0 else (), pwrites=() if g == 0 else [p2T])
                P.op("dve", TT(TMPA[:, 0:256], ps2[:, 0:256], GSGU[:, :], ALU.mult), reads=[p2T, tLAYC], writes=[tTMPA])
                for g in range(4):
                    P.op("dve", TS(TMPA[:, g * 64:(g + 1) * 64], TMPA[:, g * 64:(g + 1) * 64], BS[:, l * 4 + g:l * 4 + g + 1], None, ALU.add),
                         reads=[tTMPA, tCONST], writes=[tTMPA])
                P.op("dve", TT(TMPA[:, 0:256], TMPA[:, 0:256], ZGp[:, 0:256], ALU.mult), reads=[tTMPA, tZGp], writes=[tTMPA])
                rms_rstd(TMPA[:, 0:256], [tTMPA], 1, 256, TMPB, tTMPB)
                P.op("dve", TS(MERGB[:, ti, 384:640], TMPA[:, 0:256], SS[:, 8:9], None, ALU.mult), reads=[tTMPA, tSS], pwrites=[tMERG[ti]])

            for ta in range(0, n, 2):
                lists = []
                for ti in range(ta, min(n, ta + 2)):
                    CUR[0] = ti % 2
                    ILV[0] = True
                    P.record_begin()
                    proj_zb(ti, tiles[ti])
                    lists.append(P.record_end())
                CUR[0] = 0
                ILV[0] = False
                P.replay_interleaved(lists)
            if st == 0:
                load_gathered_kv()
            if not ctx:
                lo, hi = max(0, 4 * st - 2), min(16, 4 * st + 6)
                s0 = lo - (4 * st - 2)
                for j in range(hi - lo):
                    P.op("sp", DMA(NAK[:, :, (s0 + j) * 128:(s0 + j + 1) * 128], nak_loc[lo + j].rearrange("p (c k) -> p c k", c=3)),
                         reads=[tNAKL], writes=[tNAK] if j == 0 else (), pwrites=() if j == 0 else [tNAK], dma="nak")
                P.op("sp", DMA(NAV[:, s0:s0 + hi - lo, :], nav_loc[lo:hi].rearrange("t p c -> p t c")), reads=[tNAVL], writes=[tNAV], dma="nav")
                if st in (0, 3):
                    for j in range(2):
                        def mk_halo_k(c3, st=st, j=j):
                            def halo_k(e):
                                rk = PIDV[0]
                                if st == 0:
                                    src_r, e0, sl = (rk + 3) % 4, 2, 0
                                else:
                                    src_r, e0, sl = (rk + 1) % 4, 0, 6
                                return e.dma_start(out=NAK[:, c3, (sl + j) * 128:(sl + j + 1) * 128],
                                                   in_=nak_eall.ap()[bass.ds(src_r * 512 + (e0 + j) * 128, 128), c3 * 128:(c3 + 1) * 128])
                            return halo_k

                        def halo_v(e, st=st, j=j):
                            rk = PIDV[0]
                            if st == 0:
                                src_r, e0, sl = (rk + 3) % 4, 2, 0
                            else:
                                src_r, e0, sl = (rk + 1) % 4, 0, 6
                            return e.dma_start(out=NAV[:, sl + j, :], in_=nav_eall.ap()[bass.ds(src_r * 512 + (e0 + j) * 128, 128), :])
                        for c3 in range(3):
                            P.op("pool", mk_halo_k(c3), reads=[tNAKEA], pwrites=[tNAK], dma="nak")
                        P.op("pool", halo_v, reads=[tNAVEA], pwrites=[tNAV], dma="nav")
            if STOP[0] == 'p2a':
                P.barrier()
                return nc, P
            for ti, i in enumerate(tiles):
                if not ctx:
                    cls = 0 if i == 0 else 1 if i == 1 else 3 if i == 14 else 4 if i == 15 else 2
                    if cur_cls[0] != (l, cls):
                        P.op("pool", DMA(NAB[:, :], nab[l * 5 + cls]), writes=[tNAB], dma="nab")
                        cur_cls[0] = (l, cls)
                oac = [PSF[3], PSF[4]]
                oacT = [PSFT[3], PSFT[4]]
                for h in range(6):
                    c, pb = h // 2, (h % 2) * 64
                    q = QTA[pb:pb + 64, c, ti * 128:(ti + 1) * 128]
                    psa, paT = next_ps(0, 3)
                    psb_, pbT = next_ps(0, 3)
                    blocks = []
                    if not ctx:
                        sl_b = [(ti + b, b) for b in range(5)]
                        if i == 0:
                            sl_b.append((ti + 5, 5))
                        if i == 15:
                            sl_b.append((ti - 1, 5))
                        sl_b += [(8, None), (9, None)]
                    else:
                        sl_b = [(8, None), (9, None)]
                    for k, (slot, b) in enumerate(sl_b):
                        if k < 4:
                            blocks.append((slot, psa[:, k * 128:(k + 1) * 128], paT, b))
                        else:
                            blocks.append((slot, psb_[:, (k - 4) * 128:(k - 3) * 128], pbT, b))
                    nB = max(0, len(sl_b) - 4)
                    seen = set()
                    for (slot, reg, rT, b) in blocks:
                        first = id(rT) not in seen
                        seen.add(id(rT))
                        P.op("pe", MM(reg, NAK[pb:pb + 64, c, slot * 128:(slot + 1) * 128], q, True, b is None), reads=[tNAK, tQTA],
                             writes=[rT] if first else (), pwrites=() if first else [rT])
                        if b is not None:
                            P.op("pe", MM(reg, IDB[:, :], NAB[:, (h * 6 + b) * 128:(h * 6 + b + 1) * 128], False, True), reads=[tNAB, tCONST], pwrites=[rT])
                    if not ctx:
                        P.op("act", ACT(PTN[:, 0:512], psa[:, :], AF.Exp), reads=[paT], writes=[tPTN, tPTNb])
                        P.op("act", ACT(PTN[:, 512:512 + nB * 128], psb_[:, 0:nB * 128], AF.Exp), reads=[pbT], pwrites=[tPTN])
                    else:
                        P.op("act", ACT(PTN[:, 0:256], psa[:, 0:256], AF.Exp), reads=[paT], writes=[tPTN, tPTNb])
                    ob, obT = oac[h // 4], oacT[h // 4]
                    oreg = ob[0:65, (h % 4) * 128:(h % 4 + 1) * 128]
                    nb_ = len(blocks)
                    for bi, (slot, reg, rT, b) in enumerate(blocks):
                        firstw = (h % 4 == 0 and bi == 0)
                        P.op("pe", MM(oreg, NAV[:, slot, h * 65:(h + 1) * 65], PTN[:, bi * 128:(bi + 1) * 128], bi == 0, bi == nb_ - 1),
                             reads=[tNAV, tPTN, tPTNb], writes=[obT] if firstw else (), pwrites=() if firstw else [obT])
                P.op("dve", CP(OT[0:65, 0:512], oac[0][0:65, :]), reads=[oacT[0]], writes=[tOT])
                P.op("dve", CP(OT[0:65, 512:768], oac[1][0:65, 0:256]), reads=[oacT[1]], pwrites=[tOT])
                tp, tpT = PSF[5], PSFT[5]
                for h in range(6):
                    P.op("pe", TR(tp[:, h * 65:(h + 1) * 65], OT[0:65, h * 128:(h + 1) * 128], IDF[0:65, 0:65]), reads=[tOT, tCONST],
                         writes=[tpT] if h == 0 else (), pwrites=() if h == 0 else [tpT])
                tpv = tp[:, 0:390].rearrange("p (h d) -> p h d", h=6)
                P.op("dve", (lambda o, i_: (lambda e: e.reciprocal(out=o, in_=i_)))(SS[:, 0:6], tpv[:, :, 64]), reads=[tpT], writes=[tSS])
                P.op("dve", TT(OATT[:, ti, :].rearrange("p (h d) -> p h d", h=6), tpv[:, :, 0:64],
                               SS[:, 0:6].unsqueeze(2).to_broadcast([128, 6, 64]), ALU.mult), reads=[tpT, tSS], writes=[tW4A, tZG])
                rms_rstd(OATT[:, ti, :], [tW4A, tZG], 1, 384, TMPA, tTMPA)
                P.op("dve", TS(MERGB[:, ti, 0:384], OATT[:, ti, :], SS[:, 8:9], None, ALU.mult), reads=[tW4A, tZG, tSS], pwrites=[tMERG[ti]])

            if STOP[0] == 'p2na':
                P.barrier()
                return nc, P
            nq = n * 128
            kts = list(range(66)) if not ctx else [64, 65]
            nk = len(kts)
            for c in range(3):
                qs = [QTC[g * 64:g * 64 + 64, c, 0:nq] for g in range(2)]
                obs = [(PSF[4 + g], PSFT[4 + g]) for g in range(2)]
                ptb = [[(PT[0], tPT[0]), (PT[1], tPT[1])], [(PTN[:, 0:512], tPTN), (PTN[:, 512:1024], tPTNb)]]

                def S(g, k):
                    ps, pT = PSF[g * 2 + k % 2], PSFT[g * 2 + k % 2]
                    kt = kts[k]
                    P.op("pe", MM(ps[:, 0:nq], KTC[g * 64:g * 64 + 64, kt * 128:(kt + 1) * 128], qs[g], True, True), reads=[tKTC, tQTC], writes=[pT])
                for k0 in range(min(2, nk)):
                    for g in range(2):
                        S(g, k0)
                for k in range(nk):
                    for g in range(2):
                        ps, pT = PSF[g * 2 + k % 2], PSFT[g * 2 + k % 2]
                        pbuf, pbT = ptb[g][k % 2]
                        P.op("act", ACT(pbuf[:, 0:nq], ps[:, 0:nq], AF.Exp), reads=[pT], writes=[pbT])
                    if k + 2 < nk:
                        for g in range(2):
                            S(g, k + 2)
                    kt = kts[k]
                    for g in range(2):
                        ob, obT = obs[g]
                        pbuf, pbT = ptb[g][k % 2]
                        P.op("pe", MM(ob[0:65, 0:nq], VC[:, kt, g * 65:(g + 1) * 65], pbuf[:, 0:nq], k == 0, k == nk - 1),
                             reads=[tVC, pbT], writes=[obT] if k == 0 else (), pwrites=() if k == 0 else [obT])
                for g in range(2):
                    h = c + 3 * g
                    ob, obT = obs[g]
                    P.op("dve", CP(OT[0:65, 0:nq], ob[0:65, 0:nq]), reads=[obT], writes=[tOT])
                    tp, tpT = PSF[g], PSFT[g]
                    for ti in range(n):
                        P.op("pe", TR(tp[:, ti * 65:(ti + 1) * 65], OT[0:65, ti * 128:(ti + 1) * 128], IDF[0:65, 0:65]), reads=[tOT, tCONST],
                             writes=[tpT] if ti == 0 else (), pwrites=() if ti == 0 else [tpT])
                    tpv = tp[:, 0:65 * n].rearrange("p (t d) -> p t d", t=n)
                    P.op("dve", (lambda o, i_: (lambda e: e.reciprocal(out=o, in_=i_)))(SS[:, 0:n], tpv[:, :, 64]), reads=[tpT], writes=[tSS])
                    P.op("dve", TT(OATT[:, 0:n, h * 64:(h + 1) * 64], tpv[:, :, 0:64],
                                   SS[:, 0:n].unsqueeze(2).to_broadcast([128, n, 64]), ALU.mult), reads=[tpT, tSS], pwrites=[tW4A, tZG])
            for ti in range(n):
                rms_rstd(OATT[:, ti, :], [tW4A, tZG], 1, 384, TMPA, tTMPA)
                P.op("dve", TS(MERGB[:, ti, 640:1024], OATT[:, ti, :], SS[:, 8:9], None, ALU.mult), reads=[tW4A, tZG, tSS], pwrites=[tMERG[ti]])

            if STOP[0] == 'p2gqa':
                P.barrier()
                return nc, P
            for ti in range(n):
                def evac_m(kc, pap, ppT, ti=ti):
                    P.op("dve", TS(HT4[:, ti, kc, :], pap, GOUT[:, l * 8 + kc:l * 8 + kc + 1], None, ALU.mult), reads=[ppT, tCONST], pwrites=[tHT[ti]])
                transpose_tile(MERGB[:, ti, :], [tMERG[ti]], 8, None, None, evac_m)
            wo0, wo0T = load_wblock(wo[l * 2 + 0])
            wo1, wo1T = load_wblock(wo[l * 2 + 1])
            for ti, i in enumerate(tiles):
                for nb, (wb_, wbT_) in enumerate(((wo0, wo0T), (wo1, wo1T))):
                    ps, pT = PSF[3 + nb], PSFT[3 + nb]
                    for kc in range(8):
                        P.op("pe", MM(ps[:, :], HT4[:, ti, kc, :], wb_[:, kc, :], kc == 0, kc == 7), reads=[tHT[ti], wbT_],
                             writes=[pT] if kc == 0 else (), pwrites=() if kc == 0 else [pT])
                deepnorm(i, l, lambda nb: PSF[3 + nb][:, :], [PSFT[3], PSFT[4]])
        P.barrier()
        if STOP[0] == 'p2':
            return nc, P

        load_ln(l, 1)
        sts = [list(range(4 * st, 4 * st + 4)) for st in range(4)]
        if l < nlayers - 1:
            sts.append([16, 17])
        HTB = [(HT4, tHT), (HT4b, tHTb)]

        def p3_ln(i, j):
            w2, w2T, mv, mvT, st6, st6T = W2P[j], tW2P[j], MVP[j], tMVP[j], ST6P[j], tST6P[j]
            for ci in range(2):
                P.op("dve", (lambda o, i_: (lambda e: e.bn_stats(out=o, in_=i_)))(st6[:, ci * 6:(ci + 1) * 6], X[:, i, ci * 512:(ci + 1) * 512]),
                     reads=[XT[i]], writes=[st6T] if ci == 0 else (), pwrites=() if ci == 0 else [st6T])
            P.op("dve", (lambda o, i_: (lambda e: e.bn_aggr(out=o, in_=i_)))(mv[:, 0:2], st6[:, 0:12].rearrange('p (n s) -> p n s', s=6)), reads=[st6T], writes=[mvT])
            P.op("dve", TS(mv[:, 2:3], mv[:, 1:2], EPS, None, ALU.add), reads=[mvT], writes=[mvT])
            P.op("act", ACT(mv[:, 2:3], mv[:, 2:3], AF.Ln), reads=[mvT], writes=[mvT])
            P.op("act", ACT(mv[:, 3:4], mv[:, 2:3], AF.Exp, scale=-0.5), reads=[mvT], writes=[mvT])
            P.op("pool", TS(w2[:, :], X[:, i, :], mv[:, 0:1], mv[:, 3:4], ALU.subtract, ALU.mult), reads=[XT[i], mvT], writes=[w2T])

        def p3_tr(i, j, hb, ti):
            hbuf, hT = hb
            s_ = 1 if i >= 16 else 0

            def evac(kc, pap, pT):
                sc = MODF[:, l, 24 + kc, s_:s_ + 1]
                sh = MODF[:, l, 16 + kc, s_:s_ + 1]
                P.op("dve", TS(hbuf[:, ti, kc, :], pap, sc, sh, ALU.mult, ALU.add), reads=[pT, tMODF], pwrites=[hT[ti]])
            transpose_tile(W2P[j], [tW2P[j]], 8, None, None, evac)

        def p3_dn(i, ti):
            deepnorm_from(i, Y4[:, ti, :], [tY4[ti]], ew="pool")
            if l == nlayers - 1 and i < 16:
                P.op("sp", DMA(y_out[i], X[:, i, :]), reads=[XT[i]], pwrites=[tY], dma="yout")

        for ti, i in enumerate(sts[0]):
            p3_ln(i, ti % 2)
            p3_tr(i, ti % 2, HTB[0], ti)
        for si, tiles in enumerate(sts):
            n = len(tiles)
            ctx = tiles[0] >= 16
            hbuf, hT = HTB[si % 2]
            prev = sts[si - 1] if si > 0 else []
            nxt = sts[si + 1] if si + 1 < len(sts) else []
            load_bc(l, 1, ctx)
            for nb in range(11):
                wb_, wbT_ = load_wblock(wffi[l * 11 + nb])
                for ti, i in enumerate(tiles):
                    ps, pT = next_ps(0, 3)
                    for kc in range(8):
                        P.op("pe", MM(ps[:, :], hbuf[:, ti, kc, :], wb_[:, kc, :], kc == 0, kc == 7), reads=[hT[ti], wbT_],
                             writes=[pT] if kc == 0 else (), pwrites=() if kc == 0 else [pT])
                    P.op("act", ACT(TMPA[:, 0:256], ps[:, 0:256], AF.Silu), reads=[pT], writes=[tTMPA])
                    P.op("dve", TT(GTOK4[:, ti, nb * 256:(nb + 1) * 256], TMPA[:, 0:256], ps[:, 256:512], ALU.mult), reads=[tTMPA, pT], pwrites=[tG4[ti]])
                if nb % 2 == 0 and nb // 2 < len(prev):
                    p3_dn(prev[nb // 2], nb // 2)
                if nb % 2 == 1 and nb // 2 < len(nxt):
                    p3_ln(nxt[nb // 2], (nb // 2) % 2)
                if nb >= 3 and nb % 2 == 1 and (nb - 3) // 2 < len(nxt):
                    t2 = (nb - 3) // 2
                    p3_tr(nxt[t2], t2 % 2, HTB[(si + 1) % 2], t2)
            for t2 in range(len(nxt)):
                if 3 + 2 * t2 > 10:
                    p3_tr(nxt[t2], t2 % 2, HTB[(si + 1) % 2], t2)
            for ti, i in enumerate(tiles):
                GT = GTTa if ti < 2 else GTTb
                for grp in range(3):
                    nch = 8 if grp < 2 else 6

                    def evac_gg(kc, pap, ppT, grp=grp, ti=ti, GT=GT):
                        kk = grp * 8 + kc
                        P.op("dve", CP(GT[:, ti % 2, kk, :], pap), reads=[ppT], pwrites=[tGTT4[ti]])
                    transpose_tile(GTOK4[:, ti, grp * 1024:grp * 1024 + nch * 128], [tG4[ti]], nch, None, None, evac_gg)
            for nb2 in range(2):
                for kg in range(3):
                    nch = 8 if kg < 2 else 6
                    wb_, wbT_ = load_wblock(wffo[l * 6 + nb2 * 3 + kg])
                    for ti, i in enumerate(tiles):
                        GT = GTTa if ti < 2 else GTTb
                        ps, pT = PSF[ti], PSFT[ti]
                        for kcl in range(nch):
                            kc = kg * 8 + kcl
                            P.op("pe", MM(ps[:, :], GT[:, ti % 2, kc, :], wb_[:, kcl, :], kc == 0, kc == 21), reads=[tGTT4[ti], wbT_],
                                 writes=[pT] if kc == 0 else (), pwrites=() if kc == 0 else [pT])
                for ti, i in enumerate(tiles):
                    P.op("dve", TT(Y4[:, ti, nb2 * 512:(nb2 + 1) * 512], PSF[ti][:, :], BC[:, 0, nb2 * 512:(nb2 + 1) * 512], ALU.mult),
                         reads=[PSFT[ti], tBC[0]], pwrites=[tY4[ti]])
        for ti, i in enumerate(sts[-1]):
            p3_dn(i, ti)
        P.barrier()

    P.op("sp", None, reads=[tY])
    return nc, P


def finish(nc, P):
    from contextlib import ExitStack
    keys = list(ENGS) + ["dma:" + k for k in P.dma_cnt]
    with ExitStack() as es:
        sems = {k: es.enter_context(nc.semaphore(k.replace(":", "_"))) for k in keys}
        block = es.enter_context(nc.Block())
        P.emit(nc, block, sems)
    return nc


def _blocks(w, nblk, width=512):
    K = w.shape[0]
    kc = K // 128
    return np.ascontiguousarray(w.reshape(kc, 128, nblk, width).transpose(2, 1, 0, 3)).reshape(nblk, 128, kc * width)


def prep_shared(inp, NL):
    L = 4
    f = np.float32
    w_mod, b_mod, w_in = inp["w_mod"], inp["b_mod"], inp["w_in"]
    sh = {}
    sh["wmod"] = np.concatenate([_blocks(w_mod[l], 12) for l in range(NL)], 0)
    bases = [0, 1024, 3072, 4096]
    bt = np.zeros((L, 128, 32), f)
    for l in range(L):
        for j, b0 in enumerate(bases):
            bt[l, :, j * 8:(j + 1) * 8] = b_mod[l, b0:b0 + 1024].reshape(8, 128).T
    sh["bmodT"] = bt
    sh["bmodg"] = np.ascontiguousarray(np.stack([np.stack([b_mod[l, 2048:3072], b_mod[l, 5120:6144]]) for l in range(L)]).reshape(L * 2, 1024))
    perm = np.concatenate([np.arange(h * 64, (h + 1) * 64) for h in (0, 3, 1, 4, 2, 5)])
    wl = []
    for l in range(NL):
        w = w_in[l]
        qa, ka, va, zb, qc, kc_, vc = w[:, 0:384], w[:, 384:768], w[:, 768:1152], w[:, 1152:1664], w[:, 1664:2048], w[:, 2048:2176], w[:, 2176:2304]
        z128 = np.zeros((1024, 128), f)
        blks = [np.concatenate([va, vc], 1), np.concatenate([ka, kc_], 1), np.concatenate([qa, z128], 1),
                np.concatenate([qc[:, perm], z128], 1), zb]
        wl.append(_blocks(np.concatenate(blks, 1), 5))
    sh["win"] = np.concatenate(wl, 0)
    sh["wo"] = np.concatenate([_blocks(inp["w_o"][l], 2) for l in range(NL)], 0)
    wl = []
    for l in range(NL):
        w = inp["w_ffn_in"][l]
        a, b = w[:, :2816].reshape(1024, 11, 256), w[:, 2816:].reshape(1024, 11, 256)
        wl.append(_blocks(np.concatenate([a, b], 2).reshape(1024, 11 * 512), 11))
    sh["wffi"] = np.concatenate(wl, 0)
    wl = []
    for l in range(NL):
        w = np.zeros((3072, 1024), f)
        w[:2816] = inp["w_ffn_out"][l]
        wl.append(np.ascontiguousarray(w.reshape(3, 8, 128, 2, 512).transpose(3, 0, 2, 1, 4)).reshape(6, 128, 4096))
    sh["wffo"] = np.concatenate(wl, 0)
    sh["lnp"] = np.ascontiguousarray(np.stack([np.stack([inp["ln1_g"][l], inp["ln1_b"][l], inp["ln2_g"][l], inp["ln2_b"][l]]) for l in range(L)]).reshape(L * 4, 1024))
    sh["gout"] = np.ascontiguousarray(np.concatenate([inp["g_out"][l].reshape(8, 128).T for l in range(L)], 1))
    sh["gsgu"] = np.ascontiguousarray(inp["g_sgu"])
    sh["gqk"] = np.ascontiguousarray(np.concatenate([inp["g_q"], inp["g_k"]], 1))
    sh["wsT"] = np.ascontiguousarray(np.stack([inp["w_s"][l].transpose(2, 0, 1).reshape(128, 512) for l in range(L)]))
    sh["bsd"] = np.ascontiguousarray(np.concatenate([inp["b_s"][l].T for l in range(L)], 1))
    sh["identd"] = np.eye(128, dtype=f)
    return sh


def rope_table(tok):
    f = np.float32
    row = (tok // 64).astype(f)
    col = (tok % 64).astype(f)
    inv = (1.0 / (np.float32(10000.0) ** (np.arange(16, dtype=f) / np.float32(16)))).astype(f)
    ar = row[:, None] * inv[None, :]
    ac = col[:, None] * inv[None, :]
    return np.concatenate([np.cos(ar), np.cos(ac), np.sin(ar), np.sin(ac)], 1).astype(f)


def nab_tables(rpb, qi, L):
    out = np.full((L, 5, 128, 6, 6, 128), np.float32(-30000.0), np.float32)
    p = np.arange(128)
    for ci, i in enumerate((0, 1, 2, 14, 15)):
        G = 16 * qi + i
        r = 2 * G + p // 64
        c = p % 64
        rs = np.clip(r - 4, 0, 120)
        cs = np.clip(c - 8, 0, 48)
        for b in range(6):
            if b < 5:
                Gk = G - 2 + b
            elif i == 0:
                Gk = G + 3
            elif i == 15:
                Gk = G - 3
            else:
                continue
            if Gk < 0 or Gk > 63:
                continue
            kr = 2 * Gk + p // 64
            kc = p % 64
            ok = (kr[:, None] >= rs[None, :]) & (kr[:, None] < rs[None, :] + 8) & (kc[:, None] >= cs[None, :]) & (kc[:, None] < cs[None, :] + 16)
            dr = np.clip(kr[:, None] - r[None, :] + 7, 0, 14)
            dc = np.clip(kc[:, None] - c[None, :] + 15, 0, 30)
            for l in range(L):
                vals = rpb[l][:, dr, dc]
                out[l, ci, :, :, b, :] = np.where(ok[None], vals, np.float32(-30000.0)).transpose(1, 0, 2)
    return out.reshape(L * 5, 128, 4608)


_CACHE = {}


def kernel(**inp):
    inp = {k: np.asarray(v, dtype=np.float32) for k, v in inp.items()}
    if "nc" not in _CACHE:
        nc, P = build(NLAYERS_BUILD)
        _CACHE["nc"] = finish(nc, P)
    nc = _CACHE["nc"]
    import time as _t
    t0 = _t.time()
    NL = NLAYERS_BUILD
    sh = prep_shared(inp, NL)
    print("prep shared", _t.time() - t0, flush=True)
    in_maps = []
    for core in range(8):
        b, qi = core // 4, core % 4
        m = dict(sh)
        xs = inp["x"][b, 2048 * qi:2048 * (qi + 1)].reshape(16, 128, 1024)
        m["x_in"] = np.ascontiguousarray(np.concatenate([xs, inp["ctx"][b].reshape(2, 128, 1024)], 0))
        cv = np.empty((128, 16), np.float32)
        cv[:, 0::2] = inp["c"][b].reshape(8, 128).T
        cv[:, 1::2] = inp["c_ctx"].reshape(8, 128).T
        m["cvec"] = cv
        rp = np.empty((NT, 128, 64), np.float32)
        for i in range(16):
            rp[i] = rope_table(2048 * qi + 128 * i + np.arange(128))
        rp[16:, :, 0:32] = 1.0
        rp[16:, :, 32:64] = 0.0
        m["rope"] = rp
        m["nab"] = nab_tables(inp["rpb"], qi, NL)
        in_maps.append(m)
    print("prep all", _t.time() - t0, flush=True)
    res = run_bass_kernel_spmd(nc, in_maps, core_ids=list(range(8)))
    print("run done", _t.time() - t0, flush=True)
    out = np.empty((2, 8192, 1024), np.float32)
    for core in range(8):
        b, qi = core // 4, core % 4
        out[b, 2048 * qi:2048 * (qi + 1)] = np.asarray(res.results[core]["y_out"]).reshape(2048, 1024)
    return out
```

```python
import numpy as np
import concourse.bass as bass
import concourse.mybir as mybir
from concourse.bass_utils import run_bass_kernel_spmd

F32 = mybir.dt.float32
BF16 = mybir.dt.bfloat16
AF = mybir.ActivationFunctionType
ALU = mybir.AluOpType
AX = mybir.AxisListType

L = 4
NT = 18
ALPHA = float(8.0 ** 0.25)
EPS = 1e-6
NLAYERS_BUILD = L


class T:
    __slots__ = ("name", "w", "r")

    def __init__(self, name):
        self.name = name
        self.w = {}
        self.r = {}


ENGS = ["pe", "dve", "act", "pool", "sp"]
PIDV = [None]


class Plan:
    def __init__(self):
        self.ops = {e: [] for e in ENGS}
        self.known = {e: {} for e in ENGS}
        self.dma_cnt = {}
        self.waited = {e: set() for e in ENGS}

    def _res(self, hs):
        out = []
        for t in hs:
            r = t.resolve() if hasattr(t, "resolve") else t
            if isinstance(r, (list, tuple)):
                out.extend(r)
            else:
                out.append(r)
        return out

    def record_begin(self):
        self._rec = []

    def record_end(self):
        r, self._rec = self._rec, None
        return r

    def replay_interleaved(self, lists):
        its = [list(l) for l in lists]
        pos = [0] * len(its)
        while any(pos[j] < len(its[j]) for j in range(len(its))):
            for j in range(len(its)):
                if pos[j] < len(its[j]):
                    self.op(*its[j][pos[j]])
                    pos[j] += 1

    def op(self, eng, fn, reads=(), writes=(), pwrites=(), dma=None):
        reads, writes, pwrites = self._res(reads), self._res(writes), self._res(pwrites)
        if getattr(self, "_rec", None) is not None:
            self._rec.append((eng, fn, reads, writes, pwrites, dma))
            return
        idx = len(self.ops[eng])
        if dma is None:
            tok = (eng, idx + 1)
        else:
            self.dma_cnt[dma] = self.dma_cnt.get(dma, 0) + 1
            tok = ("dma:" + dma, self.dma_cnt[dma])
        need = {}

        def addw(d):
            for k, v in d.items():
                if need.get(k, 0) < v:
                    need[k] = v

        for t in reads:
            addw(t.w)
        for t in writes:
            addw(t.w)
            addw(t.r)
        for t in pwrites:
            addw(t.r)
        waits = []
        kn = self.known[eng]
        for k, v in need.items():
            if kn.get(k, 0) < v:
                kn[k] = v
                waits.append((k, v))
                if not k.startswith("dma:"):
                    self.waited[k].add(v)
        if fn is not None:
            for t in reads:
                if t.r.get(tok[0], 0) < tok[1]:
                    t.r[tok[0]] = tok[1]
            for t in writes:
                t.w = {tok[0]: tok[1]}
                t.r = {}
            for t in pwrites:
                if t.r:
                    t.w = {tok[0]: tok[1]}
                    t.r = {}
                else:
                    t.w[tok[0]] = max(t.w.get(tok[0], 0), tok[1])
        self.ops[eng].append((waits, fn, dma))

    def barrier(self):
        latest = {}
        for e in ENGS:
            n = len(self.ops[e])
            if n:
                latest[e] = n
        for k, v in self.dma_cnt.items():
            latest["dma:" + k] = v
        for e in ENGS:
            waits = []
            kn = self.known[e]
            for k, v in latest.items():
                if k == e:
                    continue
                if not k.startswith("dma:"):
                    vv = v
                    while vv > 0 and (self.ops[k][vv - 1][1] is None or self.ops[k][vv - 1][2] is not None):
                        vv -= 1
                    if vv == 0:
                        continue
                    v = vv
                if kn.get(k, 0) < v:
                    kn[k] = v
                    waits.append((k, v))
                    if not k.startswith("dma:"):
                        self.waited[k].add(v)
            self.ops[e].append((waits, None, None))

    def emit(self, nc, block, sems):
        rank = {}
        for e in ENGS:
            s = sorted(self.waited[e])
            rank[e] = {v: i + 1 for i, v in enumerate(s)}
        plan = self

        def run(eng_name):
            def body(e):
                if eng_name == "pool":
                    PIDV[0] = e.partition_id()
                for i, (waits, fn, dma) in enumerate(plan.ops[eng_name]):
                    for k, v in waits:
                        if k.startswith("dma:"):
                            if k.startswith("dma:cc"):
                                e.wait_ge(sems[k], 1)
                            else:
                                e.wait_ge(sems[k], 16 * v)
                        else:
                            e.wait_ge(sems[k], rank[k][v])
                    if fn is None:
                        continue
                    ins = fn(e)
                    if dma is not None:
                        if dma.startswith("cc"):
                            ins.then_inc(sems["dma:" + dma])
                        else:
                            ins.then_inc(sems["dma:" + dma], 16)
                    elif (i + 1) in rank[eng_name]:
                        ins.then_inc(sems[eng_name], 1)
            return body

        block.tensor(run("pe"))
        block.vector(run("dve"))
        block.scalar(run("act"))
        block.gpsimd(run("pool"))
        block.sync(run("sp"))


def MM(out, l, r, st, sp):
    return lambda e: e.matmul(out, lhsT=l, rhs=r, start=st, stop=sp)


def TR(out, in_, ident):
    return lambda e: e.transpose(out=out, in_=in_, identity=ident)


def ACT(out, in_, func, scale=None, bias=None):
    kw = {}
    if scale is not None:
        kw["scale"] = scale
    if bias is not None:
        kw["bias"] = bias
    return lambda e: e.activation(out=out, in_=in_, func=func, **kw)


def TS(out, in0, s1, s2, op0, op1=None):
    if op1 is None:
        return lambda e: e.tensor_scalar(out=out, in0=in0, scalar1=s1, scalar2=None, op0=op0)
    return lambda e: e.tensor_scalar(out=out, in0=in0, scalar1=s1, scalar2=s2, op0=op0, op1=op1)


def TT(out, in0, in1, op):
    return lambda e: e.tensor_tensor(out=out, in0=in0, in1=in1, op=op)


def STT(out, in0, scalar, in1, op0, op1):
    return lambda e: e.scalar_tensor_tensor(out=out, in0=in0, scalar=scalar, in1=in1, op0=op0, op1=op1)


def CP(out, in_):
    return lambda e: e.tensor_copy(out=out, in_=in_)


def DMA(out, in_):
    return lambda e: e.dma_start(out=out, in_=in_)


def MEMSET(ap, v):
    return lambda e: e.memset(ap, v)


STOP = [None]


def build(nlayers=L, debug_x=False):
    nc = bass.Bass("TRN2", target_bir_lowering=False)
    P = Plan()

    def din(name, shape):
        return nc.dram_tensor(name, shape, F32, kind="ExternalInput").ap()

    x_in = din("x_in", [NT, 128, 1024])
    cvec = din("cvec", [128, 16])
    wmod = din("wmod", [nlayers * 12, 128, 4096])
    bmodT = din("bmodT", [L, 128, 32])
    bmodg = din("bmodg", [L * 2, 1024])
    win = din("win", [nlayers * 5, 128, 4096])
    wo = din("wo", [nlayers * 2, 128, 4096])
    wffi = din("wffi", [nlayers * 11, 128, 4096])
    wffo = din("wffo", [nlayers * 6, 128, 4096])
    lnp = din("lnp", [L * 4, 1024])
    gout = din("gout", [128, L * 8])
    gsgu = din("gsgu", [L, 256])
    gqk = din("gqk", [L, 128])
    wsT = din("wsT", [L, 128, 512])
    bsd = din("bsd", [128, L * 4])
    rope = din("rope", [NT, 128, 64])
    nab = din("nab", [nlayers * 5, 128, 4608])
    identd = din("identd", [128, 128])
    y_out = nc.dram_tensor("y_out", [16, 128, 1024], F32, kind="ExternalOutput").ap()

    gst = nc.dram_tensor("gst", [L * 4, 128, 1024], F32).ap()
    hts = nc.dram_tensor("hts", [NT, 128, 1024], BF16).ap()
    nak_loc = nc.dram_tensor("nak_loc", [16, 128, 384], BF16).ap()
    nav_loc = nc.dram_tensor("nav_loc", [16, 128, 390], BF16).ap()
    ktc_loc = nc.dram_tensor("ktc_loc", [128, 2048], BF16)
    ktc_all = nc.dram_tensor("ktc_all", [512, 2048], BF16)
    vc_loc = nc.dram_tensor("vc_loc", [2048, 130], BF16)
    vc_all = nc.dram_tensor("vc_all", [8192, 130], BF16)
    nak_edge = nc.dram_tensor("nak_edge", [512, 384], BF16)
    nak_eall = nc.dram_tensor("nak_eall", [2048, 384], BF16)
    nav_edge = nc.dram_tensor("nav_edge", [512, 390], BF16)
    nav_eall = nc.dram_tensor("nav_eall", [2048, 390], BF16)

    off = [16512]
    OFF = {}
    LIMIT = 229344

    def sb(name, shape, dt, at=None):
        nbytes = int(np.prod(shape[1:])) * (4 if dt == F32 else 2)
        nbytes = (nbytes + 63) // 64 * 64
        if at is None:
            o = off[0]
            off[0] += nbytes
            assert off[0] <= LIMIT, (name, off[0])
        else:
            o = at
        t = nc.alloc_sbuf_tensor_at(name, shape, dt, offset=o)
        OFF[name] = o
        return t

    X = sb("X", [128, NT, 1024], F32)
    KTC = sb("KTC", [128, 8448], BF16)
    VC = sb("VC", [128, 66, 130], BF16)
    NAK = sb("NAK", [128, 3, 1280], BF16)
    NAV = sb("NAV", [128, 10, 390], BF16)
    NAB = sb("NAB", [128, 4608], BF16)
    BC = sb("BC", [128, 3, 1024], F32)
    WB = [sb(f"WB{i}", [128, 8, 512], BF16) for i in range(2)]
    HT4 = sb("HT4", [128, 4, 8, 128], BF16)
    MERGB = sb("MERGB", [128, 4, 1024], BF16)
    att0 = off[0]
    QTA = sb("QTA", [128, 3, 512], BF16)
    QTC = sb("QTC", [128, 3, 512], BF16)
    PT = [sb(f"PT{i}", [128, 512], BF16) for i in range(2)]
    PTN = sb("PTN", [128, 1024], BF16)
    OT = sb("OT", [128, 768], F32)
    att1 = off[0]
    assert att0 + 11264 <= att1
    GTOK4 = sb("GTOK4", [128, 4, 2816], BF16, at=OFF["KTC"])
    GTTa = sb("GTTa", [128, 2, 22, 128], BF16, at=OFF["KTC"] + 22528)
    GTTb = sb("GTTb", [128, 2, 22, 128], BF16, at=att0)
    assert 22528 + 11264 <= 16896 + 17160
    Y4 = sb("Y4", [128, 4, 1024], F32, at=OFF["NAK"])
    assert OFF["NAB"] + 9216 - OFF["NAK"] >= 16384
    WBX = [sb(f"WBX{i}", [128, 8, 512], BF16, at=OFF["KTC"] + 8192 * i) for i in range(4)]
    RLAT = sb("RLAT", [128, 8, 128], BF16, at=att0)
    RCTX = sb("RCTX", [128, 8, 128], BF16, at=att0 + 2048)
    W4A = sb("W4A", [128, 1024], F32)
    ZG = sb("ZG", [128, 512], F32)
    OATT = sb("OATT", [128, 4, 384], F32, at=OFF["W4A"])
    ZG_1 = sb("ZG1", [128, 512], F32, at=OFF["W4A"])
    W2A_0 = sb("W2A", [128, 1024], BF16)
    TMPA_0 = sb("TMPA", [128, 512], F32)
    TMPB_0 = sb("TMPB", [128, 512], F32)
    KQB_0 = sb("KQB", [128, 512], BF16)
    TMPA_1 = sb("TMPA1", [128, 512], F32, at=OFF["PTN"])
    TMPB_1 = sb("TMPB1", [128, 512], F32, at=OFF["OT"])
    KQB_1 = sb("KQB1", [128, 512], BF16, at=OFF["PT0"])
    W2A_1 = sb("W2A1", [128, 1024], BF16, at=OFF["QTA"])
    KAT_1 = sb("KAT1", [128, 4, 128], BF16, at=OFF["QTA"] + 2048)
    VAT_1 = sb("VAT1", [128, 6, 65], BF16, at=OFF["QTC"])
    VCT_1 = sb("VCT1", [128, 2, 65], BF16, at=OFF["QTC"] + 896)
    ROPE_1 = sb("ROPE1", [128, 64], F32, at=OFF["PT1"])
    SS_1 = sb("SS1", [128, 16], F32, at=OFF["PT1"] + 256)
    MV_1 = sb("MV1", [128, 8], F32, at=OFF["PT1"] + 320)
    ST6_1 = sb("ST61", [128, 12], F32, at=OFF["PT1"] + 384)
    VAT_0 = sb("VAT", [128, 6, 65], BF16)
    VCT_0 = sb("VCT", [128, 2, 65], BF16)
    KAT_0 = sb("KAT", [128, 4, 128], BF16)
    IDB = sb("IDB", [128, 128], BF16)
    IDF = sb("IDF", [128, 128], F32)
    ROPE_0 = sb("ROPE", [128, 64], F32)
    S2F = sb("S2F", [128, 16], F32)
    S2B = sb("S2B", [128, 8, 2], BF16)
    ONESB = sb("ONESB", [128, 128], BF16)
    MODF = sb("MODF", [128, L, 32, 2], F32)
    BMT = sb("BMT", [128, 32], F32)
    GOUT = sb("GOUT", [128, L * 8], F32)
    GSGU = sb("GSGU", [128, 256], F32)
    GQK = sb("GQK", [128, 128], F32)
    WST = sb("WST", [128, 512], BF16)
    BS = sb("BS", [128, L * 4], F32)
    ST6_0 = sb("ST6", [128, 12], F32)
    MV_0 = sb("MV", [128, 8], F32)
    SS_0 = sb("SS", [128, 16], F32)
    print("SBUF used", off[0], "of", LIMIT)

    PSF = [nc.alloc_psum_tensor(f"PSF{i}", [128, 512], F32) for i in range(6)]
    PST = [nc.alloc_psum_tensor(f"PST{i}", [128, 1024], BF16) for i in range(2)]
    PSFT = [T(f"psf{i}") for i in range(8)]
    pst_rr = [0]

    XT = [T(f"x{i}") for i in range(NT)]
    tKTC, tVC, tNAK, tNAV, tNAB = T("ktc"), T("vc"), T("nak"), T("nav"), T("nab")
    tBC = [T("bc0"), T("bc1"), T("bc2")]
    tWB = [T("wb0"), T("wb1")]
    tHT = [T(f"ht{i}") for i in range(4)]
    tMERG = [T(f"mg{i}") for i in range(4)]
    tQTA, tQTC = T("qta"), T("qtc")
    tPT = [T("pt0"), T("pt1")]
    tPTN, tOT = T("ptn"), T("ot")
    tPTNb = T("ptnb")
    tGTOK = T("gtok")
    tY4 = [T(f"y4{i}") for i in range(4)]
    tG4 = [T(f"g4{i}") for i in range(4)]
    tGTT4 = [T(f"gtt{i}") for i in range(4)]
    tW4A, tZG = T("w4a"), T("zg")
    CUR = [0]

    class BufP:
        def __init__(self, bufs):
            self.bufs = bufs

        def __getitem__(self, k):
            return self.bufs[CUR[0]][k]

    class HP:
        def __init__(self, hs):
            self.hs = hs

        def resolve(self):
            return self.hs[CUR[0]]
    TMPA, TMPB, KQB = BufP([TMPA_0, TMPA_1]), BufP([TMPB_0, TMPB_1]), BufP([KQB_0, KQB_1])
    W2A, KAT, VAT, VCT = BufP([W2A_0, W2A_1]), BufP([KAT_0, KAT_1]), BufP([VAT_0, VAT_1]), BufP([VCT_0, VCT_1])
    ILV = [False]
    ZGp = BufP([ZG, ZG_1])
    tZGp = HP([tZG, tW4A])
    tVAT_0, tVCT_0, tKAT_0, tW2A_0 = T("vat"), T("vct"), T("kat"), T("w2a")
    tCONST, tMODF, tLAYC = T("const"), T("modf"), T("layc")
    tW2A = HP([tW2A_0, tQTA])
    tKAT = HP([tKAT_0, tQTA])
    tVAT = HP([tVAT_0, tQTC])
    tVCT = HP([tVCT_0, tQTC])
    tTMPA = HP([T("tmpa"), [tPTN, tPTNb]])
    tTMPB = HP([T("tmpb"), tOT])
    tKQB = HP([T("kqb"), tPT[0]])
    tROPE = HP([T("rope"), tPT[1]])
    tST6 = HP([T("st6"), tPT[1]])
    tMV = HP([T("mv"), tPT[1]])
    tSS = HP([T("ss"), tPT[1]])
    ROPE, SS, MV, ST6 = BufP([ROPE_0, ROPE_1]), BufP([SS_0, SS_1]), BufP([MV_0, MV_1]), BufP([ST6_0, ST6_1])
    tGST, tHTS = T("gst"), [T(f"hts{i}") for i in range(NT)]
    tNAKL, tNAVL, tKTCL, tVCL, tNAKE, tNAVE = T("nakl"), T("navl"), T("ktcl"), T("vcl"), T("nake"), T("nave")
    tKTCA, tVCA, tNAKEA, tNAVEA = T("ktca"), T("vca"), T("nakea"), T("navea")
    tY = T("y")

    wb_rr = [0]
    WBALL = WB + WBX
    tWBALL = tWB + [T(f"wbx{i}") for i in range(4)]
    wb_pool = [6]

    def load_wblock(src_ap):
        i = wb_rr[0] % wb_pool[0]
        wb_rr[0] += 1
        P.op("pool", DMA(WBALL[i][:].rearrange("p a b -> p (a b)"), src_ap), writes=[tWBALL[i]], dma=f"wb{i}")
        return WBALL[i], tWBALL[i]

    ps_rr = [0]

    ps_set_rr = [0, 0]

    def next_ps(lo=0, hi=3):
        if ILV[0]:
            c = CUR[0]
            i = 2 * c + ps_set_rr[c] % 2
            ps_set_rr[c] += 1
            return PSF[i], PSFT[i]
        i = lo + ps_rr[0] % (hi - lo)
        ps_rr[0] += 1
        return PSF[i], PSFT[i]

    def ln_stats(src, srcT, n):
        nch = max(1, n // 512)
        w = n // nch
        for ci in range(nch):
            P.op("dve", (lambda o, i: (lambda e: e.bn_stats(out=o, in_=i)))(ST6[:, ci * 6:(ci + 1) * 6], src[:, ci * w:(ci + 1) * w]),
                 reads=srcT, writes=[tST6] if ci == 0 else (), pwrites=() if ci == 0 else [tST6])
        P.op("dve", (lambda o, i: (lambda e: e.bn_aggr(out=o, in_=i)))(MV[:, 0:2], ST6[:, 0:6 * nch].rearrange('p (n s) -> p n s', s=6)), reads=[tST6], writes=[tMV])
        P.op("dve", TS(MV[:, 2:3], MV[:, 1:2], EPS, None, ALU.add), reads=[tMV], writes=[tMV])
        P.op("act", ACT(MV[:, 2:3], MV[:, 2:3], AF.Ln), reads=[tMV], writes=[tMV])
        P.op("act", ACT(MV[:, 3:4], MV[:, 2:3], AF.Exp, scale=-0.5), reads=[tMV], writes=[tMV])

    def rms_rstd(src, srcT, G, W, tmp, tmpT):
        P.op("dve", TT(tmp[:, 0:G * W], src, src, ALU.mult), reads=srcT, writes=[tmpT])
        P.op("dve", (lambda o, i: (lambda e: e.reduce_sum(out=o, in_=i, axis=AX.X)))(SS[:, 0:G], tmp[:, 0:G * W].rearrange("p (g w) -> p g w", g=G)),
             reads=[tmpT], writes=[tSS])
        P.op("dve", TS(SS[:, 0:G], SS[:, 0:G], 1.0 / W, EPS, ALU.mult, ALU.add), reads=[tSS], writes=[tSS])
        P.op("act", ACT(SS[:, 0:G], SS[:, 0:G], AF.Ln), reads=[tSS], writes=[tSS])
        P.op("act", ACT(SS[:, 8:8 + G], SS[:, 0:G], AF.Exp, scale=-0.5), reads=[tSS], writes=[tSS])


    def qk_norm_rope(src, srcT, H, gain, dst, oscale, dstT, full):
        n = H * 64
        P.op("act", ACT(TMPA[:, 0:n], src, AF.Identity), reads=srcT, writes=[tTMPA])
        rms_rstd(TMPA[:, 0:n], [tTMPA], H, 64, TMPB, tTMPB)
        v3 = lambda ap: ap.rearrange("p (h d) -> p h d", h=H)
        P.op("dve", TT(v3(TMPA[:, 0:n]), v3(TMPA[:, 0:n]), SS[:, 8:8 + H].unsqueeze(2).to_broadcast([128, H, 64]), ALU.mult), reads=[tTMPA, tSS], writes=[tTMPA])
        P.op("dve", STT(v3(TMPA[:, 0:n]), v3(TMPA[:, 0:n]), oscale, gain.unsqueeze(1).to_broadcast([128, H, 64]), ALU.mult, ALU.mult),
             reads=[tTMPA, tLAYC], writes=[tTMPA])
        v5 = lambda ap: ap.rearrange("p (h a s f) -> p h a s f", h=H, a=2, s=2)
        x1 = v5(TMPA[:, 0:n])[:, :, :, 0, :]
        x2 = v5(TMPA[:, 0:n])[:, :, :, 1, :]
        d1 = v5(dst)[:, :, :, 0, :]
        d2 = v5(dst)[:, :, :, 1, :]
        C = ROPE[:, 0:32].rearrange("p (a f) -> p a f", a=2).unsqueeze(1).to_broadcast([128, H, 2, 16])
        S_ = ROPE[:, 32:64].rearrange("p (a f) -> p a f", a=2).unsqueeze(1).to_broadcast([128, H, 2, 16])
        v4 = lambda ap: ap.rearrange("p (h a f) -> p h a f", h=H, a=2)
        t1 = v4(TMPB[:, 0:H * 32])
        t2 = v4(TMPB[:, H * 32:H * 64])
        P.op("dve", TT(t1, x1, C, ALU.mult), reads=[tTMPA, tROPE], writes=[tTMPB])
        P.op("dve", TT(t2, x2, S_, ALU.mult), reads=[tTMPA, tROPE], pwrites=[tTMPB])
        P.op("dve", TT(d1, t1, t2, ALU.subtract), reads=[tTMPB], writes=dstT if full else (), pwrites=() if full else dstT)
        P.op("dve", TT(t1, x2, C, ALU.mult), reads=[tTMPA, tROPE], writes=[tTMPB])
        P.op("dve", TT(t2, x1, S_, ALU.mult), reads=[tTMPA, tROPE], pwrites=[tTMPB])
        P.op("dve", TT(d2, t1, t2, ALU.add), reads=[tTMPB], pwrites=dstT)

    def transpose_tile(src_bf, srcT, nch, dst_fn, dstT, evac):
        if ILV[0]:
            j = CUR[0]
        else:
            j = pst_rr[0] % 2
            pst_rr[0] += 1
        psb, pT = PST[j], PSFT[6 + j]
        for kc in range(nch):
            P.op("pe", TR(psb[:, kc * 128:(kc + 1) * 128], src_bf[:, kc * 128:(kc + 1) * 128], IDB[:, :]),
                 reads=srcT + [tCONST], writes=[pT] if kc == 0 else (), pwrites=() if kc == 0 else [pT])
        for kc in range(nch):
            evac(kc, psb[:, kc * 128:(kc + 1) * 128], pT)

    P.op("sp", DMA(IDF[:, :], identd[:, :]), writes=[tCONST], dma="c0")
    P.op("sp", DMA(S2F[:, :], cvec[:, :]), pwrites=[tCONST], dma="c0")
    P.op("sp", DMA(GOUT[:, :], gout[:, :]), pwrites=[tCONST], dma="c0")
    P.op("sp", DMA(BS[:, :], bsd[:, :]), pwrites=[tCONST], dma="c0")
    for i in range(NT):
        P.op("sp", DMA(X[:, i, :], x_in[i]), writes=[XT[i]], dma="xin")
    P.op("dve", CP(IDB[:, :], IDF[:, :]), reads=[tCONST], pwrites=[tCONST])
    P.op("dve", MEMSET(ONESB[:, :], 1.0), pwrites=[tCONST])
    P.op("act", ACT(S2F[:, :], S2F[:, :], AF.Silu), reads=[tCONST], writes=[tCONST])
    P.op("dve", CP(S2B[:].rearrange("p a b -> p (a b)"), S2F[:, :]), reads=[tCONST], writes=[tCONST])
    for kc in range(8):
        P.op("dve", TS(RLAT[:, kc, :], ONESB[:, :], S2F[:, 2 * kc:2 * kc + 1], None, ALU.mult), reads=[tCONST], pwrites=[tGTOK])
        P.op("dve", TS(RCTX[:, kc, :], ONESB[:, :], S2F[:, 2 * kc + 1:2 * kc + 2], None, ALU.mult), reads=[tCONST], pwrites=[tGTOK])
    P.op("dve", MEMSET(VAT[:, :, :], 1.0), writes=[tVAT])
    P.op("dve", MEMSET(VCT[:, :, :], 1.0), writes=[tVCT])

    for l in range(nlayers):
        P.op("sp", DMA(BMT[:, :], bmodT[l]), writes=[tLAYC], dma="c1")
        psm, psmT = PSF[5], PSFT[5]
        for jj, nbs in enumerate([(0, 1), (2, 3), (6, 7), (8, 9)]):
            for half, nb in enumerate(nbs):
                wbuf, wT = load_wblock(wmod[l * 12 + nb])
                for oc4 in range(4):
                    col = (jj * 8 + half * 4 + oc4) * 2
                    for kc in range(8):
                        first = (jj == 0 and half == 0 and oc4 == 0 and kc == 0)
                        P.op("pe", MM(psm[:, col:col + 2], wbuf[:, kc, oc4 * 128:(oc4 + 1) * 128], S2B[:, kc, :], kc == 0, kc == 7),
                             reads=[wT, tCONST], writes=[psmT] if first else (), pwrites=() if first else [psmT])
        pv = psm[:, 0:64].rearrange("p (a b) -> p a b", b=2)
        for s in range(2):
            P.op("dve", TT(MODF[:, l, :, s], pv[:, :, s], BMT[:, :], ALU.add), reads=[psmT, tLAYC], pwrites=[tMODF])
        for a0 in (8, 24):
            P.op("dve", TS(MODF[:, l, a0:a0 + 8, :], MODF[:, l, a0:a0 + 8, :], 1.0, None, ALU.add), reads=[tMODF], writes=[tMODF])
        for gi, nbs in enumerate([(4, 5), (10, 11)]):
            for half, nb in enumerate(nbs):
                wbuf, wT = load_wblock(wmod[l * 12 + nb])
                P.op("sp", DMA(TMPA[:, :], bmodg[l * 2 + gi, half * 512:(half + 1) * 512].partition_broadcast(128)),
                     writes=[tTMPA], dma="tmpa")
                for s, R in enumerate((RLAT, RCTX)):
                    ps, pT = next_ps(0, 3)
                    for kc in range(8):
                        P.op("pe", MM(ps[:, :], R[:, kc, :], wbuf[:, kc, :], kc == 0, kc == 7), reads=[wT, tGTOK],
                             writes=[pT] if kc == 0 else (), pwrites=() if kc == 0 else [pT])
                    P.op("dve", TT(TMPB[:, :], ps[:, :], TMPA[:, :], ALU.add), reads=[pT, tTMPA], writes=[tTMPB])
                    P.op("sp", DMA(gst[l * 4 + gi * 2 + s][:, half * 512:(half + 1) * 512], TMPB[:, :]), reads=[tTMPB], pwrites=[tGST], dma="gst")
    P.barrier()
    wb_pool[0] = 2
    wb_rr[0] = 0
    if STOP[0] == 'p0':
        return nc, P

    def load_bc(l, sub, ctx):
        P.op("sp", DMA(BC[:, 0, :], gst[l * 4 + sub * 2 + (1 if ctx else 0)]), reads=[tGST], writes=[tBC[0]], dma="bc0")

    def load_ln(l, sub):
        for j in range(2):
            r = l * 4 + sub * 2 + j
            P.op("sp", DMA(BC[:, 1 + j, :], lnp[r, :].partition_broadcast(128)), writes=[tBC[1 + j]], dma=f"bc{1 + j}")

    def ln_mod_transpose(i, l, which, slot):
        s = 1 if i >= 16 else 0
        ln_stats(X[:, i, :], [XT[i]], 1024)
        P.op("dve", TS(W2A[:, :], X[:, i, :], MV[:, 0:1], MV[:, 3:4], ALU.subtract, ALU.mult), reads=[XT[i], tMV], writes=[tW2A])

        def evac(kc, pap, pT):
            sc = MODF[:, l, which * 16 + 8 + kc, s:s + 1]
            sh = MODF[:, l, which * 16 + kc, s:s + 1]
            P.op("dve", TS(HT4[:, slot, kc, :], pap, sc, sh, ALU.mult, ALU.add), reads=[pT, tMODF], pwrites=[tHT[slot]])
        transpose_tile(W2A, [tW2A], 8, None, None, evac)

    def deepnorm_from(i, zsrc, zT):
        P.op("dve", STT(W4A[:, :], X[:, i, :], ALPHA, zsrc, ALU.mult, ALU.add), reads=[XT[i]] + zT, writes=[tW4A])
        ln_stats(W4A[:, :], [tW4A], 1024)
        P.op("dve", TS(W4A[:, :], W4A[:, :], MV[:, 0:1], MV[:, 3:4], ALU.subtract, ALU.mult), reads=[tW4A, tMV], writes=[tW4A])
        P.op("dve", TT(W4A[:, :], W4A[:, :], BC[:, 1, :], ALU.mult), reads=[tW4A, tBC[1]], writes=[tW4A])
        P.op("dve", TT(X[:, i, :], W4A[:, :], BC[:, 2, :], ALU.add), reads=[tW4A, tBC[2]], writes=[XT[i]])

    def deepnorm(i, l, ysrc_fn, yT):
        for nb in range(2):
            P.op("dve", TT(W4A[:, nb * 512:(nb + 1) * 512], ysrc_fn(nb), BC[:, 0, nb * 512:(nb + 1) * 512], ALU.mult),
                 reads=[yT[nb], tBC[0]], writes=[tW4A] if nb == 0 else (), pwrites=() if nb == 0 else [tW4A])
        deepnorm_from(i, W4A[:, :], [tW4A])

    for l in range(nlayers):
        P.op("sp", DMA(GSGU[:, :], gsgu[l, :].partition_broadcast(128)), writes=[tLAYC], dma="c1")
        P.op("sp", DMA(GQK[:, :], gqk[l, :].partition_broadcast(128)), pwrites=[tLAYC], dma="c1")
        P.op("pool", DMA(WST[:, :], wsT[l]), pwrites=[tLAYC], dma="c2")

        CUR[0] = 1
        P.op("dve", MEMSET(VAT[:, :, :], 1.0), writes=[tVAT])
        P.op("dve", MEMSET(VCT[:, :, :], 1.0), pwrites=[tVCT])
        CUR[0] = 0
        P.op("dve", MEMSET(VC[:, 64:66, :], 1.0), writes=[tVC])
        P.op("dve", MEMSET(NAV[:, 8:10, :], 1.0), writes=[tNAV])
        wv, wvT = load_wblock(win[l * 5 + 0])
        wk, wkT = load_wblock(win[l * 5 + 1])
        def p1_tile(i):
            lat = i < 16
            sl = i % 2
            P.op("sp", DMA(ROPE[:, :], rope[i]), writes=[tROPE], dma="rope")
            ln_mod_transpose(i, l, 0, sl)
            P.op("sp", DMA(hts[i], HT4[:, sl].rearrange("p a b -> p (a b)")), reads=[tHT[sl]], writes=[tHTS[i]], dma="hts")
            ps, pT = next_ps(0, 3)
            for kc in range(8):
                P.op("pe", MM(ps[:, :], HT4[:, sl, kc, :], wv[:, kc, :], kc == 0, kc == 7), reads=[tHT[sl], wvT],
                     writes=[pT] if kc == 0 else (), pwrites=() if kc == 0 else [pT])
            if lat:
                P.op("dve", CP(VAT[:, :, 0:64], ps[:, 0:384].rearrange("p (h d) -> p h d", h=6)), reads=[pT], pwrites=[tVAT])
                P.op("dve", CP(VCT[:, :, 0:64], ps[:, 384:512].rearrange("p (h d) -> p h d", h=2)), reads=[pT], pwrites=[tVCT])
                P.op("sp", DMA(nav_loc[i], VAT[:].rearrange("p a b -> p (a b)")), reads=[tVAT], pwrites=[tNAVL], dma="navl")
                P.op("sp", DMA(vc_loc.ap()[i * 128:(i + 1) * 128, :], VCT[:].rearrange("p a b -> p (a b)")), reads=[tVCT], pwrites=[tVCL], dma="vcl")
                if i in (0, 1, 14, 15):
                    e = i if i < 2 else i - 12
                    P.op("sp", DMA(nav_edge.ap()[e * 128:(e + 1) * 128, :], VAT[:].rearrange("p a b -> p (a b)")), reads=[tVAT], pwrites=[tNAVE], dma="nave")
            else:
                j = i - 16
                P.op("dve", CP(NAV[:, 8 + j, :].rearrange("p (h d) -> p h d", h=6)[:, :, 0:64], ps[:, 0:384].rearrange("p (h d) -> p h d", h=6)), reads=[pT], pwrites=[tNAV])
                P.op("dve", CP(VC[:, 64 + j, :].rearrange("p (h d) -> p h d", h=2)[:, :, 0:64], ps[:, 384:512].rearrange("p (h d) -> p h d", h=2)), reads=[pT], pwrites=[tVC])
            ps, pT = next_ps(0, 3)
            for kc in range(8):
                P.op("pe", MM(ps[:, :], HT4[:, sl, kc, :], wk[:, kc, :], kc == 0, kc == 7), reads=[tHT[sl], wkT],
                     writes=[pT] if kc == 0 else (), pwrites=() if kc == 0 else [pT])
            P.op("act", ACT(KQB[:, 0:384], ps[:, 0:384], AF.Identity), reads=[pT], writes=[tKQB])
            qk_norm_rope(ps[:, 384:512], [pT], 2, GQK[:, 64:128], KQB[:, 384:512], 1.0, [tKQB], False)
            kdst = []

            def evac_k(kc, pap, ppT, i=i, lat=lat):
                if kc < 3:
                    if lat:
                        P.op("dve", CP(KAT[:, kc, :], pap), reads=[ppT], pwrites=[tKAT])
                    else:
                        P.op("dve", CP(NAK[:, kc, (8 + i - 16) * 128:(9 + i - 16) * 128], pap), reads=[ppT], pwrites=[tNAK])
                else:
                    if lat:
                        P.op("dve", CP(KAT[:, 3, :], pap), reads=[ppT], pwrites=[tKAT])
                    else:
                        P.op("dve", CP(KTC[:, 8192 + (i - 16) * 128:8192 + (i - 15) * 128], pap), reads=[ppT], pwrites=[tKTC])
            transpose_tile(KQB, [tKQB], 4, None, None, evac_k)
            if lat:
                P.op("sp", DMA(nak_loc[i], KAT[:, 0:3, :].rearrange("p a b -> p (a b)")), reads=[tKAT], pwrites=[tNAKL], dma="nakl")
                P.op("sp", DMA(ktc_loc.ap()[:, i * 128:(i + 1) * 128], KAT[:, 3, :]), reads=[tKAT], pwrites=[tKTCL], dma="ktcl")
                if i in (0, 1, 14, 15):
                    e = i if i < 2 else i - 12
                    P.op("sp", DMA(nak_edge.ap()[e * 128:(e + 1) * 128, :], KAT[:, 0:3, :].rearrange("p a b -> p (a b)")), reads=[tKAT], pwrites=[tNAKE], dma="nake")

        for ia in range(0, NT, 2):
            lists = []
            for i in (ia, ia + 1):
                CUR[0] = i % 2
                ILV[0] = True
                P.record_begin()
                p1_tile(i)
                lists.append(P.record_end())
            CUR[0] = 0
            ILV[0] = False
            P.replay_interleaved(lists)

        if STOP[0] == 'p1':
            return nc, P
        def coll(src, srcT, dst, dstT, key):
            P.op("pool", (lambda s_, d_: (lambda e: e.collective_compute(
                "AllGather", ALU.bypass, replica_groups=[[0, 1, 2, 3], [4, 5, 6, 7]],
                ins=[s_.ap().opt()], outs=[d_.ap().opt()])))(src, dst), reads=[srcT], writes=[dstT], dma=key)
        coll(ktc_loc, tKTCL, ktc_all, tKTCA, f"cc{l}a")
        coll(vc_loc, tVCL, vc_all, tVCA, f"cc{l}b")
        coll(nak_edge, tNAKE, nak_eall, tNAKEA, f"cc{l}c")
        coll(nav_edge, tNAVE, nav_eall, tNAVEA, f"cc{l}d")
        def load_gathered_kv():
            for r in range(4):
                P.op("sp", DMA(KTC[:, r * 2048:(r + 1) * 2048], ktc_all.ap()[r * 128:(r + 1) * 128, :]), reads=[tKTCA], pwrites=[tKTC], dma="ktc")
            for r in range(4):
                P.op("sp", DMA(VC[:, r * 16:(r + 1) * 16, :], vc_all.ap()[r * 2048:(r + 1) * 2048, :].rearrange("(k p) c -> p k c", p=128)),
                     reads=[tVCA], pwrites=[tVC], dma="vc")

        if STOP[0] == 'ex':
            P.barrier()
            return nc, P
        load_ln(l, 0)
        cur_cls = [None]
        for st in range(5):
            tiles = list(range(4 * st, 4 * st + 4)) if st < 4 else [16, 17]
            if l == nlayers - 1 and st == 4:
                continue
            n = len(tiles)
            ctx = st == 4
            load_bc(l, 0, ctx)
            for ti, i in enumerate(tiles):
                P.op("sp", DMA(HT4[:, ti].rearrange("p a b -> p (a b)"), hts[i]), reads=[tHTS[i]], writes=[tHT[ti]], dma=f"ht{ti}")
            wq, wqT = load_wblock(win[l * 5 + 2])
            def proj_qa(ti, i):
                ps, pT = next_ps(0, 3)
                for kc in range(8):
                    P.op("pe", MM(ps[:, 0:384], HT4[:, ti, kc, :], wq[:, kc, 0:384], kc == 0, kc == 7), reads=[tHT[ti], wqT],
                         writes=[pT] if kc == 0 else (), pwrites=() if kc == 0 else [pT])
                P.op("act", ACT(KQB[:, 0:384], ps[:, 0:384], AF.Identity, scale=0.125), reads=[pT], writes=[tKQB])

                def evac_qa(kc, pap, ppT, ti=ti):
                    P.op("dve", CP(QTA[:, kc, ti * 128:(ti + 1) * 128], pap), reads=[ppT], pwrites=[tQTA])
                transpose_tile(KQB, [tKQB], 3, None, None, evac_qa)
            for ta in range(0, n, 2):
                lists = []
                for ti in range(ta, min(n, ta + 2)):
                    CUR[0] = ti % 2
                    ILV[0] = True
                    P.record_begin()
                    proj_qa(ti, tiles[ti])
                    lists.append(P.record_end())
                CUR[0] = 0
                ILV[0] = False
                P.replay_interleaved(lists)
            wq, wqT = load_wblock(win[l * 5 + 3])
            def proj_qc(ti, i):
                P.op("sp", DMA(ROPE[:, :], rope[i]), writes=[tROPE], dma="rope")
                ps, pT = next_ps(0, 3)
                for kc in range(8):
                    P.op("pe", MM(ps[:, 0:384], HT4[:, ti, kc, :], wq[:, kc, 0:384], kc == 0, kc == 7), reads=[tHT[ti], wqT],
                         writes=[pT] if kc == 0 else (), pwrites=() if kc == 0 else [pT])
                qk_norm_rope(ps[:, 0:384], [pT], 6, GQK[:, 0:64], KQB[:, 0:384], 0.125, [tKQB], True)

                def evac_qc(kc, pap, ppT, ti=ti):
                    P.op("dve", CP(QTC[:, kc, ti * 128:(ti + 1) * 128], pap), reads=[ppT], pwrites=[tQTC])
                transpose_tile(KQB, [tKQB], 3, None, None, evac_qc)
            for ta in range(0, n, 2):
                lists = []
                for ti in range(ta, min(n, ta + 2)):
                    CUR[0] = ti % 2
                    ILV[0] = True
                    P.record_begin()
                    proj_qc(ti, tiles[ti])
                    lists.append(P.record_end())
                CUR[0] = 0
                ILV[0] = False
                P.replay_interleaved(lists)
            wq, wqT = load_wblock(win[l * 5 + 4])
            def proj_zb(ti, i):
                ps, pT = next_ps(0, 3)
                for kc in range(8):
                    P.op("pe", MM(ps[:, :], HT4[:, ti, kc, :], wq[:, kc, :], kc == 0, kc == 7), reads=[tHT[ti], wqT],
                         writes=[pT] if kc == 0 else (), pwrites=() if kc == 0 else [pT])
                P.op("act", ACT(ZGp[:, :], ps[:, :], AF.Gelu_apprx_tanh), reads=[pT], writes=[tZGp])
                ln_stats(ZGp[:, 256:512], [tZGp], 256)
                P.op("dve", TS(KQB[:, 0:256], ZGp[:, 256:512], MV[:, 0:1], MV[:, 3:4], ALU.subtract, ALU.mult), reads=[tZGp, tMV], writes=[tKQB])
                ps2, p2T = (PSF[4 + CUR[0]], PSFT[4 + CUR[0]]) if ILV[0] else (PSF[5], PSFT[5])
                for g in range(4):
                    P.op("pe", MM(ps2[:, g * 64:(g + 1) * 64], WST[:, g * 128:(g + 1) * 128], KQB[:, g * 64:(g + 1) * 64], True, True),
                         reads=[tKQB, tLAYC], writes=[p2T] if g == 0 else (), pwrites=() if g == 0 else [p2T])
                P.op("dve", TT(TMPA[:, 0:256], ps2[:, 0:256], GSGU[:, :], ALU.mult), reads=[p2T, tLAYC], writes=[tTMPA])
                for g in range(4):
                    P.op("dve", TS(TMPA[:, g * 64:(g + 1) * 64], TMPA[:, g * 64:(g + 1) * 64], BS[:, l * 4 + g:l * 4 + g + 1], None, ALU.add),
                         reads=[tTMPA, tCONST], writes=[tTMPA])
                P.op("dve", TT(TMPA[:, 0:256], TMPA[:, 0:256], ZGp[:, 0:256], ALU.mult), reads=[tTMPA, tZGp], writes=[tTMPA])
                rms_rstd(TMPA[:, 0:256], [tTMPA], 1, 256, TMPB, tTMPB)
                P.op("dve", TS(MERGB[:, ti, 384:640], TMPA[:, 0:256], SS[:, 8:9], None, ALU.mult), reads=[tTMPA, tSS], pwrites=[tMERG[ti]])

            for ta in range(0, n, 2):
                lists = []
                for ti in range(ta, min(n, ta + 2)):
                    CUR[0] = ti % 2
                    ILV[0] = True
                    P.record_begin()
                    proj_zb(ti, tiles[ti])
                    lists.append(P.record_end())
                CUR[0] = 0
                ILV[0] = False
                P.replay_interleaved(lists)
            if st == 0:
                load_gathered_kv()
            if not ctx:
                lo, hi = max(0, 4 * st - 2), min(16, 4 * st + 6)
                s0 = lo - (4 * st - 2)
                for j in range(hi - lo):
                    P.op("sp", DMA(NAK[:, :, (s0 + j) * 128:(s0 + j + 1) * 128], nak_loc[lo + j].rearrange("p (c k) -> p c k", c=3)),
                         reads=[tNAKL], writes=[tNAK] if j == 0 else (), pwrites=() if j == 0 else [tNAK], dma="nak")
                P.op("sp", DMA(NAV[:, s0:s0 + hi - lo, :], nav_loc[lo:hi].rearrange("t p c -> p t c")), reads=[tNAVL], writes=[tNAV], dma="nav")
                if st in (0, 3):
                    for j in range(2):
                        def mk_halo_k(c3, st=st, j=j):
                            def halo_k(e):
                                rk = PIDV[0]
                                if st == 0:
                                    src_r, e0, sl = (rk + 3) % 4, 2, 0
                                else:
                                    src_r, e0, sl = (rk + 1) % 4, 0, 6
                                return e.dma_start(out=NAK[:, c3, (sl + j) * 128:(sl + j + 1) * 128],
                                                   in_=nak_eall.ap()[bass.ds(src_r * 512 + (e0 + j) * 128, 128), c3 * 128:(c3 + 1) * 128])
                            return halo_k

                        def halo_v(e, st=st, j=j):
                            rk = PIDV[0]
                            if st == 0:
                                src_r, e0, sl = (rk + 3) % 4, 2, 0
                            else:
                                src_r, e0, sl = (rk + 1) % 4, 0, 6
                            return e.dma_start(out=NAV[:, sl + j, :], in_=nav_eall.ap()[bass.ds(src_r * 512 + (e0 + j) * 128, 128), :])
                        for c3 in range(3):
                            P.op("pool", mk_halo_k(c3), reads=[tNAKEA], pwrites=[tNAK], dma="nak")
                        P.op("pool", halo_v, reads=[tNAVEA], pwrites=[tNAV], dma="nav")
            if STOP[0] == 'p2a':
                P.barrier()
                return nc, P
            for ti, i in enumerate(tiles):
                if not ctx:
                    cls = 0 if i == 0 else 1 if i == 1 else 3 if i == 14 else 4 if i == 15 else 2
                    if cur_cls[0] != (l, cls):
                        P.op("pool", DMA(NAB[:, :], nab[l * 5 + cls]), writes=[tNAB], dma="nab")
                        cur_cls[0] = (l, cls)
                oac = [PSF[3], PSF[4]]
                oacT = [PSFT[3], PSFT[4]]
                for h in range(6):
                    c, pb = h // 2, (h % 2) * 64
                    q = QTA[pb:pb + 64, c, ti * 128:(ti + 1) * 128]
                    psa, paT = next_ps(0, 3)
                    psb_, pbT = next_ps(0, 3)
                    blocks = []
                    if not ctx:
                        sl_b = [(ti + b, b) for b in range(5)]
                        if i == 0:
                            sl_b.append((ti + 5, 5))
                        if i == 15:
                            sl_b.append((ti - 1, 5))
                        sl_b += [(8, None), (9, None)]
                    else:
                        sl_b = [(8, None), (9, None)]
                    for k, (slot, b) in enumerate(sl_b):
                        if k < 4:
                            blocks.append((slot, psa[:, k * 128:(k + 1) * 128], paT, b))
                        else:
                            blocks.append((slot, psb_[:, (k - 4) * 128:(k - 3) * 128], pbT, b))
                    nB = max(0, len(sl_b) - 4)
                    seen = set()
                    for (slot, reg, rT, b) in blocks:
                        first = id(rT) not in seen
                        seen.add(id(rT))
                        P.op("pe", MM(reg, NAK[pb:pb + 64, c, slot * 128:(slot + 1) * 128], q, True, b is None), reads=[tNAK, tQTA],
                             writes=[rT] if first else (), pwrites=() if first else [rT])
                        if b is not None:
                            P.op("pe", MM(reg, IDB[:, :], NAB[:, (h * 6 + b) * 128:(h * 6 + b + 1) * 128], False, True), reads=[tNAB, tCONST], pwrites=[rT])
                    if not ctx:
                        P.op("act", ACT(PTN[:, 0:512], psa[:, :], AF.Exp), reads=[paT], writes=[tPTN, tPTNb])
                        P.op("act", ACT(PTN[:, 512:512 + nB * 128], psb_[:, 0:nB * 128], AF.Exp), reads=[pbT], pwrites=[tPTN])
                    else:
                        P.op("act", ACT(PTN[:, 0:256], psa[:, 0:256], AF.Exp), reads=[paT], writes=[tPTN, tPTNb])
                    ob, obT = oac[h // 4], oacT[h // 4]
                    oreg = ob[0:65, (h % 4) * 128:(h % 4 + 1) * 128]
                    nb_ = len(blocks)
                    for bi, (slot, reg, rT, b) in enumerate(blocks):
                        firstw = (h % 4 == 0 and bi == 0)
                        P.op("pe", MM(oreg, NAV[:, slot, h * 65:(h + 1) * 65], PTN[:, bi * 128:(bi + 1) * 128], bi == 0, bi == nb_ - 1),
                             reads=[tNAV, tPTN, tPTNb], writes=[obT] if firstw else (), pwrites=() if firstw else [obT])
                P.op("dve", CP(OT[0:65, 0:512], oac[0][0:65, :]), reads=[oacT[0]], writes=[tOT])
                P.op("dve", CP(OT[0:65, 512:768], oac[1][0:65, 0:256]), reads=[oacT[1]], pwrites=[tOT])
                tp, tpT = PSF[5], PSFT[5]
                for h in range(6):
                    P.op("pe", TR(tp[:, h * 65:(h + 1) * 65], OT[0:65, h * 128:(h + 1) * 128], IDF[0:65, 0:65]), reads=[tOT, tCONST],
                         writes=[tpT] if h == 0 else (), pwrites=() if h == 0 else [tpT])
                tpv = tp[:, 0:390].rearrange("p (h d) -> p h d", h=6)
                P.op("dve", (lambda o, i_: (lambda e: e.reciprocal(out=o, in_=i_)))(SS[:, 0:6], tpv[:, :, 64]), reads=[tpT], writes=[tSS])
                P.op("dve", TT(OATT[:, ti, :].rearrange("p (h d) -> p h d", h=6), tpv[:, :, 0:64],
                               SS[:, 0:6].unsqueeze(2).to_broadcast([128, 6, 64]), ALU.mult), reads=[tpT, tSS], writes=[tW4A, tZG])
                rms_rstd(OATT[:, ti, :], [tW4A, tZG], 1, 384, TMPA, tTMPA)
                P.op("dve", TS(MERGB[:, ti, 0:384], OATT[:, ti, :], SS[:, 8:9], None, ALU.mult), reads=[tW4A, tZG, tSS], pwrites=[tMERG[ti]])

            if STOP[0] == 'p2na':
                P.barrier()
                return nc, P
            nq = n * 128
            kts = list(range(66)) if not ctx else [64, 65]
            nk = len(kts)
            for c in range(3):
                qs = [QTC[g * 64:g * 64 + 64, c, 0:nq] for g in range(2)]
                obs = [(PSF[4 + g], PSFT[4 + g]) for g in range(2)]
                ptb = [[(PT[0], tPT[0]), (PT[1], tPT[1])], [(PTN[:, 0:512], tPTN), (PTN[:, 512:1024], tPTNb)]]

                def S(g, k):
                    ps, pT = PSF[g * 2 + k % 2], PSFT[g * 2 + k % 2]
                    kt = kts[k]
                    P.op("pe", MM(ps[:, 0:nq], KTC[g * 64:g * 64 + 64, kt * 128:(kt + 1) * 128], qs[g], True, True), reads=[tKTC, tQTC], writes=[pT])
                for k0 in range(min(2, nk)):
                    for g in range(2):
                        S(g, k0)
                for k in range(nk):
                    for g in range(2):
                        ps, pT = PSF[g * 2 + k % 2], PSFT[g * 2 + k % 2]
                        pbuf, pbT = ptb[g][k % 2]
                        P.op("act", ACT(pbuf[:, 0:nq], ps[:, 0:nq], AF.Exp), reads=[pT], writes=[pbT])
                    if k + 2 < nk:
                        for g in range(2):
                            S(g, k + 2)
                    kt = kts[k]
                    for g in range(2):
                        ob, obT = obs[g]
                        pbuf, pbT = ptb[g][k % 2]
                        P.op("pe", MM(ob[0:65, 0:nq], VC[:, kt, g * 65:(g + 1) * 65], pbuf[:, 0:nq], k == 0, k == nk - 1),
                             reads=[tVC, pbT], writes=[obT] if k == 0 else (), pwrites=() if k == 0 else [obT])
                for g in range(2):
                    h = c + 3 * g
                    ob, obT = obs[g]
                    P.op("dve", CP(OT[0:65, 0:nq], ob[0:65, 0:nq]), reads=[obT], writes=[tOT])
                    tp, tpT = PSF[g], PSFT[g]
                    for ti in range(n):
                        P.op("pe", TR(tp[:, ti * 65:(ti + 1) * 65], OT[0:65, ti * 128:(ti + 1) * 128], IDF[0:65, 0:65]), reads=[tOT, tCONST],
                             writes=[tpT] if ti == 0 else (), pwrites=() if ti == 0 else [tpT])
                    tpv = tp[:, 0:65 * n].rearrange("p (t d) -> p t d", t=n)
                    P.op("dve", (lambda o, i_: (lambda e: e.reciprocal(out=o, in_=i_)))(SS[:, 0:n], tpv[:, :, 64]), reads=[tpT], writes=[tSS])
                    P.op("dve", TT(OATT[:, 0:n, h * 64:(h + 1) * 64], tpv[:, :, 0:64],
                                   SS[:, 0:n].unsqueeze(2).to_broadcast([128, n, 64]), ALU.mult), reads=[tpT, tSS], pwrites=[tW4A, tZG])
            for ti in range(n):
                rms_rstd(OATT[:, ti, :], [tW4A, tZG], 1, 384, TMPA, tTMPA)
                P.op("dve", TS(MERGB[:, ti, 640:1024], OATT[:, ti, :], SS[:, 8:9], None, ALU.mult), reads=[tW4A, tZG, tSS], pwrites=[tMERG[ti]])

            if STOP[0] == 'p2gqa':
                P.barrier()
                return nc, P
            for ti in range(n):
                def evac_m(kc, pap, ppT, ti=ti):
                    P.op("dve", TS(HT4[:, ti, kc, :], pap, GOUT[:, l * 8 + kc:l * 8 + kc + 1], None, ALU.mult), reads=[ppT, tCONST], pwrites=[tHT[ti]])
                transpose_tile(MERGB[:, ti, :], [tMERG[ti]], 8, None, None, evac_m)
            wo0, wo0T = load_wblock(wo[l * 2 + 0])
            wo1, wo1T = load_wblock(wo[l * 2 + 1])
            for ti, i in enumerate(tiles):
                for nb, (wb_, wbT_) in enumerate(((wo0, wo0T), (wo1, wo1T))):
                    ps, pT = PSF[3 + nb], PSFT[3 + nb]
                    for kc in range(8):
                        P.op("pe", MM(ps[:, :], HT4[:, ti, kc, :], wb_[:, kc, :], kc == 0, kc == 7), reads=[tHT[ti], wbT_],
                             writes=[pT] if kc == 0 else (), pwrites=() if kc == 0 else [pT])
                deepnorm(i, l, lambda nb: PSF[3 + nb][:, :], [PSFT[3], PSFT[4]])
        P.barrier()
        if STOP[0] == 'p2':
            return nc, P

        load_ln(l, 1)
        for st in range(5):
            tiles = list(range(4 * st, 4 * st + 4)) if st < 4 else [16, 17]
            if l == nlayers - 1 and st == 4:
                continue
            n = len(tiles)
            ctx = st == 4
            load_bc(l, 1, ctx)
            for ti, i in enumerate(tiles):
                ln_mod_transpose(i, l, 1, ti)
            for nb in range(11):
                wb_, wbT_ = load_wblock(wffi[l * 11 + nb])
                for ti, i in enumerate(tiles):
                    ps, pT = next_ps(0, 3)
                    for kc in range(8):
                        P.op("pe", MM(ps[:, :], HT4[:, ti, kc, :], wb_[:, kc, :], kc == 0, kc == 7), reads=[tHT[ti], wbT_],
                             writes=[pT] if kc == 0 else (), pwrites=() if kc == 0 else [pT])
                    P.op("act", ACT(TMPA[:, 0:256], ps[:, 0:256], AF.Silu), reads=[pT], writes=[tTMPA])
                    P.op("dve", TT(GTOK4[:, ti, nb * 256:(nb + 1) * 256], TMPA[:, 0:256], ps[:, 256:512], ALU.mult), reads=[tTMPA, pT], pwrites=[tG4[ti]])
            for ti, i in enumerate(tiles):
                GT = GTTa if ti < 2 else GTTb
                for grp in range(3):
                    nch = 8 if grp < 2 else 6

                    def evac_gg(kc, pap, ppT, grp=grp, ti=ti, GT=GT):
                        kk = grp * 8 + kc
                        P.op("dve", CP(GT[:, ti % 2, kk, :], pap), reads=[ppT], pwrites=[tGTT4[ti]])
                    transpose_tile(GTOK4[:, ti, grp * 1024:grp * 1024 + nch * 128], [tG4[ti]], nch, None, None, evac_gg)
            for nb2 in range(2):
                for kg in range(3):
                    nch = 8 if kg < 2 else 6
                    wb_, wbT_ = load_wblock(wffo[l * 6 + nb2 * 3 + kg])
                    for ti, i in enumerate(tiles):
                        GT = GTTa if ti < 2 else GTTb
                        ps, pT = PSF[ti], PSFT[ti]
                        for kcl in range(nch):
                            kc = kg * 8 + kcl
                            P.op("pe", MM(ps[:, :], GT[:, ti % 2, kc, :], wb_[:, kcl, :], kc == 0, kc == 21), reads=[tGTT4[ti], wbT_],
                                 writes=[pT] if kc == 0 else (), pwrites=() if kc == 0 else [pT])
                for ti, i in enumerate(tiles):
                    P.op("dve", TT(Y4[:, ti, nb2 * 512:(nb2 + 1) * 512], PSF[ti][:, :], BC[:, 0, nb2 * 512:(nb2 + 1) * 512], ALU.mult),
                         reads=[PSFT[ti], tBC[0]], pwrites=[tY4[ti]])
            for ti, i in enumerate(tiles):
                deepnorm_from(i, Y4[:, ti, :], [tY4[ti]])
                if l == nlayers - 1 and i < 16:
                    P.op("sp", DMA(y_out[i], X[:, i, :]), reads=[XT[i]], pwrites=[tY], dma="yout")
        P.barrier()

    P.op("sp", None, reads=[tY])
    return nc, P


def finish(nc, P):
    from contextlib import ExitStack
    keys = list(ENGS) + ["dma:" + k for k in P.dma_cnt]
    with ExitStack() as es:
        sems = {k: es.enter_context(nc.semaphore(k.replace(":", "_"))) for k in keys}
        block = es.enter_context(nc.Block())
        P.emit(nc, block, sems)
    return nc


def _blocks(w, nblk, width=512):
    K = w.shape[0]
    kc = K // 128
    return np.ascontiguousarray(w.reshape(kc, 128, nblk, width).transpose(2, 1, 0, 3)).reshape(nblk, 128, kc * width)


def prep_shared(inp, NL):
    L = 4
    f = np.float32
    w_mod, b_mod, w_in = inp["w_mod"], inp["b_mod"], inp["w_in"]
    sh = {}
    sh["wmod"] = np.concatenate([_blocks(w_mod[l], 12) for l in range(NL)], 0)
    bases = [0, 1024, 3072, 4096]
    bt = np.zeros((L, 128, 32), f)
    for l in range(L):
        for j, b0 in enumerate(bases):
            bt[l, :, j * 8:(j + 1) * 8] = b_mod[l, b0:b0 + 1024].reshape(8, 128).T
    sh["bmodT"] = bt
    sh["bmodg"] = np.ascontiguousarray(np.stack([np.stack([b_mod[l, 2048:3072], b_mod[l, 5120:6144]]) for l in range(L)]).reshape(L * 2, 1024))
    perm = np.concatenate([np.arange(h * 64, (h + 1) * 64) for h in (0, 3, 1, 4, 2, 5)])
    wl = []
    for l in range(NL):
        w = w_in[l]
        qa, ka, va, zb, qc, kc_, vc = w[:, 0:384], w[:, 384:768], w[:, 768:1152], w[:, 1152:1664], w[:, 1664:2048], w[:, 2048:2176], w[:, 2176:2304]
        z128 = np.zeros((1024, 128), f)
        blks = [np.concatenate([va, vc], 1), np.concatenate([ka, kc_], 1), np.concatenate([qa, z128], 1),
                np.concatenate([qc[:, perm], z128], 1), zb]
        wl.append(_blocks(np.concatenate(blks, 1), 5))
    sh["win"] = np.concatenate(wl, 0)
    sh["wo"] = np.concatenate([_blocks(inp["w_o"][l], 2) for l in range(NL)], 0)
    wl = []
    for l in range(NL):
        w = inp["w_ffn_in"][l]
        a, b = w[:, :2816].reshape(1024, 11, 256), w[:, 2816:].reshape(1024, 11, 256)
        wl.append(_blocks(np.concatenate([a, b], 2).reshape(1024, 11 * 512), 11))
    sh["wffi"] = np.concatenate(wl, 0)
    wl = []
    for l in range(NL):
        w = np.zeros((3072, 1024), f)
        w[:2816] = inp["w_ffn_out"][l]
        wl.append(np.ascontiguousarray(w.reshape(3, 8, 128, 2, 512).transpose(3, 0, 2, 1, 4)).reshape(6, 128, 4096))
    sh["wffo"] = np.concatenate(wl, 0)
    sh["lnp"] = np.ascontiguousarray(np.stack([np.stack([inp["ln1_g"][l], inp["ln1_b"][l], inp["ln2_g"][l], inp["ln2_b"][l]]) for l in range(L)]).reshape(L * 4, 1024))
    sh["gout"] = np.ascontiguousarray(np.concatenate([inp["g_out"][l].reshape(8, 128).T for l in range(L)], 1))
    sh["gsgu"] = np.ascontiguousarray(inp["g_sgu"])
    sh["gqk"] = np.ascontiguousarray(np.concatenate([inp["g_q"], inp["g_k"]], 1))
    sh["wsT"] = np.ascontiguousarray(np.stack([inp["w_s"][l].transpose(2, 0, 1).reshape(128, 512) for l in range(L)]))
    sh["bsd"] = np.ascontiguousarray(np.concatenate([inp["b_s"][l].T for l in range(L)], 1))
    sh["identd"] = np.eye(128, dtype=f)
    return sh


def rope_table(tok):
    f = np.float32
    row = (tok // 64).astype(f)
    col = (tok % 64).astype(f)
    inv = (1.0 / (np.float32(10000.0) ** (np.arange(16, dtype=f) / np.float32(16)))).astype(f)
    ar = row[:, None] * inv[None, :]
    ac = col[:, None] * inv[None, :]
    return np.concatenate([np.cos(ar), np.cos(ac), np.sin(ar), np.sin(ac)], 1).astype(f)


def nab_tables(rpb, qi, L):
    out = np.full((L, 5, 128, 6, 6, 128), np.float32(-30000.0), np.float32)
    p = np.arange(128)
    for ci, i in enumerate((0, 1, 2, 14, 15)):
        G = 16 * qi + i
        r = 2 * G + p // 64
        c = p % 64
        rs = np.clip(r - 4, 0, 120)
        cs = np.clip(c - 8, 0, 48)
        for b in range(6):
            if b < 5:
                Gk = G - 2 + b
            elif i == 0:
                Gk = G + 3
            elif i == 15:
                Gk = G - 3
            else:
                continue
            if Gk < 0 or Gk > 63:
                continue
            kr = 2 * Gk + p // 64
            kc = p % 64
            ok = (kr[:, None] >= rs[None, :]) & (kr[:, None] < rs[None, :] + 8) & (kc[:, None] >= cs[None, :]) & (kc[:, None] < cs[None, :] + 16)
            dr = np.clip(kr[:, None] - r[None, :] + 7, 0, 14)
            dc = np.clip(kc[:, None] - c[None, :] + 15, 0, 30)
            for l in range(L):
                vals = rpb[l][:, dr, dc]
                out[l, ci, :, :, b, :] = np.where(ok[None], vals, np.float32(-30000.0)).transpose(1, 0, 2)
    return out.reshape(L * 5, 128, 4608)


_CACHE = {}


def kernel(**inp):
    inp = {k: np.asarray(v, dtype=np.float32) for k, v in inp.items()}
    if "nc" not in _CACHE:
        nc, P = build(NLAYERS_BUILD)
        _CACHE["nc"] = finish(nc, P)
    nc = _CACHE["nc"]
    import time as _t
    t0 = _t.time()
    NL = NLAYERS_BUILD
    sh = prep_shared(inp, NL)
    print("prep shared", _t.time() - t0, flush=True)
    in_maps = []
    for core in range(8):
        b, qi = core // 4, core % 4
        m = dict(sh)
        xs = inp["x"][b, 2048 * qi:2048 * (qi + 1)].reshape(16, 128, 1024)
        m["x_in"] = np.ascontiguousarray(np.concatenate([xs, inp["ctx"][b].reshape(2, 128, 1024)], 0))
        cv = np.empty((128, 16), np.float32)
        cv[:, 0::2] = inp["c"][b].reshape(8, 128).T
        cv[:, 1::2] = inp["c_ctx"].reshape(8, 128).T
        m["cvec"] = cv
        rp = np.empty((NT, 128, 64), np.float32)
        for i in range(16):
            rp[i] = rope_table(2048 * qi + 128 * i + np.arange(128))
        rp[16:, :, 0:32] = 1.0
        rp[16:, :, 32:64] = 0.0
        m["rope"] = rp
        m["nab"] = nab_tables(inp["rpb"], qi, NL)
        in_maps.append(m)
    print("prep all", _t.time() - t0, flush=True)
    res = run_bass_kernel_spmd(nc, in_maps, core_ids=list(range(8)))
    print("run done", _t.time() - t0, flush=True)
    out = np.empty((2, 8192, 1024), np.float32)
    for core in range(8):
        b, qi = core // 4, core % 4
        out[b, 2048 * qi:2048 * (qi + 1)] = np.asarray(res.results[core]["y_out"]).reshape(2048, 1024)
    return out
```

```python
import numpy as np
import concourse.bass as bass
import concourse.mybir as mybir
from concourse.bass_utils import run_bass_kernel_spmd

F32 = mybir.dt.float32
BF16 = mybir.dt.bfloat16
AF = mybir.ActivationFunctionType
ALU = mybir.AluOpType
AX = mybir.AxisListType

L = 4
NT = 18
ALPHA = float(8.0 ** 0.25)
EPS = 1e-6
NLAYERS_BUILD = L


class T:
    __slots__ = ("name", "w", "r")

    def __init__(self, name):
        self.name = name
        self.w = {}
        self.r = {}


ENGS = ["pe", "dve", "act", "pool", "sp"]
PIDV = [None]


class Plan:
    def __init__(self):
        self.ops = {e: [] for e in ENGS}
        self.known = {e: {} for e in ENGS}
        self.dma_cnt = {}
        self.waited = {e: set() for e in ENGS}

    def _res(self, hs):
        out = []
        for t in hs:
            r = t.resolve() if hasattr(t, "resolve") else t
            if isinstance(r, (list, tuple)):
                out.extend(r)
            else:
                out.append(r)
        return out

    def record_begin(self):
        self._rec = []

    def record_end(self):
        r, self._rec = self._rec, None
        return r

    def replay_interleaved(self, lists):
        its = [list(l) for l in lists]
        pos = [0] * len(its)
        while any(pos[j] < len(its[j]) for j in range(len(its))):
            for j in range(len(its)):
                if pos[j] < len(its[j]):
                    self.op(*its[j][pos[j]])
                    pos[j] += 1

    def op(self, eng, fn, reads=(), writes=(), pwrites=(), dma=None):
        reads, writes, pwrites = self._res(reads), self._res(writes), self._res(pwrites)
        if getattr(self, "_rec", None) is not None:
            self._rec.append((eng, fn, reads, writes, pwrites, dma))
            return
        idx = len(self.ops[eng])
        if dma is None:
            tok = (eng, idx + 1)
        else:
            self.dma_cnt[dma] = self.dma_cnt.get(dma, 0) + 1
            tok = ("dma:" + dma, self.dma_cnt[dma])
        need = {}

        def addw(d):
            for k, v in d.items():
                if need.get(k, 0) < v:
                    need[k] = v

        for t in reads:
            addw(t.w)
        for t in writes:
            addw(t.w)
            addw(t.r)
        for t in pwrites:
            addw(t.r)
        waits = []
        kn = self.known[eng]
        for k, v in need.items():
            if kn.get(k, 0) < v:
                kn[k] = v
                waits.append((k, v))
                if not k.startswith("dma:"):
                    self.waited[k].add(v)
        if fn is not None:
            for t in reads:
                if t.r.get(tok[0], 0) < tok[1]:
                    t.r[tok[0]] = tok[1]
            for t in writes:
                t.w = {tok[0]: tok[1]}
                t.r = {}
            for t in pwrites:
                if t.r:
                    t.w = {tok[0]: tok[1]}
                    t.r = {}
                else:
                    t.w[tok[0]] = max(t.w.get(tok[0], 0), tok[1])
        self.ops[eng].append((waits, fn, dma))

    def barrier(self):
        latest = {}
        for e in ENGS:
            n = len(self.ops[e])
            if n:
                latest[e] = n
        for k, v in self.dma_cnt.items():
            latest["dma:" + k] = v
        for e in ENGS:
            waits = []
            kn = self.known[e]
            for k, v in latest.items():
                if k == e:
                    continue
                if not k.startswith("dma:"):
                    vv = v
                    while vv > 0 and (self.ops[k][vv - 1][1] is None or self.ops[k][vv - 1][2] is not None):
                        vv -= 1
                    if vv == 0:
                        continue
                    v = vv
                if kn.get(k, 0) < v:
                    kn[k] = v
                    waits.append((k, v))
                    if not k.startswith("dma:"):
                        self.waited[k].add(v)
            self.ops[e].append((waits, None, None))

    def emit(self, nc, block, sems):
        rank = {}
        for e in ENGS:
            s = sorted(self.waited[e])
            rank[e] = {v: i + 1 for i, v in enumerate(s)}
        plan = self

        def run(eng_name):
            def body(e):
                if eng_name == "pool":
                    PIDV[0] = e.partition_id()
                for i, (waits, fn, dma) in enumerate(plan.ops[eng_name]):
                    for k, v in waits:
                        if k.startswith("dma:"):
                            if k.startswith("dma:cc"):
                                e.wait_ge(sems[k], 1)
                            else:
                                e.wait_ge(sems[k], 16 * v)
                        else:
                            e.wait_ge(sems[k], rank[k][v])
                    if fn is None:
                        continue
                    ins = fn(e)
                    if dma is not None:
                        if dma.startswith("cc"):
                            ins.then_inc(sems["dma:" + dma])
                        else:
                            ins.then_inc(sems["dma:" + dma], 16)
                    elif (i + 1) in rank[eng_name]:
                        ins.then_inc(sems[eng_name], 1)
            return body

        block.tensor(run("pe"))
        block.vector(run("dve"))
        block.scalar(run("act"))
        block.gpsimd(run("pool"))
        block.sync(run("sp"))


def MM(out, l, r, st, sp):
    return lambda e: e.matmul(out, lhsT=l, rhs=r, start=st, stop=sp)


def TR(out, in_, ident):
    return lambda e: e.transpose(out=out, in_=in_, identity=ident)


def ACT(out, in_, func, scale=None, bias=None):
    kw = {}
    if scale is not None:
        kw["scale"] = scale
    if bias is not None:
        kw["bias"] = bias
    return lambda e: e.activation(out=out, in_=in_, func=func, **kw)


def TS(out, in0, s1, s2, op0, op1=None):
    if op1 is None:
        return lambda e: e.tensor_scalar(out=out, in0=in0, scalar1=s1, scalar2=None, op0=op0)
    return lambda e: e.tensor_scalar(out=out, in0=in0, scalar1=s1, scalar2=s2, op0=op0, op1=op1)


def TT(out, in0, in1, op):
    return lambda e: e.tensor_tensor(out=out, in0=in0, in1=in1, op=op)


def STT(out, in0, scalar, in1, op0, op1):
    return lambda e: e.scalar_tensor_tensor(out=out, in0=in0, scalar=scalar, in1=in1, op0=op0, op1=op1)


def CP(out, in_):
    return lambda e: e.tensor_copy(out=out, in_=in_)


def DMA(out, in_):
    return lambda e: e.dma_start(out=out, in_=in_)


def MEMSET(ap, v):
    return lambda e: e.memset(ap, v)


STOP = [None]


def build(nlayers=L, debug_x=False):
    nc = bass.Bass("TRN2", target_bir_lowering=False)
    P = Plan()

    def din(name, shape):
        return nc.dram_tensor(name, shape, F32, kind="ExternalInput").ap()

    x_in = din("x_in", [NT, 128, 1024])
    cvec = din("cvec", [128, 16])
    wmod = din("wmod", [nlayers * 12, 128, 4096])
    bmodT = din("bmodT", [L, 128, 32])
    bmodg = din("bmodg", [L * 2, 1024])
    win = din("win", [nlayers * 5, 128, 4096])
    wo = din("wo", [nlayers * 2, 128, 4096])
    wffi = din("wffi", [nlayers * 11, 128, 4096])
    wffo = din("wffo", [nlayers * 6, 128, 4096])
    lnp = din("lnp", [L * 4, 1024])
    gout = din("gout", [128, L * 8])
    gsgu = din("gsgu", [L, 256])
    gqk = din("gqk", [L, 128])
    wsT = din("wsT", [L, 128, 512])
    bsd = din("bsd", [128, L * 4])
    rope = din("rope", [NT, 128, 64])
    nab = din("nab", [nlayers * 5, 128, 4608])
    identd = din("identd", [128, 128])
    y_out = nc.dram_tensor("y_out", [16, 128, 1024], F32, kind="ExternalOutput").ap()

    gst = nc.dram_tensor("gst", [L * 4, 128, 1024], F32).ap()
    hts = nc.dram_tensor("hts", [NT, 128, 1024], BF16).ap()
    nak_loc = nc.dram_tensor("nak_loc", [16, 128, 384], BF16).ap()
    nav_loc = nc.dram_tensor("nav_loc", [16, 128, 390], BF16).ap()
    ktc_loc = nc.dram_tensor("ktc_loc", [128, 2048], BF16)
    ktc_all = nc.dram_tensor("ktc_all", [512, 2048], BF16)
    vc_loc = nc.dram_tensor("vc_loc", [2048, 130], BF16)
    vc_all = nc.dram_tensor("vc_all", [8192, 130], BF16)
    nak_edge = nc.dram_tensor("nak_edge", [512, 384], BF16)
    nak_eall = nc.dram_tensor("nak_eall", [2048, 384], BF16)
    nav_edge = nc.dram_tensor("nav_edge", [512, 390], BF16)
    nav_eall = nc.dram_tensor("nav_eall", [2048, 390], BF16)

    off = [16512]
    OFF = {}
    LIMIT = 229344

    def sb(name, shape, dt, at=None):
        nbytes = int(np.prod(shape[1:])) * (4 if dt == F32 else 2)
        nbytes = (nbytes + 63) // 64 * 64
        if at is None:
            o = off[0]
            off[0] += nbytes
            assert off[0] <= LIMIT, (name, off[0])
        else:
            o = at
        t = nc.alloc_sbuf_tensor_at(name, shape, dt, offset=o)
        OFF[name] = o
        return t

    X = sb("X", [128, NT, 1024], F32)
    KTC = sb("KTC", [128, 8448], BF16)
    VC = sb("VC", [128, 66, 130], BF16)
    NAK = sb("NAK", [128, 3, 1280], BF16)
    NAV = sb("NAV", [128, 10, 390], BF16)
    NAB = sb("NAB", [128, 4608], BF16)
    BC = sb("BC", [128, 3, 1024], F32)
    WB = [sb(f"WB{i}", [128, 8, 512], BF16) for i in range(2)]
    HT4 = sb("HT4", [128, 4, 8, 128], BF16)
    MERGB = sb("MERGB", [128, 4, 1024], BF16)
    att0 = off[0]
    QTA = sb("QTA", [128, 3, 512], BF16)
    QTC = sb("QTC", [128, 3, 512], BF16)
    PT = [sb(f"PT{i}", [128, 512], BF16) for i in range(2)]
    PTN = sb("PTN", [128, 1024], BF16)
    OT = sb("OT", [128, 768], F32)
    att1 = off[0]
    assert att0 + 11264 <= att1
    GTOK4 = sb("GTOK4", [128, 4, 2816], BF16, at=OFF["KTC"])
    GTTa = sb("GTTa", [128, 2, 22, 128], BF16, at=OFF["KTC"] + 22528)
    GTTb = sb("GTTb", [128, 2, 22, 128], BF16, at=att0)
    assert 22528 + 11264 <= 16896 + 17160
    Y4 = sb("Y4", [128, 4, 1024], F32, at=OFF["NAK"])
    assert OFF["NAB"] + 9216 - OFF["NAK"] >= 16384
    WBX = [sb(f"WBX{i}", [128, 8, 512], BF16, at=OFF["KTC"] + 8192 * i) for i in range(4)]
    RLAT = sb("RLAT", [128, 8, 128], BF16, at=att0)
    RCTX = sb("RCTX", [128, 8, 128], BF16, at=att0 + 2048)
    W4A = sb("W4A", [128, 1024], F32)
    ZG = sb("ZG", [128, 512], F32)
    OATT = sb("OATT", [128, 4, 384], F32, at=OFF["W4A"])
    ZG_1 = sb("ZG1", [128, 512], F32, at=OFF["W4A"])
    W2A_0 = sb("W2A", [128, 1024], BF16)
    TMPA_0 = sb("TMPA", [128, 512], F32)
    TMPB_0 = sb("TMPB", [128, 512], F32)
    KQB_0 = sb("KQB", [128, 512], BF16)
    TMPA_1 = sb("TMPA1", [128, 512], F32, at=OFF["PTN"])
    TMPB_1 = sb("TMPB1", [128, 512], F32, at=OFF["OT"])
    KQB_1 = sb("KQB1", [128, 512], BF16, at=OFF["PT0"])
    W2A_1 = sb("W2A1", [128, 1024], BF16, at=OFF["QTA"])
    KAT_1 = sb("KAT1", [128, 4, 128], BF16, at=OFF["QTA"] + 2048)
    VAT_1 = sb("VAT1", [128, 6, 65], BF16, at=OFF["QTC"])
    VCT_1 = sb("VCT1", [128, 2, 65], BF16, at=OFF["QTC"] + 896)
    ROPE_1 = sb("ROPE1", [128, 64], F32, at=OFF["PT1"])
    SS_1 = sb("SS1", [128, 16], F32, at=OFF["PT1"] + 256)
    MV_1 = sb("MV1", [128, 8], F32, at=OFF["PT1"] + 320)
    ST6_1 = sb("ST61", [128, 12], F32, at=OFF["PT1"] + 384)
    VAT_0 = sb("VAT", [128, 6, 65], BF16)
    VCT_0 = sb("VCT", [128, 2, 65], BF16)
    KAT_0 = sb("KAT", [128, 4, 128], BF16)
    IDB = sb("IDB", [128, 128], BF16)
    IDF = sb("IDF", [128, 128], F32)
    ROPE_0 = sb("ROPE", [128, 64], F32)
    S2F = sb("S2F", [128, 16], F32)
    S2B = sb("S2B", [128, 8, 2], BF16)
    ONESB = sb("ONESB", [128, 128], BF16)
    MODF = sb("MODF", [128, L, 32, 2], F32)
    BMT = sb("BMT", [128, 32], F32)
    GOUT = sb("GOUT", [128, L * 8], F32)
    GSGU = sb("GSGU", [128, 256], F32)
    GQK = sb("GQK", [128, 128], F32)
    WST = sb("WST", [128, 512], BF16)
    BS = sb("BS", [128, L * 4], F32)
    ST6_0 = sb("ST6", [128, 12], F32)
    MV_0 = sb("MV", [128, 8], F32)
    SS_0 = sb("SS", [128, 16], F32)
    print("SBUF used", off[0], "of", LIMIT)

    PSF = [nc.alloc_psum_tensor(f"PSF{i}", [128, 512], F32) for i in range(6)]
    PST = [nc.alloc_psum_tensor(f"PST{i}", [128, 1024], BF16) for i in range(2)]
    PSFT = [T(f"psf{i}") for i in range(8)]
    pst_rr = [0]

    XT = [T(f"x{i}") for i in range(NT)]
    tKTC, tVC, tNAK, tNAV, tNAB = T("ktc"), T("vc"), T("nak"), T("nav"), T("nab")
    tBC = [T("bc0"), T("bc1"), T("bc2")]
    tWB = [T("wb0"), T("wb1")]
    tHT = [T(f"ht{i}") for i in range(4)]
    tMERG = [T(f"mg{i}") for i in range(4)]
    tQTA, tQTC = T("qta"), T("qtc")
    tPT = [T("pt0"), T("pt1")]
    tPTN, tOT = T("ptn"), T("ot")
    tPTNb = T("ptnb")
    tGTOK = T("gtok")
    tY4 = [T(f"y4{i}") for i in range(4)]
    tG4 = [T(f"g4{i}") for i in range(4)]
    tGTT4 = [T(f"gtt{i}") for i in range(4)]
    tW4A, tZG = T("w4a"), T("zg")
    CUR = [0]

    class BufP:
        def __init__(self, bufs):
            self.bufs = bufs

        def __getitem__(self, k):
            return self.bufs[CUR[0]][k]

    class HP:
        def __init__(self, hs):
            self.hs = hs

        def resolve(self):
            return self.hs[CUR[0]]
    TMPA, TMPB, KQB = BufP([TMPA_0, TMPA_1]), BufP([TMPB_0, TMPB_1]), BufP([KQB_0, KQB_1])
    W2A, KAT, VAT, VCT = BufP([W2A_0, W2A_1]), BufP([KAT_0, KAT_1]), BufP([VAT_0, VAT_1]), BufP([VCT_0, VCT_1])
    ILV = [False]
    ZGp = BufP([ZG, ZG_1])
    tZGp = HP([tZG, tW4A])
    tVAT_0, tVCT_0, tKAT_0, tW2A_0 = T("vat"), T("vct"), T("kat"), T("w2a")
    tCONST, tMODF, tLAYC = T("const"), T("modf"), T("layc")
    tW2A = HP([tW2A_0, tQTA])
    tKAT = HP([tKAT_0, tQTA])
    tVAT = HP([tVAT_0, tQTC])
    tVCT = HP([tVCT_0, tQTC])
    tTMPA = HP([T("tmpa"), [tPTN, tPTNb]])
    tTMPB = HP([T("tmpb"), tOT])
    tKQB = HP([T("kqb"), tPT[0]])
    tROPE = HP([T("rope"), tPT[1]])
    tST6 = HP([T("st6"), tPT[1]])
    tMV = HP([T("mv"), tPT[1]])
    tSS = HP([T("ss"), tPT[1]])
    ROPE, SS, MV, ST6 = BufP([ROPE_0, ROPE_1]), BufP([SS_0, SS_1]), BufP([MV_0, MV_1]), BufP([ST6_0, ST6_1])
    tGST, tHTS = T("gst"), [T(f"hts{i}") for i in range(NT)]
    tNAKL, tNAVL, tKTCL, tVCL, tNAKE, tNAVE = T("nakl"), T("navl"), T("ktcl"), T("vcl"), T("nake"), T("nave")
    tKTCA, tVCA, tNAKEA, tNAVEA = T("ktca"), T("vca"), T("nakea"), T("navea")
    tY = T("y")

    wb_rr = [0]
    WBALL = WB + WBX
    tWBALL = tWB + [T(f"wbx{i}") for i in range(4)]
    wb_pool = [6]

    def load_wblock(src_ap):
        i = wb_rr[0] % wb_pool[0]
        wb_rr[0] += 1
        P.op("pool", DMA(WBALL[i][:].rearrange("p a b -> p (a b)"), src_ap), writes=[tWBALL[i]], dma=f"wb{i}")
        return WBALL[i], tWBALL[i]

    ps_rr = [0]

    ps_set_rr = [0, 0]

    def next_ps(lo=0, hi=3):
        if ILV[0]:
            c = CUR[0]
            i = 2 * c + ps_set_rr[c] % 2
            ps_set_rr[c] += 1
            return PSF[i], PSFT[i]
        i = lo + ps_rr[0] % (hi - lo)
        ps_rr[0] += 1
        return PSF[i], PSFT[i]

    def ln_stats(src, srcT, n):
        nch = max(1, n // 512)
        w = n // nch
        for ci in range(nch):
            P.op("dve", (lambda o, i: (lambda e: e.bn_stats(out=o, in_=i)))(ST6[:, ci * 6:(ci + 1) * 6], src[:, ci * w:(ci + 1) * w]),
                 reads=srcT, writes=[tST6] if ci == 0 else (), pwrites=() if ci == 0 else [tST6])
        P.op("dve", (lambda o, i: (lambda e: e.bn_aggr(out=o, in_=i)))(MV[:, 0:2], ST6[:, 0:6 * nch].rearrange('p (n s) -> p n s', s=6)), reads=[tST6], writes=[tMV])
        P.op("dve", TS(MV[:, 2:3], MV[:, 1:2], EPS, None, ALU.add), reads=[tMV], writes=[tMV])
        P.op("act", ACT(MV[:, 2:3], MV[:, 2:3], AF.Ln), reads=[tMV], writes=[tMV])
        P.op("act", ACT(MV[:, 3:4], MV[:, 2:3], AF.Exp, scale=-0.5), reads=[tMV], writes=[tMV])

    def rms_rstd(src, srcT, G, W, tmp, tmpT):
        P.op("dve", TT(tmp[:, 0:G * W], src, src, ALU.mult), reads=srcT, writes=[tmpT])
        P.op("dve", (lambda o, i: (lambda e: e.reduce_sum(out=o, in_=i, axis=AX.X)))(SS[:, 0:G], tmp[:, 0:G * W].rearrange("p (g w) -> p g w", g=G)),
             reads=[tmpT], writes=[tSS])
        P.op("dve", TS(SS[:, 0:G], SS[:, 0:G], 1.0 / W, EPS, ALU.mult, ALU.add), reads=[tSS], writes=[tSS])
        P.op("act", ACT(SS[:, 0:G], SS[:, 0:G], AF.Ln), reads=[tSS], writes=[tSS])
        P.op("act", ACT(SS[:, 8:8 + G], SS[:, 0:G], AF.Exp, scale=-0.5), reads=[tSS], writes=[tSS])


    def qk_norm_rope(src, srcT, H, gain, dst, oscale, dstT, full):
        n = H * 64
        P.op("act", ACT(TMPA[:, 0:n], src, AF.Identity), reads=srcT, writes=[tTMPA])
        rms_rstd(TMPA[:, 0:n], [tTMPA], H, 64, TMPB, tTMPB)
        v3 = lambda ap: ap.rearrange("p (h d) -> p h d", h=H)
        P.op("dve", TT(v3(TMPA[:, 0:n]), v3(TMPA[:, 0:n]), SS[:, 8:8 + H].unsqueeze(2).to_broadcast([128, H, 64]), ALU.mult), reads=[tTMPA, tSS], writes=[tTMPA])
        P.op("dve", STT(v3(TMPA[:, 0:n]), v3(TMPA[:, 0:n]), oscale, gain.unsqueeze(1).to_broadcast([128, H, 64]), ALU.mult, ALU.mult),
             reads=[tTMPA, tLAYC], writes=[tTMPA])
        v5 = lambda ap: ap.rearrange("p (h a s f) -> p h a s f", h=H, a=2, s=2)
        x1 = v5(TMPA[:, 0:n])[:, :, :, 0, :]
        x2 = v5(TMPA[:, 0:n])[:, :, :, 1, :]
        d1 = v5(dst)[:, :, :, 0, :]
        d2 = v5(dst)[:, :, :, 1, :]
        C = ROPE[:, 0:32].rearrange("p (a f) -> p a f", a=2).unsqueeze(1).to_broadcast([128, H, 2, 16])
        S_ = ROPE[:, 32:64].rearrange("p (a f) -> p a f", a=2).unsqueeze(1).to_broadcast([128, H, 2, 16])
        v4 = lambda ap: ap.rearrange("p (h a f) -> p h a f", h=H, a=2)
        t1 = v4(TMPB[:, 0:H * 32])
        t2 = v4(TMPB[:, H * 32:H * 64])
        P.op("dve", TT(t1, x1, C, ALU.mult), reads=[tTMPA, tROPE], writes=[tTMPB])
        P.op("dve", TT(t2, x2, S_, ALU.mult), reads=[tTMPA, tROPE], pwrites=[tTMPB])
        P.op("dve", TT(d1, t1, t2, ALU.subtract), reads=[tTMPB], writes=dstT if full else (), pwrites=() if full else dstT)
        P.op("dve", TT(t1, x2, C, ALU.mult), reads=[tTMPA, tROPE], writes=[tTMPB])
        P.op("dve", TT(t2, x1, S_, ALU.mult), reads=[tTMPA, tROPE], pwrites=[tTMPB])
        P.op("dve", TT(d2, t1, t2, ALU.add), reads=[tTMPB], pwrites=dstT)

    def transpose_tile(src_bf, srcT, nch, dst_fn, dstT, evac):
        if ILV[0]:
            j = CUR[0]
        else:
            j = pst_rr[0] % 2
            pst_rr[0] += 1
        psb, pT = PST[j], PSFT[6 + j]
        for kc in range(nch):
            P.op("pe", TR(psb[:, kc * 128:(kc + 1) * 128], src_bf[:, kc * 128:(kc + 1) * 128], IDB[:, :]),
                 reads=srcT + [tCONST], writes=[pT] if kc == 0 else (), pwrites=() if kc == 0 else [pT])
        for kc in range(nch):
            evac(kc, psb[:, kc * 128:(kc + 1) * 128], pT)

    P.op("sp", DMA(IDF[:, :], identd[:, :]), writes=[tCONST], dma="c0")
    P.op("sp", DMA(S2F[:, :], cvec[:, :]), pwrites=[tCONST], dma="c0")
    P.op("sp", DMA(GOUT[:, :], gout[:, :]), pwrites=[tCONST], dma="c0")
    P.op("sp", DMA(BS[:, :], bsd[:, :]), pwrites=[tCONST], dma="c0")
    for i in range(NT):
        P.op("sp", DMA(X[:, i, :], x_in[i]), writes=[XT[i]], dma="xin")
    P.op("dve", CP(IDB[:, :], IDF[:, :]), reads=[tCONST], pwrites=[tCONST])
    P.op("dve", MEMSET(ONESB[:, :], 1.0), pwrites=[tCONST])
    P.op("act", ACT(S2F[:, :], S2F[:, :], AF.Silu), reads=[tCONST], writes=[tCONST])
    P.op("dve", CP(S2B[:].rearrange("p a b -> p (a b)"), S2F[:, :]), reads=[tCONST], writes=[tCONST])
    for kc in range(8):
        P.op("dve", TS(RLAT[:, kc, :], ONESB[:, :], S2F[:, 2 * kc:2 * kc + 1], None, ALU.mult), reads=[tCONST], pwrites=[tGTOK])
        P.op("dve", TS(RCTX[:, kc, :], ONESB[:, :], S2F[:, 2 * kc + 1:2 * kc + 2], None, ALU.mult), reads=[tCONST], pwrites=[tGTOK])
    P.op("dve", MEMSET(VAT[:, :, :], 1.0), writes=[tVAT])
    P.op("dve", MEMSET(VCT[:, :, :], 1.0), writes=[tVCT])

    for l in range(nlayers):
        P.op("sp", DMA(BMT[:, :], bmodT[l]), writes=[tLAYC], dma="c1")
        psm, psmT = PSF[5], PSFT[5]
        for jj, nbs in enumerate([(0, 1), (2, 3), (6, 7), (8, 9)]):
            for half, nb in enumerate(nbs):
                wbuf, wT = load_wblock(wmod[l * 12 + nb])
                for oc4 in range(4):
                    col = (jj * 8 + half * 4 + oc4) * 2
                    for kc in range(8):
                        first = (jj == 0 and half == 0 and oc4 == 0 and kc == 0)
                        P.op("pe", MM(psm[:, col:col + 2], wbuf[:, kc, oc4 * 128:(oc4 + 1) * 128], S2B[:, kc, :], kc == 0, kc == 7),
                             reads=[wT, tCONST], writes=[psmT] if first else (), pwrites=() if first else [psmT])
        pv = psm[:, 0:64].rearrange("p (a b) -> p a b", b=2)
        for s in range(2):
            P.op("dve", TT(MODF[:, l, :, s], pv[:, :, s], BMT[:, :], ALU.add), reads=[psmT, tLAYC], pwrites=[tMODF])
        for a0 in (8, 24):
            P.op("dve", TS(MODF[:, l, a0:a0 + 8, :], MODF[:, l, a0:a0 + 8, :], 1.0, None, ALU.add), reads=[tMODF], writes=[tMODF])
        for gi, nbs in enumerate([(4, 5), (10, 11)]):
            for half, nb in enumerate(nbs):
                wbuf, wT = load_wblock(wmod[l * 12 + nb])
                P.op("sp", DMA(TMPA[:, :], bmodg[l * 2 + gi, half * 512:(half + 1) * 512].partition_broadcast(128)),
                     writes=[tTMPA], dma="tmpa")
                for s, R in enumerate((RLAT, RCTX)):
                    ps, pT = next_ps(0, 3)
                    for kc in range(8):
                        P.op("pe", MM(ps[:, :], R[:, kc, :], wbuf[:, kc, :], kc == 0, kc == 7), reads=[wT, tGTOK],
                             writes=[pT] if kc == 0 else (), pwrites=() if kc == 0 else [pT])
                    P.op("dve", TT(TMPB[:, :], ps[:, :], TMPA[:, :], ALU.add), reads=[pT, tTMPA], writes=[tTMPB])
                    P.op("sp", DMA(gst[l * 4 + gi * 2 + s][:, half * 512:(half + 1) * 512], TMPB[:, :]), reads=[tTMPB], pwrites=[tGST], dma="gst")
    P.barrier()
    wb_pool[0] = 2
    wb_rr[0] = 0
    if STOP[0] == 'p0':
        return nc, P

    def load_bc(l, sub, ctx):
        P.op("sp", DMA(BC[:, 0, :], gst[l * 4 + sub * 2 + (1 if ctx else 0)]), reads=[tGST], writes=[tBC[0]], dma="bc0")

    def load_ln(l, sub):
        for j in range(2):
            r = l * 4 + sub * 2 + j
            P.op("sp", DMA(BC[:, 1 + j, :], lnp[r, :].partition_broadcast(128)), writes=[tBC[1 + j]], dma=f"bc{1 + j}")

    def ln_mod_transpose(i, l, which, slot):
        s = 1 if i >= 16 else 0
        ln_stats(X[:, i, :], [XT[i]], 1024)
        P.op("dve", TS(W2A[:, :], X[:, i, :], MV[:, 0:1], MV[:, 3:4], ALU.subtract, ALU.mult), reads=[XT[i], tMV], writes=[tW2A])

        def evac(kc, pap, pT):
            sc = MODF[:, l, which * 16 + 8 + kc, s:s + 1]
            sh = MODF[:, l, which * 16 + kc, s:s + 1]
            P.op("dve", TS(HT4[:, slot, kc, :], pap, sc, sh, ALU.mult, ALU.add), reads=[pT, tMODF], pwrites=[tHT[slot]])
        transpose_tile(W2A, [tW2A], 8, None, None, evac)

    def deepnorm_from(i, zsrc, zT):
        P.op("dve", STT(W4A[:, :], X[:, i, :], ALPHA, zsrc, ALU.mult, ALU.add), reads=[XT[i]] + zT, writes=[tW4A])
        ln_stats(W4A[:, :], [tW4A], 1024)
        P.op("dve", STT(W4A[:, :], W4A[:, :], MV[:, 0:1], BC[:, 1, :], ALU.subtract, ALU.mult), reads=[tW4A, tMV, tBC[1]], writes=[tW4A])
        P.op("dve", STT(X[:, i, :], W4A[:, :], MV[:, 3:4], BC[:, 2, :], ALU.mult, ALU.add), reads=[tW4A, tMV, tBC[2]], writes=[XT[i]])

    def deepnorm(i, l, ysrc_fn, yT):
        for nb in range(2):
            P.op("dve", TT(W4A[:, nb * 512:(nb + 1) * 512], ysrc_fn(nb), BC[:, 0, nb * 512:(nb + 1) * 512], ALU.mult),
                 reads=[yT[nb], tBC[0]], writes=[tW4A] if nb == 0 else (), pwrites=() if nb == 0 else [tW4A])
        deepnorm_from(i, W4A[:, :], [tW4A])

    for l in range(nlayers):
        P.op("sp", DMA(GSGU[:, :], gsgu[l, :].partition_broadcast(128)), writes=[tLAYC], dma="c1")
        P.op("sp", DMA(GQK[:, :], gqk[l, :].partition_broadcast(128)), pwrites=[tLAYC], dma="c1")
        P.op("pool", DMA(WST[:, :], wsT[l]), pwrites=[tLAYC], dma="c2")

        CUR[0] = 1
        P.op("dve", MEMSET(VAT[:, :, :], 1.0), writes=[tVAT])
        P.op("dve", MEMSET(VCT[:, :, :], 1.0), pwrites=[tVCT])
        CUR[0] = 0
        P.op("dve", MEMSET(VC[:, 64:66, :], 1.0), writes=[tVC])
        P.op("dve", MEMSET(NAV[:, 8:10, :], 1.0), writes=[tNAV])
        wv, wvT = load_wblock(win[l * 5 + 0])
        wk, wkT = load_wblock(win[l * 5 + 1])
        def p1_tile(i):
            lat = i < 16
            sl = i % 2
            P.op("sp", DMA(ROPE[:, :], rope[i]), writes=[tROPE], dma="rope")
            ln_mod_transpose(i, l, 0, sl)
            P.op("sp", DMA(hts[i], HT4[:, sl].rearrange("p a b -> p (a b)")), reads=[tHT[sl]], writes=[tHTS[i]], dma="hts")
            ps, pT = next_ps(0, 3)
            for kc in range(8):
                P.op("pe", MM(ps[:, :], HT4[:, sl, kc, :], wv[:, kc, :], kc == 0, kc == 7), reads=[tHT[sl], wvT],
                     writes=[pT] if kc == 0 else (), pwrites=() if kc == 0 else [pT])
            if lat:
                P.op("dve", CP(VAT[:, :, 0:64], ps[:, 0:384].rearrange("p (h d) -> p h d", h=6)), reads=[pT], pwrites=[tVAT])
                P.op("dve", CP(VCT[:, :, 0:64], ps[:, 384:512].rearrange("p (h d) -> p h d", h=2)), reads=[pT], pwrites=[tVCT])
                P.op("sp", DMA(nav_loc[i], VAT[:].rearrange("p a b -> p (a b)")), reads=[tVAT], pwrites=[tNAVL], dma="navl")
                P.op("sp", DMA(vc_loc.ap()[i * 128:(i + 1) * 128, :], VCT[:].rearrange("p a b -> p (a b)")), reads=[tVCT], pwrites=[tVCL], dma="vcl")
                if i in (0, 1, 14, 15):
                    e = i if i < 2 else i - 12
                    P.op("sp", DMA(nav_edge.ap()[e * 128:(e + 1) * 128, :], VAT[:].rearrange("p a b -> p (a b)")), reads=[tVAT], pwrites=[tNAVE], dma="nave")
            else:
                j = i - 16
                P.op("dve", CP(NAV[:, 8 + j, :].rearrange("p (h d) -> p h d", h=6)[:, :, 0:64], ps[:, 0:384].rearrange("p (h d) -> p h d", h=6)), reads=[pT], pwrites=[tNAV])
                P.op("dve", CP(VC[:, 64 + j, :].rearrange("p (h d) -> p h d", h=2)[:, :, 0:64], ps[:, 384:512].rearrange("p (h d) -> p h d", h=2)), reads=[pT], pwrites=[tVC])
            ps, pT = next_ps(0, 3)
            for kc in range(8):
                P.op("pe", MM(ps[:, :], HT4[:, sl, kc, :], wk[:, kc, :], kc == 0, kc == 7), reads=[tHT[sl], wkT],
                     writes=[pT] if kc == 0 else (), pwrites=() if kc == 0 else [pT])
            P.op("act", ACT(KQB[:, 0:384], ps[:, 0:384], AF.Identity), reads=[pT], writes=[tKQB])
            qk_norm_rope(ps[:, 384:512], [pT], 2, GQK[:, 64:128], KQB[:, 384:512], 1.0, [tKQB], False)
            kdst = []

            def evac_k(kc, pap, ppT, i=i, lat=lat):
                if kc < 3:
                    if lat:
                        P.op("dve", CP(KAT[:, kc, :], pap), reads=[ppT], pwrites=[tKAT])
                    else:
                        P.op("dve", CP(NAK[:, kc, (8 + i - 16) * 128:(9 + i - 16) * 128], pap), reads=[ppT], pwrites=[tNAK])
                else:
                    if lat:
                        P.op("dve", CP(KAT[:, 3, :], pap), reads=[ppT], pwrites=[tKAT])
                    else:
                        P.op("dve", CP(KTC[:, 8192 + (i - 16) * 128:8192 + (i - 15) * 128], pap), reads=[ppT], pwrites=[tKTC])
            transpose_tile(KQB, [tKQB], 4, None, None, evac_k)
            if lat:
                P.op("sp", DMA(nak_loc[i], KAT[:, 0:3, :].rearrange("p a b -> p (a b)")), reads=[tKAT], pwrites=[tNAKL], dma="nakl")
                P.op("sp", DMA(ktc_loc.ap()[:, i * 128:(i + 1) * 128], KAT[:, 3, :]), reads=[tKAT], pwrites=[tKTCL], dma="ktcl")
                if i in (0, 1, 14, 15):
                    e = i if i < 2 else i - 12
                    P.op("sp", DMA(nak_edge.ap()[e * 128:(e + 1) * 128, :], KAT[:, 0:3, :].rearrange("p a b -> p (a b)")), reads=[tKAT], pwrites=[tNAKE], dma="nake")

        for ia in range(0, NT, 2):
            lists = []
            for i in (ia, ia + 1):
                CUR[0] = i % 2
                ILV[0] = True
                P.record_begin()
                p1_tile(i)
                lists.append(P.record_end())
            CUR[0] = 0
            ILV[0] = False
            P.replay_interleaved(lists)

        if STOP[0] == 'p1':
            return nc, P
        def coll(src, srcT, dst, dstT, key):
            P.op("pool", (lambda s_, d_: (lambda e: e.collective_compute(
                "AllGather", ALU.bypass, replica_groups=[[0, 1, 2, 3], [4, 5, 6, 7]],
                ins=[s_.ap().opt()], outs=[d_.ap().opt()])))(src, dst), reads=[srcT], writes=[dstT], dma=key)
        coll(ktc_loc, tKTCL, ktc_all, tKTCA, f"cc{l}a")
        coll(vc_loc, tVCL, vc_all, tVCA, f"cc{l}b")
        coll(nak_edge, tNAKE, nak_eall, tNAKEA, f"cc{l}c")
        coll(nav_edge, tNAVE, nav_eall, tNAVEA, f"cc{l}d")
        def load_gathered_kv():
            for r in range(4):
                P.op("sp", DMA(KTC[:, r * 2048:(r + 1) * 2048], ktc_all.ap()[r * 128:(r + 1) * 128, :]), reads=[tKTCA], pwrites=[tKTC], dma="ktc")
            for r in range(4):
                P.op("sp", DMA(VC[:, r * 16:(r + 1) * 16, :], vc_all.ap()[r * 2048:(r + 1) * 2048, :].rearrange("(k p) c -> p k c", p=128)),
                     reads=[tVCA], pwrites=[tVC], dma="vc")

        if STOP[0] == 'ex':
            P.barrier()
            return nc, P
        load_ln(l, 0)
        cur_cls = [None]
        for st in range(5):
            tiles = list(range(4 * st, 4 * st + 4)) if st < 4 else [16, 17]
            if l == nlayers - 1 and st == 4:
                continue
            n = len(tiles)
            ctx = st == 4
            load_bc(l, 0, ctx)
            for ti, i in enumerate(tiles):
                P.op("sp", DMA(HT4[:, ti].rearrange("p a b -> p (a b)"), hts[i]), reads=[tHTS[i]], writes=[tHT[ti]], dma=f"ht{ti}")
            wq, wqT = load_wblock(win[l * 5 + 2])
            def proj_qa(ti, i):
                ps, pT = next_ps(0, 3)
                for kc in range(8):
                    P.op("pe", MM(ps[:, 0:384], HT4[:, ti, kc, :], wq[:, kc, 0:384], kc == 0, kc == 7), reads=[tHT[ti], wqT],
                         writes=[pT] if kc == 0 else (), pwrites=() if kc == 0 else [pT])
                P.op("act", ACT(KQB[:, 0:384], ps[:, 0:384], AF.Identity, scale=0.125), reads=[pT], writes=[tKQB])

                def evac_qa(kc, pap, ppT, ti=ti):
                    P.op("dve", CP(QTA[:, kc, ti * 128:(ti + 1) * 128], pap), reads=[ppT], pwrites=[tQTA])
                transpose_tile(KQB, [tKQB], 3, None, None, evac_qa)
            for ta in range(0, n, 2):
                lists = []
                for ti in range(ta, min(n, ta + 2)):
                    CUR[0] = ti % 2
                    ILV[0] = True
                    P.record_begin()
                    proj_qa(ti, tiles[ti])
                    lists.append(P.record_end())
                CUR[0] = 0
                ILV[0] = False
                P.replay_interleaved(lists)
            wq, wqT = load_wblock(win[l * 5 + 3])
            def proj_qc(ti, i):
                P.op("sp", DMA(ROPE[:, :], rope[i]), writes=[tROPE], dma="rope")
                ps, pT = next_ps(0, 3)
                for kc in range(8):
                    P.op("pe", MM(ps[:, 0:384], HT4[:, ti, kc, :], wq[:, kc, 0:384], kc == 0, kc == 7), reads=[tHT[ti], wqT],
                         writes=[pT] if kc == 0 else (), pwrites=() if kc == 0 else [pT])
                qk_norm_rope(ps[:, 0:384], [pT], 6, GQK[:, 0:64], KQB[:, 0:384], 0.125, [tKQB], True)

                def evac_qc(kc, pap, ppT, ti=ti):
                    P.op("dve", CP(QTC[:, kc, ti * 128:(ti + 1) * 128], pap), reads=[ppT], pwrites=[tQTC])
                transpose_tile(KQB, [tKQB], 3, None, None, evac_qc)
            for ta in range(0, n, 2):
                lists = []
                for ti in range(ta, min(n, ta + 2)):
                    CUR[0] = ti % 2
                    ILV[0] = True
                    P.record_begin()
                    proj_qc(ti, tiles[ti])
                    lists.append(P.record_end())
                CUR[0] = 0
                ILV[0] = False
                P.replay_interleaved(lists)
            wq, wqT = load_wblock(win[l * 5 + 4])
            def proj_zb(ti, i):
                ps, pT = next_ps(0, 3)
                for kc in range(8):
                    P.op("pe", MM(ps[:, :], HT4[:, ti, kc, :], wq[:, kc, :], kc == 0, kc == 7), reads=[tHT[ti], wqT],
                         writes=[pT] if kc == 0 else (), pwrites=() if kc == 0 else [pT])
                P.op("act", ACT(ZGp[:, :], ps[:, :], AF.Gelu_apprx_tanh), reads=[pT], writes=[tZGp])
                ln_stats(ZGp[:, 256:512], [tZGp], 256)
                P.op("dve", TS(KQB[:, 0:256], ZGp[:, 256:512], MV[:, 0:1], MV[:, 3:4], ALU.subtract, ALU.mult), reads=[tZGp, tMV], writes=[tKQB])
                ps2, p2T = (PSF[4 + CUR[0]], PSFT[4 + CUR[0]]) if ILV[0] else (PSF[5], PSFT[5])
                for g in range(4):
                    P.op("pe", MM(ps2[:, g * 64:(g + 1) * 64], WST[:, g * 128:(g + 1) * 128], KQB[:, g * 64:(g + 1) * 64], True, True),
                         reads=[tKQB, tLAYC], writes=[p2T] if g == 0 else (), pwrites=() if g == 0 else [p2T])
                P.op("dve", TT(TMPA[:, 0:256], ps2[:, 0:256], GSGU[:, :], ALU.mult), reads=[p2T, tLAYC], writes=[tTMPA])
                for g in range(4):
                    P.op("dve", TS(TMPA[:, g * 64:(g + 1) * 64], TMPA[:, g * 64:(g + 1) * 64], BS[:, l * 4 + g:l * 4 + g + 1], None, ALU.add),
                         reads=[tTMPA, tCONST], writes=[tTMPA])
                P.op("dve", TT(TMPA[:, 0:256], TMPA[:, 0:256], ZGp[:, 0:256], ALU.mult), reads=[tTMPA, tZGp], writes=[tTMPA])
                rms_rstd(TMPA[:, 0:256], [tTMPA], 1, 256, TMPB, tTMPB)
                P.op("dve", TS(MERGB[:, ti, 384:640], TMPA[:, 0:256], SS[:, 8:9], None, ALU.mult), reads=[tTMPA, tSS], pwrites=[tMERG[ti]])

            for ta in range(0, n, 2):
                lists = []
                for ti in range(ta, min(n, ta + 2)):
                    CUR[0] = ti % 2
                    ILV[0] = True
                    P.record_begin()
                    proj_zb(ti, tiles[ti])
                    lists.append(P.record_end())
                CUR[0] = 0
                ILV[0] = False
                P.replay_interleaved(lists)
            if st == 0:
                load_gathered_kv()
            if not ctx:
                lo, hi = max(0, 4 * st - 2), min(16, 4 * st + 6)
                s0 = lo - (4 * st - 2)
                for j in range(hi - lo):
                    P.op("sp", DMA(NAK[:, :, (s0 + j) * 128:(s0 + j + 1) * 128], nak_loc[lo + j].rearrange("p (c k) -> p c k", c=3)),
                         reads=[tNAKL], writes=[tNAK] if j == 0 else (), pwrites=() if j == 0 else [tNAK], dma="nak")
                P.op("sp", DMA(NAV[:, s0:s0 + hi - lo, :], nav_loc[lo:hi].rearrange("t p c -> p t c")), reads=[tNAVL], writes=[tNAV], dma="nav")
                if st in (0, 3):
                    for j in range(2):
                        def mk_halo_k(c3, st=st, j=j):
                            def halo_k(e):
                                rk = PIDV[0]
                                if st == 0:
                                    src_r, e0, sl = (rk + 3) % 4, 2, 0
                                else:
                                    src_r, e0, sl = (rk + 1) % 4, 0, 6
                                return e.dma_start(out=NAK[:, c3, (sl + j) * 128:(sl + j + 1) * 128],
                                                   in_=nak_eall.ap()[bass.ds(src_r * 512 + (e0 + j) * 128, 128), c3 * 128:(c3 + 1) * 128])
                            return halo_k

                        def halo_v(e, st=st, j=j):
                            rk = PIDV[0]
                            if st == 0:
                                src_r, e0, sl = (rk + 3) % 4, 2, 0
                            else:
                                src_r, e0, sl = (rk + 1) % 4, 0, 6
                            return e.dma_start(out=NAV[:, sl + j, :], in_=nav_eall.ap()[bass.ds(src_r * 512 + (e0 + j) * 128, 128), :])
                        for c3 in range(3):
                            P.op("pool", mk_halo_k(c3), reads=[tNAKEA], pwrites=[tNAK], dma="nak")
                        P.op("pool", halo_v, reads=[tNAVEA], pwrites=[tNAV], dma="nav")
            if STOP[0] == 'p2a':
                P.barrier()
                return nc, P
            for ti, i in enumerate(tiles):
                if not ctx:
                    cls = 0 if i == 0 else 1 if i == 1 else 3 if i == 14 else 4 if i == 15 else 2
                    if cur_cls[0] != (l, cls):
                        P.op("pool", DMA(NAB[:, :], nab[l * 5 + cls]), writes=[tNAB], dma="nab")
                        cur_cls[0] = (l, cls)
                oac = [PSF[3], PSF[4]]
                oacT = [PSFT[3], PSFT[4]]
                for h in range(6):
                    c, pb = h // 2, (h % 2) * 64
                    q = QTA[pb:pb + 64, c, ti * 128:(ti + 1) * 128]
                    psa, paT = next_ps(0, 3)
                    psb_, pbT = next_ps(0, 3)
                    blocks = []
                    if not ctx:
                        sl_b = [(ti + b, b) for b in range(5)]
                        if i == 0:
                            sl_b.append((ti + 5, 5))
                        if i == 15:
                            sl_b.append((ti - 1, 5))
                        sl_b += [(8, None), (9, None)]
                    else:
                        sl_b = [(8, None), (9, None)]
                    for k, (slot, b) in enumerate(sl_b):
                        if k < 4:
                            blocks.append((slot, psa[:, k * 128:(k + 1) * 128], paT, b))
                        else:
                            blocks.append((slot, psb_[:, (k - 4) * 128:(k - 3) * 128], pbT, b))
                    nB = max(0, len(sl_b) - 4)
                    seen = set()
                    for (slot, reg, rT, b) in blocks:
                        first = id(rT) not in seen
                        seen.add(id(rT))
                        P.op("pe", MM(reg, NAK[pb:pb + 64, c, slot * 128:(slot + 1) * 128], q, True, b is None), reads=[tNAK, tQTA],
                             writes=[rT] if first else (), pwrites=() if first else [rT])
                        if b is not None:
                            P.op("pe", MM(reg, IDB[:, :], NAB[:, (h * 6 + b) * 128:(h * 6 + b + 1) * 128], False, True), reads=[tNAB, tCONST], pwrites=[rT])
                    if not ctx:
                        P.op("act", ACT(PTN[:, 0:512], psa[:, :], AF.Exp), reads=[paT], writes=[tPTN, tPTNb])
                        P.op("act", ACT(PTN[:, 512:512 + nB * 128], psb_[:, 0:nB * 128], AF.Exp), reads=[pbT], pwrites=[tPTN])
                    else:
                        P.op("act", ACT(PTN[:, 0:256], psa[:, 0:256], AF.Exp), reads=[paT], writes=[tPTN, tPTNb])
                    ob, obT = oac[h // 4], oacT[h // 4]
                    oreg = ob[0:65, (h % 4) * 128:(h % 4 + 1) * 128]
                    nb_ = len(blocks)
                    for bi, (slot, reg, rT, b) in enumerate(blocks):
                        firstw = (h % 4 == 0 and bi == 0)
                        P.op("pe", MM(oreg, NAV[:, slot, h * 65:(h + 1) * 65], PTN[:, bi * 128:(bi + 1) * 128], bi == 0, bi == nb_ - 1),
                             reads=[tNAV, tPTN, tPTNb], writes=[obT] if firstw else (), pwrites=() if firstw else [obT])
                P.op("dve", CP(OT[0:65, 0:512], oac[0][0:65, :]), reads=[oacT[0]], writes=[tOT])
                P.op("dve", CP(OT[0:65, 512:768], oac[1][0:65, 0:256]), reads=[oacT[1]], pwrites=[tOT])
                tp, tpT = PSF[5], PSFT[5]
                for h in range(6):
                    P.op("pe", TR(tp[:, h * 65:(h + 1) * 65], OT[0:65, h * 128:(h + 1) * 128], IDF[0:65, 0:65]), reads=[tOT, tCONST],
                         writes=[tpT] if h == 0 else (), pwrites=() if h == 0 else [tpT])
                tpv = tp[:, 0:390].rearrange("p (h d) -> p h d", h=6)
                P.op("dve", (lambda o, i_: (lambda e: e.reciprocal(out=o, in_=i_)))(SS[:, 0:6], tpv[:, :, 64]), reads=[tpT], writes=[tSS])
                P.op("dve", TT(OATT[:, ti, :].rearrange("p (h d) -> p h d", h=6), tpv[:, :, 0:64],
                               SS[:, 0:6].unsqueeze(2).to_broadcast([128, 6, 64]), ALU.mult), reads=[tpT, tSS], writes=[tW4A, tZG])
                rms_rstd(OATT[:, ti, :], [tW4A, tZG], 1, 384, TMPA, tTMPA)
                P.op("dve", TS(MERGB[:, ti, 0:384], OATT[:, ti, :], SS[:, 8:9], None, ALU.mult), reads=[tW4A, tZG, tSS], pwrites=[tMERG[ti]])

            if STOP[0] == 'p2na':
                P.barrier()
                return nc, P
            nq = n * 128
            kts = list(range(66)) if not ctx else [64, 65]
            nk = len(kts)
            for c in range(3):
                qs = [QTC[g * 64:g * 64 + 64, c, 0:nq] for g in range(2)]
                obs = [(PSF[4 + g], PSFT[4 + g]) for g in range(2)]
                ptb = [[(PT[0], tPT[0]), (PT[1], tPT[1])], [(PTN[:, 0:512], tPTN), (PTN[:, 512:1024], tPTNb)]]

                def S(g, k):
                    ps, pT = PSF[g * 2 + k % 2], PSFT[g * 2 + k % 2]
                    kt = kts[k]
                    P.op("pe", MM(ps[:, 0:nq], KTC[g * 64:g * 64 + 64, kt * 128:(kt + 1) * 128], qs[g], True, True), reads=[tKTC, tQTC], writes=[pT])
                for k0 in range(min(2, nk)):
                    for g in range(2):
                        S(g, k0)
                for k in range(nk):
                    for g in range(2):
                        ps, pT = PSF[g * 2 + k % 2], PSFT[g * 2 + k % 2]
                        pbuf, pbT = ptb[g][k % 2]
                        P.op("act", ACT(pbuf[:, 0:nq], ps[:, 0:nq], AF.Exp), reads=[pT], writes=[pbT])
                    if k + 2 < nk:
                        for g in range(2):
                            S(g, k + 2)
                    kt = kts[k]
                    for g in range(2):
                        ob, obT = obs[g]
                        pbuf, pbT = ptb[g][k % 2]
                        P.op("pe", MM(ob[0:65, 0:nq], VC[:, kt, g * 65:(g + 1) * 65], pbuf[:, 0:nq], k == 0, k == nk - 1),
                             reads=[tVC, pbT], writes=[obT] if k == 0 else (), pwrites=() if k == 0 else [obT])
                for g in range(2):
                    h = c + 3 * g
                    ob, obT = obs[g]
                    P.op("dve", CP(OT[0:65, 0:nq], ob[0:65, 0:nq]), reads=[obT], writes=[tOT])
                    tp, tpT = PSF[g], PSFT[g]
                    for ti in range(n):
                        P.op("pe", TR(tp[:, ti * 65:(ti + 1) * 65], OT[0:65, ti * 128:(ti + 1) * 128], IDF[0:65, 0:65]), reads=[tOT, tCONST],
                             writes=[tpT] if ti == 0 else (), pwrites=() if ti == 0 else [tpT])
                    tpv = tp[:, 0:65 * n].rearrange("p (t d) -> p t d", t=n)
                    P.op("dve", (lambda o, i_: (lambda e: e.reciprocal(out=o, in_=i_)))(SS[:, 0:n], tpv[:, :, 64]), reads=[tpT], writes=[tSS])
                    P.op("dve", TT(OATT[:, 0:n, h * 64:(h + 1) * 64], tpv[:, :, 0:64],
                                   SS[:, 0:n].unsqueeze(2).to_broadcast([128, n, 64]), ALU.mult), reads=[tpT, tSS], pwrites=[tW4A, tZG])
            for ti in range(n):
                rms_rstd(OATT[:, ti, :], [tW4A, tZG], 1, 384, TMPA, tTMPA)
                P.op("dve", TS(MERGB[:, ti, 640:1024], OATT[:, ti, :], SS[:, 8:9], None, ALU.mult), reads=[tW4A, tZG, tSS], pwrites=[tMERG[ti]])

            if STOP[0] == 'p2gqa':
                P.barrier()
                return nc, P
            for ti in range(n):
                def evac_m(kc, pap, ppT, ti=ti):
                    P.op("dve", TS(HT4[:, ti, kc, :], pap, GOUT[:, l * 8 + kc:l * 8 + kc + 1], None, ALU.mult), reads=[ppT, tCONST], pwrites=[tHT[ti]])
                transpose_tile(MERGB[:, ti, :], [tMERG[ti]], 8, None, None, evac_m)
            wo0, wo0T = load_wblock(wo[l * 2 + 0])
            wo1, wo1T = load_wblock(wo[l * 2 + 1])
            for ti, i in enumerate(tiles):
                for nb, (wb_, wbT_) in enumerate(((wo0, wo0T), (wo1, wo1T))):
                    ps, pT = PSF[3 + nb], PSFT[3 + nb]
                    for kc in range(8):
                        P.op("pe", MM(ps[:, :], HT4[:, ti, kc, :], wb_[:, kc, :], kc == 0, kc == 7), reads=[tHT[ti], wbT_],
                             writes=[pT] if kc == 0 else (), pwrites=() if kc == 0 else [pT])
                deepnorm(i, l, lambda nb: PSF[3 + nb][:, :], [PSFT[3], PSFT[4]])
        P.barrier()
        if STOP[0] == 'p2':
            return nc, P

        load_ln(l, 1)
        for st in range(5):
            tiles = list(range(4 * st, 4 * st + 4)) if st < 4 else [16, 17]
            if l == nlayers - 1 and st == 4:
                continue
            n = len(tiles)
            ctx = st == 4
            load_bc(l, 1, ctx)
            for ti, i in enumerate(tiles):
                ln_mod_transpose(i, l, 1, ti)
            for nb in range(11):
                wb_, wbT_ = load_wblock(wffi[l * 11 + nb])
                for ti, i in enumerate(tiles):
                    ps, pT = next_ps(0, 3)
                    for kc in range(8):
                        P.op("pe", MM(ps[:, :], HT4[:, ti, kc, :], wb_[:, kc, :], kc == 0, kc == 7), reads=[tHT[ti], wbT_],
                             writes=[pT] if kc == 0 else (), pwrites=() if kc == 0 else [pT])
                    P.op("act", ACT(TMPA[:, 0:256], ps[:, 0:256], AF.Silu), reads=[pT], writes=[tTMPA])
                    P.op("dve", TT(GTOK4[:, ti, nb * 256:(nb + 1) * 256], TMPA[:, 0:256], ps[:, 256:512], ALU.mult), reads=[tTMPA, pT], pwrites=[tG4[ti]])
            for ti, i in enumerate(tiles):
                GT = GTTa if ti < 2 else GTTb
                for grp in range(3):
                    nch = 8 if grp < 2 else 6

                    def evac_gg(kc, pap, ppT, grp=grp, ti=ti, GT=GT):
                        kk = grp * 8 + kc
                        P.op("dve", CP(GT[:, ti % 2, kk, :], pap), reads=[ppT], pwrites=[tGTT4[ti]])
                    transpose_tile(GTOK4[:, ti, grp * 1024:grp * 1024 + nch * 128], [tG4[ti]], nch, None, None, evac_gg)
            for nb2 in range(2):
                for kg in range(3):
                    nch = 8 if kg < 2 else 6
                    wb_, wbT_ = load_wblock(wffo[l * 6 + nb2 * 3 + kg])
                    for ti, i in enumerate(tiles):
                        GT = GTTa if ti < 2 else GTTb
                        ps, pT = PSF[ti], PSFT[ti]
                        for kcl in range(nch):
                            kc = kg * 8 + kcl
                            P.op("pe", MM(ps[:, :], GT[:, ti % 2, kc, :], wb_[:, kcl, :], kc == 0, kc == 21), reads=[tGTT4[ti], wbT_],
                                 writes=[pT] if kc == 0 else (), pwrites=() if kc == 0 else [pT])
                for ti, i in enumerate(tiles):
                    P.op("dve", TT(Y4[:, ti, nb2 * 512:(nb2 + 1) * 512], PSF[ti][:, :], BC[:, 0, nb2 * 512:(nb2 + 1) * 512], ALU.mult),
                         reads=[PSFT[ti], tBC[0]], pwrites=[tY4[ti]])
            for ti, i in enumerate(tiles):
                deepnorm_from(i, Y4[:, ti, :], [tY4[ti]])
                if l == nlayers - 1 and i < 16:
                    P.op("sp", DMA(y_out[i], X[:, i, :]), reads=[XT[i]], pwrites=[tY], dma="yout")
        P.barrier()

    P.op("sp", None, reads=[tY])
    return nc, P


def finish(nc, P):
    from contextlib import ExitStack
    keys = list(ENGS) + ["dma:" + k for k in P.dma_cnt]
    with ExitStack() as es:
        sems = {k: es.enter_context(nc.semaphore(k.replace(":", "_"))) for k in keys}
        block = es.enter_context(nc.Block())
        P.emit(nc, block, sems)
    return nc


def _blocks(w, nblk, width=512):
    K = w.shape[0]
    kc = K // 128
    return np.ascontiguousarray(w.reshape(kc, 128, nblk, width).transpose(2, 1, 0, 3)).reshape(nblk, 128, kc * width)


def prep_shared(inp, NL):
    L = 4
    f = np.float32
    w_mod, b_mod, w_in = inp["w_mod"], inp["b_mod"], inp["w_in"]
    sh = {}
    sh["wmod"] = np.concatenate([_blocks(w_mod[l], 12) for l in range(NL)], 0)
    bases = [0, 1024, 3072, 4096]
    bt = np.zeros((L, 128, 32), f)
    for l in range(L):
        for j, b0 in enumerate(bases):
            bt[l, :, j * 8:(j + 1) * 8] = b_mod[l, b0:b0 + 1024].reshape(8, 128).T
    sh["bmodT"] = bt
    sh["bmodg"] = np.ascontiguousarray(np.stack([np.stack([b_mod[l, 2048:3072], b_mod[l, 5120:6144]]) for l in range(L)]).reshape(L * 2, 1024))
    perm = np.concatenate([np.arange(h * 64, (h + 1) * 64) for h in (0, 3, 1, 4, 2, 5)])
    wl = []
    for l in range(NL):
        w = w_in[l]
        qa, ka, va, zb, qc, kc_, vc = w[:, 0:384], w[:, 384:768], w[:, 768:1152], w[:, 1152:1664], w[:, 1664:2048], w[:, 2048:2176], w[:, 2176:2304]
        z128 = np.zeros((1024, 128), f)
        blks = [np.concatenate([va, vc], 1), np.concatenate([ka, kc_], 1), np.concatenate([qa, z128], 1),
                np.concatenate([qc[:, perm], z128], 1), zb]
        wl.append(_blocks(np.concatenate(blks, 1), 5))
    sh["win"] = np.concatenate(wl, 0)
    sh["wo"] = np.concatenate([_blocks(inp["w_o"][l], 2) for l in range(NL)], 0)
    wl = []
    for l in range(NL):
        w = inp["w_ffn_in"][l]
        a, b = w[:, :2816].reshape(1024, 11, 256), w[:, 2816:].reshape(1024, 11, 256)
        wl.append(_blocks(np.concatenate([a, b], 2).reshape(1024, 11 * 512), 11))
    sh["wffi"] = np.concatenate(wl, 0)
    wl = []
    for l in range(NL):
        w = np.zeros((3072, 1024), f)
        w[:2816] = inp["w_ffn_out"][l]
        wl.append(np.ascontiguousarray(w.reshape(3, 8, 128, 2, 512).transpose(3, 0, 2, 1, 4)).reshape(6, 128, 4096))
    sh["wffo"] = np.concatenate(wl, 0)
    sh["lnp"] = np.ascontiguousarray(np.stack([np.stack([inp["ln1_g"][l], inp["ln1_b"][l], inp["ln2_g"][l], inp["ln2_b"][l]]) for l in range(L)]).reshape(L * 4, 1024))
    sh["gout"] = np.ascontiguousarray(np.concatenate([inp["g_out"][l].reshape(8, 128).T for l in range(L)], 1))
    sh["gsgu"] = np.ascontiguousarray(inp["g_sgu"])
    sh["gqk"] = np.ascontiguousarray(np.concatenate([inp["g_q"], inp["g_k"]], 1))
    sh["wsT"] = np.ascontiguousarray(np.stack([inp["w_s"][l].transpose(2, 0, 1).reshape(128, 512) for l in range(L)]))
    sh["bsd"] = np.ascontiguousarray(np.concatenate([inp["b_s"][l].T for l in range(L)], 1))
    sh["identd"] = np.eye(128, dtype=f)
    return sh


def rope_table(tok):
    f = np.float32
    row = (tok // 64).astype(f)
    col = (tok % 64).astype(f)
    inv = (1.0 / (np.float32(10000.0) ** (np.arange(16, dtype=f) / np.float32(16)))).astype(f)
    ar = row[:, None] * inv[None, :]
    ac = col[:, None] * inv[None, :]
    return np.concatenate([np.cos(ar), np.cos(ac), np.sin(ar), np.sin(ac)], 1).astype(f)


def nab_tables(rpb, qi, L):
    out = np.full((L, 5, 128, 6, 6, 128), np.float32(-30000.0), np.float32)
    p = np.arange(128)
    for ci, i in enumerate((0, 1, 2, 14, 15)):
        G = 16 * qi + i
        r = 2 * G + p // 64
        c = p % 64
        rs = np.clip(r - 4, 0, 120)
        cs = np.clip(c - 8, 0, 48)
        for b in range(6):
            if b < 5:
                Gk = G - 2 + b
            elif i == 0:
                Gk = G + 3
            elif i == 15:
                Gk = G - 3
            else:
                continue
            if Gk < 0 or Gk > 63:
                continue
            kr = 2 * Gk + p // 64
            kc = p % 64
            ok = (kr[:, None] >= rs[None, :]) & (kr[:, None] < rs[None, :] + 8) & (kc[:, None] >= cs[None, :]) & (kc[:, None] < cs[None, :] + 16)
            dr = np.clip(kr[:, None] - r[None, :] + 7, 0, 14)
            dc = np.clip(kc[:, None] - c[None, :] + 15, 0, 30)
            for l in range(L):
                vals = rpb[l][:, dr, dc]
                out[l, ci, :, :, b, :] = np.where(ok[None], vals, np.float32(-30000.0)).transpose(1, 0, 2)
    return out.reshape(L * 5, 128, 4608)


_CACHE = {}


def kernel(**inp):
    inp = {k: np.asarray(v, dtype=np.float32) for k, v in inp.items()}
    if "nc" not in _CACHE:
        nc, P = build(NLAYERS_BUILD)
        _CACHE["nc"] = finish(nc, P)
    nc = _CACHE["nc"]
    import time as _t
    t0 = _t.time()
    NL = NLAYERS_BUILD
    sh = prep_shared(inp, NL)
    print("prep shared", _t.time() - t0, flush=True)
    in_maps = []
    for core in range(8):
        b, qi = core // 4, core % 4
        m = dict(sh)
        xs = inp["x"][b, 2048 * qi:2048 * (qi + 1)].reshape(16, 128, 1024)
        m["x_in"] = np.ascontiguousarray(np.concatenate([xs, inp["ctx"][b].reshape(2, 128, 1024)], 0))
        cv = np.empty((128, 16), np.float32)
        cv[:, 0::2] = inp["c"][b].reshape(8, 128).T
        cv[:, 1::2] = inp["c_ctx"].reshape(8, 128).T
        m["cvec"] = cv
        rp = np.empty((NT, 128, 64), np.float32)
        for i in range(16):
            rp[i] = rope_table(2048 * qi + 128 * i + np.arange(128))
        rp[16:, :, 0:32] = 1.0
        rp[16:, :, 32:64] = 0.0
        m["rope"] = rp
        m["nab"] = nab_tables(inp["rpb"], qi, NL)
        in_maps.append(m)
    print("prep all", _t.time() - t0, flush=True)
    res = run_bass_kernel_spmd(nc, in_maps, core_ids=list(range(8)))
    print("run done", _t.time() - t0, flush=True)
    out = np.empty((2, 8192, 1024), np.float32)
    for core in range(8):
        b, qi = core // 4, core % 4
        out[b, 2048 * qi:2048 * (qi + 1)] = np.asarray(res.results[core]["y_out"]).reshape(2048, 1024)
    return out
```

```python
import numpy as np
import concourse.bass as bass
import concourse.mybir as mybir
from concourse.bass_utils import run_bass_kernel_spmd

F32 = mybir.dt.float32
BF16 = mybir.dt.bfloat16
AF = mybir.ActivationFunctionType
ALU = mybir.AluOpType
AX = mybir.AxisListType

L = 4
NT = 18
ALPHA = float(8.0 ** 0.25)
EPS = 1e-6
NLAYERS_BUILD = L


class T:
    __slots__ = ("name", "w", "r")

    def __init__(self, name):
        self.name = name
        self.w = {}
        self.r = {}


ENGS = ["pe", "dve", "act", "pool", "sp"]
PIDV = [None]


class Plan:
    def __init__(self):
        self.ops = {e: [] for e in ENGS}
        self.known = {e: {} for e in ENGS}
        self.dma_cnt = {}
        self.waited = {e: set() for e in ENGS}

    def _res(self, hs):
        out = []
        for t in hs:
            r = t.resolve() if hasattr(t, "resolve") else t
            if isinstance(r, (list, tuple)):
                out.extend(r)
            else:
                out.append(r)
        return out

    def record_begin(self):
        self._rec = []

    def record_end(self):
        r, self._rec = self._rec, None
        return r

    def replay_interleaved(self, lists):
        its = [list(l) for l in lists]
        pos = [0] * len(its)
        while any(pos[j] < len(its[j]) for j in range(len(its))):
            for j in range(len(its)):
                if pos[j] < len(its[j]):
                    self.op(*its[j][pos[j]])
                    pos[j] += 1

    def op(self, eng, fn, reads=(), writes=(), pwrites=(), dma=None):
        reads, writes, pwrites = self._res(reads), self._res(writes), self._res(pwrites)
        if getattr(self, "_rec", None) is not None:
            self._rec.append((eng, fn, reads, writes, pwrites, dma))
            return
        idx = len(self.ops[eng])
        if dma is None:
            tok = (eng, idx + 1)
        else:
            self.dma_cnt[dma] = self.dma_cnt.get(dma, 0) + 1
            tok = ("dma:" + dma, self.dma_cnt[dma])
        need = {}

        def addw(d):
            for k, v in d.items():
                if need.get(k, 0) < v:
                    need[k] = v

        for t in reads:
            addw(t.w)
        for t in writes:
            addw(t.w)
            addw(t.r)
        for t in pwrites:
            addw(t.r)
        waits = []
        kn = self.known[eng]
        for k, v in need.items():
            if kn.get(k, 0) < v:
                kn[k] = v
                waits.append((k, v))
                if not k.startswith("dma:"):
                    self.waited[k].add(v)
        if fn is not None:
            for t in reads:
                if t.r.get(tok[0], 0) < tok[1]:
                    t.r[tok[0]] = tok[1]
            for t in writes:
                t.w = {tok[0]: tok[1]}
                t.r = {}
            for t in pwrites:
                if t.r:
                    t.w = {tok[0]: tok[1]}
                    t.r = {}
                else:
                    t.w[tok[0]] = max(t.w.get(tok[0], 0), tok[1])
        self.ops[eng].append((waits, fn, dma))

    def barrier(self):
        latest = {}
        for e in ENGS:
            n = len(self.ops[e])
            if n:
                latest[e] = n
        for k, v in self.dma_cnt.items():
            latest["dma:" + k] = v
        for e in ENGS:
            waits = []
            kn = self.known[e]
            for k, v in latest.items():
                if k == e:
                    continue
                if not k.startswith("dma:"):
                    vv = v
                    while vv > 0 and (self.ops[k][vv - 1][1] is None or self.ops[k][vv - 1][2] is not None):
                        vv -= 1
                    if vv == 0:
                        continue
                    v = vv
                if kn.get(k, 0) < v:
                    kn[k] = v
                    waits.append((k, v))
                    if not k.startswith("dma:"):
                        self.waited[k].add(v)
            self.ops[e].append((waits, None, None))

    def emit(self, nc, block, sems):
        rank = {}
        for e in ENGS:
            s = sorted(self.waited[e])
            rank[e] = {v: i + 1 for i, v in enumerate(s)}
        plan = self

        def run(eng_name):
            def body(e):
                if eng_name == "pool":
                    PIDV[0] = e.partition_id()
                for i, (waits, fn, dma) in enumerate(plan.ops[eng_name]):
                    for k, v in waits:
                        if k.startswith("dma:"):
                            if k.startswith("dma:cc"):
                                e.wait_ge(sems[k], 1)
                            else:
                                e.wait_ge(sems[k], 16 * v)
                        else:
                            e.wait_ge(sems[k], rank[k][v])
                    if fn is None:
                        continue
                    ins = fn(e)
                    if dma is not None:
                        if dma.startswith("cc"):
                            ins.then_inc(sems["dma:" + dma])
                        else:
                            ins.then_inc(sems["dma:" + dma], 16)
                    elif (i + 1) in rank[eng_name]:
                        ins.then_inc(sems[eng_name], 1)
            return body

        block.tensor(run("pe"))
        block.vector(run("dve"))
        block.scalar(run("act"))
        block.gpsimd(run("pool"))
        block.sync(run("sp"))


def MM(out, l, r, st, sp):
    return lambda e: e.matmul(out, lhsT=l, rhs=r, start=st, stop=sp)


def TR(out, in_, ident):
    return lambda e: e.transpose(out=out, in_=in_, identity=ident)


def ACT(out, in_, func, scale=None, bias=None):
    kw = {}
    if scale is not None:
        kw["scale"] = scale
    if bias is not None:
        kw["bias"] = bias
    return lambda e: e.activation(out=out, in_=in_, func=func, **kw)


def TS(out, in0, s1, s2, op0, op1=None):
    if op1 is None:
        return lambda e: e.tensor_scalar(out=out, in0=in0, scalar1=s1, scalar2=None, op0=op0)
    return lambda e: e.tensor_scalar(out=out, in0=in0, scalar1=s1, scalar2=s2, op0=op0, op1=op1)


def TT(out, in0, in1, op):
    return lambda e: e.tensor_tensor(out=out, in0=in0, in1=in1, op=op)


def STT(out, in0, scalar, in1, op0, op1):
    return lambda e: e.scalar_tensor_tensor(out=out, in0=in0, scalar=scalar, in1=in1, op0=op0, op1=op1)


def CP(out, in_):
    return lambda e: e.tensor_copy(out=out, in_=in_)


def DMA(out, in_):
    return lambda e: e.dma_start(out=out, in_=in_)


def MEMSET(ap, v):
    return lambda e: e.memset(ap, v)


STOP = [None]


def build(nlayers=L, debug_x=False):
    nc = bass.Bass("TRN2", target_bir_lowering=False)
    P = Plan()

    def din(name, shape):
        return nc.dram_tensor(name, shape, F32, kind="ExternalInput").ap()

    x_in = din("x_in", [NT, 128, 1024])
    cvec = din("cvec", [128, 16])
    wmod = din("wmod", [nlayers * 12, 128, 4096])
    bmodT = din("bmodT", [L, 128, 32])
    bmodg = din("bmodg", [L * 2, 1024])
    win = din("win", [nlayers * 5, 128, 4096])
    wo = din("wo", [nlayers * 2, 128, 4096])
    wffi = din("wffi", [nlayers * 11, 128, 4096])
    wffo = din("wffo", [nlayers * 6, 128, 4096])
    lnp = din("lnp", [L * 4, 1024])
    gout = din("gout", [128, L * 8])
    gsgu = din("gsgu", [L, 256])
    gqk = din("gqk", [L, 128])
    wsT = din("wsT", [L, 128, 512])
    bsd = din("bsd", [128, L * 4])
    rope = din("rope", [NT, 128, 64])
    nab = din("nab", [nlayers * 5, 128, 4608])
    identd = din("identd", [128, 128])
    y_out = nc.dram_tensor("y_out", [16, 128, 1024], F32, kind="ExternalOutput").ap()

    gst = nc.dram_tensor("gst", [L * 4, 128, 1024], F32).ap()
    hts = nc.dram_tensor("hts", [NT, 128, 1024], BF16).ap()
    nak_loc = nc.dram_tensor("nak_loc", [16, 128, 384], BF16).ap()
    nav_loc = nc.dram_tensor("nav_loc", [16, 128, 390], BF16).ap()
    ktc_loc = nc.dram_tensor("ktc_loc", [128, 2048], BF16)
    ktc_all = nc.dram_tensor("ktc_all", [512, 2048], BF16)
    vc_loc = nc.dram_tensor("vc_loc", [2048, 130], BF16)
    vc_all = nc.dram_tensor("vc_all", [8192, 130], BF16)
    nak_edge = nc.dram_tensor("nak_edge", [512, 384], BF16)
    nak_eall = nc.dram_tensor("nak_eall", [2048, 384], BF16)
    nav_edge = nc.dram_tensor("nav_edge", [512, 390], BF16)
    nav_eall = nc.dram_tensor("nav_eall", [2048, 390], BF16)

    off = [16512]
    OFF = {}
    LIMIT = 229344

    def sb(name, shape, dt, at=None):
        nbytes = int(np.prod(shape[1:])) * (4 if dt == F32 else 2)
        nbytes = (nbytes + 63) // 64 * 64
        if at is None:
            o = off[0]
            off[0] += nbytes
            assert off[0] <= LIMIT, (name, off[0])
        else:
            o = at
        t = nc.alloc_sbuf_tensor_at(name, shape, dt, offset=o)
        OFF[name] = o
        return t

    X = sb("X", [128, NT, 1024], F32)
    KTC = sb("KTC", [128, 8448], BF16)
    VC = sb("VC", [128, 66, 130], BF16)
    NAK = sb("NAK", [128, 3, 1280], BF16)
    NAV = sb("NAV", [128, 10, 390], BF16)
    NAB = sb("NAB", [128, 4608], BF16)
    BC = sb("BC", [128, 3, 1024], F32)
    WB = [sb(f"WB{i}", [128, 8, 512], BF16) for i in range(2)]
    HT4 = sb("HT4", [128, 4, 8, 128], BF16)
    MERGB = sb("MERGB", [128, 4, 1024], BF16)
    att0 = off[0]
    QTA = sb("QTA", [128, 3, 512], BF16)
    QTC = sb("QTC", [128, 3, 512], BF16)
    PT = [sb(f"PT{i}", [128, 512], BF16) for i in range(2)]
    PTN = sb("PTN", [128, 1024], BF16)
    OT = sb("OT", [128, 768], F32)
    att1 = off[0]
    assert att0 + 11264 <= att1
    GTOK4 = sb("GTOK4", [128, 4, 2816], BF16, at=OFF["KTC"])
    GTTa = sb("GTTa", [128, 2, 22, 128], BF16, at=OFF["KTC"] + 22528)
    GTTb = sb("GTTb", [128, 2, 22, 128], BF16, at=att0)
    assert 22528 + 11264 <= 16896 + 17160
    Y4 = sb("Y4", [128, 4, 1024], F32, at=OFF["NAK"])
    assert OFF["NAB"] + 9216 - OFF["NAK"] >= 16384
    WBX = [sb(f"WBX{i}", [128, 8, 512], BF16, at=OFF["KTC"] + 8192 * i) for i in range(4)]
    RLAT = sb("RLAT", [128, 8, 128], BF16, at=att0)
    RCTX = sb("RCTX", [128, 8, 128], BF16, at=att0 + 2048)
    W4A = sb("W4A", [128, 1024], F32)
    ZG = sb("ZG", [128, 512], F32)
    OATT = sb("OATT", [128, 4, 384], F32, at=OFF["W4A"])
    ZG_1 = sb("ZG1", [128, 512], F32, at=OFF["W4A"])
    W2A_0 = sb("W2A", [128, 1024], BF16)
    TMPA_0 = sb("TMPA", [128, 512], F32)
    TMPB_0 = sb("TMPB", [128, 512], F32)
    KQB_0 = sb("KQB", [128, 512], BF16)
    TMPA_1 = sb("TMPA1", [128, 512], F32, at=OFF["PTN"])
    TMPB_1 = sb("TMPB1", [128, 512], F32, at=OFF["OT"])
    KQB_1 = sb("KQB1", [128, 512], BF16, at=OFF["PT0"])
    W2A_1 = sb("W2A1", [128, 1024], BF16, at=OFF["QTA"])
    KAT_1 = sb("KAT1", [128, 4, 128], BF16, at=OFF["QTA"] + 2048)
    VAT_1 = sb("VAT1", [128, 6, 65], BF16, at=OFF["QTC"])
    VCT_1 = sb("VCT1", [128, 2, 65], BF16, at=OFF["QTC"] + 896)
    ROPE_1 = sb("ROPE1", [128, 64], F32, at=OFF["PT1"])
    SS_1 = sb("SS1", [128, 16], F32, at=OFF["PT1"] + 256)
    MV_1 = sb("MV1", [128, 8], F32, at=OFF["PT1"] + 320)
    ST6_1 = sb("ST61", [128, 12], F32, at=OFF["PT1"] + 384)
    VAT_0 = sb("VAT", [128, 6, 65], BF16)
    VCT_0 = sb("VCT", [128, 2, 65], BF16)
    KAT_0 = sb("KAT", [128, 4, 128], BF16)
    IDB = sb("IDB", [128, 128], BF16)
    IDF = sb("IDF", [128, 128], F32)
    ROPE_0 = sb("ROPE", [128, 64], F32)
    S2F = sb("S2F", [128, 16], F32)
    S2B = sb("S2B", [128, 8, 2], BF16)
    ONESB = sb("ONESB", [128, 128], BF16)
    MODF = sb("MODF", [128, L, 32, 2], F32)
    BMT = sb("BMT", [128, 32], F32)
    GOUT = sb("GOUT", [128, L * 8], F32)
    GSGU = sb("GSGU", [128, 256], F32)
    GQK = sb("GQK", [128, 128], F32)
    WST = sb("WST", [128, 512], BF16)
    BS = sb("BS", [128, L * 4], F32)
    ST6_0 = sb("ST6", [128, 12], F32)
    MV_0 = sb("MV", [128, 8], F32)
    SS_0 = sb("SS", [128, 16], F32)
    print("SBUF used", off[0], "of", LIMIT)

    PSF = [nc.alloc_psum_tensor(f"PSF{i}", [128, 512], F32) for i in range(6)]
    PST = [nc.alloc_psum_tensor(f"PST{i}", [128, 1024], BF16) for i in range(2)]
    PSFT = [T(f"psf{i}") for i in range(8)]
    pst_rr = [0]

    XT = [T(f"x{i}") for i in range(NT)]
    tKTC, tVC, tNAK, tNAV, tNAB = T("ktc"), T("vc"), T("nak"), T("nav"), T("nab")
    tBC = [T("bc0"), T("bc1"), T("bc2")]
    tWB = [T("wb0"), T("wb1")]
    tHT = [T(f"ht{i}") for i in range(4)]
    tMERG = [T(f"mg{i}") for i in range(4)]
    tQTA, tQTC = T("qta"), T("qtc")
    tPT = [T("pt0"), T("pt1")]
    tPTN, tOT = T("ptn"), T("ot")
    tPTNb = T("ptnb")
    tGTOK = T("gtok")
    tY4 = [T(f"y4{i}") for i in range(4)]
    tG4 = [T(f"g4{i}") for i in range(4)]
    tGTT4 = [T(f"gtt{i}") for i in range(4)]
    tW4A, tZG = T("w4a"), T("zg")
    CUR = [0]

    class BufP:
        def __init__(self, bufs):
            self.bufs = bufs

        def __getitem__(self, k):
            return self.bufs[CUR[0]][k]

    class HP:
        def __init__(self, hs):
            self.hs = hs

        def resolve(self):
            return self.hs[CUR[0]]
    TMPA, TMPB, KQB = BufP([TMPA_0, TMPA_1]), BufP([TMPB_0, TMPB_1]), BufP([KQB_0, KQB_1])
    W2A, KAT, VAT, VCT = BufP([W2A_0, W2A_1]), BufP([KAT_0, KAT_1]), BufP([VAT_0, VAT_1]), BufP([VCT_0, VCT_1])
    ILV = [False]
    ZGp = BufP([ZG, ZG_1])
    tZGp = HP([tZG, tW4A])
    tVAT_0, tVCT_0, tKAT_0, tW2A_0 = T("vat"), T("vct"), T("kat"), T("w2a")
    tCONST, tMODF, tLAYC = T("const"), T("modf"), T("layc")
    tW2A = HP([tW2A_0, tQTA])
    tKAT = HP([tKAT_0, tQTA])
    tVAT = HP([tVAT_0, tQTC])
    tVCT = HP([tVCT_0, tQTC])
    tTMPA = HP([T("tmpa"), [tPTN, tPTNb]])
    tTMPB = HP([T("tmpb"), tOT])
    tKQB = HP([T("kqb"), tPT[0]])
    tROPE = HP([T("rope"), tPT[1]])
    tST6 = HP([T("st6"), tPT[1]])
    tMV = HP([T("mv"), tPT[1]])
    tSS = HP([T("ss"), tPT[1]])
    ROPE, SS, MV, ST6 = BufP([ROPE_0, ROPE_1]), BufP([SS_0, SS_1]), BufP([MV_0, MV_1]), BufP([ST6_0, ST6_1])
    tGST, tHTS = T("gst"), [T(f"hts{i}") for i in range(NT)]
    tNAKL, tNAVL, tKTCL, tVCL, tNAKE, tNAVE = T("nakl"), T("navl"), T("ktcl"), T("vcl"), T("nake"), T("nave")
    tKTCA, tVCA, tNAKEA, tNAVEA = T("ktca"), T("vca"), T("nakea"), T("navea")
    tY = T("y")

    wb_rr = [0]
    WBALL = WB + WBX
    tWBALL = tWB + [T(f"wbx{i}") for i in range(4)]
    wb_pool = [6]

    def load_wblock(src_ap):
        i = wb_rr[0] % wb_pool[0]
        wb_rr[0] += 1
        P.op("pool", DMA(WBALL[i][:].rearrange("p a b -> p (a b)"), src_ap), writes=[tWBALL[i]], dma=f"wb{i}")
        return WBALL[i], tWBALL[i]

    ps_rr = [0]

    ps_set_rr = [0, 0]

    def next_ps(lo=0, hi=3):
        if ILV[0]:
            c = CUR[0]
            i = 2 * c + ps_set_rr[c] % 2
            ps_set_rr[c] += 1
            return PSF[i], PSFT[i]
        i = lo + ps_rr[0] % (hi - lo)
        ps_rr[0] += 1
        return PSF[i], PSFT[i]

    def ln_stats(src, srcT, n):
        nch = max(1, n // 512)
        w = n // nch
        for ci in range(nch):
            P.op("dve", (lambda o, i: (lambda e: e.bn_stats(out=o, in_=i)))(ST6[:, ci * 6:(ci + 1) * 6], src[:, ci * w:(ci + 1) * w]),
                 reads=srcT, writes=[tST6] if ci == 0 else (), pwrites=() if ci == 0 else [tST6])
        P.op("dve", (lambda o, i: (lambda e: e.bn_aggr(out=o, in_=i)))(MV[:, 0:2], ST6[:, 0:6 * nch].rearrange('p (n s) -> p n s', s=6)), reads=[tST6], writes=[tMV])
        P.op("act", ACT(MV[:, 2:3], MV[:, 1:2], AF.Ln, bias=EPS), reads=[tMV], writes=[tMV])
        P.op("act", ACT(MV[:, 3:4], MV[:, 2:3], AF.Exp, scale=-0.5), reads=[tMV], writes=[tMV])

    def rms_rstd(src, srcT, G, W, tmp, tmpT):
        P.op("dve", TT(tmp[:, 0:G * W], src, src, ALU.mult), reads=srcT, writes=[tmpT])
        P.op("dve", (lambda o, i: (lambda e: e.reduce_sum(out=o, in_=i, axis=AX.X)))(SS[:, 0:G], tmp[:, 0:G * W].rearrange("p (g w) -> p g w", g=G)),
             reads=[tmpT], writes=[tSS])
        P.op("act", ACT(SS[:, 0:G], SS[:, 0:G], AF.Ln, scale=1.0 / W, bias=EPS), reads=[tSS], writes=[tSS])
        P.op("act", ACT(SS[:, 8:8 + G], SS[:, 0:G], AF.Exp, scale=-0.5), reads=[tSS], writes=[tSS])


    def qk_norm_rope(src, srcT, H, gain, dst, oscale, dstT, full):
        n = H * 64
        P.op("act", ACT(TMPA[:, 0:n], src, AF.Identity), reads=srcT, writes=[tTMPA])
        rms_rstd(TMPA[:, 0:n], [tTMPA], H, 64, TMPB, tTMPB)
        v3 = lambda ap: ap.rearrange("p (h d) -> p h d", h=H)
        P.op("dve", TT(v3(TMPA[:, 0:n]), v3(TMPA[:, 0:n]), SS[:, 8:8 + H].unsqueeze(2).to_broadcast([128, H, 64]), ALU.mult), reads=[tTMPA, tSS], writes=[tTMPA])
        P.op("dve", STT(v3(TMPA[:, 0:n]), v3(TMPA[:, 0:n]), oscale, gain.unsqueeze(1).to_broadcast([128, H, 64]), ALU.mult, ALU.mult),
             reads=[tTMPA, tLAYC], writes=[tTMPA])
        v5 = lambda ap: ap.rearrange("p (h a s f) -> p h a s f", h=H, a=2, s=2)
        x1 = v5(TMPA[:, 0:n])[:, :, :, 0, :]
        x2 = v5(TMPA[:, 0:n])[:, :, :, 1, :]
        d1 = v5(dst)[:, :, :, 0, :]
        d2 = v5(dst)[:, :, :, 1, :]
        C = ROPE[:, 0:32].rearrange("p (a f) -> p a f", a=2).unsqueeze(1).to_broadcast([128, H, 2, 16])
        S_ = ROPE[:, 32:64].rearrange("p (a f) -> p a f", a=2).unsqueeze(1).to_broadcast([128, H, 2, 16])
        v4 = lambda ap: ap.rearrange("p (h a f) -> p h a f", h=H, a=2)
        t1 = v4(TMPB[:, 0:H * 32])
        t2 = v4(TMPB[:, H * 32:H * 64])
        P.op("dve", TT(t1, x1, C, ALU.mult), reads=[tTMPA, tROPE], writes=[tTMPB])
        P.op("dve", TT(t2, x2, S_, ALU.mult), reads=[tTMPA, tROPE], pwrites=[tTMPB])
        P.op("dve", TT(d1, t1, t2, ALU.subtract), reads=[tTMPB], writes=dstT if full else (), pwrites=() if full else dstT)
        P.op("dve", TT(t1, x2, C, ALU.mult), reads=[tTMPA, tROPE], writes=[tTMPB])
        P.op("dve", TT(t2, x1, S_, ALU.mult), reads=[tTMPA, tROPE], pwrites=[tTMPB])
        P.op("dve", TT(d2, t1, t2, ALU.add), reads=[tTMPB], pwrites=dstT)

    def transpose_tile(src_bf, srcT, nch, dst_fn, dstT, evac):
        if ILV[0]:
            j = CUR[0]
        else:
            j = pst_rr[0] % 2
            pst_rr[0] += 1
        psb, pT = PST[j], PSFT[6 + j]
        for kc in range(nch):
            P.op("pe", TR(psb[:, kc * 128:(kc + 1) * 128], src_bf[:, kc * 128:(kc + 1) * 128], IDB[:, :]),
                 reads=srcT + [tCONST], writes=[pT] if kc == 0 else (), pwrites=() if kc == 0 else [pT])
        for kc in range(nch):
            evac(kc, psb[:, kc * 128:(kc + 1) * 128], pT)

    P.op("sp", DMA(IDF[:, :], identd[:, :]), writes=[tCONST], dma="c0")
    P.op("sp", DMA(S2F[:, :], cvec[:, :]), pwrites=[tCONST], dma="c0")
    P.op("sp", DMA(GOUT[:, :], gout[:, :]), pwrites=[tCONST], dma="c0")
    P.op("sp", DMA(BS[:, :], bsd[:, :]), pwrites=[tCONST], dma="c0")
    for i in range(NT):
        P.op("sp", DMA(X[:, i, :], x_in[i]), writes=[XT[i]], dma="xin")
    P.op("dve", CP(IDB[:, :], IDF[:, :]), reads=[tCONST], pwrites=[tCONST])
    P.op("dve", MEMSET(ONESB[:, :], 1.0), pwrites=[tCONST])
    P.op("act", ACT(S2F[:, :], S2F[:, :], AF.Silu), reads=[tCONST], writes=[tCONST])
    P.op("dve", CP(S2B[:].rearrange("p a b -> p (a b)"), S2F[:, :]), reads=[tCONST], writes=[tCONST])
    for kc in range(8):
        P.op("dve", TS(RLAT[:, kc, :], ONESB[:, :], S2F[:, 2 * kc:2 * kc + 1], None, ALU.mult), reads=[tCONST], pwrites=[tGTOK])
        P.op("dve", TS(RCTX[:, kc, :], ONESB[:, :], S2F[:, 2 * kc + 1:2 * kc + 2], None, ALU.mult), reads=[tCONST], pwrites=[tGTOK])
    P.op("dve", MEMSET(VAT[:, :, :], 1.0), writes=[tVAT])
    P.op("dve", MEMSET(VCT[:, :, :], 1.0), writes=[tVCT])

    for l in range(nlayers):
        P.op("sp", DMA(BMT[:, :], bmodT[l]), writes=[tLAYC], dma="c1")
        psm, psmT = PSF[5], PSFT[5]
        for jj, nbs in enumerate([(0, 1), (2, 3), (6, 7), (8, 9)]):
            for half, nb in enumerate(nbs):
                wbuf, wT = load_wblock(wmod[l * 12 + nb])
                for oc4 in range(4):
                    col = (jj * 8 + half * 4 + oc4) * 2
                    for kc in range(8):
                        first = (jj == 0 and half == 0 and oc4 == 0 and kc == 0)
                        P.op("pe", MM(psm[:, col:col + 2], wbuf[:, kc, oc4 * 128:(oc4 + 1) * 128], S2B[:, kc, :], kc == 0, kc == 7),
                             reads=[wT, tCONST], writes=[psmT] if first else (), pwrites=() if first else [psmT])
        pv = psm[:, 0:64].rearrange("p (a b) -> p a b", b=2)
        for s in range(2):
            P.op("dve", TT(MODF[:, l, :, s], pv[:, :, s], BMT[:, :], ALU.add), reads=[psmT, tLAYC], pwrites=[tMODF])
        for a0 in (8, 24):
            P.op("dve", TS(MODF[:, l, a0:a0 + 8, :], MODF[:, l, a0:a0 + 8, :], 1.0, None, ALU.add), reads=[tMODF], writes=[tMODF])
        for gi, nbs in enumerate([(4, 5), (10, 11)]):
            for half, nb in enumerate(nbs):
                wbuf, wT = load_wblock(wmod[l * 12 + nb])
                P.op("sp", DMA(TMPA[:, :], bmodg[l * 2 + gi, half * 512:(half + 1) * 512].partition_broadcast(128)),
                     writes=[tTMPA], dma="tmpa")
                for s, R in enumerate((RLAT, RCTX)):
                    ps, pT = next_ps(0, 3)
                    for kc in range(8):
                        P.op("pe", MM(ps[:, :], R[:, kc, :], wbuf[:, kc, :], kc == 0, kc == 7), reads=[wT, tGTOK],
                             writes=[pT] if kc == 0 else (), pwrites=() if kc == 0 else [pT])
                    P.op("dve", TT(TMPB[:, :], ps[:, :], TMPA[:, :], ALU.add), reads=[pT, tTMPA], writes=[tTMPB])
                    P.op("sp", DMA(gst[l * 4 + gi * 2 + s][:, half * 512:(half + 1) * 512], TMPB[:, :]), reads=[tTMPB], pwrites=[tGST], dma="gst")
    P.barrier()
    wb_pool[0] = 2
    wb_rr[0] = 0
    if STOP[0] == 'p0':
        return nc, P

    def load_bc(l, sub, ctx):
        P.op("sp", DMA(BC[:, 0, :], gst[l * 4 + sub * 2 + (1 if ctx else 0)]), reads=[tGST], writes=[tBC[0]], dma="bc0")

    def load_ln(l, sub):
        for j in range(2):
            r = l * 4 + sub * 2 + j
            P.op("sp", DMA(BC[:, 1 + j, :], lnp[r, :].partition_broadcast(128)), writes=[tBC[1 + j]], dma=f"bc{1 + j}")

    def ln_mod_transpose(i, l, which, slot):
        s = 1 if i >= 16 else 0
        ln_stats(X[:, i, :], [XT[i]], 1024)
        P.op("dve", TS(W2A[:, :], X[:, i, :], MV[:, 0:1], MV[:, 3:4], ALU.subtract, ALU.mult), reads=[XT[i], tMV], writes=[tW2A])

        def evac(kc, pap, pT):
            sc = MODF[:, l, which * 16 + 8 + kc, s:s + 1]
            sh = MODF[:, l, which * 16 + kc, s:s + 1]
            P.op("dve", TS(HT4[:, slot, kc, :], pap, sc, sh, ALU.mult, ALU.add), reads=[pT, tMODF], pwrites=[tHT[slot]])
        transpose_tile(W2A, [tW2A], 8, None, None, evac)

    def deepnorm_from(i, zsrc, zT):
        P.op("dve", STT(W4A[:, :], X[:, i, :], ALPHA, zsrc, ALU.mult, ALU.add), reads=[XT[i]] + zT, writes=[tW4A])
        ln_stats(W4A[:, :], [tW4A], 1024)
        P.op("dve", STT(W4A[:, :], W4A[:, :], MV[:, 0:1], BC[:, 1, :], ALU.subtract, ALU.mult), reads=[tW4A, tMV, tBC[1]], writes=[tW4A])
        P.op("dve", STT(X[:, i, :], W4A[:, :], MV[:, 3:4], BC[:, 2, :], ALU.mult, ALU.add), reads=[tW4A, tMV, tBC[2]], writes=[XT[i]])

    def deepnorm(i, l, ysrc_fn, yT):
        for nb in range(2):
            P.op("dve", TT(W4A[:, nb * 512:(nb + 1) * 512], ysrc_fn(nb), BC[:, 0, nb * 512:(nb + 1) * 512], ALU.mult),
                 reads=[yT[nb], tBC[0]], writes=[tW4A] if nb == 0 else (), pwrites=() if nb == 0 else [tW4A])
        deepnorm_from(i, W4A[:, :], [tW4A])

    for l in range(nlayers):
        P.op("sp", DMA(GSGU[:, :], gsgu[l, :].partition_broadcast(128)), writes=[tLAYC], dma="c1")
        P.op("sp", DMA(GQK[:, :], gqk[l, :].partition_broadcast(128)), pwrites=[tLAYC], dma="c1")
        P.op("pool", DMA(WST[:, :], wsT[l]), pwrites=[tLAYC], dma="c2")

        CUR[0] = 1
        P.op("dve", MEMSET(VAT[:, :, :], 1.0), writes=[tVAT])
        P.op("dve", MEMSET(VCT[:, :, :], 1.0), pwrites=[tVCT])
        CUR[0] = 0
        P.op("dve", MEMSET(VC[:, 64:66, :], 1.0), writes=[tVC])
        P.op("dve", MEMSET(NAV[:, 8:10, :], 1.0), writes=[tNAV])
        wv, wvT = load_wblock(win[l * 5 + 0])
        wk, wkT = load_wblock(win[l * 5 + 1])
        def p1_tile(i):
            lat = i < 16
            sl = i % 2
            P.op("sp", DMA(ROPE[:, :], rope[i]), writes=[tROPE], dma="rope")
            ln_mod_transpose(i, l, 0, sl)
            P.op("sp", DMA(hts[i], HT4[:, sl].rearrange("p a b -> p (a b)")), reads=[tHT[sl]], writes=[tHTS[i]], dma="hts")
            ps, pT = next_ps(0, 3)
            for kc in range(8):
                P.op("pe", MM(ps[:, :], HT4[:, sl, kc, :], wv[:, kc, :], kc == 0, kc == 7), reads=[tHT[sl], wvT],
                     writes=[pT] if kc == 0 else (), pwrites=() if kc == 0 else [pT])
            if lat:
                P.op("dve", CP(VAT[:, :, 0:64], ps[:, 0:384].rearrange("p (h d) -> p h d", h=6)), reads=[pT], pwrites=[tVAT])
                P.op("dve", CP(VCT[:, :, 0:64], ps[:, 384:512].rearrange("p (h d) -> p h d", h=2)), reads=[pT], pwrites=[tVCT])
                P.op("sp", DMA(nav_loc[i], VAT[:].rearrange("p a b -> p (a b)")), reads=[tVAT], pwrites=[tNAVL], dma="navl")
                P.op("sp", DMA(vc_loc.ap()[i * 128:(i + 1) * 128, :], VCT[:].rearrange("p a b -> p (a b)")), reads=[tVCT], pwrites=[tVCL], dma="vcl")
                if i in (0, 1, 14, 15):
                    e = i if i < 2 else i - 12
                    P.op("sp", DMA(nav_edge.ap()[e * 128:(e + 1) * 128, :], VAT[:].rearrange("p a b -> p (a b)")), reads=[tVAT], pwrites=[tNAVE], dma="nave")
            else:
                j = i - 16
                P.op("dve", CP(NAV[:, 8 + j, :].rearrange("p (h d) -> p h d", h=6)[:, :, 0:64], ps[:, 0:384].rearrange("p (h d) -> p h d", h=6)), reads=[pT], pwrites=[tNAV])
                P.op("dve", CP(VC[:, 64 + j, :].rearrange("p (h d) -> p h d", h=2)[:, :, 0:64], ps[:, 384:512].rearrange("p (h d) -> p h d", h=2)), reads=[pT], pwrites=[tVC])
            ps, pT = next_ps(0, 3)
            for kc in range(8):
                P.op("pe", MM(ps[:, :], HT4[:, sl, kc, :], wk[:, kc, :], kc == 0, kc == 7), reads=[tHT[sl], wkT],
                     writes=[pT] if kc == 0 else (), pwrites=() if kc == 0 else [pT])
            P.op("act", ACT(KQB[:, 0:384], ps[:, 0:384], AF.Identity), reads=[pT], writes=[tKQB])
            qk_norm_rope(ps[:, 384:512], [pT], 2, GQK[:, 64:128], KQB[:, 384:512], 1.0, [tKQB], False)
            kdst = []

            def evac_k(kc, pap, ppT, i=i, lat=lat):
                if kc < 3:
                    if lat:
                        P.op("dve", CP(KAT[:, kc, :], pap), reads=[ppT], pwrites=[tKAT])
                    else:
                        P.op("dve", CP(NAK[:, kc, (8 + i - 16) * 128:(9 + i - 16) * 128], pap), reads=[ppT], pwrites=[tNAK])
                else:
                    if lat:
                        P.op("dve", CP(KAT[:, 3, :], pap), reads=[ppT], pwrites=[tKAT])
                    else:
                        P.op("dve", CP(KTC[:, 8192 + (i - 16) * 128:8192 + (i - 15) * 128], pap), reads=[ppT], pwrites=[tKTC])
            transpose_tile(KQB, [tKQB], 4, None, None, evac_k)
            if lat:
                P.op("sp", DMA(nak_loc[i], KAT[:, 0:3, :].rearrange("p a b -> p (a b)")), reads=[tKAT], pwrites=[tNAKL], dma="nakl")
                P.op("sp", DMA(ktc_loc.ap()[:, i * 128:(i + 1) * 128], KAT[:, 3, :]), reads=[tKAT], pwrites=[tKTCL], dma="ktcl")
                if i in (0, 1, 14, 15):
                    e = i if i < 2 else i - 12
                    P.op("sp", DMA(nak_edge.ap()[e * 128:(e + 1) * 128, :], KAT[:, 0:3, :].rearrange("p a b -> p (a b)")), reads=[tKAT], pwrites=[tNAKE], dma="nake")

        for ia in range(0, NT, 2):
            lists = []
            for i in (ia, ia + 1):
                CUR[0] = i % 2
                ILV[0] = True
                P.record_begin()
                p1_tile(i)
                lists.append(P.record_end())
            CUR[0] = 0
            ILV[0] = False
            P.replay_interleaved(lists)

        if STOP[0] == 'p1':
            return nc, P
        def coll(src, srcT, dst, dstT, key):
            P.op("pool", (lambda s_, d_: (lambda e: e.collective_compute(
                "AllGather", ALU.bypass, replica_groups=[[0, 1, 2, 3], [4, 5, 6, 7]],
                ins=[s_.ap().opt()], outs=[d_.ap().opt()])))(src, dst), reads=[srcT], writes=[dstT], dma=key)
        coll(ktc_loc, tKTCL, ktc_all, tKTCA, f"cc{l}a")
        coll(vc_loc, tVCL, vc_all, tVCA, f"cc{l}b")
        coll(nak_edge, tNAKE, nak_eall, tNAKEA, f"cc{l}c")
        coll(nav_edge, tNAVE, nav_eall, tNAVEA, f"cc{l}d")
        def load_gathered_kv():
            for r in range(4):
                P.op("sp", DMA(KTC[:, r * 2048:(r + 1) * 2048], ktc_all.ap()[r * 128:(r + 1) * 128, :]), reads=[tKTCA], pwrites=[tKTC], dma="ktc")
            for r in range(4):
                P.op("sp", DMA(VC[:, r * 16:(r + 1) * 16, :], vc_all.ap()[r * 2048:(r + 1) * 2048, :].rearrange("(k p) c -> p k c", p=128)),
                     reads=[tVCA], pwrites=[tVC], dma="vc")

        if STOP[0] == 'ex':
            P.barrier()
            return nc, P
        load_ln(l, 0)
        cur_cls = [None]
        for st in range(5):
            tiles = list(range(4 * st, 4 * st + 4)) if st < 4 else [16, 17]
            if l == nlayers - 1 and st == 4:
                continue
            n = len(tiles)
            ctx = st == 4
            load_bc(l, 0, ctx)
            for ti, i in enumerate(tiles):
                P.op("sp", DMA(HT4[:, ti].rearrange("p a b -> p (a b)"), hts[i]), reads=[tHTS[i]], writes=[tHT[ti]], dma=f"ht{ti}")
            wq, wqT = load_wblock(win[l * 5 + 2])
            def proj_qa(ti, i):
                ps, pT = next_ps(0, 3)
                for kc in range(8):
                    P.op("pe", MM(ps[:, 0:384], HT4[:, ti, kc, :], wq[:, kc, 0:384], kc == 0, kc == 7), reads=[tHT[ti], wqT],
                         writes=[pT] if kc == 0 else (), pwrites=() if kc == 0 else [pT])
                P.op("act", ACT(KQB[:, 0:384], ps[:, 0:384], AF.Identity, scale=0.125), reads=[pT], writes=[tKQB])

                def evac_qa(kc, pap, ppT, ti=ti):
                    P.op("dve", CP(QTA[:, kc, ti * 128:(ti + 1) * 128], pap), reads=[ppT], pwrites=[tQTA])
                transpose_tile(KQB, [tKQB], 3, None, None, evac_qa)
            for ta in range(0, n, 2):
                lists = []
                for ti in range(ta, min(n, ta + 2)):
                    CUR[0] = ti % 2
                    ILV[0] = True
                    P.record_begin()
                    proj_qa(ti, tiles[ti])
                    lists.append(P.record_end())
                CUR[0] = 0
                ILV[0] = False
                P.replay_interleaved(lists)
            wq, wqT = load_wblock(win[l * 5 + 3])
            def proj_qc(ti, i):
                P.op("sp", DMA(ROPE[:, :], rope[i]), writes=[tROPE], dma="rope")
                ps, pT = next_ps(0, 3)
                for kc in range(8):
                    P.op("pe", MM(ps[:, 0:384], HT4[:, ti, kc, :], wq[:, kc, 0:384], kc == 0, kc == 7), reads=[tHT[ti], wqT],
                         writes=[pT] if kc == 0 else (), pwrites=() if kc == 0 else [pT])
                qk_norm_rope(ps[:, 0:384], [pT], 6, GQK[:, 0:64], KQB[:, 0:384], 0.125, [tKQB], True)

                def evac_qc(kc, pap, ppT, ti=ti):
                    P.op("dve", CP(QTC[:, kc, ti * 128:(ti + 1) * 128], pap), reads=[ppT], pwrites=[tQTC])
                transpose_tile(KQB, [tKQB], 3, None, None, evac_qc)
            for ta in range(0, n, 2):
                lists = []
                for ti in range(ta, min(n, ta + 2)):
                    CUR[0] = ti % 2
                    ILV[0] = True
                    P.record_begin()
                    proj_qc(ti, tiles[ti])
                    lists.append(P.record_end())
                CUR[0] = 0
                ILV[0] = False
                P.replay_interleaved(lists)
            wq, wqT = load_wblock(win[l * 5 + 4])
            def proj_zb(ti, i):
                ps, pT = next_ps(0, 3)
                for kc in range(8):
                    P.op("pe", MM(ps[:, :], HT4[:, ti, kc, :], wq[:, kc, :], kc == 0, kc == 7), reads=[tHT[ti], wqT],
                         writes=[pT] if kc == 0 else (), pwrites=() if kc == 0 else [pT])
                P.op("act", ACT(ZGp[:, :], ps[:, :], AF.Gelu_apprx_tanh), reads=[pT], writes=[tZGp])
                ln_stats(ZGp[:, 256:512], [tZGp], 256)
                P.op("dve", TS(KQB[:, 0:256], ZGp[:, 256:512], MV[:, 0:1], MV[:, 3:4], ALU.subtract, ALU.mult), reads=[tZGp, tMV], writes=[tKQB])
                ps2, p2T = (PSF[4 + CUR[0]], PSFT[4 + CUR[0]]) if ILV[0] else (PSF[5], PSFT[5])
                for g in range(4):
                    P.op("pe", MM(ps2[:, g * 64:(g + 1) * 64], WST[:, g * 128:(g + 1) * 128], KQB[:, g * 64:(g + 1) * 64], True, True),
                         reads=[tKQB, tLAYC], writes=[p2T] if g == 0 else (), pwrites=() if g == 0 else [p2T])
                P.op("dve", TT(TMPA[:, 0:256], ps2[:, 0:256], GSGU[:, :], ALU.mult), reads=[p2T, tLAYC], writes=[tTMPA])
                for g in range(4):
                    P.op("dve", TS(TMPA[:, g * 64:(g + 1) * 64], TMPA[:, g * 64:(g + 1) * 64], BS[:, l * 4 + g:l * 4 + g + 1], None, ALU.add),
                         reads=[tTMPA, tCONST], writes=[tTMPA])
                P.op("dve", TT(TMPA[:, 0:256], TMPA[:, 0:256], ZGp[:, 0:256], ALU.mult), reads=[tTMPA, tZGp], writes=[tTMPA])
                rms_rstd(TMPA[:, 0:256], [tTMPA], 1, 256, TMPB, tTMPB)
                P.op("dve", TS(MERGB[:, ti, 384:640], TMPA[:, 0:256], SS[:, 8:9], None, ALU.mult), reads=[tTMPA, tSS], pwrites=[tMERG[ti]])

            for ta in range(0, n, 2):
                lists = []
                for ti in range(ta, min(n, ta + 2)):
                    CUR[0] = ti % 2
                    ILV[0] = True
                    P.record_begin()
                    proj_zb(ti, tiles[ti])
                    lists.append(P.record_end())
                CUR[0] = 0
                ILV[0] = False
                P.replay_interleaved(lists)
            if st == 0:
                load_gathered_kv()
            if not ctx:
                lo, hi = max(0, 4 * st - 2), min(16, 4 * st + 6)
                s0 = lo - (4 * st - 2)
                for j in range(hi - lo):
                    P.op("sp", DMA(NAK[:, :, (s0 + j) * 128:(s0 + j + 1) * 128], nak_loc[lo + j].rearrange("p (c k) -> p c k", c=3)),
                         reads=[tNAKL], writes=[tNAK] if j == 0 else (), pwrites=() if j == 0 else [tNAK], dma="nak")
                P.op("sp", DMA(NAV[:, s0:s0 + hi - lo, :], nav_loc[lo:hi].rearrange("t p c -> p t c")), reads=[tNAVL], writes=[tNAV], dma="nav")
                if st in (0, 3):
                    for j in range(2):
                        def mk_halo_k(c3, st=st, j=j):
                            def halo_k(e):
                                rk = PIDV[0]
                                if st == 0:
                                    src_r, e0, sl = (rk + 3) % 4, 2, 0
                                else:
                                    src_r, e0, sl = (rk + 1) % 4, 0, 6
                                return e.dma_start(out=NAK[:, c3, (sl + j) * 128:(sl + j + 1) * 128],
                                                   in_=nak_eall.ap()[bass.ds(src_r * 512 + (e0 + j) * 128, 128), c3 * 128:(c3 + 1) * 128])
                            return halo_k

                        def halo_v(e, st=st, j=j):
                            rk = PIDV[0]
                            if st == 0:
                                src_r, e0, sl = (rk + 3) % 4, 2, 0
                            else:
                                src_r, e0, sl = (rk + 1) % 4, 0, 6
                            return e.dma_start(out=NAV[:, sl + j, :], in_=nav_eall.ap()[bass.ds(src_r * 512 + (e0 + j) * 128, 128), :])
                        for c3 in range(3):
                            P.op("pool", mk_halo_k(c3), reads=[tNAKEA], pwrites=[tNAK], dma="nak")
                        P.op("pool", halo_v, reads=[tNAVEA], pwrites=[tNAV], dma="nav")
            if STOP[0] == 'p2a':
                P.barrier()
                return nc, P
            for ti, i in enumerate(tiles):
                if not ctx:
                    cls = 0 if i == 0 else 1 if i == 1 else 3 if i == 14 else 4 if i == 15 else 2
                    if cur_cls[0] != (l, cls):
                        P.op("pool", DMA(NAB[:, :], nab[l * 5 + cls]), writes=[tNAB], dma="nab")
                        cur_cls[0] = (l, cls)
                oac = [PSF[3], PSF[4]]
                oacT = [PSFT[3], PSFT[4]]
                for h in range(6):
                    c, pb = h // 2, (h % 2) * 64
                    q = QTA[pb:pb + 64, c, ti * 128:(ti + 1) * 128]
                    psa, paT = next_ps(0, 3)
                    psb_, pbT = next_ps(0, 3)
                    blocks = []
                    if not ctx:
                        sl_b = [(ti + b, b) for b in range(5)]
                        if i == 0:
                            sl_b.append((ti + 5, 5))
                        if i == 15:
                            sl_b.append((ti - 1, 5))
                        sl_b += [(8, None), (9, None)]
                    else:
                        sl_b = [(8, None), (9, None)]
                    for k, (slot, b) in enumerate(sl_b):
                        if k < 4:
                            blocks.append((slot, psa[:, k * 128:(k + 1) * 128], paT, b))
                        else:
                            blocks.append((slot, psb_[:, (k - 4) * 128:(k - 3) * 128], pbT, b))
                    nB = max(0, len(sl_b) - 4)
                    seen = set()
                    for (slot, reg, rT, b) in blocks:
                        first = id(rT) not in seen
                        seen.add(id(rT))
                        P.op("pe", MM(reg, NAK[pb:pb + 64, c, slot * 128:(slot + 1) * 128], q, True, b is None), reads=[tNAK, tQTA],
                             writes=[rT] if first else (), pwrites=() if first else [rT])
                        if b is not None:
                            P.op("pe", MM(reg, IDB[:, :], NAB[:, (h * 6 + b) * 128:(h * 6 + b + 1) * 128], False, True), reads=[tNAB, tCONST], pwrites=[rT])
                    if not ctx:
                        P.op("act", ACT(PTN[:, 0:512], psa[:, :], AF.Exp), reads=[paT], writes=[tPTN, tPTNb])
                        P.op("act", ACT(PTN[:, 512:512 + nB * 128], psb_[:, 0:nB * 128], AF.Exp), reads=[pbT], pwrites=[tPTN])
                    else:
                        P.op("act", ACT(PTN[:, 0:256], psa[:, 0:256], AF.Exp), reads=[paT], writes=[tPTN, tPTNb])
                    ob, obT = oac[h // 4], oacT[h // 4]
                    oreg = ob[0:65, (h % 4) * 128:(h % 4 + 1) * 128]
                    nb_ = len(blocks)
                    for bi, (slot, reg, rT, b) in enumerate(blocks):
                        firstw = (h % 4 == 0 and bi == 0)
                        P.op("pe", MM(oreg, NAV[:, slot, h * 65:(h + 1) * 65], PTN[:, bi * 128:(bi + 1) * 128], bi == 0, bi == nb_ - 1),
                             reads=[tNAV, tPTN, tPTNb], writes=[obT] if firstw else (), pwrites=() if firstw else [obT])
                P.op("dve", CP(OT[0:65, 0:512], oac[0][0:65, :]), reads=[oacT[0]], writes=[tOT])
                P.op("dve", CP(OT[0:65, 512:768], oac[1][0:65, 0:256]), reads=[oacT[1]], pwrites=[tOT])
                tp, tpT = PSF[5], PSFT[5]
                for h in range(6):
                    P.op("pe", TR(tp[:, h * 65:(h + 1) * 65], OT[0:65, h * 128:(h + 1) * 128], IDF[0:65, 0:65]), reads=[tOT, tCONST],
                         writes=[tpT] if h == 0 else (), pwrites=() if h == 0 else [tpT])
                tpv = tp[:, 0:390].rearrange("p (h d) -> p h d", h=6)
                P.op("dve", (lambda o, i_: (lambda e: e.reciprocal(out=o, in_=i_)))(SS[:, 0:6], tpv[:, :, 64]), reads=[tpT], writes=[tSS])
                P.op("dve", TT(OATT[:, ti, :].rearrange("p (h d) -> p h d", h=6), tpv[:, :, 0:64],
                               SS[:, 0:6].unsqueeze(2).to_broadcast([128, 6, 64]), ALU.mult), reads=[tpT, tSS], writes=[tW4A, tZG])
                rms_rstd(OATT[:, ti, :], [tW4A, tZG], 1, 384, TMPA, tTMPA)
                P.op("dve", TS(MERGB[:, ti, 0:384], OATT[:, ti, :], SS[:, 8:9], None, ALU.mult), reads=[tW4A, tZG, tSS], pwrites=[tMERG[ti]])

            if STOP[0] == 'p2na':
                P.barrier()
                return nc, P
            nq = n * 128
            kts = list(range(66)) if not ctx else [64, 65]
            nk = len(kts)
            for c in range(3):
                qs = [QTC[g * 64:g * 64 + 64, c, 0:nq] for g in range(2)]
                obs = [(PSF[4 + g], PSFT[4 + g]) for g in range(2)]
                ptb = [[(PT[0], tPT[0]), (PT[1], tPT[1])], [(PTN[:, 0:512], tPTN), (PTN[:, 512:1024], tPTNb)]]

                def S(g, k):
                    ps, pT = PSF[g * 2 + k % 2], PSFT[g * 2 + k % 2]
                    kt = kts[k]
                    P.op("pe", MM(ps[:, 0:nq], KTC[g * 64:g * 64 + 64, kt * 128:(kt + 1) * 128], qs[g], True, True), reads=[tKTC, tQTC], writes=[pT])
                for k0 in range(min(2, nk)):
                    for g in range(2):
                        S(g, k0)
                for k in range(nk):
                    for g in range(2):
                        ps, pT = PSF[g * 2 + k % 2], PSFT[g * 2 + k % 2]
                        pbuf, pbT = ptb[g][k % 2]
                        P.op("act", ACT(pbuf[:, 0:nq], ps[:, 0:nq], AF.Exp), reads=[pT], writes=[pbT])
                    if k + 2 < nk:
                        for g in range(2):
                            S(g, k + 2)
                    kt = kts[k]
                    for g in range(2):
                        ob, obT = obs[g]
                        pbuf, pbT = ptb[g][k % 2]
                        P.op("pe", MM(ob[0:65, 0:nq], VC[:, kt, g * 65:(g + 1) * 65], pbuf[:, 0:nq], k == 0, k == nk - 1),
                             reads=[tVC, pbT], writes=[obT] if k == 0 else (), pwrites=() if k == 0 else [obT])
                for g in range(2):
                    h = c + 3 * g
                    ob, obT = obs[g]
                    P.op("dve", CP(OT[0:65, 0:nq], ob[0:65, 0:nq]), reads=[obT], writes=[tOT])
                    tp, tpT = PSF[g], PSFT[g]
                    for ti in range(n):
                        P.op("pe", TR(tp[:, ti * 65:(ti + 1) * 65], OT[0:65, ti * 128:(ti + 1) * 128], IDF[0:65, 0:65]), reads=[tOT, tCONST],
                             writes=[tpT] if ti == 0 else (), pwrites=() if ti == 0 else [tpT])
                    tpv = tp[:, 0:65 * n].rearrange("p (t d) -> p t d", t=n)
                    P.op("dve", (lambda o, i_: (lambda e: e.reciprocal(out=o, in_=i_)))(SS[:, 0:n], tpv[:, :, 64]), reads=[tpT], writes=[tSS])
                    P.op("dve", TT(OATT[:, 0:n, h * 64:(h + 1) * 64], tpv[:, :, 0:64],
                                   SS[:, 0:n].unsqueeze(2).to_broadcast([128, n, 64]), ALU.mult), reads=[tpT, tSS], pwrites=[tW4A, tZG])
            for ti in range(n):
                rms_rstd(OATT[:, ti, :], [tW4A, tZG], 1, 384, TMPA, tTMPA)
                P.op("dve", TS(MERGB[:, ti, 640:1024], OATT[:, ti, :], SS[:, 8:9], None, ALU.mult), reads=[tW4A, tZG, tSS], pwrites=[tMERG[ti]])

            if STOP[0] == 'p2gqa':
                P.barrier()
                return nc, P
            for ti in range(n):
                def evac_m(kc, pap, ppT, ti=ti):
                    P.op("dve", TS(HT4[:, ti, kc, :], pap, GOUT[:, l * 8 + kc:l * 8 + kc + 1], None, ALU.mult), reads=[ppT, tCONST], pwrites=[tHT[ti]])
                transpose_tile(MERGB[:, ti, :], [tMERG[ti]], 8, None, None, evac_m)
            wo0, wo0T = load_wblock(wo[l * 2 + 0])
            wo1, wo1T = load_wblock(wo[l * 2 + 1])
            for ti, i in enumerate(tiles):
                for nb, (wb_, wbT_) in enumerate(((wo0, wo0T), (wo1, wo1T))):
                    ps, pT = PSF[3 + nb], PSFT[3 + nb]
                    for kc in range(8):
                        P.op("pe", MM(ps[:, :], HT4[:, ti, kc, :], wb_[:, kc, :], kc == 0, kc == 7), reads=[tHT[ti], wbT_],
                             writes=[pT] if kc == 0 else (), pwrites=() if kc == 0 else [pT])
                deepnorm(i, l, lambda nb: PSF[3 + nb][:, :], [PSFT[3], PSFT[4]])
        P.barrier()
        if STOP[0] == 'p2':
            return nc, P

        load_ln(l, 1)
        for st in range(5):
            tiles = list(range(4 * st, 4 * st + 4)) if st < 4 else [16, 17]
            if l == nlayers - 1 and st == 4:
                continue
            n = len(tiles)
            ctx = st == 4
            load_bc(l, 1, ctx)
            for ti, i in enumerate(tiles):
                ln_mod_transpose(i, l, 1, ti)
            for nb in range(11):
                wb_, wbT_ = load_wblock(wffi[l * 11 + nb])
                for ti, i in enumerate(tiles):
                    ps, pT = next_ps(0, 3)
                    for kc in range(8):
                        P.op("pe", MM(ps[:, :], HT4[:, ti, kc, :], wb_[:, kc, :], kc == 0, kc == 7), reads=[tHT[ti], wbT_],
                             writes=[pT] if kc == 0 else (), pwrites=() if kc == 0 else [pT])
                    P.op("act", ACT(TMPA[:, 0:256], ps[:, 0:256], AF.Silu), reads=[pT], writes=[tTMPA])
                    P.op("dve", TT(GTOK4[:, ti, nb * 256:(nb + 1) * 256], TMPA[:, 0:256], ps[:, 256:512], ALU.mult), reads=[tTMPA, pT], pwrites=[tG4[ti]])
            for ti, i in enumerate(tiles):
                GT = GTTa if ti < 2 else GTTb
                for grp in range(3):
                    nch = 8 if grp < 2 else 6

                    def evac_gg(kc, pap, ppT, grp=grp, ti=ti, GT=GT):
                        kk = grp * 8 + kc
                        P.op("dve", CP(GT[:, ti % 2, kk, :], pap), reads=[ppT], pwrites=[tGTT4[ti]])
                    transpose_tile(GTOK4[:, ti, grp * 1024:grp * 1024 + nch * 128], [tG4[ti]], nch, None, None, evac_gg)
            for nb2 in range(2):
                for kg in range(3):
                    nch = 8 if kg < 2 else 6
                    wb_, wbT_ = load_wblock(wffo[l * 6 + nb2 * 3 + kg])
                    for ti, i in enumerate(tiles):
                        GT = GTTa if ti < 2 else GTTb
                        ps, pT = PSF[ti], PSFT[ti]
                        for kcl in range(nch):
                            kc = kg * 8 + kcl
                            P.op("pe", MM(ps[:, :], GT[:, ti % 2, kc, :], wb_[:, kcl, :], kc == 0, kc == 21), reads=[tGTT4[ti], wbT_],
                                 writes=[pT] if kc == 0 else (), pwrites=() if kc == 0 else [pT])
                for ti, i in enumerate(tiles):
                    P.op("dve", TT(Y4[:, ti, nb2 * 512:(nb2 + 1) * 512], PSF[ti][:, :], BC[:, 0, nb2 * 512:(nb2 + 1) * 512], ALU.mult),
                         reads=[PSFT[ti], tBC[0]], pwrites=[tY4[ti]])
            for ti, i in enumerate(tiles):
                deepnorm_from(i, Y4[:, ti, :], [tY4[ti]])
                if l == nlayers - 1 and i < 16:
                    P.op("sp", DMA(y_out[i], X[:, i, :]), reads=[XT[i]], pwrites=[tY], dma="yout")
        P.barrier()

    P.op("sp", None, reads=[tY])
    return nc, P


def finish(nc, P):
    from contextlib import ExitStack
    keys = list(ENGS) + ["dma:" + k for k in P.dma_cnt]
    with ExitStack() as es:
        sems = {k: es.enter_context(nc.semaphore(k.replace(":", "_"))) for k in keys}
        block = es.enter_context(nc.Block())
        P.emit(nc, block, sems)
    return nc


def _blocks(w, nblk, width=512):
    K = w.shape[0]
    kc = K // 128
    return np.ascontiguousarray(w.reshape(kc, 128, nblk, width).transpose(2, 1, 0, 3)).reshape(nblk, 128, kc * width)


def prep_shared(inp, NL):
    L = 4
    f = np.float32
    w_mod, b_mod, w_in = inp["w_mod"], inp["b_mod"], inp["w_in"]
    sh = {}
    sh["wmod"] = np.concatenate([_blocks(w_mod[l], 12) for l in range(NL)], 0)
    bases = [0, 1024, 3072, 4096]
    bt = np.zeros((L, 128, 32), f)
    for l in range(L):
        for j, b0 in enumerate(bases):
            bt[l, :, j * 8:(j + 1) * 8] = b_mod[l, b0:b0 + 1024].reshape(8, 128).T
    sh["bmodT"] = bt
    sh["bmodg"] = np.ascontiguousarray(np.stack([np.stack([b_mod[l, 2048:3072], b_mod[l, 5120:6144]]) for l in range(L)]).reshape(L * 2, 1024))
    perm = np.concatenate([np.arange(h * 64, (h + 1) * 64) for h in (0, 3, 1, 4, 2, 5)])
    wl = []
    for l in range(NL):
        w = w_in[l]
        qa, ka, va, zb, qc, kc_, vc = w[:, 0:384], w[:, 384:768], w[:, 768:1152], w[:, 1152:1664], w[:, 1664:2048], w[:, 2048:2176], w[:, 2176:2304]
        z128 = np.zeros((1024, 128), f)
        blks = [np.concatenate([va, vc], 1), np.concatenate([ka, kc_], 1), np.concatenate([qa, z128], 1),
                np.concatenate([qc[:, perm], z128], 1), zb]
        wl.append(_blocks(np.concatenate(blks, 1), 5))
    sh["win"] = np.concatenate(wl, 0)
    sh["wo"] = np.concatenate([_blocks(inp["w_o"][l], 2) for l in range(NL)], 0)
    wl = []
    for l in range(NL):
        w = inp["w_ffn_in"][l]
        a, b = w[:, :2816].reshape(1024, 11, 256), w[:, 2816:].reshape(1024, 11, 256)
        wl.append(_blocks(np.concatenate([a, b], 2).reshape(1024, 11 * 512), 11))
    sh["wffi"] = np.concatenate(wl, 0)
    wl = []
    for l in range(NL):
        w = np.zeros((3072, 1024), f)
        w[:2816] = inp["w_ffn_out"][l]
        wl.append(np.ascontiguousarray(w.reshape(3, 8, 128, 2, 512).transpose(3, 0, 2, 1, 4)).reshape(6, 128, 4096))
    sh["wffo"] = np.concatenate(wl, 0)
    sh["lnp"] = np.ascontiguousarray(np.stack([np.stack([inp["ln1_g"][l], inp["ln1_b"][l], inp["ln2_g"][l], inp["ln2_b"][l]]) for l in range(L)]).reshape(L * 4, 1024))
    sh["gout"] = np.ascontiguousarray(np.concatenate([inp["g_out"][l].reshape(8, 128).T for l in range(L)], 1))
    sh["gsgu"] = np.ascontiguousarray(inp["g_sgu"])
    sh["gqk"] = np.ascontiguousarray(np.concatenate([inp["g_q"], inp["g_k"]], 1))
    sh["wsT"] = np.ascontiguousarray(np.stack([inp["w_s"][l].transpose(2, 0, 1).reshape(128, 512) for l in range(L)]))
    sh["bsd"] = np.ascontiguousarray(np.concatenate([inp["b_s"][l].T for l in range(L)], 1))
    sh["identd"] = np.eye(128, dtype=f)
    return sh


def rope_table(tok):
    f = np.float32
    row = (tok // 64).astype(f)
    col = (tok % 64).astype(f)
    inv = (1.0 / (np.float32(10000.0) ** (np.arange(16, dtype=f) / np.float32(16)))).astype(f)
    ar = row[:, None] * inv[None, :]
    ac = col[:, None] * inv[None, :]
    return np.concatenate([np.cos(ar), np.cos(ac), np.sin(ar), np.sin(ac)], 1).astype(f)


def nab_tables(rpb, qi, L):
    out = np.full((L, 5, 128, 6, 6, 128), np.float32(-30000.0), np.float32)
    p = np.arange(128)
    for ci, i in enumerate((0, 1, 2, 14, 15)):
        G = 16 * qi + i
        r = 2 * G + p // 64
        c = p % 64
        rs = np.clip(r - 4, 0, 120)
        cs = np.clip(c - 8, 0, 48)
        for b in range(6):
            if b < 5:
                Gk = G - 2 + b
            elif i == 0:
                Gk = G + 3
            elif i == 15:
                Gk = G - 3
            else:
                continue
            if Gk < 0 or Gk > 63:
                continue
            kr = 2 * Gk + p // 64
            kc = p % 64
            ok = (kr[:, None] >= rs[None, :]) & (kr[:, None] < rs[None, :] + 8) & (kc[:, None] >= cs[None, :]) & (kc[:, None] < cs[None, :] + 16)
            dr = np.clip(kr[:, None] - r[None, :] + 7, 0, 14)
            dc = np.clip(kc[:, None] - c[None, :] + 15, 0, 30)
            for l in range(L):
                vals = rpb[l][:, dr, dc]
                out[l, ci, :, :, b, :] = np.where(ok[None], vals, np.float32(-30000.0)).transpose(1, 0, 2)
    return out.reshape(L * 5, 128, 4608)


_CACHE = {}


def kernel(**inp):
    inp = {k: np.asarray(v, dtype=np.float32) for k, v in inp.items()}
    if "nc" not in _CACHE:
        nc, P = build(NLAYERS_BUILD)
        _CACHE["nc"] = finish(nc, P)
    nc = _CACHE["nc"]
    import time as _t
    t0 = _t.time()
    NL = NLAYERS_BUILD
    sh = prep_shared(inp, NL)
    print("prep shared", _t.time() - t0, flush=True)
    in_maps = []
    for core in range(8):
        b, qi = core // 4, core % 4
        m = dict(sh)
        xs = inp["x"][b, 2048 * qi:2048 * (qi + 1)].reshape(16, 128, 1024)
        m["x_in"] = np.ascontiguousarray(np.concatenate([xs, inp["ctx"][b].reshape(2, 128, 1024)], 0))
        cv = np.empty((128, 16), np.float32)
        cv[:, 0::2] = inp["c"][b].reshape(8, 128).T
        cv[:, 1::2] = inp["c_ctx"].reshape(8, 128).T
        m["cvec"] = cv
        rp = np.empty((NT, 128, 64), np.float32)
        for i in range(16):
            rp[i] = rope_table(2048 * qi + 128 * i + np.arange(128))
        rp[16:, :, 0:32] = 1.0
        rp[16:, :, 32:64] = 0.0
        m["rope"] = rp
        m["nab"] = nab_tables(inp["rpb"], qi, NL)
        in_maps.append(m)
    print("prep all", _t.time() - t0, flush=True)
    res = run_bass_kernel_spmd(nc, in_maps, core_ids=list(range(8)))
    print("run done", _t.time() - t0, flush=True)
    out = np.empty((2, 8192, 1024), np.float32)
    for core in range(8):
        b, qi = core // 4, core % 4
        out[b, 2048 * qi:2048 * (qi + 1)] = np.asarray(res.results[core]["y_out"]).reshape(2048, 1024)
    return out
```

```python
import numpy as np
import concourse.bass as bass
import concourse.mybir as mybir
from concourse.bass_utils import run_bass_kernel_spmd

F32 = mybir.dt.float32
BF16 = mybir.dt.bfloat16
AF = mybir.ActivationFunctionType
ALU = mybir.AluOpType
AX = mybir.AxisListType

L = 4
NT = 18
ALPHA = float(8.0 ** 0.25)
EPS = 1e-6
NLAYERS_BUILD = L


class T:
    __slots__ = ("name", "w", "r")

    def __init__(self, name):
        self.name = name
        self.w = {}
        self.r = {}


ENGS = ["pe", "dve", "act", "pool", "sp"]
PIDV = [None]


class Plan:
    def __init__(self):
        self.ops = {e: [] for e in ENGS}
        self.known = {e: {} for e in ENGS}
        self.dma_cnt = {}
        self.waited = {e: set() for e in ENGS}

    def _res(self, hs):
        out = []
        for t in hs:
            r = t.resolve() if hasattr(t, "resolve") else t
            if isinstance(r, (list, tuple)):
                out.extend(r)
            else:
                out.append(r)
        return out

    def record_begin(self):
        self._rec = []

    def record_end(self):
        r, self._rec = self._rec, None
        return r

    def replay_interleaved(self, lists):
        its = [list(l) for l in lists]
        pos = [0] * len(its)
        while any(pos[j] < len(its[j]) for j in range(len(its))):
            for j in range(len(its)):
                if pos[j] < len(its[j]):
                    self.op(*its[j][pos[j]])
                    pos[j] += 1

    def op(self, eng, fn, reads=(), writes=(), pwrites=(), dma=None):
        reads, writes, pwrites = self._res(reads), self._res(writes), self._res(pwrites)
        if getattr(self, "_rec", None) is not None:
            self._rec.append((eng, fn, reads, writes, pwrites, dma))
            return
        idx = len(self.ops[eng])
        if dma is None:
            tok = (eng, idx + 1)
        else:
            self.dma_cnt[dma] = self.dma_cnt.get(dma, 0) + 1
            tok = ("dma:" + dma, self.dma_cnt[dma])
        need = {}

        def addw(d):
            for k, v in d.items():
                if need.get(k, 0) < v:
                    need[k] = v

        for t in reads:
            addw(t.w)
        for t in writes:
            addw(t.w)
            addw(t.r)
        for t in pwrites:
            addw(t.r)
        waits = []
        kn = self.known[eng]
        for k, v in need.items():
            if kn.get(k, 0) < v:
                kn[k] = v
                waits.append((k, v))
                if not k.startswith("dma:"):
                    self.waited[k].add(v)
        if fn is not None:
            for t in reads:
                if t.r.get(tok[0], 0) < tok[1]:
                    t.r[tok[0]] = tok[1]
            for t in writes:
                t.w = {tok[0]: tok[1]}
                t.r = {}
            for t in pwrites:
                if t.r:
                    t.w = {tok[0]: tok[1]}
                    t.r = {}
                else:
                    t.w[tok[0]] = max(t.w.get(tok[0], 0), tok[1])
        self.ops[eng].append((waits, fn, dma))

    def barrier(self):
        latest = {}
        for e in ENGS:
            n = len(self.ops[e])
            if n:
                latest[e] = n
        for k, v in self.dma_cnt.items():
            latest["dma:" + k] = v
        for e in ENGS:
            waits = []
            kn = self.known[e]
            for k, v in latest.items():
                if k == e:
                    continue
                if not k.startswith("dma:"):
                    vv = v
                    while vv > 0 and (self.ops[k][vv - 1][1] is None or self.ops[k][vv - 1][2] is not None):
                        vv -= 1
                    if vv == 0:
                        continue
                    v = vv
                if kn.get(k, 0) < v:
                    kn[k] = v
                    waits.append((k, v))
                    if not k.startswith("dma:"):
                        self.waited[k].add(v)
            self.ops[e].append((waits, None, None))

    def emit(self, nc, block, sems):
        rank = {}
        for e in ENGS:
            s = sorted(self.waited[e])
            rank[e] = {v: i + 1 for i, v in enumerate(s)}
        plan = self

        def run(eng_name):
            def body(e):
                if eng_name == "pool":
                    PIDV[0] = e.partition_id()
                for i, (waits, fn, dma) in enumerate(plan.ops[eng_name]):
                    for k, v in waits:
                        if k.startswith("dma:"):
                            if k.startswith("dma:cc"):
                                e.wait_ge(sems[k], 1)
                            else:
                                e.wait_ge(sems[k], 16 * v)
                        else:
                            e.wait_ge(sems[k], rank[k][v])
                    if fn is None:
                        continue
                    ins = fn(e)
                    if dma is not None:
                        if dma.startswith("cc"):
                            ins.then_inc(sems["dma:" + dma])
                        else:
                            ins.then_inc(sems["dma:" + dma], 16)
                    elif (i + 1) in rank[eng_name]:
                        ins.then_inc(sems[eng_name], 1)
            return body

        block.tensor(run("pe"))
        block.vector(run("dve"))
        block.scalar(run("act"))
        block.gpsimd(run("pool"))
        block.sync(run("sp"))


def MM(out, l, r, st, sp):
    return lambda e: e.matmul(out, lhsT=l, rhs=r, start=st, stop=sp)


def TR(out, in_, ident):
    return lambda e: e.transpose(out=out, in_=in_, identity=ident)


def ACT(out, in_, func, scale=None, bias=None):
    kw = {}
    if scale is not None:
        kw["scale"] = scale
    if bias is not None:
        kw["bias"] = bias
    return lambda e: e.activation(out=out, in_=in_, func=func, **kw)


def TS(out, in0, s1, s2, op0, op1=None):
    if op1 is None:
        return lambda e: e.tensor_scalar(out=out, in0=in0, scalar1=s1, scalar2=None, op0=op0)
    return lambda e: e.tensor_scalar(out=out, in0=in0, scalar1=s1, scalar2=s2, op0=op0, op1=op1)


def TT(out, in0, in1, op):
    return lambda e: e.tensor_tensor(out=out, in0=in0, in1=in1, op=op)


def STT(out, in0, scalar, in1, op0, op1):
    return lambda e: e.scalar_tensor_tensor(out=out, in0=in0, scalar=scalar, in1=in1, op0=op0, op1=op1)


def CP(out, in_):
    return lambda e: e.tensor_copy(out=out, in_=in_)


def DMA(out, in_):
    return lambda e: e.dma_start(out=out, in_=in_)


def MEMSET(ap, v):
    return lambda e: e.memset(ap, v)


STOP = [None]


def build(nlayers=L, debug_x=False):
    nc = bass.Bass("TRN2", target_bir_lowering=False)
    P = Plan()

    def din(name, shape):
        return nc.dram_tensor(name, shape, F32, kind="ExternalInput").ap()

    x_in = din("x_in", [NT, 128, 1024])
    cvec = din("cvec", [128, 16])
    wmod = din("wmod", [nlayers * 12, 128, 4096])
    bmodT = din("bmodT", [L, 128, 32])
    bmodg = din("bmodg", [L * 2, 1024])
    win = din("win", [nlayers * 5, 128, 4096])
    wo = din("wo", [nlayers * 2, 128, 4096])
    wffi = din("wffi", [nlayers * 11, 128, 4096])
    wffo = din("wffo", [nlayers * 6, 128, 4096])
    lnp = din("lnp", [L * 4, 1024])
    gout = din("gout", [128, L * 8])
    gsgu = din("gsgu", [L, 256])
    gqk = din("gqk", [L, 128])
    wsT = din("wsT", [L, 128, 512])
    bsd = din("bsd", [128, L * 4])
    rope = din("rope", [NT, 128, 64])
    nab = din("nab", [nlayers * 5, 128, 4608])
    identd = din("identd", [128, 128])
    y_out = nc.dram_tensor("y_out", [16, 128, 1024], F32, kind="ExternalOutput").ap()

    gst = nc.dram_tensor("gst", [L * 4, 128, 1024], F32).ap()
    hts = nc.dram_tensor("hts", [NT, 128, 1024], BF16).ap()
    nak_loc = nc.dram_tensor("nak_loc", [16, 128, 384], BF16).ap()
    nav_loc = nc.dram_tensor("nav_loc", [16, 128, 390], BF16).ap()
    ktc_loc = nc.dram_tensor("ktc_loc", [128, 2048], BF16)
    ktc_all = nc.dram_tensor("ktc_all", [512, 2048], BF16)
    vc_loc = nc.dram_tensor("vc_loc", [2048, 130], BF16)
    vc_all = nc.dram_tensor("vc_all", [8192, 130], BF16)
    nak_edge = nc.dram_tensor("nak_edge", [512, 384], BF16)
    nak_eall = nc.dram_tensor("nak_eall", [2048, 384], BF16)
    nav_edge = nc.dram_tensor("nav_edge", [512, 390], BF16)
    nav_eall = nc.dram_tensor("nav_eall", [2048, 390], BF16)

    off = [16512]
    OFF = {}
    LIMIT = 229344

    def sb(name, shape, dt, at=None):
        nbytes = int(np.prod(shape[1:])) * (4 if dt == F32 else 2)
        nbytes = (nbytes + 63) // 64 * 64
        if at is None:
            o = off[0]
            off[0] += nbytes
            assert off[0] <= LIMIT, (name, off[0])
        else:
            o = at
        t = nc.alloc_sbuf_tensor_at(name, shape, dt, offset=o)
        OFF[name] = o
        return t

    X = sb("X", [128, NT, 1024], F32)
    KTC = sb("KTC", [128, 8448], BF16)
    VC = sb("VC", [128, 66, 130], BF16)
    NAK = sb("NAK", [128, 3, 1280], BF16)
    NAV = sb("NAV", [128, 10, 390], BF16)
    NAB = sb("NAB", [128, 4608], BF16)
    BC = sb("BC", [128, 3, 1024], F32)
    WB = [sb(f"WB{i}", [128, 8, 512], BF16) for i in range(2)]
    HT4 = sb("HT4", [128, 4, 8, 128], BF16)
    MERGB = sb("MERGB", [128, 4, 1024], BF16)
    att0 = off[0]
    QTA = sb("QTA", [128, 3, 512], BF16)
    QTC = sb("QTC", [128, 3, 512], BF16)
    PT = [sb(f"PT{i}", [128, 512], BF16) for i in range(2)]
    PTN = sb("PTN", [128, 1024], BF16)
    OT = sb("OT", [128, 768], F32)
    att1 = off[0]
    assert att0 + 11264 <= att1
    GTOK4 = sb("GTOK4", [128, 4, 2816], BF16, at=OFF["KTC"])
    GTTa = sb("GTTa", [128, 2, 22, 128], BF16, at=OFF["KTC"] + 22528)
    GTTb = sb("GTTb", [128, 2, 22, 128], BF16, at=att0)
    assert 22528 + 11264 <= 16896 + 17160
    Y4 = sb("Y4", [128, 4, 1024], F32, at=OFF["NAK"])
    HT4b = sb("HT4b", [128, 4, 8, 128], BF16, at=OFF["MERGB"])
    W2P1 = sb("W2P1", [128, 1024], BF16, at=OFF["NAK"] + 16384)
    MVP_ = [sb(f"MVP{j}", [128, 8], F32, at=OFF["NAK"] + 16384 + 2048 + 64 * j) for j in range(2)]
    ST6P_ = [sb(f"ST6P{j}", [128, 12], F32, at=OFF["NAK"] + 16384 + 2048 + 128 + 64 * j) for j in range(2)]
    assert OFF["NAK"] + 16384 + 2048 + 256 <= OFF["NAB"] + 9216
    assert OFF["NAB"] + 9216 - OFF["NAK"] >= 16384
    WBX = [sb(f"WBX{i}", [128, 8, 512], BF16, at=OFF["KTC"] + 8192 * i) for i in range(4)]
    RLAT = sb("RLAT", [128, 8, 128], BF16, at=att0)
    RCTX = sb("RCTX", [128, 8, 128], BF16, at=att0 + 2048)
    W4A = sb("W4A", [128, 1024], F32)
    ZG = sb("ZG", [128, 512], F32)
    OATT = sb("OATT", [128, 4, 384], F32, at=OFF["W4A"])
    ZG_1 = sb("ZG1", [128, 512], F32, at=OFF["W4A"])
    W2A_0 = sb("W2A", [128, 1024], BF16)
    TMPA_0 = sb("TMPA", [128, 512], F32)
    TMPB_0 = sb("TMPB", [128, 512], F32)
    KQB_0 = sb("KQB", [128, 512], BF16)
    TMPA_1 = sb("TMPA1", [128, 512], F32, at=OFF["PTN"])
    TMPB_1 = sb("TMPB1", [128, 512], F32, at=OFF["OT"])
    KQB_1 = sb("KQB1", [128, 512], BF16, at=OFF["PT0"])
    W2A_1 = sb("W2A1", [128, 1024], BF16, at=OFF["QTA"])
    KAT_1 = sb("KAT1", [128, 4, 128], BF16, at=OFF["QTA"] + 2048)
    VAT_1 = sb("VAT1", [128, 6, 65], BF16, at=OFF["QTC"])
    VCT_1 = sb("VCT1", [128, 2, 65], BF16, at=OFF["QTC"] + 896)
    ROPE_1 = sb("ROPE1", [128, 64], F32, at=OFF["PT1"])
    SS_1 = sb("SS1", [128, 16], F32, at=OFF["PT1"] + 256)
    MV_1 = sb("MV1", [128, 8], F32, at=OFF["PT1"] + 320)
    ST6_1 = sb("ST61", [128, 12], F32, at=OFF["PT1"] + 384)
    VAT_0 = sb("VAT", [128, 6, 65], BF16)
    VCT_0 = sb("VCT", [128, 2, 65], BF16)
    KAT_0 = sb("KAT", [128, 4, 128], BF16)
    IDB = sb("IDB", [128, 128], BF16)
    IDF = sb("IDF", [128, 128], F32)
    ROPE_0 = sb("ROPE", [128, 64], F32)
    S2F = sb("S2F", [128, 16], F32)
    S2B = sb("S2B", [128, 8, 2], BF16)
    ONESB = sb("ONESB", [128, 128], BF16)
    MODF = sb("MODF", [128, L, 32, 2], F32)
    BMT = sb("BMT", [128, 32], F32)
    GOUT = sb("GOUT", [128, L * 8], F32)
    GSGU = sb("GSGU", [128, 256], F32)
    GQK = sb("GQK", [128, 128], F32)
    WST = sb("WST", [128, 512], BF16)
    BS = sb("BS", [128, L * 4], F32)
    ST6_0 = sb("ST6", [128, 12], F32)
    MV_0 = sb("MV", [128, 8], F32)
    SS_0 = sb("SS", [128, 16], F32)
    print("SBUF used", off[0], "of", LIMIT)

    PSF = [nc.alloc_psum_tensor(f"PSF{i}", [128, 512], F32) for i in range(6)]
    PST = [nc.alloc_psum_tensor(f"PST{i}", [128, 1024], BF16) for i in range(2)]
    PSFT = [T(f"psf{i}") for i in range(8)]
    pst_rr = [0]

    XT = [T(f"x{i}") for i in range(NT)]
    tKTC, tVC, tNAK, tNAV, tNAB = T("ktc"), T("vc"), T("nak"), T("nav"), T("nab")
    tBC = [T("bc0"), T("bc1"), T("bc2")]
    tWB = [T("wb0"), T("wb1")]
    tHT = [T(f"ht{i}") for i in range(4)]
    tMERG = [T(f"mg{i}") for i in range(4)]
    tQTA, tQTC = T("qta"), T("qtc")
    tPT = [T("pt0"), T("pt1")]
    tPTN, tOT = T("ptn"), T("ot")
    tPTNb = T("ptnb")
    tGTOK = T("gtok")
    tY4 = [T(f"y4{i}") for i in range(4)]
    tG4 = [T(f"g4{i}") for i in range(4)]
    tGTT4 = [T(f"gtt{i}") for i in range(4)]
    tW4A, tZG = T("w4a"), T("zg")
    CUR = [0]

    class BufP:
        def __init__(self, bufs):
            self.bufs = bufs

        def __getitem__(self, k):
            return self.bufs[CUR[0]][k]

    class HP:
        def __init__(self, hs):
            self.hs = hs

        def resolve(self):
            return self.hs[CUR[0]]
    TMPA, TMPB, KQB = BufP([TMPA_0, TMPA_1]), BufP([TMPB_0, TMPB_1]), BufP([KQB_0, KQB_1])
    W2A, KAT, VAT, VCT = BufP([W2A_0, W2A_1]), BufP([KAT_0, KAT_1]), BufP([VAT_0, VAT_1]), BufP([VCT_0, VCT_1])
    ILV = [False]
    ZGp = BufP([ZG, ZG_1])
    tZGp = HP([tZG, tW4A])
    tVAT_0, tVCT_0, tKAT_0, tW2A_0 = T("vat"), T("vct"), T("kat"), T("w2a")
    W2P = [W2A_0, W2P1]
    MVP, ST6P = MVP_, ST6P_
    tHTb = [T(f"htb{i}") for i in range(4)]
    tW2P = [tW2A_0, T("w2p1")]
    tMVP = [T("mvp0"), T("mvp1")]
    tST6P = [T("st6p0"), T("st6p1")]
    tCONST, tMODF, tLAYC = T("const"), T("modf"), T("layc")
    tW2A = HP([tW2A_0, tQTA])
    tKAT = HP([tKAT_0, tQTA])
    tVAT = HP([tVAT_0, tQTC])
    tVCT = HP([tVCT_0, tQTC])
    tTMPA = HP([T("tmpa"), [tPTN, tPTNb]])
    tTMPB = HP([T("tmpb"), tOT])
    tKQB = HP([T("kqb"), tPT[0]])
    tROPE = HP([T("rope"), tPT[1]])
    tST6 = HP([T("st6"), tPT[1]])
    tMV = HP([T("mv"), tPT[1]])
    tSS = HP([T("ss"), tPT[1]])
    ROPE, SS, MV, ST6 = BufP([ROPE_0, ROPE_1]), BufP([SS_0, SS_1]), BufP([MV_0, MV_1]), BufP([ST6_0, ST6_1])
    tGST, tHTS = T("gst"), [T(f"hts{i}") for i in range(NT)]
    tNAKL, tNAVL, tKTCL, tVCL, tNAKE, tNAVE = T("nakl"), T("navl"), T("ktcl"), T("vcl"), T("nake"), T("nave")
    tKTCA, tVCA, tNAKEA, tNAVEA = T("ktca"), T("vca"), T("nakea"), T("navea")
    tY = T("y")

    wb_rr = [0]
    WBALL = WB + WBX
    tWBALL = tWB + [T(f"wbx{i}") for i in range(4)]
    wb_pool = [6]

    def load_wblock(src_ap):
        i = wb_rr[0] % wb_pool[0]
        wb_rr[0] += 1
        P.op("pool", DMA(WBALL[i][:].rearrange("p a b -> p (a b)"), src_ap), writes=[tWBALL[i]], dma=f"wb{i}")
        return WBALL[i], tWBALL[i]

    ps_rr = [0]

    ps_set_rr = [0, 0]

    def next_ps(lo=0, hi=3):
        if ILV[0]:
            c = CUR[0]
            i = 2 * c + ps_set_rr[c] % 2
            ps_set_rr[c] += 1
            return PSF[i], PSFT[i]
        i = lo + ps_rr[0] % (hi - lo)
        ps_rr[0] += 1
        return PSF[i], PSFT[i]

    def ln_stats(src, srcT, n):
        nch = max(1, n // 512)
        w = n // nch
        for ci in range(nch):
            P.op("dve", (lambda o, i: (lambda e: e.bn_stats(out=o, in_=i)))(ST6[:, ci * 6:(ci + 1) * 6], src[:, ci * w:(ci + 1) * w]),
                 reads=srcT, writes=[tST6] if ci == 0 else (), pwrites=() if ci == 0 else [tST6])
        P.op("dve", (lambda o, i: (lambda e: e.bn_aggr(out=o, in_=i)))(MV[:, 0:2], ST6[:, 0:6 * nch].rearrange('p (n s) -> p n s', s=6)), reads=[tST6], writes=[tMV])
        P.op("act", ACT(MV[:, 2:3], MV[:, 1:2], AF.Ln, bias=EPS), reads=[tMV], writes=[tMV])
        P.op("act", ACT(MV[:, 3:4], MV[:, 2:3], AF.Exp, scale=-0.5), reads=[tMV], writes=[tMV])

    def rms_rstd(src, srcT, G, W, tmp, tmpT):
        P.op("dve", TT(tmp[:, 0:G * W], src, src, ALU.mult), reads=srcT, writes=[tmpT])
        P.op("dve", (lambda o, i: (lambda e: e.reduce_sum(out=o, in_=i, axis=AX.X)))(SS[:, 0:G], tmp[:, 0:G * W].rearrange("p (g w) -> p g w", g=G)),
             reads=[tmpT], writes=[tSS])
        P.op("act", ACT(SS[:, 0:G], SS[:, 0:G], AF.Ln, scale=1.0 / W, bias=EPS), reads=[tSS], writes=[tSS])
        P.op("act", ACT(SS[:, 8:8 + G], SS[:, 0:G], AF.Exp, scale=-0.5), reads=[tSS], writes=[tSS])


    def qk_norm_rope(src, srcT, H, gain, dst, oscale, dstT, full):
        n = H * 64
        P.op("act", ACT(TMPA[:, 0:n], src, AF.Identity), reads=srcT, writes=[tTMPA])
        rms_rstd(TMPA[:, 0:n], [tTMPA], H, 64, TMPB, tTMPB)
        v3 = lambda ap: ap.rearrange("p (h d) -> p h d", h=H)
        P.op("dve", TT(v3(TMPA[:, 0:n]), v3(TMPA[:, 0:n]), SS[:, 8:8 + H].unsqueeze(2).to_broadcast([128, H, 64]), ALU.mult), reads=[tTMPA, tSS], writes=[tTMPA])
        P.op("dve", STT(v3(TMPA[:, 0:n]), v3(TMPA[:, 0:n]), oscale, gain.unsqueeze(1).to_broadcast([128, H, 64]), ALU.mult, ALU.mult),
             reads=[tTMPA, tLAYC], writes=[tTMPA])
        v5 = lambda ap: ap.rearrange("p (h a s f) -> p h a s f", h=H, a=2, s=2)
        x1 = v5(TMPA[:, 0:n])[:, :, :, 0, :]
        x2 = v5(TMPA[:, 0:n])[:, :, :, 1, :]
        d1 = v5(dst)[:, :, :, 0, :]
        d2 = v5(dst)[:, :, :, 1, :]
        C = ROPE[:, 0:32].rearrange("p (a f) -> p a f", a=2).unsqueeze(1).to_broadcast([128, H, 2, 16])
        S_ = ROPE[:, 32:64].rearrange("p (a f) -> p a f", a=2).unsqueeze(1).to_broadcast([128, H, 2, 16])
        v4 = lambda ap: ap.rearrange("p (h a f) -> p h a f", h=H, a=2)
        t1 = v4(TMPB[:, 0:H * 32])
        t2 = v4(TMPB[:, H * 32:H * 64])
        P.op("dve", TT(t1, x1, C, ALU.mult), reads=[tTMPA, tROPE], writes=[tTMPB])
        P.op("dve", TT(t2, x2, S_, ALU.mult), reads=[tTMPA, tROPE], pwrites=[tTMPB])
        P.op("dve", TT(d1, t1, t2, ALU.subtract), reads=[tTMPB], writes=dstT if full else (), pwrites=() if full else dstT)
        P.op("dve", TT(t1, x2, C, ALU.mult), reads=[tTMPA, tROPE], writes=[tTMPB])
        P.op("dve", TT(t2, x1, S_, ALU.mult), reads=[tTMPA, tROPE], pwrites=[tTMPB])
        P.op("dve", TT(d2, t1, t2, ALU.add), reads=[tTMPB], pwrites=dstT)

    def transpose_tile(src_bf, srcT, nch, dst_fn, dstT, evac):
        if ILV[0]:
            j = CUR[0]
        else:
            j = pst_rr[0] % 2
            pst_rr[0] += 1
        psb, pT = PST[j], PSFT[6 + j]
        for kc in range(nch):
            P.op("pe", TR(psb[:, kc * 128:(kc + 1) * 128], src_bf[:, kc * 128:(kc + 1) * 128], IDB[:, :]),
                 reads=srcT + [tCONST], writes=[pT] if kc == 0 else (), pwrites=() if kc == 0 else [pT])
        for kc in range(nch):
            evac(kc, psb[:, kc * 128:(kc + 1) * 128], pT)

    P.op("sp", DMA(IDF[:, :], identd[:, :]), writes=[tCONST], dma="c0")
    P.op("sp", DMA(S2F[:, :], cvec[:, :]), pwrites=[tCONST], dma="c0")
    P.op("sp", DMA(GOUT[:, :], gout[:, :]), pwrites=[tCONST], dma="c0")
    P.op("sp", DMA(BS[:, :], bsd[:, :]), pwrites=[tCONST], dma="c0")
    for i in range(NT):
        P.op("sp", DMA(X[:, i, :], x_in[i]), writes=[XT[i]], dma="xin")
    P.op("dve", CP(IDB[:, :], IDF[:, :]), reads=[tCONST], pwrites=[tCONST])
    P.op("dve", MEMSET(ONESB[:, :], 1.0), pwrites=[tCONST])
    P.op("act", ACT(S2F[:, :], S2F[:, :], AF.Silu), reads=[tCONST], writes=[tCONST])
    P.op("dve", CP(S2B[:].rearrange("p a b -> p (a b)"), S2F[:, :]), reads=[tCONST], writes=[tCONST])
    for kc in range(8):
        P.op("dve", TS(RLAT[:, kc, :], ONESB[:, :], S2F[:, 2 * kc:2 * kc + 1], None, ALU.mult), reads=[tCONST], pwrites=[tGTOK])
        P.op("dve", TS(RCTX[:, kc, :], ONESB[:, :], S2F[:, 2 * kc + 1:2 * kc + 2], None, ALU.mult), reads=[tCONST], pwrites=[tGTOK])
    P.op("dve", MEMSET(VAT[:, :, :], 1.0), writes=[tVAT])
    P.op("dve", MEMSET(VCT[:, :, :], 1.0), writes=[tVCT])

    for l in range(nlayers):
        P.op("sp", DMA(BMT[:, :], bmodT[l]), writes=[tLAYC], dma="c1")
        psm, psmT = PSF[5], PSFT[5]
        for jj, nbs in enumerate([(0, 1), (2, 3), (6, 7), (8, 9)]):
            for half, nb in enumerate(nbs):
                wbuf, wT = load_wblock(wmod[l * 12 + nb])
                for oc4 in range(4):
                    col = (jj * 8 + half * 4 + oc4) * 2
                    for kc in range(8):
                        first = (jj == 0 and half == 0 and oc4 == 0 and kc == 0)
                        P.op("pe", MM(psm[:, col:col + 2], wbuf[:, kc, oc4 * 128:(oc4 + 1) * 128], S2B[:, kc, :], kc == 0, kc == 7),
                             reads=[wT, tCONST], writes=[psmT] if first else (), pwrites=() if first else [psmT])
        pv = psm[:, 0:64].rearrange("p (a b) -> p a b", b=2)
        for s in range(2):
            P.op("dve", TT(MODF[:, l, :, s], pv[:, :, s], BMT[:, :], ALU.add), reads=[psmT, tLAYC], pwrites=[tMODF])
        for a0 in (8, 24):
            P.op("dve", TS(MODF[:, l, a0:a0 + 8, :], MODF[:, l, a0:a0 + 8, :], 1.0, None, ALU.add), reads=[tMODF], writes=[tMODF])
        for gi, nbs in enumerate([(4, 5), (10, 11)]):
            for half, nb in enumerate(nbs):
                wbuf, wT = load_wblock(wmod[l * 12 + nb])
                P.op("sp", DMA(TMPA[:, :], bmodg[l * 2 + gi, half * 512:(half + 1) * 512].partition_broadcast(128)),
                     writes=[tTMPA], dma="tmpa")
                for s, R in enumerate((RLAT, RCTX)):
                    ps, pT = next_ps(0, 3)
                    for kc in range(8):
                        P.op("pe", MM(ps[:, :], R[:, kc, :], wbuf[:, kc, :], kc == 0, kc == 7), reads=[wT, tGTOK],
                             writes=[pT] if kc == 0 else (), pwrites=() if kc == 0 else [pT])
                    P.op("dve", TT(TMPB[:, :], ps[:, :], TMPA[:, :], ALU.add), reads=[pT, tTMPA], writes=[tTMPB])
                    P.op("sp", DMA(gst[l * 4 + gi * 2 + s][:, half * 512:(half + 1) * 512], TMPB[:, :]), reads=[tTMPB], pwrites=[tGST], dma="gst")
    P.barrier()
    wb_pool[0] = 2
    wb_rr[0] = 0
    if STOP[0] == 'p0':
        return nc, P

    def load_bc(l, sub, ctx):
        P.op("sp", DMA(BC[:, 0, :], gst[l * 4 + sub * 2 + (1 if ctx else 0)]), reads=[tGST], writes=[tBC[0]], dma="bc0")

    def load_ln(l, sub):
        for j in range(2):
            r = l * 4 + sub * 2 + j
            P.op("sp", DMA(BC[:, 1 + j, :], lnp[r, :].partition_broadcast(128)), writes=[tBC[1 + j]], dma=f"bc{1 + j}")

    def ln_mod_transpose(i, l, which, slot):
        s = 1 if i >= 16 else 0
        ln_stats(X[:, i, :], [XT[i]], 1024)
        P.op("dve", TS(W2A[:, :], X[:, i, :], MV[:, 0:1], MV[:, 3:4], ALU.subtract, ALU.mult), reads=[XT[i], tMV], writes=[tW2A])

        def evac(kc, pap, pT):
            sc = MODF[:, l, which * 16 + 8 + kc, s:s + 1]
            sh = MODF[:, l, which * 16 + kc, s:s + 1]
            P.op("dve", TS(HT4[:, slot, kc, :], pap, sc, sh, ALU.mult, ALU.add), reads=[pT, tMODF], pwrites=[tHT[slot]])
        transpose_tile(W2A, [tW2A], 8, None, None, evac)

    def deepnorm_from(i, zsrc, zT):
        P.op("dve", STT(W4A[:, :], X[:, i, :], ALPHA, zsrc, ALU.mult, ALU.add), reads=[XT[i]] + zT, writes=[tW4A])
        ln_stats(W4A[:, :], [tW4A], 1024)
        P.op("dve", STT(W4A[:, :], W4A[:, :], MV[:, 0:1], BC[:, 1, :], ALU.subtract, ALU.mult), reads=[tW4A, tMV, tBC[1]], writes=[tW4A])
        P.op("dve", STT(X[:, i, :], W4A[:, :], MV[:, 3:4], BC[:, 2, :], ALU.mult, ALU.add), reads=[tW4A, tMV, tBC[2]], writes=[XT[i]])

    def deepnorm(i, l, ysrc_fn, yT):
        for nb in range(2):
            P.op("dve", TT(W4A[:, nb * 512:(nb + 1) * 512], ysrc_fn(nb), BC[:, 0, nb * 512:(nb + 1) * 512], ALU.mult),
                 reads=[yT[nb], tBC[0]], writes=[tW4A] if nb == 0 else (), pwrites=() if nb == 0 else [tW4A])
        deepnorm_from(i, W4A[:, :], [tW4A])

    for l in range(nlayers):
        P.op("sp", DMA(GSGU[:, :], gsgu[l, :].partition_broadcast(128)), writes=[tLAYC], dma="c1")
        P.op("sp", DMA(GQK[:, :], gqk[l, :].partition_broadcast(128)), pwrites=[tLAYC], dma="c1")
        P.op("pool", DMA(WST[:, :], wsT[l]), pwrites=[tLAYC], dma="c2")

        CUR[0] = 1
        P.op("dve", MEMSET(VAT[:, :, :], 1.0), writes=[tVAT])
        P.op("dve", MEMSET(VCT[:, :, :], 1.0), pwrites=[tVCT])
        CUR[0] = 0
        P.op("dve", MEMSET(VC[:, 64:66, :], 1.0), writes=[tVC])
        P.op("dve", MEMSET(NAV[:, 8:10, :], 1.0), writes=[tNAV])
        wv, wvT = load_wblock(win[l * 5 + 0])
        wk, wkT = load_wblock(win[l * 5 + 1])
        def p1_tile(i):
            lat = i < 16
            sl = i % 2
            P.op("sp", DMA(ROPE[:, :], rope[i]), writes=[tROPE], dma="rope")
            ln_mod_transpose(i, l, 0, sl)
            P.op("sp", DMA(hts[i], HT4[:, sl].rearrange("p a b -> p (a b)")), reads=[tHT[sl]], writes=[tHTS[i]], dma="hts")
            ps, pT = next_ps(0, 3)
            for kc in range(8):
                P.op("pe", MM(ps[:, :], HT4[:, sl, kc, :], wv[:, kc, :], kc == 0, kc == 7), reads=[tHT[sl], wvT],
                     writes=[pT] if kc == 0 else (), pwrites=() if kc == 0 else [pT])
            if lat:
                P.op("dve", CP(VAT[:, :, 0:64], ps[:, 0:384].rearrange("p (h d) -> p h d", h=6)), reads=[pT], pwrites=[tVAT])
                P.op("dve", CP(VCT[:, :, 0:64], ps[:, 384:512].rearrange("p (h d) -> p h d", h=2)), reads=[pT], pwrites=[tVCT])
                P.op("sp", DMA(nav_loc[i], VAT[:].rearrange("p a b -> p (a b)")), reads=[tVAT], pwrites=[tNAVL], dma="navl")
                P.op("sp", DMA(vc_loc.ap()[i * 128:(i + 1) * 128, :], VCT[:].rearrange("p a b -> p (a b)")), reads=[tVCT], pwrites=[tVCL], dma="vcl")
                if i in (0, 1, 14, 15):
                    e = i if i < 2 else i - 12
                    P.op("sp", DMA(nav_edge.ap()[e * 128:(e + 1) * 128, :], VAT[:].rearrange("p a b -> p (a b)")), reads=[tVAT], pwrites=[tNAVE], dma="nave")
            else:
                j = i - 16
                P.op("dve", CP(NAV[:, 8 + j, :].rearrange("p (h d) -> p h d", h=6)[:, :, 0:64], ps[:, 0:384].rearrange("p (h d) -> p h d", h=6)), reads=[pT], pwrites=[tNAV])
                P.op("dve", CP(VC[:, 64 + j, :].rearrange("p (h d) -> p h d", h=2)[:, :, 0:64], ps[:, 384:512].rearrange("p (h d) -> p h d", h=2)), reads=[pT], pwrites=[tVC])
            ps, pT = next_ps(0, 3)
            for kc in range(8):
                P.op("pe", MM(ps[:, :], HT4[:, sl, kc, :], wk[:, kc, :], kc == 0, kc == 7), reads=[tHT[sl], wkT],
                     writes=[pT] if kc == 0 else (), pwrites=() if kc == 0 else [pT])
            P.op("act", ACT(KQB[:, 0:384], ps[:, 0:384], AF.Identity), reads=[pT], writes=[tKQB])
            qk_norm_rope(ps[:, 384:512], [pT], 2, GQK[:, 64:128], KQB[:, 384:512], 1.0, [tKQB], False)
            kdst = []

            def evac_k(kc, pap, ppT, i=i, lat=lat):
                if kc < 3:
                    if lat:
                        P.op("dve", CP(KAT[:, kc, :], pap), reads=[ppT], pwrites=[tKAT])
                    else:
                        P.op("dve", CP(NAK[:, kc, (8 + i - 16) * 128:(9 + i - 16) * 128], pap), reads=[ppT], pwrites=[tNAK])
                else:
                    if lat:
                        P.op("dve", CP(KAT[:, 3, :], pap), reads=[ppT], pwrites=[tKAT])
                    else:
                        P.op("dve", CP(KTC[:, 8192 + (i - 16) * 128:8192 + (i - 15) * 128], pap), reads=[ppT], pwrites=[tKTC])
            transpose_tile(KQB, [tKQB], 4, None, None, evac_k)
            if lat:
                P.op("sp", DMA(nak_loc[i], KAT[:, 0:3, :].rearrange("p a b -> p (a b)")), reads=[tKAT], pwrites=[tNAKL], dma="nakl")
                P.op("sp", DMA(ktc_loc.ap()[:, i * 128:(i + 1) * 128], KAT[:, 3, :]), reads=[tKAT], pwrites=[tKTCL], dma="ktcl")
                if i in (0, 1, 14, 15):
                    e = i if i < 2 else i - 12
                    P.op("sp", DMA(nak_edge.ap()[e * 128:(e + 1) * 128, :], KAT[:, 0:3, :].rearrange("p a b -> p (a b)")), reads=[tKAT], pwrites=[tNAKE], dma="nake")

        for ia in range(0, NT, 2):
            lists = []
            for i in (ia, ia + 1):
                CUR[0] = i % 2
                ILV[0] = True
                P.record_begin()
                p1_tile(i)
                lists.append(P.record_end())
            CUR[0] = 0
            ILV[0] = False
            P.replay_interleaved(lists)

        if STOP[0] == 'p1':
            return nc, P
        def coll(src, srcT, dst, dstT, key):
            P.op("pool", (lambda s_, d_: (lambda e: e.collective_compute(
                "AllGather", ALU.bypass, replica_groups=[[0, 1, 2, 3], [4, 5, 6, 7]],
                ins=[s_.ap().opt()], outs=[d_.ap().opt()])))(src, dst), reads=[srcT], writes=[dstT], dma=key)
        coll(ktc_loc, tKTCL, ktc_all, tKTCA, f"cc{l}a")
        coll(vc_loc, tVCL, vc_all, tVCA, f"cc{l}b")
        coll(nak_edge, tNAKE, nak_eall, tNAKEA, f"cc{l}c")
        coll(nav_edge, tNAVE, nav_eall, tNAVEA, f"cc{l}d")
        def load_gathered_kv():
            for r in range(4):
                P.op("sp", DMA(KTC[:, r * 2048:(r + 1) * 2048], ktc_all.ap()[r * 128:(r + 1) * 128, :]), reads=[tKTCA], pwrites=[tKTC], dma="ktc")
            for r in range(4):
                P.op("sp", DMA(VC[:, r * 16:(r + 1) * 16, :], vc_all.ap()[r * 2048:(r + 1) * 2048, :].rearrange("(k p) c -> p k c", p=128)),
                     reads=[tVCA], pwrites=[tVC], dma="vc")

        if STOP[0] == 'ex':
            P.barrier()
            return nc, P
        load_ln(l, 0)
        cur_cls = [None]
        for st in range(5):
            tiles = list(range(4 * st, 4 * st + 4)) if st < 4 else [16, 17]
            if l == nlayers - 1 and st == 4:
                continue
            n = len(tiles)
            ctx = st == 4
            load_bc(l, 0, ctx)
            for ti, i in enumerate(tiles):
                P.op("sp", DMA(HT4[:, ti].rearrange("p a b -> p (a b)"), hts[i]), reads=[tHTS[i]], writes=[tHT[ti]], dma=f"ht{ti}")
            wq, wqT = load_wblock(win[l * 5 + 2])
            def proj_qa(ti, i):
                ps, pT = next_ps(0, 3)
                for kc in range(8):
                    P.op("pe", MM(ps[:, 0:384], HT4[:, ti, kc, :], wq[:, kc, 0:384], kc == 0, kc == 7), reads=[tHT[ti], wqT],
                         writes=[pT] if kc == 0 else (), pwrites=() if kc == 0 else [pT])
                P.op("act", ACT(KQB[:, 0:384], ps[:, 0:384], AF.Identity, scale=0.125), reads=[pT], writes=[tKQB])

                def evac_qa(kc, pap, ppT, ti=ti):
                    P.op("dve", CP(QTA[:, kc, ti * 128:(ti + 1) * 128], pap), reads=[ppT], pwrites=[tQTA])
                transpose_tile(KQB, [tKQB], 3, None, None, evac_qa)
            for ta in range(0, n, 2):
                lists = []
                for ti in range(ta, min(n, ta + 2)):
                    CUR[0] = ti % 2
                    ILV[0] = True
                    P.record_begin()
                    proj_qa(ti, tiles[ti])
                    lists.append(P.record_end())
                CUR[0] = 0
                ILV[0] = False
                P.replay_interleaved(lists)
            wq, wqT = load_wblock(win[l * 5 + 3])
            def proj_qc(ti, i):
                P.op("sp", DMA(ROPE[:, :], rope[i]), writes=[tROPE], dma="rope")
                ps, pT = next_ps(0, 3)
                for kc in range(8):
                    P.op("pe", MM(ps[:, 0:384], HT4[:, ti, kc, :], wq[:, kc, 0:384], kc == 0, kc == 7), reads=[tHT[ti], wqT],
                         writes=[pT] if kc == 0 else (), pwrites=() if kc == 0 else [pT])
                qk_norm_rope(ps[:, 0:384], [pT], 6, GQK[:, 0:64], KQB[:, 0:384], 0.125, [tKQB], True)

                def evac_qc(kc, pap, ppT, ti=ti):
                    P.op("dve", CP(QTC[:, kc, ti * 128:(ti + 1) * 128], pap), reads=[ppT], pwrites=[tQTC])
                transpose_tile(KQB, [tKQB], 3, None, None, evac_qc)
            for ta in range(0, n, 2):
                lists = []
                for ti in range(ta, min(n, ta + 2)):
                    CUR[0] = ti % 2
                    ILV[0] = True
                    P.record_begin()
                    proj_qc(ti, tiles[ti])
                    lists.append(P.record_end())
                CUR[0] = 0
                ILV[0] = False
                P.replay_interleaved(lists)
            wq, wqT = load_wblock(win[l * 5 + 4])
            def proj_zb(ti, i):
                ps, pT = next_ps(0, 3)
                for kc in range(8):
                    P.op("pe", MM(ps[:, :], HT4[:, ti, kc, :], wq[:, kc, :], kc == 0, kc == 7), reads=[tHT[ti], wqT],
                         writes=[pT] if kc == 0 else (), pwrites=() if kc == 0 else [pT])
                P.op("act", ACT(ZGp[:, :], ps[:, :], AF.Gelu_apprx_tanh), reads=[pT], writes=[tZGp])
                ln_stats(ZGp[:, 256:512], [tZGp], 256)
                P.op("dve", TS(KQB[:, 0:256], ZGp[:, 256:512], MV[:, 0:1], MV[:, 3:4], ALU.subtract, ALU.mult), reads=[tZGp, tMV], writes=[tKQB])
                ps2, p2T = (PSF[4 + CUR[0]], PSFT[4 + CUR[0]]) if ILV[0] else (PSF[5], PSFT[5])
                for g in range(4):
                    P.op("pe", MM(ps2[:, g * 64:(g + 1) * 64], WST[:, g * 128:(g + 1) * 128], KQB[:, g * 64:(g + 1) * 64], True, True),
                         reads=[tKQB, tLAYC], writes=[p2T] if g == 0 else (), pwrites=() if g == 0 else [p2T])
                P.op("dve", TT(TMPA[:, 0:256], ps2[:, 0:256], GSGU[:, :], ALU.mult), reads=[p2T, tLAYC], writes=[tTMPA])
                for g in range(4):
                    P.op("dve", TS(TMPA[:, g * 64:(g + 1) * 64], TMPA[:, g * 64:(g + 1) * 64], BS[:, l * 4 + g:l * 4 + g + 1], None, ALU.add),
                         reads=[tTMPA, tCONST], writes=[tTMPA])
                P.op("dve", TT(TMPA[:, 0:256], TMPA[:, 0:256], ZGp[:, 0:256], ALU.mult), reads=[tTMPA, tZGp], writes=[tTMPA])
                rms_rstd(TMPA[:, 0:256], [tTMPA], 1, 256, TMPB, tTMPB)
                P.op("dve", TS(MERGB[:, ti, 384:640], TMPA[:, 0:256], SS[:, 8:9], None, ALU.mult), reads=[tTMPA, tSS], pwrites=[tMERG[ti]])

            for ta in range(0, n, 2):
                lists = []
                for ti in range(ta, min(n, ta + 2)):
                    CUR[0] = ti % 2
                    ILV[0] = True
                    P.record_begin()
                    proj_zb(ti, tiles[ti])
                    lists.append(P.record_end())
                CUR[0] = 0
                ILV[0] = False
                P.replay_interleaved(lists)
            if st == 0:
                load_gathered_kv()
            if not ctx:
                lo, hi = max(0, 4 * st - 2), min(16, 4 * st + 6)
                s0 = lo - (4 * st - 2)
                for j in range(hi - lo):
                    P.op("sp", DMA(NAK[:, :, (s0 + j) * 128:(s0 + j + 1) * 128], nak_loc[lo + j].rearrange("p (c k) -> p c k", c=3)),
                         reads=[tNAKL], writes=[tNAK] if j == 0 else (), pwrites=() if j == 0 else [tNAK], dma="nak")
                P.op("sp", DMA(NAV[:, s0:s0 + hi - lo, :], nav_loc[lo:hi].rearrange("t p c -> p t c")), reads=[tNAVL], writes=[tNAV], dma="nav")
                if st in (0, 3):
                    for j in range(2):
                        def mk_halo_k(c3, st=st, j=j):
                            def halo_k(e):
                                rk = PIDV[0]
                                if st == 0:
                                    src_r, e0, sl = (rk + 3) % 4, 2, 0
                                else:
                                    src_r, e0, sl = (rk + 1) % 4, 0, 6
                                return e.dma_start(out=NAK[:, c3, (sl + j) * 128:(sl + j + 1) * 128],
                                                   in_=nak_eall.ap()[bass.ds(src_r * 512 + (e0 + j) * 128, 128), c3 * 128:(c3 + 1) * 128])
                            return halo_k

                        def halo_v(e, st=st, j=j):
                            rk = PIDV[0]
                            if st == 0:
                                src_r, e0, sl = (rk + 3) % 4, 2, 0
                            else:
                                src_r, e0, sl = (rk + 1) % 4, 0, 6
                            return e.dma_start(out=NAV[:, sl + j, :], in_=nav_eall.ap()[bass.ds(src_r * 512 + (e0 + j) * 128, 128), :])
                        for c3 in range(3):
                            P.op("pool", mk_halo_k(c3), reads=[tNAKEA], pwrites=[tNAK], dma="nak")
                        P.op("pool", halo_v, reads=[tNAVEA], pwrites=[tNAV], dma="nav")
            if STOP[0] == 'p2a':
                P.barrier()
                return nc, P
            for ti, i in enumerate(tiles):
                if not ctx:
                    cls = 0 if i == 0 else 1 if i == 1 else 3 if i == 14 else 4 if i == 15 else 2
                    if cur_cls[0] != (l, cls):
                        P.op("pool", DMA(NAB[:, :], nab[l * 5 + cls]), writes=[tNAB], dma="nab")
                        cur_cls[0] = (l, cls)
                oac = [PSF[3], PSF[4]]
                oacT = [PSFT[3], PSFT[4]]
                for h in range(6):
                    c, pb = h // 2, (h % 2) * 64
                    q = QTA[pb:pb + 64, c, ti * 128:(ti + 1) * 128]
                    psa, paT = next_ps(0, 3)
                    psb_, pbT = next_ps(0, 3)
                    blocks = []
                    if not ctx:
                        sl_b = [(ti + b, b) for b in range(5)]
                        if i == 0:
                            sl_b.append((ti + 5, 5))
                        if i == 15:
                            sl_b.append((ti - 1, 5))
                        sl_b += [(8, None), (9, None)]
                    else:
                        sl_b = [(8, None), (9, None)]
                    for k, (slot, b) in enumerate(sl_b):
                        if k < 4:
                            blocks.append((slot, psa[:, k * 128:(k + 1) * 128], paT, b))
                        else:
                            blocks.append((slot, psb_[:, (k - 4) * 128:(k - 3) * 128], pbT, b))
                    nB = max(0, len(sl_b) - 4)
                    seen = set()
                    for (slot, reg, rT, b) in blocks:
                        first = id(rT) not in seen
                        seen.add(id(rT))
                        P.op("pe", MM(reg, NAK[pb:pb + 64, c, slot * 128:(slot + 1) * 128], q, True, b is None), reads=[tNAK, tQTA],
                             writes=[rT] if first else (), pwrites=() if first else [rT])
                        if b is not None:
                            P.op("pe", MM(reg, IDB[:, :], NAB[:, (h * 6 + b) * 128:(h * 6 + b + 1) * 128], False, True), reads=[tNAB, tCONST], pwrites=[rT])
                    if not ctx:
                        P.op("act", ACT(PTN[:, 0:512], psa[:, :], AF.Exp), reads=[paT], writes=[tPTN, tPTNb])
                        P.op("act", ACT(PTN[:, 512:512 + nB * 128], psb_[:, 0:nB * 128], AF.Exp), reads=[pbT], pwrites=[tPTN])
                    else:
                        P.op("act", ACT(PTN[:, 0:256], psa[:, 0:256], AF.Exp), reads=[paT], writes=[tPTN, tPTNb])
                    ob, obT = oac[h // 4], oacT[h // 4]
                    oreg = ob[0:65, (h % 4) * 128:(h % 4 + 1) * 128]
                    nb_ = len(blocks)
                    for bi, (slot, reg, rT, b) in enumerate(blocks):
                        firstw = (h % 4 == 0 and bi == 0)
                        P.op("pe", MM(oreg, NAV[:, slot, h * 65:(h + 1) * 65], PTN[:, bi * 128:(bi + 1) * 128], bi == 0, bi == nb_ - 1),
                             reads=[tNAV, tPTN, tPTNb], writes=[obT] if firstw else (), pwrites=() if firstw else [obT])
                P.op("dve", CP(OT[0:65, 0:512], oac[0][0:65, :]), reads=[oacT[0]], writes=[tOT])
                P.op("dve", CP(OT[0:65, 512:768], oac[1][0:65, 0:256]), reads=[oacT[1]], pwrites=[tOT])
                tp, tpT = PSF[5], PSFT[5]
                for h in range(6):
                    P.op("pe", TR(tp[:, h * 65:(h + 1) * 65], OT[0:65, h * 128:(h + 1) * 128], IDF[0:65, 0:65]), reads=[tOT, tCONST],
                         writes=[tpT] if h == 0 else (), pwrites=() if h == 0 else [tpT])
                tpv = tp[:, 0:390].rearrange("p (h d) -> p h d", h=6)
                P.op("dve", (lambda o, i_: (lambda e: e.reciprocal(out=o, in_=i_)))(SS[:, 0:6], tpv[:, :, 64]), reads=[tpT], writes=[tSS])
                P.op("dve", TT(OATT[:, ti, :].rearrange("p (h d) -> p h d", h=6), tpv[:, :, 0:64],
                               SS[:, 0:6].unsqueeze(2).to_broadcast([128, 6, 64]), ALU.mult), reads=[tpT, tSS], writes=[tW4A, tZG])
                rms_rstd(OATT[:, ti, :], [tW4A, tZG], 1, 384, TMPA, tTMPA)
                P.op("dve", TS(MERGB[:, ti, 0:384], OATT[:, ti, :], SS[:, 8:9], None, ALU.mult), reads=[tW4A, tZG, tSS], pwrites=[tMERG[ti]])

            if STOP[0] == 'p2na':
                P.barrier()
                return nc, P
            nq = n * 128
            kts = list(range(66)) if not ctx else [64, 65]
            nk = len(kts)
            for c in range(3):
                qs = [QTC[g * 64:g * 64 + 64, c, 0:nq] for g in range(2)]
                obs = [(PSF[4 + g], PSFT[4 + g]) for g in range(2)]
                ptb = [[(PT[0], tPT[0]), (PT[1], tPT[1])], [(PTN[:, 0:512], tPTN), (PTN[:, 512:1024], tPTNb)]]

                def S(g, k):
                    ps, pT = PSF[g * 2 + k % 2], PSFT[g * 2 + k % 2]
                    kt = kts[k]
                    P.op("pe", MM(ps[:, 0:nq], KTC[g * 64:g * 64 + 64, kt * 128:(kt + 1) * 128], qs[g], True, True), reads=[tKTC, tQTC], writes=[pT])
                for k0 in range(min(2, nk)):
                    for g in range(2):
                        S(g, k0)
                for k in range(nk):
                    for g in range(2):
                        ps, pT = PSF[g * 2 + k % 2], PSFT[g * 2 + k % 2]
                        pbuf, pbT = ptb[g][k % 2]
                        P.op("act", ACT(pbuf[:, 0:nq], ps[:, 0:nq], AF.Exp), reads=[pT], writes=[pbT])
                    if k + 2 < nk:
                        for g in range(2):
                            S(g, k + 2)
                    kt = kts[k]
                    for g in range(2):
                        ob, obT = obs[g]
                        pbuf, pbT = ptb[g][k % 2]
                        P.op("pe", MM(ob[0:65, 0:nq], VC[:, kt, g * 65:(g + 1) * 65], pbuf[:, 0:nq], k == 0, k == nk - 1),
                             reads=[tVC, pbT], writes=[obT] if k == 0 else (), pwrites=() if k == 0 else [obT])
                for g in range(2):
                    h = c + 3 * g
                    ob, obT = obs[g]
                    P.op("dve", CP(OT[0:65, 0:nq], ob[0:65, 0:nq]), reads=[obT], writes=[tOT])
                    tp, tpT = PSF[g], PSFT[g]
                    for ti in range(n):
                        P.op("pe", TR(tp[:, ti * 65:(ti + 1) * 65], OT[0:65, ti * 128:(ti + 1) * 128], IDF[0:65, 0:65]), reads=[tOT, tCONST],
                             writes=[tpT] if ti == 0 else (), pwrites=() if ti == 0 else [tpT])
                    tpv = tp[:, 0:65 * n].rearrange("p (t d) -> p t d", t=n)
                    P.op("dve", (lambda o, i_: (lambda e: e.reciprocal(out=o, in_=i_)))(SS[:, 0:n], tpv[:, :, 64]), reads=[tpT], writes=[tSS])
                    P.op("dve", TT(OATT[:, 0:n, h * 64:(h + 1) * 64], tpv[:, :, 0:64],
                                   SS[:, 0:n].unsqueeze(2).to_broadcast([128, n, 64]), ALU.mult), reads=[tpT, tSS], pwrites=[tW4A, tZG])
            for ti in range(n):
                rms_rstd(OATT[:, ti, :], [tW4A, tZG], 1, 384, TMPA, tTMPA)
                P.op("dve", TS(MERGB[:, ti, 640:1024], OATT[:, ti, :], SS[:, 8:9], None, ALU.mult), reads=[tW4A, tZG, tSS], pwrites=[tMERG[ti]])

            if STOP[0] == 'p2gqa':
                P.barrier()
                return nc, P
            for ti in range(n):
                def evac_m(kc, pap, ppT, ti=ti):
                    P.op("dve", TS(HT4[:, ti, kc, :], pap, GOUT[:, l * 8 + kc:l * 8 + kc + 1], None, ALU.mult), reads=[ppT, tCONST], pwrites=[tHT[ti]])
                transpose_tile(MERGB[:, ti, :], [tMERG[ti]], 8, None, None, evac_m)
            wo0, wo0T = load_wblock(wo[l * 2 + 0])
            wo1, wo1T = load_wblock(wo[l * 2 + 1])
            for ti, i in enumerate(tiles):
                for nb, (wb_, wbT_) in enumerate(((wo0, wo0T), (wo1, wo1T))):
                    ps, pT = PSF[3 + nb], PSFT[3 + nb]
                    for kc in range(8):
                        P.op("pe", MM(ps[:, :], HT4[:, ti, kc, :], wb_[:, kc, :], kc == 0, kc == 7), reads=[tHT[ti], wbT_],
                             writes=[pT] if kc == 0 else (), pwrites=() if kc == 0 else [pT])
                deepnorm(i, l, lambda nb: PSF[3 + nb][:, :], [PSFT[3], PSFT[4]])
        P.barrier()
        if STOP[0] == 'p2':
            return nc, P

        load_ln(l, 1)
        sts = [list(range(4 * st, 4 * st + 4)) for st in range(4)]
        if l < nlayers - 1:
            sts.append([16, 17])
        HTB = [(HT4, tHT), (HT4b, tHTb)]

        def p3_ln(i, j):
            w2, w2T, mv, mvT, st6, st6T = W2P[j], tW2P[j], MVP[j], tMVP[j], ST6P[j], tST6P[j]
            for ci in range(2):
                P.op("dve", (lambda o, i_: (lambda e: e.bn_stats(out=o, in_=i_)))(st6[:, ci * 6:(ci + 1) * 6], X[:, i, ci * 512:(ci + 1) * 512]),
                     reads=[XT[i]], writes=[st6T] if ci == 0 else (), pwrites=() if ci == 0 else [st6T])
            P.op("dve", (lambda o, i_: (lambda e: e.bn_aggr(out=o, in_=i_)))(mv[:, 0:2], st6[:, 0:12].rearrange('p (n s) -> p n s', s=6)), reads=[st6T], writes=[mvT])
            P.op("act", ACT(mv[:, 2:3], mv[:, 1:2], AF.Ln, bias=EPS), reads=[mvT], writes=[mvT])
            P.op("act", ACT(mv[:, 3:4], mv[:, 2:3], AF.Exp, scale=-0.5), reads=[mvT], writes=[mvT])
            P.op("dve", TS(w2[:, :], X[:, i, :], mv[:, 0:1], mv[:, 3:4], ALU.subtract, ALU.mult), reads=[XT[i], mvT], writes=[w2T])

        def p3_tr(i, j, hb, ti):
            hbuf, hT = hb
            s_ = 1 if i >= 16 else 0

            def evac(kc, pap, pT):
                sc = MODF[:, l, 24 + kc, s_:s_ + 1]
                sh = MODF[:, l, 16 + kc, s_:s_ + 1]
                P.op("dve", TS(hbuf[:, ti, kc, :], pap, sc, sh, ALU.mult, ALU.add), reads=[pT, tMODF], pwrites=[hT[ti]])
            transpose_tile(W2P[j], [tW2P[j]], 8, None, None, evac)

        def p3_dn(i, ti):
            deepnorm_from(i, Y4[:, ti, :], [tY4[ti]])
            if l == nlayers - 1 and i < 16:
                P.op("sp", DMA(y_out[i], X[:, i, :]), reads=[XT[i]], pwrites=[tY], dma="yout")

        for ti, i in enumerate(sts[0]):
            p3_ln(i, ti % 2)
            p3_tr(i, ti % 2, HTB[0], ti)
        for si, tiles in enumerate(sts):
            n = len(tiles)
            ctx = tiles[0] >= 16
            hbuf, hT = HTB[si % 2]
            prev = sts[si - 1] if si > 0 else []
            nxt = sts[si + 1] if si + 1 < len(sts) else []
            load_bc(l, 1, ctx)
            for nb in range(11):
                wb_, wbT_ = load_wblock(wffi[l * 11 + nb])
                for ti, i in enumerate(tiles):
                    ps, pT = next_ps(0, 3)
                    for kc in range(8):
                        P.op("pe", MM(ps[:, :], hbuf[:, ti, kc, :], wb_[:, kc, :], kc == 0, kc == 7), reads=[hT[ti], wbT_],
                             writes=[pT] if kc == 0 else (), pwrites=() if kc == 0 else [pT])
                    P.op("act", ACT(TMPA[:, 0:256], ps[:, 0:256], AF.Silu), reads=[pT], writes=[tTMPA])
                    P.op("dve", TT(GTOK4[:, ti, nb * 256:(nb + 1) * 256], TMPA[:, 0:256], ps[:, 256:512], ALU.mult), reads=[tTMPA, pT], pwrites=[tG4[ti]])
                if nb % 2 == 0 and nb // 2 < len(prev):
                    p3_dn(prev[nb // 2], nb // 2)
                if nb % 2 == 1 and nb // 2 < len(nxt):
                    p3_ln(nxt[nb // 2], (nb // 2) % 2)
                if nb >= 3 and nb % 2 == 1 and (nb - 3) // 2 < len(nxt):
                    t2 = (nb - 3) // 2
                    p3_tr(nxt[t2], t2 % 2, HTB[(si + 1) % 2], t2)
            for t2 in range(len(nxt)):
                if 3 + 2 * t2 > 10:
                    p3_tr(nxt[t2], t2 % 2, HTB[(si + 1) % 2], t2)
            for ti, i in enumerate(tiles):
                GT = GTTa if ti < 2 else GTTb
                for grp in range(3):
                    nch = 8 if grp < 2 else 6

                    def evac_gg(kc, pap, ppT, grp=grp, ti=ti, GT=GT):
                        kk = grp * 8 + kc
                        P.op("dve", CP(GT[:, ti % 2, kk, :], pap), reads=[ppT], pwrites=[tGTT4[ti]])
                    transpose_tile(GTOK4[:, ti, grp * 1024:grp * 1024 + nch * 128], [tG4[ti]], nch, None, None, evac_gg)
            for nb2 in range(2):
                for kg in range(3):
                    nch = 8 if kg < 2 else 6
                    wb_, wbT_ = load_wblock(wffo[l * 6 + nb2 * 3 + kg])
                    for ti, i in enumerate(tiles):
                        GT = GTTa if ti < 2 else GTTb
                        ps, pT = PSF[ti], PSFT[ti]
                        for kcl in range(nch):
                            kc = kg * 8 + kcl
                            P.op("pe", MM(ps[:, :], GT[:, ti % 2, kc, :], wb_[:, kcl, :], kc == 0, kc == 21), reads=[tGTT4[ti], wbT_],
                                 writes=[pT] if kc == 0 else (), pwrites=() if kc == 0 else [pT])
                for ti, i in enumerate(tiles):
                    P.op("dve", TT(Y4[:, ti, nb2 * 512:(nb2 + 1) * 512], PSF[ti][:, :], BC[:, 0, nb2 * 512:(nb2 + 1) * 512], ALU.mult),
                         reads=[PSFT[ti], tBC[0]], pwrites=[tY4[ti]])
        for ti, i in enumerate(sts[-1]):
            p3_dn(i, ti)
        P.barrier()

    P.op("sp", None, reads=[tY])
    return nc, P


def finish(nc, P):
    from contextlib import ExitStack
    keys = list(ENGS) + ["dma:" + k for k in P.dma_cnt]
    with ExitStack() as es:
        sems = {k: es.enter_context(nc.semaphore(k.replace(":", "_"))) for k in keys}
        block = es.enter_context(nc.Block())
        P.emit(nc, block, sems)
    return nc


def _blocks(w, nblk, width=512):
    K = w.shape[0]
    kc = K // 128
    return np.ascontiguousarray(w.reshape(kc, 128, nblk, width).transpose(2, 1, 0, 3)).reshape(nblk, 128, kc * width)


def prep_shared(inp, NL):
    L = 4
    f = np.float32
    w_mod, b_mod, w_in = inp["w_mod"], inp["b_mod"], inp["w_in"]
    sh = {}
    sh["wmod"] = np.concatenate([_blocks(w_mod[l], 12) for l in range(NL)], 0)
    bases = [0, 1024, 3072, 4096]
    bt = np.zeros((L, 128, 32), f)
    for l in range(L):
        for j, b0 in enumerate(bases):
            bt[l, :, j * 8:(j + 1) * 8] = b_mod[l, b0:b0 + 1024].reshape(8, 128).T
    sh["bmodT"] = bt
    sh["bmodg"] = np.ascontiguousarray(np.stack([np.stack([b_mod[l, 2048:3072], b_mod[l, 5120:6144]]) for l in range(L)]).reshape(L * 2, 1024))
    perm = np.concatenate([np.arange(h * 64, (h + 1) * 64) for h in (0, 3, 1, 4, 2, 5)])
    wl = []
    for l in range(NL):
        w = w_in[l]
        qa, ka, va, zb, qc, kc_, vc = w[:, 0:384], w[:, 384:768], w[:, 768:1152], w[:, 1152:1664], w[:, 1664:2048], w[:, 2048:2176], w[:, 2176:2304]
        z128 = np.zeros((1024, 128), f)
        blks = [np.concatenate([va, vc], 1), np.concatenate([ka, kc_], 1), np.concatenate([qa, z128], 1),
                np.concatenate([qc[:, perm], z128], 1), zb]
        wl.append(_blocks(np.concatenate(blks, 1), 5))
    sh["win"] = np.concatenate(wl, 0)
    sh["wo"] = np.concatenate([_blocks(inp["w_o"][l], 2) for l in range(NL)], 0)
    wl = []
    for l in range(NL):
        w = inp["w_ffn_in"][l]
        a, b = w[:, :2816].reshape(1024, 11, 256), w[:, 2816:].reshape(1024, 11, 256)
        wl.append(_blocks(np.concatenate([a, b], 2).reshape(1024, 11 * 512), 11))
    sh["wffi"] = np.concatenate(wl, 0)
    wl = []
    for l in range(NL):
        w = np.zeros((3072, 1024), f)
        w[:2816] = inp["w_ffn_out"][l]
        wl.append(np.ascontiguousarray(w.reshape(3, 8, 128, 2, 512).transpose(3, 0, 2, 1, 4)).reshape(6, 128, 4096))
    sh["wffo"] = np.concatenate(wl, 0)
    sh["lnp"] = np.ascontiguousarray(np.stack([np.stack([inp["ln1_g"][l], inp["ln1_b"][l], inp["ln2_g"][l], inp["ln2_b"][l]]) for l in range(L)]).reshape(L * 4, 1024))
    sh["gout"] = np.ascontiguousarray(np.concatenate([inp["g_out"][l].reshape(8, 128).T for l in range(L)], 1))
    sh["gsgu"] = np.ascontiguousarray(inp["g_sgu"])
    sh["gqk"] = np.ascontiguousarray(np.concatenate([inp["g_q"], inp["g_k"]], 1))
    sh["wsT"] = np.ascontiguousarray(np.stack([inp["w_s"][l].transpose(2, 0, 1).reshape(128, 512) for l in range(L)]))
    sh["bsd"] = np.ascontiguousarray(np.concatenate([inp["b_s"][l].T for l in range(L)], 1))
    sh["identd"] = np.eye(128, dtype=f)
    return sh


def rope_table(tok):
    f = np.float32
    row = (tok // 64).astype(f)
    col = (tok % 64).astype(f)
    inv = (1.0 / (np.float32(10000.0) ** (np.arange(16, dtype=f) / np.float32(16)))).astype(f)
    ar = row[:, None] * inv[None, :]
    ac = col[:, None] * inv[None, :]
    return np.concatenate([np.cos(ar), np.cos(ac), np.sin(ar), np.sin(ac)], 1).astype(f)


def nab_tables(rpb, qi, L):
    out = np.full((L, 5, 128, 6, 6, 128), np.float32(-30000.0), np.float32)
    p = np.arange(128)
    for ci, i in enumerate((0, 1, 2, 14, 15)):
        G = 16 * qi + i
        r = 2 * G + p // 64
        c = p % 64
        rs = np.clip(r - 4, 0, 120)
        cs = np.clip(c - 8, 0, 48)
        for b in range(6):
            if b < 5:
                Gk = G - 2 + b
            elif i == 0:
                Gk = G + 3
            elif i == 15:
                Gk = G - 3
            else:
                continue
            if Gk < 0 or Gk > 63:
                continue
            kr = 2 * Gk + p // 64
            kc = p % 64
            ok = (kr[:, None] >= rs[None, :]) & (kr[:, None] < rs[None, :] + 8) & (kc[:, None] >= cs[None, :]) & (kc[:, None] < cs[None, :] + 16)
            dr = np.clip(kr[:, None] - r[None, :] + 7, 0, 14)
            dc = np.clip(kc[:, None] - c[None, :] + 15, 0, 30)
            for l in range(L):
                vals = rpb[l][:, dr, dc]
                out[l, ci, :, :, b, :] = np.where(ok[None], vals, np.float32(-30000.0)).transpose(1, 0, 2)
    return out.reshape(L * 5, 128, 4608)


_CACHE = {}


def kernel(**inp):
    inp = {k: np.asarray(v, dtype=np.float32) for k, v in inp.items()}
    if "nc" not in _CACHE:
        nc, P = build(NLAYERS_BUILD)
        _CACHE["nc"] = finish(nc, P)
    nc = _CACHE["nc"]
    import time as _t
    t0 = _t.time()
    NL = NLAYERS_BUILD
    sh = prep_shared(inp, NL)
    print("prep shared", _t.time() - t0, flush=True)
    in_maps = []
    for core in range(8):
        b, qi = core // 4, core % 4
        m = dict(sh)
        xs = inp["x"][b, 2048 * qi:2048 * (qi + 1)].reshape(16, 128, 1024)
        m["x_in"] = np.ascontiguousarray(np.concatenate([xs, inp["ctx"][b].reshape(2, 128, 1024)], 0))
        cv = np.empty((128, 16), np.float32)
        cv[:, 0::2] = inp["c"][b].reshape(8, 128).T
        cv[:, 1::2] = inp["c_ctx"].reshape(8, 128).T
        m["cvec"] = cv
        rp = np.empty((NT, 128, 64), np.float32)
        for i in range(16):
            rp[i] = rope_table(2048 * qi + 128 * i + np.arange(128))
        rp[16:, :, 0:32] = 1.0
        rp[16:, :, 32:64] = 0.0
        m["rope"] = rp
        m["nab"] = nab_tables(inp["rpb"], qi, NL)
        in_maps.append(m)
    print("prep all", _t.time() - t0, flush=True)
    res = run_bass_kernel_spmd(nc, in_maps, core_ids=list(range(8)))
    print("run done", _t.time() - t0, flush=True)
    out = np.empty((2, 8192, 1024), np.float32)
    for core in range(8):
        b, qi = core // 4, core % 4
        out[b, 2048 * qi:2048 * (qi + 1)] = np.asarray(res.results[core]["y_out"]).reshape(2048, 1024)
    return out
```

```python
import numpy as np
import concourse.bass as bass
import concourse.mybir as mybir
from concourse.bass_utils import run_bass_kernel_spmd

F32 = mybir.dt.float32
BF16 = mybir.dt.bfloat16
AF = mybir.ActivationFunctionType
ALU = mybir.AluOpType
AX = mybir.AxisListType

L = 4
NT = 18
ALPHA = float(8.0 ** 0.25)
EPS = 1e-6
NLAYERS_BUILD = L


class T:
    __slots__ = ("name", "w", "r")

    def __init__(self, name):
        self.name = name
        self.w = {}
        self.r = {}


ENGS = ["pe", "dve", "act", "pool", "sp"]
PIDV = [None]


class Plan:
    def __init__(self):
        self.ops = {e: [] for e in ENGS}
        self.known = {e: {} for e in ENGS}
        self.dma_cnt = {}
        self.waited = {e: set() for e in ENGS}

    def _res(self, hs):
        out = []
        for t in hs:
            r = t.resolve() if hasattr(t, "resolve") else t
            if isinstance(r, (list, tuple)):
                out.extend(r)
            else:
                out.append(r)
        return out

    def record_begin(self):
        self._rec = []

    def record_end(self):
        r, self._rec = self._rec, None
        return r

    def replay_interleaved(self, lists):
        its = [list(l) for l in lists]
        pos = [0] * len(its)
        while any(pos[j] < len(its[j]) for j in range(len(its))):
            for j in range(len(its)):
                if pos[j] < len(its[j]):
                    self.op(*its[j][pos[j]])
                    pos[j] += 1

    def op(self, eng, fn, reads=(), writes=(), pwrites=(), dma=None):
        reads, writes, pwrites = self._res(reads), self._res(writes), self._res(pwrites)
        if getattr(self, "_rec", None) is not None:
            self._rec.append((eng, fn, reads, writes, pwrites, dma))
            return
        idx = len(self.ops[eng])
        if dma is None:
            tok = (eng, idx + 1)
        else:
            self.dma_cnt[dma] = self.dma_cnt.get(dma, 0) + 1
            tok = ("dma:" + dma, self.dma_cnt[dma])
        need = {}

        def addw(d):
            for k, v in d.items():
                if need.get(k, 0) < v:
                    need[k] = v

        for t in reads:
            addw(t.w)
        for t in writes:
            addw(t.w)
            addw(t.r)
        for t in pwrites:
            addw(t.r)
        waits = []
        kn = self.known[eng]
        for k, v in need.items():
            if kn.get(k, 0) < v:
                kn[k] = v
                waits.append((k, v))
                if not k.startswith("dma:"):
                    self.waited[k].add(v)
        if fn is not None:
            for t in reads:
                if t.r.get(tok[0], 0) < tok[1]:
                    t.r[tok[0]] = tok[1]
            for t in writes:
                t.w = {tok[0]: tok[1]}
                t.r = {}
            for t in pwrites:
                if t.r:
                    t.w = {tok[0]: tok[1]}
                    t.r = {}
                else:
                    t.w[tok[0]] = max(t.w.get(tok[0], 0), tok[1])
        self.ops[eng].append((waits, fn, dma))

    def barrier(self):
        latest = {}
        for e in ENGS:
            n = len(self.ops[e])
            if n:
                latest[e] = n
        for k, v in self.dma_cnt.items():
            latest["dma:" + k] = v
        for e in ENGS:
            waits = []
            kn = self.known[e]
            for k, v in latest.items():
                if k == e:
                    continue
                if not k.startswith("dma:"):
                    vv = v
                    while vv > 0 and (self.ops[k][vv - 1][1] is None or self.ops[k][vv - 1][2] is not None):
                        vv -= 1
                    if vv == 0:
                        continue
                    v = vv
                if kn.get(k, 0) < v:
                    kn[k] = v
                    waits.append((k, v))
                    if not k.startswith("dma:"):
                        self.waited[k].add(v)
            self.ops[e].append((waits, None, None))

    def emit(self, nc, block, sems):
        rank = {}
        for e in ENGS:
            s = sorted(self.waited[e])
            rank[e] = {v: i + 1 for i, v in enumerate(s)}
        plan = self

        def run(eng_name):
            def body(e):
                if eng_name == "pool":
                    PIDV[0] = e.partition_id()
                for i, (waits, fn, dma) in enumerate(plan.ops[eng_name]):
                    for k, v in waits:
                        if k.startswith("dma:"):
                            if k.startswith("dma:cc"):
                                e.wait_ge(sems[k], 1)
                            else:
                                e.wait_ge(sems[k], 16 * v)
                        else:
                            e.wait_ge(sems[k], rank[k][v])
                    if fn is None:
                        continue
                    ins = fn(e)
                    if dma is not None:
                        if dma.startswith("cc"):
                            ins.then_inc(sems["dma:" + dma])
                        else:
                            ins.then_inc(sems["dma:" + dma], 16)
                    elif (i + 1) in rank[eng_name]:
                        ins.then_inc(sems[eng_name], 1)
            return body

        block.tensor(run("pe"))
        block.vector(run("dve"))
        block.scalar(run("act"))
        block.gpsimd(run("pool"))
        block.sync(run("sp"))


def MM(out, l, r, st, sp):
    return lambda e: e.matmul(out, lhsT=l, rhs=r, start=st, stop=sp)


def TR(out, in_, ident):
    return lambda e: e.transpose(out=out, in_=in_, identity=ident)


def ACT(out, in_, func, scale=None, bias=None):
    kw = {}
    if scale is not None:
        kw["scale"] = scale
    if bias is not None:
        kw["bias"] = bias
    return lambda e: e.activation(out=out, in_=in_, func=func, **kw)


def TS(out, in0, s1, s2, op0, op1=None):
    if op1 is None:
        return lambda e: e.tensor_scalar(out=out, in0=in0, scalar1=s1, scalar2=None, op0=op0)
    return lambda e: e.tensor_scalar(out=out, in0=in0, scalar1=s1, scalar2=s2, op0=op0, op1=op1)


def TT(out, in0, in1, op):
    return lambda e: e.tensor_tensor(out=out, in0=in0, in1=in1, op=op)


def STT(out, in0, scalar, in1, op0, op1):
    return lambda e: e.scalar_tensor_tensor(out=out, in0=in0, scalar=scalar, in1=in1, op0=op0, op1=op1)


def CP(out, in_):
    return lambda e: e.tensor_copy(out=out, in_=in_)


def DMA(out, in_):
    return lambda e: e.dma_start(out=out, in_=in_)


def MEMSET(ap, v):
    return lambda e: e.memset(ap, v)


STOP = [None]


def build(nlayers=L, debug_x=False):
    nc = bass.Bass("TRN2", target_bir_lowering=False)
    P = Plan()

    def din(name, shape):
        return nc.dram_tensor(name, shape, F32, kind="ExternalInput").ap()

    x_in = din("x_in", [NT, 128, 1024])
    cvec = din("cvec", [128, 16])
    wmod = din("wmod", [nlayers * 12, 128, 4096])
    bmodT = din("bmodT", [L, 128, 32])
    bmodg = din("bmodg", [L * 2, 1024])
    win = din("win", [nlayers * 5, 128, 4096])
    wo = din("wo", [nlayers * 2, 128, 4096])
    wffi = din("wffi", [nlayers * 11, 128, 4096])
    wffo = din("wffo", [nlayers * 6, 128, 4096])
    lnp = din("lnp", [L * 4, 1024])
    gout = din("gout", [128, L * 8])
    gsgu = din("gsgu", [L, 256])
    gqk = din("gqk", [L, 128])
    wsT = din("wsT", [L, 128, 512])
    bsd = din("bsd", [128, L * 4])
    rope = din("rope", [NT, 128, 64])
    nab = din("nab", [nlayers * 5, 128, 4608])
    identd = din("identd", [128, 128])
    y_out = nc.dram_tensor("y_out", [16, 128, 1024], F32, kind="ExternalOutput").ap()

    gst = nc.dram_tensor("gst", [L * 4, 128, 1024], F32).ap()
    hts = nc.dram_tensor("hts", [NT, 128, 1024], BF16).ap()
    nak_loc = nc.dram_tensor("nak_loc", [16, 128, 384], BF16).ap()
    nav_loc = nc.dram_tensor("nav_loc", [16, 128, 390], BF16).ap()
    ktc_loc = nc.dram_tensor("ktc_loc", [128, 2048], BF16)
    ktc_all = nc.dram_tensor("ktc_all", [512, 2048], BF16)
    vc_loc = nc.dram_tensor("vc_loc", [2048, 130], BF16)
    vc_all = nc.dram_tensor("vc_all", [8192, 130], BF16)
    nak_edge = nc.dram_tensor("nak_edge", [512, 384], BF16)
    nak_eall = nc.dram_tensor("nak_eall", [2048, 384], BF16)
    nav_edge = nc.dram_tensor("nav_edge", [512, 390], BF16)
    nav_eall = nc.dram_tensor("nav_eall", [2048, 390], BF16)

    off = [16512]
    OFF = {}
    LIMIT = 229344

    def sb(name, shape, dt, at=None):
        nbytes = int(np.prod(shape[1:])) * (4 if dt == F32 else 2)
        nbytes = (nbytes + 63) // 64 * 64
        if at is None:
            o = off[0]
            off[0] += nbytes
            assert off[0] <= LIMIT, (name, off[0])
        else:
            o = at
        t = nc.alloc_sbuf_tensor_at(name, shape, dt, offset=o)
        OFF[name] = o
        return t

    X = sb("X", [128, NT, 1024], F32)
    KTC = sb("KTC", [128, 8448], BF16)
    VC = sb("VC", [128, 66, 130], BF16)
    NAK = sb("NAK", [128, 3, 1280], BF16)
    NAV = sb("NAV", [128, 10, 390], BF16)
    NAB = sb("NAB", [128, 4608], BF16)
    BC = sb("BC", [128, 3, 1024], F32)
    WB = [sb(f"WB{i}", [128, 8, 512], BF16) for i in range(2)]
    HT4 = sb("HT4", [128, 4, 8, 128], BF16)
    MERGB = sb("MERGB", [128, 4, 1024], BF16)
    att0 = off[0]
    QTA = sb("QTA", [128, 3, 512], BF16)
    QTC = sb("QTC", [128, 3, 512], BF16)
    PT = [sb(f"PT{i}", [128, 512], BF16) for i in range(2)]
    PTN = sb("PTN", [128, 1024], BF16)
    OT = sb("OT", [128, 768], F32)
    att1 = off[0]
    assert att0 + 11264 <= att1
    GTOK4 = sb("GTOK4", [128, 4, 2816], BF16, at=OFF["KTC"])
    GTTa = sb("GTTa", [128, 2, 22, 128], BF16, at=OFF["KTC"] + 22528)
    GTTb = sb("GTTb", [128, 2, 22, 128], BF16, at=att0)
    assert 22528 + 11264 <= 16896 + 17160
    Y4 = sb("Y4", [128, 4, 1024], F32, at=OFF["NAK"])
    HT4b = sb("HT4b", [128, 4, 8, 128], BF16, at=OFF["MERGB"])
    W2P1 = sb("W2P1", [128, 1024], BF16, at=OFF["NAK"] + 16384)
    MVP_ = [sb(f"MVP{j}", [128, 8], F32, at=OFF["NAK"] + 16384 + 2048 + 64 * j) for j in range(2)]
    ST6P_ = [sb(f"ST6P{j}", [128, 12], F32, at=OFF["NAK"] + 16384 + 2048 + 128 + 64 * j) for j in range(2)]
    assert OFF["NAK"] + 16384 + 2048 + 256 <= OFF["NAB"] + 9216
    assert OFF["NAB"] + 9216 - OFF["NAK"] >= 16384
    WBX = [sb(f"WBX{i}", [128, 8, 512], BF16, at=OFF["KTC"] + 8192 * i) for i in range(4)]
    RLAT = sb("RLAT", [128, 8, 128], BF16, at=att0)
    RCTX = sb("RCTX", [128, 8, 128], BF16, at=att0 + 2048)
    W4A = sb("W4A", [128, 1024], F32)
    ZG = sb("ZG", [128, 512], F32)
    OATT = sb("OATT", [128, 4, 384], F32, at=OFF["W4A"])
    ZG_1 = sb("ZG1", [128, 512], F32, at=OFF["W4A"])
    W2A_0 = sb("W2A", [128, 1024], BF16)
    TMPA_0 = sb("TMPA", [128, 512], F32)
    TMPB_0 = sb("TMPB", [128, 512], F32)
    KQB_0 = sb("KQB", [128, 512], BF16)
    TMPA_1 = sb("TMPA1", [128, 512], F32, at=OFF["PTN"])
    TMPB_1 = sb("TMPB1", [128, 512], F32, at=OFF["OT"])
    KQB_1 = sb("KQB1", [128, 512], BF16, at=OFF["PT0"])
    W2A_1 = sb("W2A1", [128, 1024], BF16, at=OFF["QTA"])
    KAT_1 = sb("KAT1", [128, 4, 128], BF16, at=OFF["QTA"] + 2048)
    VAT_1 = sb("VAT1", [128, 6, 65], BF16, at=OFF["QTC"])
    VCT_1 = sb("VCT1", [128, 2, 65], BF16, at=OFF["QTC"] + 896)
    ROPE_1 = sb("ROPE1", [128, 64], F32, at=OFF["PT1"])
    SS_1 = sb("SS1", [128, 16], F32, at=OFF["PT1"] + 256)
    MV_1 = sb("MV1", [128, 8], F32, at=OFF["PT1"] + 320)
    ST6_1 = sb("ST61", [128, 12], F32, at=OFF["PT1"] + 384)
    VAT_0 = sb("VAT", [128, 6, 65], BF16)
    VCT_0 = sb("VCT", [128, 2, 65], BF16)
    KAT_0 = sb("KAT", [128, 4, 128], BF16)
    IDB = sb("IDB", [128, 128], BF16)
    IDF = sb("IDF", [128, 128], F32)
    ROPE_0 = sb("ROPE", [128, 64], F32)
    S2F = sb("S2F", [128, 16], F32)
    S2B = sb("S2B", [128, 8, 2], BF16)
    ONESB = sb("ONESB", [128, 128], BF16)
    MODF = sb("MODF", [128, L, 32, 2], F32)
    BMT = sb("BMT", [128, 32], F32)
    GOUT = sb("GOUT", [128, L * 8], F32)
    GSGU = sb("GSGU", [128, 256], F32)
    GQK = sb("GQK", [128, 128], F32)
    WST = sb("WST", [128, 512], BF16)
    BS = sb("BS", [128, L * 4], F32)
    ST6_0 = sb("ST6", [128, 12], F32)
    MV_0 = sb("MV", [128, 8], F32)
    SS_0 = sb("SS", [128, 16], F32)
    print("SBUF used", off[0], "of", LIMIT)

    PSF = [nc.alloc_psum_tensor(f"PSF{i}", [128, 512], F32) for i in range(6)]
    PST = [nc.alloc_psum_tensor(f"PST{i}", [128, 1024], BF16) for i in range(2)]
    PSFT = [T(f"psf{i}") for i in range(8)]
    pst_rr = [0]

    XT = [T(f"x{i}") for i in range(NT)]
    tKTC, tVC, tNAK, tNAV, tNAB = T("ktc"), T("vc"), T("nak"), T("nav"), T("nab")
    tBC = [T("bc0"), T("bc1"), T("bc2")]
    tWB = [T("wb0"), T("wb1")]
    tHT = [T(f"ht{i}") for i in range(4)]
    tMERG = [T(f"mg{i}") for i in range(4)]
    tQTA, tQTC = T("qta"), T("qtc")
    tPT = [T("pt0"), T("pt1")]
    tPTN, tOT = T("ptn"), T("ot")
    tPTNb = T("ptnb")
    tGTOK = T("gtok")
    tY4 = [T(f"y4{i}") for i in range(4)]
    tG4 = [T(f"g4{i}") for i in range(4)]
    tGTT4 = [T(f"gtt{i}") for i in range(4)]
    tW4A, tZG = T("w4a"), T("zg")
    CUR = [0]

    class BufP:
        def __init__(self, bufs):
            self.bufs = bufs

        def __getitem__(self, k):
            return self.bufs[CUR[0]][k]

    class HP:
        def __init__(self, hs):
            self.hs = hs

        def resolve(self):
            return self.hs[CUR[0]]
    TMPA, TMPB, KQB = BufP([TMPA_0, TMPA_1]), BufP([TMPB_0, TMPB_1]), BufP([KQB_0, KQB_1])
    W2A, KAT, VAT, VCT = BufP([W2A_0, W2A_1]), BufP([KAT_0, KAT_1]), BufP([VAT_0, VAT_1]), BufP([VCT_0, VCT_1])
    ILV = [False]
    ZGp = BufP([ZG, ZG_1])
    tZGp = HP([tZG, tW4A])
    tVAT_0, tVCT_0, tKAT_0, tW2A_0 = T("vat"), T("vct"), T("kat"), T("w2a")
    W2P = [W2A_0, W2P1]
    MVP, ST6P = MVP_, ST6P_
    tHTb = [T(f"htb{i}") for i in range(4)]
    tW2P = [tW2A_0, T("w2p1")]
    tMVP = [T("mvp0"), T("mvp1")]
    tST6P = [T("st6p0"), T("st6p1")]
    tCONST, tMODF, tLAYC = T("const"), T("modf"), T("layc")
    tW2A = HP([tW2A_0, tQTA])
    tKAT = HP([tKAT_0, tQTA])
    tVAT = HP([tVAT_0, tQTC])
    tVCT = HP([tVCT_0, tQTC])
    tTMPA = HP([T("tmpa"), [tPTN, tPTNb]])
    tTMPB = HP([T("tmpb"), tOT])
    tKQB = HP([T("kqb"), tPT[0]])
    tROPE = HP([T("rope"), tPT[1]])
    tST6 = HP([T("st6"), tPT[1]])
    tMV = HP([T("mv"), tPT[1]])
    tSS = HP([T("ss"), tPT[1]])
    ROPE, SS, MV, ST6 = BufP([ROPE_0, ROPE_1]), BufP([SS_0, SS_1]), BufP([MV_0, MV_1]), BufP([ST6_0, ST6_1])
    tGST, tHTS = T("gst"), [T(f"hts{i}") for i in range(NT)]
    tNAKL, tNAVL, tKTCL, tVCL, tNAKE, tNAVE = T("nakl"), T("navl"), T("ktcl"), T("vcl"), T("nake"), T("nave")
    tKTCA, tVCA, tNAKEA, tNAVEA = T("ktca"), T("vca"), T("nakea"), T("navea")
    tY = T("y")

    wb_rr = [0]
    WBALL = WB + WBX
    tWBALL = tWB + [T(f"wbx{i}") for i in range(4)]
    wb_pool = [6]

    def load_wblock(src_ap):
        i = wb_rr[0] % wb_pool[0]
        wb_rr[0] += 1
        P.op("pool", DMA(WBALL[i][:].rearrange("p a b -> p (a b)"), src_ap), writes=[tWBALL[i]], dma=f"wb{i}")
        return WBALL[i], tWBALL[i]

    ps_rr = [0]

    ps_set_rr = [0, 0]

    def next_ps(lo=0, hi=3):
        if ILV[0]:
            c = CUR[0]
            i = 2 * c + ps_set_rr[c] % 2
            ps_set_rr[c] += 1
            return PSF[i], PSFT[i]
        i = lo + ps_rr[0] % (hi - lo)
        ps_rr[0] += 1
        return PSF[i], PSFT[i]

    def ln_stats(src, srcT, n):
        nch = max(1, n // 512)
        w = n // nch
        for ci in range(nch):
            P.op("dve", (lambda o, i: (lambda e: e.bn_stats(out=o, in_=i)))(ST6[:, ci * 6:(ci + 1) * 6], src[:, ci * w:(ci + 1) * w]),
                 reads=srcT, writes=[tST6] if ci == 0 else (), pwrites=() if ci == 0 else [tST6])
        P.op("dve", (lambda o, i: (lambda e: e.bn_aggr(out=o, in_=i)))(MV[:, 0:2], ST6[:, 0:6 * nch].rearrange('p (n s) -> p n s', s=6)), reads=[tST6], writes=[tMV])
        P.op("act", ACT(MV[:, 2:3], MV[:, 1:2], AF.Ln, bias=EPS), reads=[tMV], writes=[tMV])
        P.op("act", ACT(MV[:, 3:4], MV[:, 2:3], AF.Exp, scale=-0.5), reads=[tMV], writes=[tMV])

    def rms_rstd(src, srcT, G, W, tmp, tmpT):
        P.op("dve", TT(tmp[:, 0:G * W], src, src, ALU.mult), reads=srcT, writes=[tmpT])
        P.op("dve", (lambda o, i: (lambda e: e.reduce_sum(out=o, in_=i, axis=AX.X)))(SS[:, 0:G], tmp[:, 0:G * W].rearrange("p (g w) -> p g w", g=G)),
             reads=[tmpT], writes=[tSS])
        P.op("act", ACT(SS[:, 0:G], SS[:, 0:G], AF.Ln, scale=1.0 / W, bias=EPS), reads=[tSS], writes=[tSS])
        P.op("act", ACT(SS[:, 8:8 + G], SS[:, 0:G], AF.Exp, scale=-0.5), reads=[tSS], writes=[tSS])


    def qk_norm_rope(src, srcT, H, gain, dst, oscale, dstT, full):
        n = H * 64
        P.op("act", ACT(TMPA[:, 0:n], src, AF.Identity), reads=srcT, writes=[tTMPA])
        rms_rstd(TMPA[:, 0:n], [tTMPA], H, 64, TMPB, tTMPB)
        v3 = lambda ap: ap.rearrange("p (h d) -> p h d", h=H)
        P.op("dve", TT(v3(TMPA[:, 0:n]), v3(TMPA[:, 0:n]), SS[:, 8:8 + H].unsqueeze(2).to_broadcast([128, H, 64]), ALU.mult), reads=[tTMPA, tSS], writes=[tTMPA])
        P.op("dve", STT(v3(TMPA[:, 0:n]), v3(TMPA[:, 0:n]), oscale, gain.unsqueeze(1).to_broadcast([128, H, 64]), ALU.mult, ALU.mult),
             reads=[tTMPA, tLAYC], writes=[tTMPA])
        v5 = lambda ap: ap.rearrange("p (h a s f) -> p h a s f", h=H, a=2, s=2)
        x1 = v5(TMPA[:, 0:n])[:, :, :, 0, :]
        x2 = v5(TMPA[:, 0:n])[:, :, :, 1, :]
        d1 = v5(dst)[:, :, :, 0, :]
        d2 = v5(dst)[:, :, :, 1, :]
        C = ROPE[:, 0:32].rearrange("p (a f) -> p a f", a=2).unsqueeze(1).to_broadcast([128, H, 2, 16])
        S_ = ROPE[:, 32:64].rearrange("p (a f) -> p a f", a=2).unsqueeze(1).to_broadcast([128, H, 2, 16])
        v4 = lambda ap: ap.rearrange("p (h a f) -> p h a f", h=H, a=2)
        t1 = v4(TMPB[:, 0:H * 32])
        t2 = v4(TMPB[:, H * 32:H * 64])
        P.op("dve", TT(t1, x1, C, ALU.mult), reads=[tTMPA, tROPE], writes=[tTMPB])
        P.op("dve", TT(t2, x2, S_, ALU.mult), reads=[tTMPA, tROPE], pwrites=[tTMPB])
        P.op("dve", TT(d1, t1, t2, ALU.subtract), reads=[tTMPB], writes=dstT if full else (), pwrites=() if full else dstT)
        P.op("dve", TT(t1, x2, C, ALU.mult), reads=[tTMPA, tROPE], writes=[tTMPB])
        P.op("dve", TT(t2, x1, S_, ALU.mult), reads=[tTMPA, tROPE], pwrites=[tTMPB])
        P.op("dve", TT(d2, t1, t2, ALU.add), reads=[tTMPB], pwrites=dstT)

    def transpose_tile(src_bf, srcT, nch, dst_fn, dstT, evac):
        if ILV[0]:
            j = CUR[0]
        else:
            j = pst_rr[0] % 2
            pst_rr[0] += 1
        psb, pT = PST[j], PSFT[6 + j]
        for kc in range(nch):
            P.op("pe", TR(psb[:, kc * 128:(kc + 1) * 128], src_bf[:, kc * 128:(kc + 1) * 128], IDB[:, :]),
                 reads=srcT + [tCONST], writes=[pT] if kc == 0 else (), pwrites=() if kc == 0 else [pT])
        for kc in range(nch):
            evac(kc, psb[:, kc * 128:(kc + 1) * 128], pT)

    P.op("sp", DMA(IDF[:, :], identd[:, :]), writes=[tCONST], dma="c0")
    P.op("sp", DMA(S2F[:, :], cvec[:, :]), pwrites=[tCONST], dma="c0")
    P.op("sp", DMA(GOUT[:, :], gout[:, :]), pwrites=[tCONST], dma="c0")
    P.op("sp", DMA(BS[:, :], bsd[:, :]), pwrites=[tCONST], dma="c0")
    for i in range(NT):
        P.op("sp", DMA(X[:, i, :], x_in[i]), writes=[XT[i]], dma=f"xin{i}")
    P.op("dve", CP(IDB[:, :], IDF[:, :]), reads=[tCONST], pwrites=[tCONST])
    P.op("dve", MEMSET(ONESB[:, :], 1.0), pwrites=[tCONST])
    P.op("act", ACT(S2F[:, :], S2F[:, :], AF.Silu), reads=[tCONST], writes=[tCONST])
    P.op("dve", CP(S2B[:].rearrange("p a b -> p (a b)"), S2F[:, :]), reads=[tCONST], writes=[tCONST])
    for kc in range(8):
        P.op("dve", TS(RLAT[:, kc, :], ONESB[:, :], S2F[:, 2 * kc:2 * kc + 1], None, ALU.mult), reads=[tCONST], pwrites=[tGTOK])
        P.op("dve", TS(RCTX[:, kc, :], ONESB[:, :], S2F[:, 2 * kc + 1:2 * kc + 2], None, ALU.mult), reads=[tCONST], pwrites=[tGTOK])
    P.op("dve", MEMSET(VAT[:, :, :], 1.0), writes=[tVAT])
    P.op("dve", MEMSET(VCT[:, :, :], 1.0), writes=[tVCT])

    for l in range(nlayers):
        P.op("sp", DMA(BMT[:, :], bmodT[l]), writes=[tLAYC], dma="c1")
        psm, psmT = PSF[5], PSFT[5]
        for jj, nbs in enumerate([(0, 1), (2, 3), (6, 7), (8, 9)]):
            for half, nb in enumerate(nbs):
                wbuf, wT = load_wblock(wmod[l * 12 + nb])
                for oc4 in range(4):
                    col = (jj * 8 + half * 4 + oc4) * 2
                    for kc in range(8):
                        first = (jj == 0 and half == 0 and oc4 == 0 and kc == 0)
                        P.op("pe", MM(psm[:, col:col + 2], wbuf[:, kc, oc4 * 128:(oc4 + 1) * 128], S2B[:, kc, :], kc == 0, kc == 7),
                             reads=[wT, tCONST], writes=[psmT] if first else (), pwrites=() if first else [psmT])
        pv = psm[:, 0:64].rearrange("p (a b) -> p a b", b=2)
        for s in range(2):
            P.op("dve", TT(MODF[:, l, :, s], pv[:, :, s], BMT[:, :], ALU.add), reads=[psmT, tLAYC], pwrites=[tMODF])
        for a0 in (8, 24):
            P.op("dve", TS(MODF[:, l, a0:a0 + 8, :], MODF[:, l, a0:a0 + 8, :], 1.0, None, ALU.add), reads=[tMODF], writes=[tMODF])
        for gi, nbs in enumerate([(4, 5), (10, 11)]):
            for half, nb in enumerate(nbs):
                wbuf, wT = load_wblock(wmod[l * 12 + nb])
                P.op("sp", DMA(TMPA[:, :], bmodg[l * 2 + gi, half * 512:(half + 1) * 512].partition_broadcast(128)),
                     writes=[tTMPA], dma="tmpa")
                for s, R in enumerate((RLAT, RCTX)):
                    ps, pT = next_ps(0, 3)
                    for kc in range(8):
                        P.op("pe", MM(ps[:, :], R[:, kc, :], wbuf[:, kc, :], kc == 0, kc == 7), reads=[wT, tGTOK],
                             writes=[pT] if kc == 0 else (), pwrites=() if kc == 0 else [pT])
                    P.op("dve", TT(TMPB[:, :], ps[:, :], TMPA[:, :], ALU.add), reads=[pT, tTMPA], writes=[tTMPB])
                    P.op("sp", DMA(gst[l * 4 + gi * 2 + s][:, half * 512:(half + 1) * 512], TMPB[:, :]), reads=[tTMPB], pwrites=[tGST], dma="gst")
    P.barrier()
    wb_pool[0] = 2
    wb_rr[0] = 0
    if STOP[0] == 'p0':
        return nc, P

    def load_bc(l, sub, ctx):
        P.op("sp", DMA(BC[:, 0, :], gst[l * 4 + sub * 2 + (1 if ctx else 0)]), reads=[tGST], writes=[tBC[0]], dma="bc0")

    def load_ln(l, sub):
        for j in range(2):
            r = l * 4 + sub * 2 + j
            P.op("sp", DMA(BC[:, 1 + j, :], lnp[r, :].partition_broadcast(128)), writes=[tBC[1 + j]], dma=f"bc{1 + j}")

    def ln_mod_transpose(i, l, which, slot):
        s = 1 if i >= 16 else 0
        ln_stats(X[:, i, :], [XT[i]], 1024)
        P.op("dve", TS(W2A[:, :], X[:, i, :], MV[:, 0:1], MV[:, 3:4], ALU.subtract, ALU.mult), reads=[XT[i], tMV], writes=[tW2A])

        def evac(kc, pap, pT):
            sc = MODF[:, l, which * 16 + 8 + kc, s:s + 1]
            sh = MODF[:, l, which * 16 + kc, s:s + 1]
            P.op("dve", TS(HT4[:, slot, kc, :], pap, sc, sh, ALU.mult, ALU.add), reads=[pT, tMODF], pwrites=[tHT[slot]])
        transpose_tile(W2A, [tW2A], 8, None, None, evac)

    def deepnorm_from(i, zsrc, zT):
        P.op("dve", STT(W4A[:, :], X[:, i, :], ALPHA, zsrc, ALU.mult, ALU.add), reads=[XT[i]] + zT, writes=[tW4A])
        ln_stats(W4A[:, :], [tW4A], 1024)
        P.op("dve", STT(W4A[:, :], W4A[:, :], MV[:, 0:1], BC[:, 1, :], ALU.subtract, ALU.mult), reads=[tW4A, tMV, tBC[1]], writes=[tW4A])
        P.op("dve", STT(X[:, i, :], W4A[:, :], MV[:, 3:4], BC[:, 2, :], ALU.mult, ALU.add), reads=[tW4A, tMV, tBC[2]], writes=[XT[i]])

    def deepnorm(i, l, ysrc_fn, yT):
        for nb in range(2):
            P.op("dve", TT(W4A[:, nb * 512:(nb + 1) * 512], ysrc_fn(nb), BC[:, 0, nb * 512:(nb + 1) * 512], ALU.mult),
                 reads=[yT[nb], tBC[0]], writes=[tW4A] if nb == 0 else (), pwrites=() if nb == 0 else [tW4A])
        deepnorm_from(i, W4A[:, :], [tW4A])

    for l in range(nlayers):
        P.op("sp", DMA(GSGU[:, :], gsgu[l, :].partition_broadcast(128)), writes=[tLAYC], dma="c1")
        P.op("sp", DMA(GQK[:, :], gqk[l, :].partition_broadcast(128)), pwrites=[tLAYC], dma="c1")
        P.op("pool", DMA(WST[:, :], wsT[l]), pwrites=[tLAYC], dma="c2")

        CUR[0] = 1
        P.op("dve", MEMSET(VAT[:, :, :], 1.0), writes=[tVAT])
        P.op("dve", MEMSET(VCT[:, :, :], 1.0), pwrites=[tVCT])
        CUR[0] = 0
        P.op("dve", MEMSET(VC[:, 64:66, :], 1.0), writes=[tVC])
        P.op("dve", MEMSET(NAV[:, 8:10, :], 1.0), writes=[tNAV])
        wv, wvT = load_wblock(win[l * 5 + 0])
        wk, wkT = load_wblock(win[l * 5 + 1])
        def p1_tile(i):
            lat = i < 16
            sl = i % 2
            P.op("sp", DMA(ROPE[:, :], rope[i]), writes=[tROPE], dma=f"rope{sl}")
            ln_mod_transpose(i, l, 0, sl)
            P.op("sp", DMA(hts[i], HT4[:, sl].rearrange("p a b -> p (a b)")), reads=[tHT[sl]], writes=[tHTS[i]], dma=f"hts{sl}")
            ps, pT = next_ps(0, 3)
            for kc in range(8):
                P.op("pe", MM(ps[:, :], HT4[:, sl, kc, :], wv[:, kc, :], kc == 0, kc == 7), reads=[tHT[sl], wvT],
                     writes=[pT] if kc == 0 else (), pwrites=() if kc == 0 else [pT])
            if lat:
                P.op("dve", CP(VAT[:, :, 0:64], ps[:, 0:384].rearrange("p (h d) -> p h d", h=6)), reads=[pT], pwrites=[tVAT])
                P.op("dve", CP(VCT[:, :, 0:64], ps[:, 384:512].rearrange("p (h d) -> p h d", h=2)), reads=[pT], pwrites=[tVCT])
                P.op("sp", DMA(nav_loc[i], VAT[:].rearrange("p a b -> p (a b)")), reads=[tVAT], pwrites=[tNAVL], dma=f"navl{sl}")
                P.op("sp", DMA(vc_loc.ap()[i * 128:(i + 1) * 128, :], VCT[:].rearrange("p a b -> p (a b)")), reads=[tVCT], pwrites=[tVCL], dma=f"vcl{sl}")
                if i in (0, 1, 14, 15):
                    e = i if i < 2 else i - 12
                    P.op("sp", DMA(nav_edge.ap()[e * 128:(e + 1) * 128, :], VAT[:].rearrange("p a b -> p (a b)")), reads=[tVAT], pwrites=[tNAVE], dma=f"nave{sl}")
            else:
                j = i - 16
                P.op("dve", CP(NAV[:, 8 + j, :].rearrange("p (h d) -> p h d", h=6)[:, :, 0:64], ps[:, 0:384].rearrange("p (h d) -> p h d", h=6)), reads=[pT], pwrites=[tNAV])
                P.op("dve", CP(VC[:, 64 + j, :].rearrange("p (h d) -> p h d", h=2)[:, :, 0:64], ps[:, 384:512].rearrange("p (h d) -> p h d", h=2)), reads=[pT], pwrites=[tVC])
            ps, pT = next_ps(0, 3)
            for kc in range(8):
                P.op("pe", MM(ps[:, :], HT4[:, sl, kc, :], wk[:, kc, :], kc == 0, kc == 7), reads=[tHT[sl], wkT],
                     writes=[pT] if kc == 0 else (), pwrites=() if kc == 0 else [pT])
            P.op("act", ACT(KQB[:, 0:384], ps[:, 0:384], AF.Identity), reads=[pT], writes=[tKQB])
            qk_norm_rope(ps[:, 384:512], [pT], 2, GQK[:, 64:128], KQB[:, 384:512], 1.0, [tKQB], False)
            kdst = []

            def evac_k(kc, pap, ppT, i=i, lat=lat):
                if kc < 3:
                    if lat:
                        P.op("dve", CP(KAT[:, kc, :], pap), reads=[ppT], pwrites=[tKAT])
                    else:
                        P.op("dve", CP(NAK[:, kc, (8 + i - 16) * 128:(9 + i - 16) * 128], pap), reads=[ppT], pwrites=[tNAK])
                else:
                    if lat:
                        P.op("dve", CP(KAT[:, 3, :], pap), reads=[ppT], pwrites=[tKAT])
                    else:
                        P.op("dve", CP(KTC[:, 8192 + (i - 16) * 128:8192 + (i - 15) * 128], pap), reads=[ppT], pwrites=[tKTC])
            transpose_tile(KQB, [tKQB], 4, None, None, evac_k)
            if lat:
                P.op("sp", DMA(nak_loc[i], KAT[:, 0:3, :].rearrange("p a b -> p (a b)")), reads=[tKAT], pwrites=[tNAKL], dma=f"nakl{sl}")
                P.op("sp", DMA(ktc_loc.ap()[:, i * 128:(i + 1) * 128], KAT[:, 3, :]), reads=[tKAT], pwrites=[tKTCL], dma=f"ktcl{sl}")
                if i in (0, 1, 14, 15):
                    e = i if i < 2 else i - 12
                    P.op("sp", DMA(nak_edge.ap()[e * 128:(e + 1) * 128, :], KAT[:, 0:3, :].rearrange("p a b -> p (a b)")), reads=[tKAT], pwrites=[tNAKE], dma=f"nake{sl}")

        for ia in range(0, NT, 2):
            lists = []
            for i in (ia, ia + 1):
                CUR[0] = i % 2
                ILV[0] = True
                P.record_begin()
                p1_tile(i)
                lists.append(P.record_end())
            CUR[0] = 0
            ILV[0] = False
            P.replay_interleaved(lists)

        if STOP[0] == 'p1':
            return nc, P
        def coll(src, srcT, dst, dstT, key):
            P.op("pool", (lambda s_, d_: (lambda e: e.collective_compute(
                "AllGather", ALU.bypass, replica_groups=[[0, 1, 2, 3], [4, 5, 6, 7]],
                ins=[s_.ap().opt()], outs=[d_.ap().opt()])))(src, dst), reads=[srcT], writes=[dstT], dma=key)
        coll(ktc_loc, tKTCL, ktc_all, tKTCA, f"cc{l}a")
        coll(vc_loc, tVCL, vc_all, tVCA, f"cc{l}b")
        coll(nak_edge, tNAKE, nak_eall, tNAKEA, f"cc{l}c")
        coll(nav_edge, tNAVE, nav_eall, tNAVEA, f"cc{l}d")
        def load_gathered_kv():
            for r in range(4):
                P.op("sp", DMA(KTC[:, r * 2048:(r + 1) * 2048], ktc_all.ap()[r * 128:(r + 1) * 128, :]), reads=[tKTCA], pwrites=[tKTC], dma="ktc")
            for r in range(4):
                P.op("sp", DMA(VC[:, r * 16:(r + 1) * 16, :], vc_all.ap()[r * 2048:(r + 1) * 2048, :].rearrange("(k p) c -> p k c", p=128)),
                     reads=[tVCA], pwrites=[tVC], dma="vc")

        if STOP[0] == 'ex':
            P.barrier()
            return nc, P
        load_ln(l, 0)
        cur_cls = [None]
        for st in range(5):
            tiles = list(range(4 * st, 4 * st + 4)) if st < 4 else [16, 17]
            if l == nlayers - 1 and st == 4:
                continue
            n = len(tiles)
            ctx = st == 4
            load_bc(l, 0, ctx)
            for ti, i in enumerate(tiles):
                P.op("sp", DMA(HT4[:, ti].rearrange("p a b -> p (a b)"), hts[i]), reads=[tHTS[i]], writes=[tHT[ti]], dma=f"ht{ti}")
            wq, wqT = load_wblock(win[l * 5 + 2])
            def proj_qa(ti, i):
                ps, pT = next_ps(0, 3)
                for kc in range(8):
                    P.op("pe", MM(ps[:, 0:384], HT4[:, ti, kc, :], wq[:, kc, 0:384], kc == 0, kc == 7), reads=[tHT[ti], wqT],
                         writes=[pT] if kc == 0 else (), pwrites=() if kc == 0 else [pT])
                P.op("act", ACT(KQB[:, 0:384], ps[:, 0:384], AF.Identity, scale=0.125), reads=[pT], writes=[tKQB])

                def evac_qa(kc, pap, ppT, ti=ti):
                    P.op("dve", CP(QTA[:, kc, ti * 128:(ti + 1) * 128], pap), reads=[ppT], pwrites=[tQTA])
                transpose_tile(KQB, [tKQB], 3, None, None, evac_qa)
            for ta in range(0, n, 2):
                lists = []
                for ti in range(ta, min(n, ta + 2)):
                    CUR[0] = ti % 2
                    ILV[0] = True
                    P.record_begin()
                    proj_qa(ti, tiles[ti])
                    lists.append(P.record_end())
                CUR[0] = 0
                ILV[0] = False
                P.replay_interleaved(lists)
            wq, wqT = load_wblock(win[l * 5 + 3])
            def proj_qc(ti, i):
                P.op("sp", DMA(ROPE[:, :], rope[i]), writes=[tROPE], dma=f"rope{CUR[0]}")
                ps, pT = next_ps(0, 3)
                for kc in range(8):
                    P.op("pe", MM(ps[:, 0:384], HT4[:, ti, kc, :], wq[:, kc, 0:384], kc == 0, kc == 7), reads=[tHT[ti], wqT],
                         writes=[pT] if kc == 0 else (), pwrites=() if kc == 0 else [pT])
                qk_norm_rope(ps[:, 0:384], [pT], 6, GQK[:, 0:64], KQB[:, 0:384], 0.125, [tKQB], True)

                def evac_qc(kc, pap, ppT, ti=ti):
                    P.op("dve", CP(QTC[:, kc, ti * 128:(ti + 1) * 128], pap), reads=[ppT], pwrites=[tQTC])
                transpose_tile(KQB, [tKQB], 3, None, None, evac_qc)
            for ta in range(0, n, 2):
                lists = []
                for ti in range(ta, min(n, ta + 2)):
                    CUR[0] = ti % 2
                    ILV[0] = True
                    P.record_begin()
                    proj_qc(ti, tiles[ti])
                    lists.append(P.record_end())
                CUR[0] = 0
                ILV[0] = False
                P.replay_interleaved(lists)
            wq, wqT = load_wblock(win[l * 5 + 4])
            def proj_zb(ti, i):
                ps, pT = next_ps(0, 3)
                for kc in range(8):
                    P.op("pe", MM(ps[:, :], HT4[:, ti, kc, :], wq[:, kc, :], kc == 0, kc == 7), reads=[tHT[ti], wqT],
                         writes=[pT] if kc == 0 else (), pwrites=() if kc == 0 else [pT])
                P.op("act", ACT(ZGp[:, :], ps[:, :], AF.Gelu_apprx_tanh), reads=[pT], writes=[tZGp])
                ln_stats(ZGp[:, 256:512], [tZGp], 256)
                P.op("dve", TS(KQB[:, 0:256], ZGp[:, 256:512], MV[:, 0:1], MV[:, 3:4], ALU.subtract, ALU.mult), reads=[tZGp, tMV], writes=[tKQB])
                ps2, p2T = (PSF[4 + CUR[0]], PSFT[4 + CUR[0]]) if ILV[0] else (PSF[5], PSFT[5])
                for g in range(4):
                    P.op("pe", MM(ps2[:, g * 64:(g + 1) * 64], WST[:, g * 128:(g + 1) * 128], KQB[:, g * 64:(g + 1) * 64], True, True),
                         reads=[tKQB, tLAYC], writes=[p2T] if g == 0 else (), pwrites=() if g == 0 else [p2T])
                P.op("dve", TT(TMPA[:, 0:256], ps2[:, 0:256], GSGU[:, :], ALU.mult), reads=[p2T, tLAYC], writes=[tTMPA])
                for g in range(4):
                    P.op("dve", TS(TMPA[:, g * 64:(g + 1) * 64], TMPA[:, g * 64:(g + 1) * 64], BS[:, l * 4 + g:l * 4 + g + 1], None, ALU.add),
                         reads=[tTMPA, tCONST], writes=[tTMPA])
                P.op("dve", TT(TMPA[:, 0:256], TMPA[:, 0:256], ZGp[:, 0:256], ALU.mult), reads=[tTMPA, tZGp], writes=[tTMPA])
                rms_rstd(TMPA[:, 0:256], [tTMPA], 1, 256, TMPB, tTMPB)
                P.op("dve", TS(MERGB[:, ti, 384:640], TMPA[:, 0:256], SS[:, 8:9], None, ALU.mult), reads=[tTMPA, tSS], pwrites=[tMERG[ti]])

            for ta in range(0, n, 2):
                lists = []
                for ti in range(ta, min(n, ta + 2)):
                    CUR[0] = ti % 2
                    ILV[0] = True
                    P.record_begin()
                    proj_zb(ti, tiles[ti])
                    lists.append(P.record_end())
                CUR[0] = 0
                ILV[0] = False
                P.replay_interleaved(lists)
            if st == 0:
                load_gathered_kv()
            if not ctx:
                lo, hi = max(0, 4 * st - 2), min(16, 4 * st + 6)
                s0 = lo - (4 * st - 2)
                for j in range(hi - lo):
                    P.op("sp", DMA(NAK[:, :, (s0 + j) * 128:(s0 + j + 1) * 128], nak_loc[lo + j].rearrange("p (c k) -> p c k", c=3)),
                         reads=[tNAKL], writes=[tNAK] if j == 0 else (), pwrites=() if j == 0 else [tNAK], dma="nak")
                P.op("sp", DMA(NAV[:, s0:s0 + hi - lo, :], nav_loc[lo:hi].rearrange("t p c -> p t c")), reads=[tNAVL], writes=[tNAV], dma="nav")
                if st in (0, 3):
                    for j in range(2):
                        def mk_halo_k(c3, st=st, j=j):
                            def halo_k(e):
                                rk = PIDV[0]
                                if st == 0:
                                    src_r, e0, sl = (rk + 3) % 4, 2, 0
                                else:
                                    src_r, e0, sl = (rk + 1) % 4, 0, 6
                                return e.dma_start(out=NAK[:, c3, (sl + j) * 128:(sl + j + 1) * 128],
                                                   in_=nak_eall.ap()[bass.ds(src_r * 512 + (e0 + j) * 128, 128), c3 * 128:(c3 + 1) * 128])
                            return halo_k

                        def halo_v(e, st=st, j=j):
                            rk = PIDV[0]
                            if st == 0:
                                src_r, e0, sl = (rk + 3) % 4, 2, 0
                            else:
                                src_r, e0, sl = (rk + 1) % 4, 0, 6
                            return e.dma_start(out=NAV[:, sl + j, :], in_=nav_eall.ap()[bass.ds(src_r * 512 + (e0 + j) * 128, 128), :])
                        for c3 in range(3):
                            P.op("pool", mk_halo_k(c3), reads=[tNAKEA], pwrites=[tNAK], dma="nak")
                        P.op("pool", halo_v, reads=[tNAVEA], pwrites=[tNAV], dma="nav")
            if STOP[0] == 'p2a':
                P.barrier()
                return nc, P
            for ti, i in enumerate(tiles):
                if not ctx:
                    cls = 0 if i == 0 else 1 if i == 1 else 3 if i == 14 else 4 if i == 15 else 2
                    if cur_cls[0] != (l, cls):
                        P.op("pool", DMA(NAB[:, :], nab[l * 5 + cls]), writes=[tNAB], dma="nab")
                        cur_cls[0] = (l, cls)
                oac = [PSF[3], PSF[4]]
                oacT = [PSFT[3], PSFT[4]]
                for h in range(6):
                    c, pb = h // 2, (h % 2) * 64
                    q = QTA[pb:pb + 64, c, ti * 128:(ti + 1) * 128]
                    psa, paT = next_ps(0, 3)
                    psb_, pbT = next_ps(0, 3)
                    blocks = []
                    if not ctx:
                        sl_b = [(ti + b, b) for b in range(5)]
                        if i == 0:
                            sl_b.append((ti + 5, 5))
                        if i == 15:
                            sl_b.append((ti - 1, 5))
                        sl_b += [(8, None), (9, None)]
                    else:
                        sl_b = [(8, None), (9, None)]
                    for k, (slot, b) in enumerate(sl_b):
                        if k < 4:
                            blocks.append((slot, psa[:, k * 128:(k + 1) * 128], paT, b))
                        else:
                            blocks.append((slot, psb_[:, (k - 4) * 128:(k - 3) * 128], pbT, b))
                    nB = max(0, len(sl_b) - 4)
                    seen = set()
                    for (slot, reg, rT, b) in blocks:
                        first = id(rT) not in seen
                        seen.add(id(rT))
                        P.op("pe", MM(reg, NAK[pb:pb + 64, c, slot * 128:(slot + 1) * 128], q, True, b is None), reads=[tNAK, tQTA],
                             writes=[rT] if first else (), pwrites=() if first else [rT])
                        if b is not None:
                            P.op("pe", MM(reg, IDB[:, :], NAB[:, (h * 6 + b) * 128:(h * 6 + b + 1) * 128], False, True), reads=[tNAB, tCONST], pwrites=[rT])
                    if not ctx:
                        P.op("act", ACT(PTN[:, 0:512], psa[:, :], AF.Exp), reads=[paT], writes=[tPTN, tPTNb])
                        P.op("act", ACT(PTN[:, 512:512 + nB * 128], psb_[:, 0:nB * 128], AF.Exp), reads=[pbT], pwrites=[tPTN])
                    else:
                        P.op("act", ACT(PTN[:, 0:256], psa[:, 0:256], AF.Exp), reads=[paT], writes=[tPTN, tPTNb])
                    ob, obT = oac[h // 4], oacT[h // 4]
                    oreg = ob[0:65, (h % 4) * 128:(h % 4 + 1) * 128]
                    nb_ = len(blocks)
                    for bi, (slot, reg, rT, b) in enumerate(blocks):
                        firstw = (h % 4 == 0 and bi == 0)
                        P.op("pe", MM(oreg, NAV[:, slot, h * 65:(h + 1) * 65], PTN[:, bi * 128:(bi + 1) * 128], bi == 0, bi == nb_ - 1),
                             reads=[tNAV, tPTN, tPTNb], writes=[obT] if firstw else (), pwrites=() if firstw else [obT])
                P.op("dve", CP(OT[0:65, 0:512], oac[0][0:65, :]), reads=[oacT[0]], writes=[tOT])
                P.op("dve", CP(OT[0:65, 512:768], oac[1][0:65, 0:256]), reads=[oacT[1]], pwrites=[tOT])
                tp, tpT = PSF[5], PSFT[5]
                for h in range(6):
                    P.op("pe", TR(tp[:, h * 65:(h + 1) * 65], OT[0:65, h * 128:(h + 1) * 128], IDF[0:65, 0:65]), reads=[tOT, tCONST],
                         writes=[tpT] if h == 0 else (), pwrites=() if h == 0 else [tpT])
                tpv = tp[:, 0:390].rearrange("p (h d) -> p h d", h=6)
                P.op("dve", (lambda o, i_: (lambda e: e.reciprocal(out=o, in_=i_)))(SS[:, 0:6], tpv[:, :, 64]), reads=[tpT], writes=[tSS])
                P.op("dve", TT(OATT[:, ti, :].rearrange("p (h d) -> p h d", h=6), tpv[:, :, 0:64],
                               SS[:, 0:6].unsqueeze(2).to_broadcast([128, 6, 64]), ALU.mult), reads=[tpT, tSS], writes=[tW4A, tZG])
                rms_rstd(OATT[:, ti, :], [tW4A, tZG], 1, 384, TMPA, tTMPA)
                P.op("dve", TS(MERGB[:, ti, 0:384], OATT[:, ti, :], SS[:, 8:9], None, ALU.mult), reads=[tW4A, tZG, tSS], pwrites=[tMERG[ti]])

            if STOP[0] == 'p2na':
                P.barrier()
                return nc, P
            nq = n * 128
            kts = list(range(66)) if not ctx else [64, 65]
            nk = len(kts)
            for c in range(3):
                qs = [QTC[g * 64:g * 64 + 64, c, 0:nq] for g in range(2)]
                obs = [(PSF[4 + g], PSFT[4 + g]) for g in range(2)]
                ptb = [[(PT[0], tPT[0]), (PT[1], tPT[1])], [(PTN[:, 0:512], tPTN), (PTN[:, 512:1024], tPTNb)]]

                def S(g, k):
                    ps, pT = PSF[g * 2 + k % 2], PSFT[g * 2 + k % 2]
                    kt = kts[k]
                    P.op("pe", MM(ps[:, 0:nq], KTC[g * 64:g * 64 + 64, kt * 128:(kt + 1) * 128], qs[g], True, True), reads=[tKTC, tQTC], writes=[pT])
                for k0 in range(min(2, nk)):
                    for g in range(2):
                        S(g, k0)
                for k in range(nk):
                    for g in range(2):
                        ps, pT = PSF[g * 2 + k % 2], PSFT[g * 2 + k % 2]
                        pbuf, pbT = ptb[g][k % 2]
                        P.op("act", ACT(pbuf[:, 0:nq], ps[:, 0:nq], AF.Exp), reads=[pT], writes=[pbT])
                    if k + 2 < nk:
                        for g in range(2):
                            S(g, k + 2)
                    kt = kts[k]
                    for g in range(2):
                        ob, obT = obs[g]
                        pbuf, pbT = ptb[g][k % 2]
                        P.op("pe", MM(ob[0:65, 0:nq], VC[:, kt, g * 65:(g + 1) * 65], pbuf[:, 0:nq], k == 0, k == nk - 1),
                             reads=[tVC, pbT], writes=[obT] if k == 0 else (), pwrites=() if k == 0 else [obT])
                for g in range(2):
                    h = c + 3 * g
                    ob, obT = obs[g]
                    P.op("dve", CP(OT[0:65, 0:nq], ob[0:65, 0:nq]), reads=[obT], writes=[tOT])
                    tp, tpT = PSF[g], PSFT[g]
                    for ti in range(n):
                        P.op("pe", TR(tp[:, ti * 65:(ti + 1) * 65], OT[0:65, ti * 128:(ti + 1) * 128], IDF[0:65, 0:65]), reads=[tOT, tCONST],
                             writes=[tpT] if ti == 0 else (), pwrites=() if ti == 0 else [tpT])
                    tpv = tp[:, 0:65 * n].rearrange("p (t d) -> p t d", t=n)
                    P.op("dve", (lambda o, i_: (lambda e: e.reciprocal(out=o, in_=i_)))(SS[:, 0:n], tpv[:, :, 64]), reads=[tpT], writes=[tSS])
                    P.op("dve", TT(OATT[:, 0:n, h * 64:(h + 1) * 64], tpv[:, :, 0:64],
                                   SS[:, 0:n].unsqueeze(2).to_broadcast([128, n, 64]), ALU.mult), reads=[tpT, tSS], pwrites=[tW4A, tZG])
            for ti in range(n):
                rms_rstd(OATT[:, ti, :], [tW4A, tZG], 1, 384, TMPA, tTMPA)
                P.op("dve", TS(MERGB[:, ti, 640:1024], OATT[:, ti, :], SS[:, 8:9], None, ALU.mult), reads=[tW4A, tZG, tSS], pwrites=[tMERG[ti]])

            if STOP[0] == 'p2gqa':
                P.barrier()
                return nc, P
            for ti in range(n):
                def evac_m(kc, pap, ppT, ti=ti):
                    P.op("dve", TS(HT4[:, ti, kc, :], pap, GOUT[:, l * 8 + kc:l * 8 + kc + 1], None, ALU.mult), reads=[ppT, tCONST], pwrites=[tHT[ti]])
                transpose_tile(MERGB[:, ti, :], [tMERG[ti]], 8, None, None, evac_m)
            wo0, wo0T = load_wblock(wo[l * 2 + 0])
            wo1, wo1T = load_wblock(wo[l * 2 + 1])
            for ti, i in enumerate(tiles):
                for nb, (wb_, wbT_) in enumerate(((wo0, wo0T), (wo1, wo1T))):
                    ps, pT = PSF[3 + nb], PSFT[3 + nb]
                    for kc in range(8):
                        P.op("pe", MM(ps[:, :], HT4[:, ti, kc, :], wb_[:, kc, :], kc == 0, kc == 7), reads=[tHT[ti], wbT_],
                             writes=[pT] if kc == 0 else (), pwrites=() if kc == 0 else [pT])
                deepnorm(i, l, lambda nb: PSF[3 + nb][:, :], [PSFT[3], PSFT[4]])
        P.barrier()
        if STOP[0] == 'p2':
            return nc, P

        load_ln(l, 1)
        sts = [list(range(4 * st, 4 * st + 4)) for st in range(4)]
        if l < nlayers - 1:
            sts.append([16, 17])
        HTB = [(HT4, tHT), (HT4b, tHTb)]

        def p3_ln(i, j):
            w2, w2T, mv, mvT, st6, st6T = W2P[j], tW2P[j], MVP[j], tMVP[j], ST6P[j], tST6P[j]
            for ci in range(2):
                P.op("dve", (lambda o, i_: (lambda e: e.bn_stats(out=o, in_=i_)))(st6[:, ci * 6:(ci + 1) * 6], X[:, i, ci * 512:(ci + 1) * 512]),
                     reads=[XT[i]], writes=[st6T] if ci == 0 else (), pwrites=() if ci == 0 else [st6T])
            P.op("dve", (lambda o, i_: (lambda e: e.bn_aggr(out=o, in_=i_)))(mv[:, 0:2], st6[:, 0:12].rearrange('p (n s) -> p n s', s=6)), reads=[st6T], writes=[mvT])
            P.op("act", ACT(mv[:, 2:3], mv[:, 1:2], AF.Ln, bias=EPS), reads=[mvT], writes=[mvT])
            P.op("act", ACT(mv[:, 3:4], mv[:, 2:3], AF.Exp, scale=-0.5), reads=[mvT], writes=[mvT])
            P.op("dve", TS(w2[:, :], X[:, i, :], mv[:, 0:1], mv[:, 3:4], ALU.subtract, ALU.mult), reads=[XT[i], mvT], writes=[w2T])

        def p3_tr(i, j, hb, ti):
            hbuf, hT = hb
            s_ = 1 if i >= 16 else 0

            def evac(kc, pap, pT):
                sc = MODF[:, l, 24 + kc, s_:s_ + 1]
                sh = MODF[:, l, 16 + kc, s_:s_ + 1]
                P.op("dve", TS(hbuf[:, ti, kc, :], pap, sc, sh, ALU.mult, ALU.add), reads=[pT, tMODF], pwrites=[hT[ti]])
            transpose_tile(W2P[j], [tW2P[j]], 8, None, None, evac)

        def p3_dn(i, ti):
            deepnorm_from(i, Y4[:, ti, :], [tY4[ti]])
            if l == nlayers - 1 and i < 16:
                P.op("sp", DMA(y_out[i], X[:, i, :]), reads=[XT[i]], pwrites=[tY], dma="yout")

        for ti, i in enumerate(sts[0]):
            p3_ln(i, ti % 2)
            p3_tr(i, ti % 2, HTB[0], ti)
        for si, tiles in enumerate(sts):
            n = len(tiles)
            ctx = tiles[0] >= 16
            hbuf, hT = HTB[si % 2]
            prev = sts[si - 1] if si > 0 else []
            nxt = sts[si + 1] if si + 1 < len(sts) else []
            load_bc(l, 1, ctx)
            for nb in range(11):
                wb_, wbT_ = load_wblock(wffi[l * 11 + nb])
                for ti, i in enumerate(tiles):
                    ps, pT = next_ps(0, 3)
                    for kc in range(8):
                        P.op("pe", MM(ps[:, :], hbuf[:, ti, kc, :], wb_[:, kc, :], kc == 0, kc == 7), reads=[hT[ti], wbT_],
                             writes=[pT] if kc == 0 else (), pwrites=() if kc == 0 else [pT])
                    P.op("act", ACT(TMPA[:, 0:256], ps[:, 0:256], AF.Silu), reads=[pT], writes=[tTMPA])
                    P.op("dve", TT(GTOK4[:, ti, nb * 256:(nb + 1) * 256], TMPA[:, 0:256], ps[:, 256:512], ALU.mult), reads=[tTMPA, pT], pwrites=[tG4[ti]])
                if nb % 2 == 0 and nb // 2 < len(prev):
                    p3_dn(prev[nb // 2], nb // 2)
                if nb % 2 == 1 and nb // 2 < len(nxt):
                    p3_ln(nxt[nb // 2], (nb // 2) % 2)
                if nb >= 3 and nb % 2 == 1 and (nb - 3) // 2 < len(nxt):
                    t2 = (nb - 3) // 2
                    p3_tr(nxt[t2], t2 % 2, HTB[(si + 1) % 2], t2)
            for t2 in range(len(nxt)):
                if 3 + 2 * t2 > 10:
                    p3_tr(nxt[t2], t2 % 2, HTB[(si + 1) % 2], t2)
            for ti, i in enumerate(tiles):
                GT = GTTa if ti < 2 else GTTb
                for grp in range(3):
                    nch = 8 if grp < 2 else 6

                    def evac_gg(kc, pap, ppT, grp=grp, ti=ti, GT=GT):
                        kk = grp * 8 + kc
                        P.op("dve", CP(GT[:, ti % 2, kk, :], pap), reads=[ppT], pwrites=[tGTT4[ti]])
                    transpose_tile(GTOK4[:, ti, grp * 1024:grp * 1024 + nch * 128], [tG4[ti]], nch, None, None, evac_gg)
            for nb2 in range(2):
                for kg in range(3):
                    nch = 8 if kg < 2 else 6
                    wb_, wbT_ = load_wblock(wffo[l * 6 + nb2 * 3 + kg])
                    for ti, i in enumerate(tiles):
                        GT = GTTa if ti < 2 else GTTb
                        ps, pT = PSF[ti], PSFT[ti]
                        for kcl in range(nch):
                            kc = kg * 8 + kcl
                            P.op("pe", MM(ps[:, :], GT[:, ti % 2, kc, :], wb_[:, kcl, :], kc == 0, kc == 21), reads=[tGTT4[ti], wbT_],
                                 writes=[pT] if kc == 0 else (), pwrites=() if kc == 0 else [pT])
                for ti, i in enumerate(tiles):
                    P.op("dve", TT(Y4[:, ti, nb2 * 512:(nb2 + 1) * 512], PSF[ti][:, :], BC[:, 0, nb2 * 512:(nb2 + 1) * 512], ALU.mult),
                         reads=[PSFT[ti], tBC[0]], pwrites=[tY4[ti]])
        for ti, i in enumerate(sts[-1]):
            p3_dn(i, ti)
        P.barrier()

    P.op("sp", None, reads=[tY])
    return nc, P


def finish(nc, P):
    from contextlib import ExitStack
    keys = list(ENGS) + ["dma:" + k for k in P.dma_cnt]
    with ExitStack() as es:
        sems = {k: es.enter_context(nc.semaphore(k.replace(":", "_"))) for k in keys}
        block = es.enter_context(nc.Block())
        P.emit(nc, block, sems)
    return nc


def _blocks(w, nblk, width=512):
    K = w.shape[0]
    kc = K // 128
    return np.ascontiguousarray(w.reshape(kc, 128, nblk, width).transpose(2, 1, 0, 3)).reshape(nblk, 128, kc * width)


def prep_shared(inp, NL):
    L = 4
    f = np.float32
    w_mod, b_mod, w_in = inp["w_mod"], inp["b_mod"], inp["w_in"]
    sh = {}
    sh["wmod"] = np.concatenate([_blocks(w_mod[l], 12) for l in range(NL)], 0)
    bases = [0, 1024, 3072, 4096]
    bt = np.zeros((L, 128, 32), f)
    for l in range(L):
        for j, b0 in enumerate(bases):
            bt[l, :, j * 8:(j + 1) * 8] = b_mod[l, b0:b0 + 1024].reshape(8, 128).T
    sh["bmodT"] = bt
    sh["bmodg"] = np.ascontiguousarray(np.stack([np.stack([b_mod[l, 2048:3072], b_mod[l, 5120:6144]]) for l in range(L)]).reshape(L * 2, 1024))
    perm = np.concatenate([np.arange(h * 64, (h + 1) * 64) for h in (0, 3, 1, 4, 2, 5)])
    wl = []
    for l in range(NL):
        w = w_in[l]
        qa, ka, va, zb, qc, kc_, vc = w[:, 0:384], w[:, 384:768], w[:, 768:1152], w[:, 1152:1664], w[:, 1664:2048], w[:, 2048:2176], w[:, 2176:2304]
        z128 = np.zeros((1024, 128), f)
        blks = [np.concatenate([va, vc], 1), np.concatenate([ka, kc_], 1), np.concatenate([qa, z128], 1),
                np.concatenate([qc[:, perm], z128], 1), zb]
        wl.append(_blocks(np.concatenate(blks, 1), 5))
    sh["win"] = np.concatenate(wl, 0)
    sh["wo"] = np.concatenate([_blocks(inp["w_o"][l], 2) for l in range(NL)], 0)
    wl = []
    for l in range(NL):
        w = inp["w_ffn_in"][l]
        a, b = w[:, :2816].reshape(1024, 11, 256), w[:, 2816:].reshape(1024, 11, 256)
        wl.append(_blocks(np.concatenate([a, b], 2).reshape(1024, 11 * 512), 11))
    sh["wffi"] = np.concatenate(wl, 0)
    wl = []
    for l in range(NL):
        w = np.zeros((3072, 1024), f)
        w[:2816] = inp["w_ffn_out"][l]
        wl.append(np.ascontiguousarray(w.reshape(3, 8, 128, 2, 512).transpose(3, 0, 2, 1, 4)).reshape(6, 128, 4096))
    sh["wffo"] = np.concatenate(wl, 0)
    sh["lnp"] = np.ascontiguousarray(np.stack([np.stack([inp["ln1_g"][l], inp["ln1_b"][l], inp["ln2_g"][l], inp["ln2_b"][l]]) for l in range(L)]).reshape(L * 4, 1024))
    sh["gout"] = np.ascontiguousarray(np.concatenate([inp["g_out"][l].reshape(8, 128).T for l in range(L)], 1))
    sh["gsgu"] = np.ascontiguousarray(inp["g_sgu"])
    sh["gqk"] = np.ascontiguousarray(np.concatenate([inp["g_q"], inp["g_k"]], 1))
    sh["wsT"] = np.ascontiguousarray(np.stack([inp["w_s"][l].transpose(2, 0, 1).reshape(128, 512) for l in range(L)]))
    sh["bsd"] = np.ascontiguousarray(np.concatenate([inp["b_s"][l].T for l in range(L)], 1))
    sh["identd"] = np.eye(128, dtype=f)
    return sh


def rope_table(tok):
    f = np.float32
    row = (tok // 64).astype(f)
    col = (tok % 64).astype(f)
    inv = (1.0 / (np.float32(10000.0) ** (np.arange(16, dtype=f) / np.float32(16)))).astype(f)
    ar = row[:, None] * inv[None, :]
    ac = col[:, None] * inv[None, :]
    return np.concatenate([np.cos(ar), np.cos(ac), np.sin(ar), np.sin(ac)], 1).astype(f)


def nab_tables(rpb, qi, L):
    out = np.full((L, 5, 128, 6, 6, 128), np.float32(-30000.0), np.float32)
    p = np.arange(128)
    for ci, i in enumerate((0, 1, 2, 14, 15)):
        G = 16 * qi + i
        r = 2 * G + p // 64
        c = p % 64
        rs = np.clip(r - 4, 0, 120)
        cs = np.clip(c - 8, 0, 48)
        for b in range(6):
            if b < 5:
                Gk = G - 2 + b
            elif i == 0:
                Gk = G + 3
            elif i == 15:
                Gk = G - 3
            else:
                continue
            if Gk < 0 or Gk > 63:
                continue
            kr = 2 * Gk + p // 64
            kc = p % 64
            ok = (kr[:, None] >= rs[None, :]) & (kr[:, None] < rs[None, :] + 8) & (kc[:, None] >= cs[None, :]) & (kc[:, None] < cs[None, :] + 16)
            dr = np.clip(kr[:, None] - r[None, :] + 7, 0, 14)
            dc = np.clip(kc[:, None] - c[None, :] + 15, 0, 30)
            for l in range(L):
                vals = rpb[l][:, dr, dc]
                out[l, ci, :, :, b, :] = np.where(ok[None], vals, np.float32(-30000.0)).transpose(1, 0, 2)
    return out.reshape(L * 5, 128, 4608)


_CACHE = {}


def kernel(**inp):
    inp = {k: np.asarray(v, dtype=np.float32) for k, v in inp.items()}
    if "nc" not in _CACHE:
        nc, P = build(NLAYERS_BUILD)
        _CACHE["nc"] = finish(nc, P)
    nc = _CACHE["nc"]
    import time as _t
    t0 = _t.time()
    NL = NLAYERS_BUILD
    sh = prep_shared(inp, NL)
    print("prep shared", _t.time() - t0, flush=True)
    in_maps = []
    for core in range(8):
        b, qi = core // 4, core % 4
        m = dict(sh)
        xs = inp["x"][b, 2048 * qi:2048 * (qi + 1)].reshape(16, 128, 1024)
        m["x_in"] = np.ascontiguousarray(np.concatenate([xs, inp["ctx"][b].reshape(2, 128, 1024)], 0))
        cv = np.empty((128, 16), np.float32)
        cv[:, 0::2] = inp["c"][b].reshape(8, 128).T
        cv[:, 1::2] = inp["c_ctx"].reshape(8, 128).T
        m["cvec"] = cv
        rp = np.empty((NT, 128, 64), np.float32)
        for i in range(16):
            rp[i] = rope_table(2048 * qi + 128 * i + np.arange(128))
        rp[16:, :, 0:32] = 1.0
        rp[16:, :, 32:64] = 0.0
        m["rope"] = rp
        m["nab"] = nab_tables(inp["rpb"], qi, NL)
        in_maps.append(m)
    print("prep all", _t.time() - t0, flush=True)
    res = run_bass_kernel_spmd(nc, in_maps, core_ids=list(range(8)))
    print("run done", _t.time() - t0, flush=True)
    out = np.empty((2, 8192, 1024), np.float32)
    for core in range(8):
        b, qi = core // 4, core % 4
        out[b, 2048 * qi:2048 * (qi + 1)] = np.asarray(res.results[core]["y_out"]).reshape(2048, 1024)
    return out
```
